# Optimizing a Trainium2 kernel written in Bass

```python
import math
import jax, jax.numpy as jnp
from jax import lax
import numpy as np

D_MODEL = 1024
BATCH = 2
SEQ = 8192
DEPTH = 1

M_HEADS = 4
M_HEAD_DIM = D_MODEL // M_HEADS
M_WIDTH = M_HEADS * M_HEAD_DIM
M_CONV = 4
M_CHUNK = 128
R_HEAD_DIM = 64
R_WIDTH = D_MODEL
R_HEADS = R_WIDTH // R_HEAD_DIM
R_DECAY_RANK = 64
R_AAA_RANK = 64
R_GATE_RANK = 128
R_GN_EPS = 64e-5
D_FF = 128 * ((8 * D_MODEL // 3 + 127) // 128)
LN_EPS = 1e-5
ALPHA = (2 * DEPTH) ** 0.25
BETA = (8 * DEPTH) ** -0.25
R_COLS = 3 * R_WIDTH + R_DECAY_RANK + R_AAA_RANK + R_GATE_RANK
W_IN_SPLITS = (M_WIDTH, M_WIDTH, M_WIDTH, M_WIDTH, M_HEADS, M_HEADS, R_COLS, D_MODEL, D_MODEL)
W_IN_COLS = 4 * M_WIDTH + 2 * M_HEADS + R_COLS + 2 * D_MODEL
R_SPLITS = (R_WIDTH, R_WIDTH, R_WIDTH, R_DECAY_RANK, R_AAA_RANK, R_GATE_RANK)

kernel_name = 'hybrid_mlstm_rwkv7_macaron_deepnorm'


def _offsets(sizes):
    return [int(o) for o in np.cumsum(np.asarray(sizes))[:-1]]


def layer_norm(x, g, b, eps=LN_EPS):
    xf = x.astype(jnp.float32)
    mu = jnp.mean(xf, axis=-1, keepdims=True)
    var = jnp.mean(jnp.square(xf - mu), axis=-1, keepdims=True)
    return ((xf - mu) * lax.rsqrt(var + eps) * g + b).astype(x.dtype)


def swiglu(x, w_gate, w_up, w_down):
    return (jax.nn.silu(x @ w_gate) * (x @ w_up)) @ w_down


def causal_dwconv(x, w, b):
    k_w = w.shape[0]
    t = x.shape[1]
    xp = jnp.pad(x, ((0, 0), (k_w - 1, 0), (0, 0)))
    out = b
    for j in range(k_w):
        out = out + xp[:, j:j + t] * w[j]
    return out


def mlstm_chunkwise(q, k, v, i_pre, log_f):
    bsz, nh, t, dh = q.shape
    nc = t // M_CHUNK
    q = q.reshape(bsz, nh, nc, M_CHUNK, dh) * dh ** -0.5
    k = k.reshape(bsz, nh, nc, M_CHUNK, dh)
    v = v.reshape(bsz, nh, nc, M_CHUNK, dh)
    i_pre = i_pre.reshape(bsz, nh, nc, M_CHUNK)
    b = jnp.cumsum(log_f.reshape(bsz, nh, nc, M_CHUNK), axis=-1)
    g = b[..., -1]
    a = g[..., None] - b + i_pre
    m_loc = jnp.max(a, axis=-1)
    wa = jnp.exp(a - m_loc[..., None])
    c_loc = jnp.einsum('bhcs,bhcsk,bhcsv->bhckv', wa, k, v)
    n_loc = jnp.einsum('bhcs,bhcsk->bhck', wa, k)

    def step(carry, xs):
        c_st, n_st, m_st = carry
        g_c, m_c, cc, nn = xs
        m_new = jnp.maximum(g_c + m_st, m_c)
        s_old = jnp.exp(g_c + m_st - m_new)
        s_new = jnp.exp(m_c - m_new)
        c_next = s_old[..., None, None] * c_st + s_new[..., None, None] * cc
        n_next = s_old[..., None] * n_st + s_new[..., None] * nn
        return (c_next, n_next, m_new), (c_st, n_st, m_st)

    init = (jnp.zeros((bsz, nh, dh, dh), q.dtype), jnp.zeros((bsz, nh, dh), q.dtype),
            jnp.zeros((bsz, nh), q.dtype))
    xs = (jnp.moveaxis(g, 2, 0), jnp.moveaxis(m_loc, 2, 0), jnp.moveaxis(c_loc, 2, 0),
          jnp.moveaxis(n_loc, 2, 0))
    _, (c_prev, n_prev, m_prev) = lax.scan(step, init, xs)
    c_prev = jnp.moveaxis(c_prev, 0, 2)
    n_prev = jnp.moveaxis(n_prev, 0, 2)
    m_prev = jnp.moveaxis(m_prev, 0, 2)

    causal = jnp.tril(jnp.ones((M_CHUNK, M_CHUNK), dtype=bool))
    d_log = b[..., :, None] - b[..., None, :] + i_pre[..., None, :]
    d_log = jnp.where(causal, d_log, -jnp.inf)
    inter = b + m_prev[..., None]
    m_t = jnp.maximum(jnp.max(d_log, axis=-1), inter)
    s = jnp.einsum('bhctd,bhcsd->bhcts', q, k) * jnp.exp(d_log - m_t[..., None])
    s_inter = jnp.exp(inter - m_t)
    num = (jnp.einsum('bhcts,bhcsv->bhctv', s, v)
           + s_inter[..., None] * jnp.einsum('bhctk,bhckv->bhctv', q, c_prev))
    den = jnp.sum(s, axis=-1) + s_inter * jnp.einsum('bhctk,bhck->bhct', q, n_prev)
    h = num / jnp.maximum(jnp.abs(den), jnp.exp(-m_t))[..., None]
    return h.reshape(bsz, nh, t, dh)


def rwkv7_recurrence(r, w, k, v, a, b):
    bsz, _, nh, n = r.shape

    def step(st, xs):
        r_t, w_t, k_t, v_t, a_t, b_t = xs
        sa = jnp.einsum('bhvk,bhk->bhv', st, a_t)
        st = st * w_t[:, :, None, :] + sa[..., None] * b_t[:, :, None, :] + v_t[..., None] * k_t[:, :, None, :]
        return st, jnp.einsum('bhvk,bhk->bhv', st, r_t)

    xs = tuple(jnp.moveaxis(z, 1, 0) for z in (r, w, k, v, a, b))
    _, y = lax.scan(step, jnp.zeros((bsz, nh, n, n), r.dtype), xs)
    return jnp.moveaxis(y, 0, 1)


def token_mixer(h, w_in, m_conv_w, m_conv_b, m_i_bias, m_f_bias, m_norm_g, r_mu, r_w0, r_w2,
                r_a0, r_a2, r_g2, r_k_k, r_k_a, r_r_k, r_gn_g, r_gn_b, w_branch_a, w_branch_b, w_out):
    bsz, t, _ = h.shape
    f32 = jnp.float32
    p = h @ w_in
    mq, mk, mv, mo, mi, mf, rc, gate_a, gate_b = jnp.split(p, _offsets(W_IN_SPLITS), axis=-1)

    qk = jax.nn.silu(causal_dwconv(jnp.concatenate([mq, mk], axis=-1), m_conv_w, m_conv_b))
    mq, mk = jnp.split(qk, 2, axis=-1)

    def mheads(z):
        return z.reshape(bsz, t, M_HEADS, M_HEAD_DIM).transpose(0, 2, 1, 3).astype(f32)

    i_pre = (mi + m_i_bias).astype(f32).transpose(0, 2, 1)
    log_f = jax.nn.log_sigmoid((mf + m_f_bias).astype(f32)).transpose(0, 2, 1)
    hm = mlstm_chunkwise(mheads(mq), mheads(mk), mheads(mv), i_pre, log_f)
    mu = jnp.mean(hm, axis=-1, keepdims=True)
    var = jnp.mean(jnp.square(hm - mu), axis=-1, keepdims=True)
    hm = ((hm - mu) * lax.rsqrt(var + LN_EPS)).transpose(0, 2, 1, 3).reshape(bsz, t, M_WIDTH).astype(h.dtype)
    hm = hm * m_norm_g * jax.nn.sigmoid(mo)
    y_a = hm @ w_branch_a

    rc_prev = jnp.pad(rc, ((0, 0), (1, 0), (0, 0)))[:, :-1]
    rc = rc + (rc_prev - rc) * r_mu
    rr, rk, rv, wd, ad, gd = jnp.split(rc, _offsets(R_SPLITS), axis=-1)
    w_log = -jax.nn.softplus(-(r_w0 + jnp.tanh(wd) @ r_w2)) - 0.5
    decay = jnp.exp(-jnp.exp(w_log.astype(f32)))
    a = jax.nn.sigmoid(r_a0 + ad @ r_a2)
    g = jax.nn.sigmoid(gd) @ r_g2

    def rheads(z):
        return z.reshape(bsz, t, R_HEADS, R_HEAD_DIM).astype(f32)

    kk = rheads(rk * r_k_k)
    kk = kk / jnp.maximum(jnp.sqrt(jnp.sum(jnp.square(kk), axis=-1, keepdims=True)), 1e-12)
    rk = rk * (1 + (a - 1) * r_k_a)
    r_h, k_h, v_h, a_h = rheads(rr), rheads(rk), rheads(rv), rheads(a)
    yr = rwkv7_recurrence(r_h, rheads(decay), k_h, v_h, -kk, kk * a_h)
    mu = jnp.mean(yr, axis=-1, keepdims=True)
    var = jnp.mean(jnp.square(yr - mu), axis=-1, keepdims=True)
    yr = ((yr - mu) * lax.rsqrt(var + R_GN_EPS)).reshape(bsz, t, R_WIDTH) * r_gn_g + r_gn_b
    bonus = jnp.sum(r_h * k_h * r_r_k, axis=-1, keepdims=True) * v_h
    yr = (yr + bonus.reshape(bsz, t, R_WIDTH)).astype(h.dtype) * g
    y_b = yr @ w_branch_b

    merged = jax.nn.sigmoid(gate_a) * y_a + jax.nn.sigmoid(gate_b) * y_b
    return merged @ w_out


def setup_inputs(seed: int = 0) -> dict:
    key = jax.random.key(seed)
    ks = iter(jax.random.split(key, 48))
    d = D_MODEL

    def nrm(shape, std):
        return jax.random.normal(next(ks), (DEPTH,) + shape, jnp.float32) * std

    x = jax.random.normal(next(ks), (BATCH, SEQ, d), jnp.float32)
    ffn1_w_gate = nrm((d, D_FF), d ** -0.5)
    ffn1_w_up = nrm((d, D_FF), BETA * d ** -0.5)
    ffn1_w_down = nrm((D_FF, d), BETA * D_FF ** -0.5)
    ln1_g = 1.0 + nrm((d,), 0.02)
    ln1_b = nrm((d,), 0.02)
    col_scale = np.ones((W_IN_COLS,), np.float32)
    mv0 = 2 * M_WIDTH
    col_scale[mv0:mv0 + M_WIDTH] = BETA
    rv0 = 4 * M_WIDTH + 2 * M_HEADS + 2 * R_WIDTH
    col_scale[rv0:rv0 + R_WIDTH] = BETA
    w_in = nrm((d, W_IN_COLS), d ** -0.5) * jnp.asarray(col_scale)
    m_conv_w = nrm((M_CONV, 2 * M_WIDTH), M_CONV ** -0.5)
    m_conv_b = nrm((2 * M_WIDTH,), 0.02)
    m_i_bias = nrm((M_HEADS,), 0.1)
    m_f_bias = jnp.linspace(3.0, 6.0, M_HEADS, dtype=jnp.float32) + nrm((M_HEADS,), 0.1)
    m_norm_g = 1.0 + nrm((M_WIDTH,), 0.02)
    r_mu = jax.random.uniform(next(ks), (DEPTH, R_COLS), jnp.float32, 0.2, 0.8)
    ramp = jnp.arange(R_WIDTH, dtype=jnp.float32) / (R_WIDTH - 1)
    r_w0 = -6.0 + 5.0 * ramp ** 0.85 + nrm((R_WIDTH,), 0.1)
    r_w2 = nrm((R_DECAY_RANK, R_WIDTH), 0.1 * R_DECAY_RANK ** -0.5)
    r_a0 = nrm((R_WIDTH,), 0.1)
    r_a2 = nrm((R_AAA_RANK, R_WIDTH), 0.1 * R_AAA_RANK ** -0.5)
    r_g2 = nrm((R_GATE_RANK, R_WIDTH), R_GATE_RANK ** -0.5)
    r_k_k = 0.85 + nrm((R_WIDTH,), 0.02)
    r_k_a = 1.0 + nrm((R_WIDTH,), 0.02)
    r_r_k = nrm((R_HEADS, R_HEAD_DIM), 0.1)
    r_gn_g = 1.0 + nrm((R_WIDTH,), 0.02)
    r_gn_b = nrm((R_WIDTH,), 0.02)
    w_branch_a = nrm((M_WIDTH, d), BETA * M_WIDTH ** -0.5)
    w_branch_b = nrm((R_WIDTH, d), BETA * R_WIDTH ** -0.5)
    w_out = nrm((d, d), BETA * d ** -0.5)
    ln2_g = 1.0 + nrm((d,), 0.02)
    ln2_b = nrm((d,), 0.02)
    ffn2_w_gate = nrm((d, D_FF), d ** -0.5)
    ffn2_w_up = nrm((d, D_FF), BETA * d ** -0.5)
    ffn2_w_down = nrm((D_FF, d), BETA * D_FF ** -0.5)
    ln3_g = 1.0 + nrm((d,), 0.02)
    ln3_b = nrm((d,), 0.02)
    return {'x': x, 'ffn1_w_gate': ffn1_w_gate, 'ffn1_w_up': ffn1_w_up, 'ffn1_w_down': ffn1_w_down,
            'ln1_g': ln1_g, 'ln1_b': ln1_b, 'w_in': w_in, 'm_conv_w': m_conv_w, 'm_conv_b': m_conv_b,
            'm_i_bias': m_i_bias, 'm_f_bias': m_f_bias, 'm_norm_g': m_norm_g, 'r_mu': r_mu, 'r_w0': r_w0,
            'r_w2': r_w2, 'r_a0': r_a0, 'r_a2': r_a2, 'r_g2': r_g2, 'r_k_k': r_k_k, 'r_k_a': r_k_a,
            'r_r_k': r_r_k, 'r_gn_g': r_gn_g, 'r_gn_b': r_gn_b, 'w_branch_a': w_branch_a,
            'w_branch_b': w_branch_b, 'w_out': w_out, 'ln2_g': ln2_g, 'ln2_b': ln2_b,
            'ffn2_w_gate': ffn2_w_gate, 'ffn2_w_up': ffn2_w_up, 'ffn2_w_down': ffn2_w_down,
            'ln3_g': ln3_g, 'ln3_b': ln3_b}


def reference(x, ffn1_w_gate, ffn1_w_up, ffn1_w_down, ln1_g, ln1_b, w_in, m_conv_w, m_conv_b,
              m_i_bias, m_f_bias, m_norm_g, r_mu, r_w0, r_w2, r_a0, r_a2, r_g2, r_k_k, r_k_a, r_r_k,
              r_gn_g, r_gn_b, w_branch_a, w_branch_b, w_out, ln2_g, ln2_b, ffn2_w_gate, ffn2_w_up,
              ffn2_w_down, ln3_g, ln3_b):
    for l in range(DEPTH):
        x = layer_norm(ALPHA * x + 0.5 * swiglu(x, ffn1_w_gate[l], ffn1_w_up[l], ffn1_w_down[l]),
                       ln1_g[l], ln1_b[l])
        mix = token_mixer(x, w_in[l], m_conv_w[l], m_conv_b[l], m_i_bias[l], m_f_bias[l], m_norm_g[l],
                          r_mu[l], r_w0[l], r_w2[l], r_a0[l], r_a2[l], r_g2[l], r_k_k[l], r_k_a[l],
                          r_r_k[l], r_gn_g[l], r_gn_b[l], w_branch_a[l], w_branch_b[l], w_out[l])
        x = layer_norm(ALPHA * x + mix, ln2_g[l], ln2_b[l])
        x = layer_norm(ALPHA * x + 0.5 * swiglu(x, ffn2_w_gate[l], ffn2_w_up[l], ffn2_w_down[l]),
                       ln3_g[l], ln3_b[l])
    return x
```

```python
import contextlib
import os
import numpy as np
import concourse.bass as bass
import concourse.mybir as mybir
from concourse.bass_utils import run_bass_kernel_spmd

F32 = mybir.dt.float32
BF16 = mybir.dt.bfloat16
AF = mybir.ActivationFunctionType
ALU = mybir.AluOpType

D = 1024
DFF = 2816
NJ = DFF // 128
SEQ = 8192
TB = 512
ALPHA = 2.0 ** 0.25
LN_EPS = 1e-5
GN_EPS = 64e-5
NCORES = 8
WCOLS = 9480
RC0 = 4104


class Res:
    __slots__ = ("w", "r", "excl")

    def __init__(self, excl=False):
        self.w = None
        self.r = {}
        self.excl = excl


class Sched:
    NDS = 40

    def __init__(self, nc, stack):
        self.nc = nc
        self.E = {"pe": nc.tensor, "act": nc.scalar, "dve": nc.vector, "pool": nc.gpsimd, "sp": nc.sync}
        self.q = {k: [] for k in self.E}
        self.cnt = {k: 0 for k in self.E}
        self.esem = {k: stack.enter_context(nc.semaphore("s_" + k)) for k in self.E}
        self.dsem = [stack.enter_context(nc.semaphore("d%d" % i)) for i in range(self.NDS)]
        self.dval = [0] * self.NDS
        self.dnext = {"sp": 0, "pool": 0}
        self.dpool = {"sp": list(range(0, 8)), "pool": list(range(8, self.NDS))}
        self.waited = {k: {} for k in self.E}

    def _deps(self, eng, reads, writes, extra=()):
        deps = {}

        def add(t):
            if t is None:
                return
            k = (t[0], t[1])
            if deps.get(k, 0) < t[2]:
                deps[k] = t[2]

        for r in reads:
            add(r.w)
        for w in writes:
            add(w.w)
            for k, v in w.r.items():
                add((k[0], k[1], v))
        for t in extra:
            add(t)
        waits = []
        for k, v in deps.items():
            if k[0] == "e" and k[1] == eng and eng == "pe":
                continue
            if self.waited[eng].get(k, 0) >= v:
                continue
            self.waited[eng][k] = v
            waits.append((k, v))
        return waits

    def _mark(self, tok, reads, writes):
        k = (tok[0], tok[1])
        for r in reads:
            if r.r.get(k, 0) < tok[2]:
                r.r[k] = tok[2]
        for w in writes:
            w.w = tok
            w.r = {}

    def op(self, eng, fn, reads=(), writes=()):
        ex = [r for r in reads if r.excl]
        if ex:
            writes = list(writes) + ex
        waits = self._deps(eng, reads, writes)
        self.cnt[eng] += 1
        tok = ("e", eng, self.cnt[eng])
        self.q[eng].append((waits, fn, None))
        self._mark(tok, reads, writes)
        return tok

    def dma(self, qeng, out, in_, reads=(), writes=()):
        pool = self.dpool[qeng]
        j = pool[self.dnext[qeng]]
        self.dnext[qeng] = (self.dnext[qeng] + 1) % len(pool)
        prev = ("d", j, self.dval[j]) if self.dval[j] else None
        waits = self._deps(qeng, reads, writes, extra=(prev,) if prev else ())
        self.dval[j] += 16
        tok = ("d", j, self.dval[j])
        self.q[qeng].append((waits, (out, in_), j))
        self._mark(tok, reads, writes)
        return tok

    def emit(self):
        nc = self.nc
        with nc.Block() as block:
            def run(kind):
                def body(eng):
                    for waits, fn, dj in self.q[kind]:
                        for k, v in waits:
                            sem = self.esem[k[1]] if k[0] == "e" else self.dsem[k[1]]
                            eng.wait_ge(sem, v)
                        if dj is None:
                            fn(eng).then_inc(self.esem[kind], 1)
                        else:
                            eng.dma_start(out=fn[0], in_=fn[1]).then_inc(self.dsem[dj], 16)
                return body
            block.tensor(run("pe"))
            block.scalar(run("act"))
            block.vector(run("dve"))
            block.gpsimd(run("pool"))
            block.sync(run("sp"))


class Ring:
    def __init__(self, items):
        self.items = items
        self.i = 0

    def next(self):
        it = self.items[self.i]
        self.i = (self.i + 1) % len(self.items)
        return it


PV = {}
_o = 0
for _n, _w in [("ln1_g", 8), ("ln1_b", 8), ("ln2_g", 8), ("ln2_b", 8), ("ln3_g", 8), ("ln3_b", 8),
               ("cw0", 16), ("cw1", 16), ("cw2", 16), ("cw3", 16), ("cb", 16), ("mng", 8),
               ("mu", 26), ("omu", 26), ("w0", 8), ("a0", 8), ("kk", 8), ("ka", 8), ("rrk", 8),
               ("gng", 8), ("gnb", 8)]:
    PV[_n] = _o
    _o += _w
NPV = _o
C_ID, C_OD, C_MUS, C_TRI, C_ONE, C_MLS, C_IST, C_BO, NCST = 0, 128, 256, 384, 512, 640, 768, 832, 960


def build(nblk):
    nc = bass.Bass("TRN2", target_bir_lowering=False)

    def din(name, shape):
        return nc.dram_tensor(name, list(shape), F32, kind="ExternalInput").ap()

    x_d = din("x", [SEQ, D])
    cst_d = din("cst", [128, NCST])
    pv_d = din("pv", [128, NPV])
    gb_d = din("gbias", [128, 8])
    rst_d = din("rst", [128, 512])
    w1g = din("ffn1_w_gate", [D, DFF]); w1u = din("ffn1_w_up", [D, DFF]); w1d = din("ffn1_w_down", [DFF, D])
    w2g = din("ffn2_w_gate", [D, DFF]); w2u = din("ffn2_w_up", [D, DFF]); w2d = din("ffn2_w_down", [DFF, D])
    win = din("w_in", [D, WCOLS])
    wa_d = din("w_branch_a", [D, D]); wb_d = din("w_branch_b", [D, D]); wo_d = din("w_out", [D, D])
    rw2_d = din("r_w2", [64, D]); ra2_d = din("r_a2", [64, D]); rg2_d = din("r_g2", [128, D])
    out_d = nc.dram_tensor("out", [SEQ, D], F32, kind="ExternalOutput").ap()

    with contextlib.ExitStack() as st:
        S = Sched(nc, st)
        _n = [0]

        def sb(shape, dt=F32):
            _n[0] += 1
            return st.enter_context(nc.sbuf_tensor("sb%d" % _n[0], list(shape), dt))

        def ring(n, shape, dt=F32):
            return Ring([(sb(shape, dt), Res()) for _ in range(n)])

        banks = [st.enter_context(nc.psum_tensor("ps%d" % i, [128, 512], F32)) for i in range(8)]
        psr = Ring([(banks[i], Res(True)) for i in range(7)])
        pyo_bank = (banks[7], Res(True))

        cst = sb([128, NCST]); cst_r = Res()
        cstb = sb([128, NCST], BF16); cstb_r = Res()
        pv = sb([128, NPV]); pv_r = Res()
        gbias = sb([128, 8]); gb_r = Res()
        S.dma("sp", cst[:, :], cst_d[:, :], writes=[cst_r])
        S.dma("sp", pv[:, :], pv_d[:, :], writes=[pv_r])
        S.dma("sp", gbias[:, :], gb_d[:, :], writes=[gb_r])
        S.op("act", lambda e: e.copy(cstb[:, :], cst[:, :]), reads=[cst_r], writes=[cstb_r])
        ident = cst[:, C_ID:C_ID + 128]
        identb = cstb[:, C_ID:C_ID + 128]
        onesdb = cstb[:, C_OD:C_OD + 128]
        tri = cst[:, C_TRI:C_TRI + 128]
        ones = cst[:, C_ONE:C_ONE + 128]
        istb = cstb[:, C_IST:C_IST + 64]
        bob = cstb[:, C_BO:C_BO + 128]
        rstb = sb([128, 512], BF16)
        S.dma("pool", rstb[:, :], rst_d[:, :], writes=[cstb_r])
        rst = rstb[:, :]
        epsln = sb([128, 1]); eps4 = sb([128, 1]); epsgn = sb([128, 1]); eps_r = Res()
        S.op("dve", lambda e: e.memset(epsln[:, :], LN_EPS), writes=[eps_r])
        S.op("dve", lambda e: e.memset(eps4[:, :], 4 * LN_EPS), writes=[eps_r])
        S.op("dve", lambda e: e.memset(epsgn[:, :], GN_EPS), writes=[eps_r])

        def pvc(name, i=0):
            c = PV[name] + i
            return pv[:, c:c + 1]

        rw2a2 = sb([128, D], BF16); rw_r = Res()
        rg2 = sb([128, D], BF16)
        S.dma("pool", rw2a2[0:64, :], rw2_d[:, :], writes=[rw_r])
        S.dma("pool", rw2a2[64:128, :], ra2_d[:, :], writes=[rw_r])
        S.dma("pool", rg2[:, :], rg2_d[:, :], writes=[rw_r])

        xT = sb([128, 8, TB]); xTb = sb([128, 8, TB], BF16); xT_r = [Res() for _ in range(8)]
        x1T = sb([128, 8, TB]); x1Tb = sb([128, 8, TB], BF16); x1_r = [Res() for _ in range(8)]
        x2T = xT; x2Tb = xTb; x2_r = xT_r
        x3T = xT; x3Tb = xTb; x3_r = xT_r
        aT = sb([128, NJ, TB], BF16); aT_r = [Res() for _ in range(NJ)]
        zT = sb([128, 8, TB]); zT_r = [Res() for _ in range(8)]
        xtok = ring(1, [128, D])
        otok = xtok
        wgu = ring(2, [128, 2, 8, 128], BF16)
        wdb = ring(2, [128, 512], BF16)
        wpj = ring(2, [128, 8, 128], BF16)
        tA = ring(3, [128, TB])
        tB = ring(2, [128, TB], BF16)
        mean_r = Res(); rstd_r = Res()
        out_r = Res()

        def load_xT(blk):
            for t in range(TB // 128):
                xt, xr = xtok.next()
                r0 = blk * TB + t * 128
                S.dma("sp", xt[:, :], x_d[r0:r0 + 128, :], writes=[xr])
                for half in range(2):
                    ps, pr = psr.next()
                    for q in range(4):
                        kc = half * 4 + q
                        S.op("pe", lambda e, ps=ps, xt=xt, kc=kc, q=q: e.matmul(
                            ps[:, q * 128:(q + 1) * 128], xt[:, kc * 128:(kc + 1) * 128], ident,
                            start=True, stop=True), reads=[xr, cst_r], writes=[pr])
                    psv = ps[:, :].rearrange("p (q n) -> p q n", q=4)
                    S.op("act", lambda e, psv=psv, half=half, t=t: e.copy(
                        xT[:, half * 4:half * 4 + 4, t * 128:(t + 1) * 128], psv),
                        reads=[pr], writes=xT_r[half * 4:half * 4 + 4])
                    S.op("dve", lambda e, psv=psv, half=half, t=t: e.tensor_copy(
                        xTb[:, half * 4:half * 4 + 4, t * 128:(t + 1) * 128], psv),
                        reads=[pr], writes=xT_r[half * 4:half * 4 + 4])

        def layer_norm(gname, bname, eps_t, outT, outTb, out_rs):
            psm, pmr = psr.next()
            pss, ssr = psr.next()
            for dc in range(8):
                tb, tr = tB.next()
                S.op("act", lambda e, tb=tb, dc=dc: e.copy(tb[:, :], zT[:, dc, :]), reads=[zT_r[dc]], writes=[tr])
                S.op("pe", lambda e, tb=tb, dc=dc: e.matmul(psm[:, :], onesdb, tb[:, :], start=(dc == 0), stop=(dc == 7)),
                     reads=[cstb_r, tr], writes=[pmr])
                tb2, tr2 = tB.next()
                S.op("act", lambda e, tb2=tb2, dc=dc: e.activation(tb2[:, :], zT[:, dc, :], AF.Square),
                     reads=[zT_r[dc]], writes=[tr2])
                S.op("pe", lambda e, tb2=tb2, dc=dc: e.matmul(pss[:, :], onesdb, tb2[:, :], start=(dc == 0), stop=(dc == 7)),
                     reads=[cstb_r, tr2], writes=[ssr])
            S.op("act", lambda e: e.copy(mean_sb[:, :], psm[:, :]), reads=[pmr], writes=[mean_r])
            t1, r1 = tA.next()
            S.op("dve", lambda e, t1=t1: e.tensor_tensor(t1[:, :], mean_sb[:, :], mean_sb[:, :], ALU.mult),
                 reads=[mean_r], writes=[r1])
            t2, r2 = tA.next()
            S.op("dve", lambda e, t1=t1, t2=t2: e.tensor_tensor(t2[:, :], pss[:, :], t1[:, :], ALU.subtract),
                 reads=[ssr, r1], writes=[r2])
            S.op("dve", lambda e, t2=t2: e.tensor_scalar_max(t2[:, :], t2[:, :], 0.0), reads=[r2], writes=[r2])
            t3, r3 = tA.next()
            S.op("act", lambda e, t2=t2, t3=t3: e.activation(t3[:, :], t2[:, :], AF.Sqrt, bias=eps_t[:, 0:1]),
                 reads=[r2, eps_r], writes=[r3])
            S.op("dve", lambda e, t3=t3: e.reciprocal(rstd_sb[:, :], t3[:, :]), reads=[r3], writes=[rstd_r])
            for dc in range(8):
                ta, tar = tA.next()
                S.op("dve", lambda e, ta=ta, dc=dc: e.tensor_tensor(ta[:, :], zT[:, dc, :], mean_sb[:, :], ALU.subtract),
                     reads=[zT_r[dc], mean_r], writes=[tar])
                S.op("dve", lambda e, ta=ta: e.tensor_tensor(ta[:, :], ta[:, :], rstd_sb[:, :], ALU.mult),
                     reads=[tar, rstd_r], writes=[tar])
                S.op("act", lambda e, ta=ta, dc=dc: e.activation(
                    outT[:, dc, :], ta[:, :], AF.Identity, scale=pvc(gname, dc), bias=pvc(bname, dc)),
                    reads=[tar, pv_r], writes=[out_rs[dc]])
                S.op("act", lambda e, ta=ta, dc=dc: e.activation(
                    outTb[:, dc, :], ta[:, :], AF.Identity, scale=pvc(gname, dc), bias=pvc(bname, dc)),
                    reads=[tar, pv_r], writes=[out_rs[dc]])

        def ffn_ln(inT, inTb, in_rs, Wg, Wu, Wd, gname, bname, outT, outTb, out_rs):
            Wg_v = Wg.rearrange("(kc p) f -> p kc f", p=128)
            Wu_v = Wu.rearrange("(kc p) f -> p kc f", p=128)
            Wd_v = Wd.rearrange("(j p) d -> p j d", p=128)
            for j in range(NJ):
                wb, wr = wgu.next()
                S.dma("pool", wb[:, 0], Wg_v[:, :, j * 128:(j + 1) * 128], writes=[wr])
                S.dma("pool", wb[:, 1], Wu_v[:, :, j * 128:(j + 1) * 128], writes=[wr])
                psg, pgr = psr.next()
                psu, pur = psr.next()
                for gu, (ps, prr) in enumerate(((psg, pgr), (psu, pur))):
                    for kc in range(8):
                        S.op("pe", lambda e, ps=ps, wb=wb, kc=kc, gu=gu: e.matmul(
                            ps[:, :], wb[:, gu, kc, :], inTb[:, kc, :], start=(kc == 0), stop=(kc == 7)),
                            reads=[wr, in_rs[kc]], writes=[prr])
                tb, tr = tA.next()
                S.op("act", lambda e, tb=tb, ps=psg: e.activation(tb[:, :], ps[:, :], AF.Silu), reads=[pgr], writes=[tr])
                S.op("dve", lambda e, tb=tb, ps=psu, j=j: e.tensor_tensor(aT[:, j, :], tb[:, :], ps[:, :], ALU.mult),
                     reads=[tr, pur], writes=[aT_r[j]])
            for half in range(2):
                pss_ = [psr.next() for _ in range(4)]
                for j in range(NJ):
                    wb, wr = wdb.next()
                    S.dma("pool", wb[:, :], Wd_v[:, j, half * 512:(half + 1) * 512], writes=[wr])
                    for q in range(4):
                        ps, pr = pss_[q]
                        S.op("pe", lambda e, ps=ps, wb=wb, j=j, q=q: e.matmul(
                            ps[:, :], wb[:, q * 128:(q + 1) * 128], aT[:, j, :], start=(j == 0), stop=(j == NJ - 1)),
                            reads=[wr, aT_r[j]], writes=[pr])
                for q in range(4):
                    dc = half * 4 + q
                    ps, pr = pss_[q]
                    S.op("dve", lambda e, ps=ps, dc=dc: e.scalar_tensor_tensor(
                        zT[:, dc, :], inT[:, dc, :], 2.0 * ALPHA, ps[:, :], ALU.mult, ALU.add),
                        reads=[pr, in_rs[dc]], writes=[zT_r[dc]])
            layer_norm(gname, bname, eps4, outT, outTb, out_rs)

        def store_T(blk, srcT, src_rs):
            for t in range(TB // 128):
                ot, orr = otok.next()
                for half in range(2):
                    ps, pr = psr.next()
                    for q in range(4):
                        kc = half * 4 + q
                        S.op("pe", lambda e, ps=ps, kc=kc, q=q, t=t: e.matmul(
                            ps[:, q * 128:(q + 1) * 128], srcT[:, kc, t * 128:(t + 1) * 128], ident,
                            start=True, stop=True), reads=[src_rs[kc], cst_r], writes=[pr])
                    S.op("act", lambda e, ps=ps, ot=ot, half=half: e.copy(ot[:, half * 512:(half + 1) * 512], ps[:, :]),
                         reads=[pr], writes=[orr])
                r0 = blk * TB + t * 128
                S.dma("sp", out_d[r0:r0 + 128, :], ot[:, :], reads=[orr], writes=[out_r])

        def projT(Wd_ap, col0, inTb, in_rs, ncols=128):
            wb, wr = wpj.next()
            Wv = Wd_ap.rearrange("(kc p) f -> p kc f", p=128)
            S.dma("pool", wb[:, :, 0:ncols], Wv[:, :, col0:col0 + ncols], writes=[wr])
            ps, pr = psr.next()
            for kc in range(8):
                S.op("pe", lambda e, ps=ps, wb=wb, kc=kc: e.matmul(
                    ps[0:ncols, :], wb[:, kc, 0:ncols], inTb[:, kc, :], start=(kc == 0), stop=(kc == 7)),
                    reads=[wr, in_rs[kc]], writes=[pr])
            return ps, pr

        NT = TB // 128
        carry = sb([128, 16, 3]); carry_r = [Res() for _ in range(16)]
        S.op("dve", lambda e: e.memset(carry[:, :, :], 0.0), writes=carry_r)
        cwork = ring(2, [128, 3 + TB])
        mean_sb = cwork.items[0][0][:, 0:TB]; rstd_sb = cwork.items[1][0][:, 0:TB]
        qkT = zT[:, :, :].rearrange("p a b -> p (a b)").bitcast(BF16).rearrange("p (c n) -> p c n", n=TB)
        qk_r = [zT_r[c // 2] for c in range(16)]
        sigmo = sb([128, 8, TB], BF16); sigmo_r = [Res() for _ in range(8)]
        sga = sigmo; sga_r = sigmo_r
        sgb = qkT[:, 8:16, :]; sgb_r = qk_r[8:16]
        vt = sb([128, NT, 4, 258], BF16); vt_r = [[Res() for _ in range(4)] for _ in range(NT)]
        S.op("dve", lambda e: e.memset(vt[:, :, :, :], 1.0), writes=[r for rr in vt_r for r in rr])
        wv = ring(1, [128, 8, 256], BF16)
        wif = sb([128, 8, 8], BF16); wif_r = Res()
        S.dma("pool", wif[:, :, :], win.rearrange("(kc p) f -> p kc f", p=128)[:, :, 4096:4104], writes=[wif_r])
        gts = sb([128, NT, 24]); gts_r = [Res() for _ in range(NT)]
        Cst = sb([128, 4, 2, 258]); C_r = [Res() for _ in range(4)]
        Cb = sb([128, 4, 2, 258], BF16); Cb_r = [Res() for _ in range(4)]
        S.op("dve", lambda e: e.memset(Cst[:, :, :, :], 0.0), writes=C_r)
        S.op("dve", lambda e: e.memset(Cb[:, :, :, :], 0.0), writes=Cb_r)
        hmTb = sb([128, 8, TB], BF16); hm_r = [Res() for _ in range(8)]
        yrTb = sb([128, 8, TB], BF16); yr_r = [Res() for _ in range(8)]
        mgTb = sigmo; mg_r = sigmo_r
        smr = ring(2, [128, 128], BF16)
        ktk = ring(1, [128, 256], BF16)
        hh = ring(1, [128, 256])
        hnb = ring(1, [128, 256], BF16)
        sm6 = ring(4, [128, 8])
        ctmp = ring(1, [128, 258])

        def mlstm(blk):
            Wv = win.rearrange("(kc p) f -> p kc f", p=128)
            for c in range(16):
                ps, pr = projT(win, c * 128, x1Tb, x1_r)
                wk, wkr = cwork.next()
                S.op("act", lambda e, wk=wk, c=c: e.copy(wk[:, 0:3], carry[:, c, :]), reads=[carry_r[c]], writes=[wkr])
                S.op("act", lambda e, wk=wk, ps=ps: e.copy(wk[:, 3:3 + TB], ps[:, :]), reads=[pr], writes=[wkr])
                S.op("act", lambda e, wk=wk, c=c: e.copy(carry[:, c, :], wk[:, TB:TB + 3]), reads=[wkr], writes=[carry_r[c]])
                ta, tar = tA.next()
                S.op("dve", lambda e, wk=wk, ta=ta, c=c: e.tensor_scalar(
                    ta[:, :], wk[:, 0:TB], pvc("cw0", c), pvc("cb", c), ALU.mult, ALU.add),
                    reads=[wkr, pv_r], writes=[tar])
                for j in (1, 2, 3):
                    S.op("dve", lambda e, wk=wk, ta=ta, c=c, j=j: e.scalar_tensor_tensor(
                        ta[:, :], wk[:, j:j + TB], pvc("cw%d" % j, c), ta[:, :], ALU.mult, ALU.add),
                        reads=[wkr, pv_r, tar], writes=[tar])
                S.op("act", lambda e, ta=ta, c=c: e.activation(qkT[:, c, :], ta[:, :], AF.Silu),
                     reads=[tar], writes=[qk_r[c]])
            for c in range(8):
                ps, pr = projT(win, 3072 + c * 128, x1Tb, x1_r)
                S.op("act", lambda e, ps=ps, c=c: e.activation(sigmo[:, c, :], ps[:, :], AF.Sigmoid),
                     reads=[pr], writes=[sigmo_r[c]])
            for t in range(NT):
                ps, pr = psr.next()
                for kc in range(8):
                    S.op("pe", lambda e, ps=ps, kc=kc, t=t: e.matmul(
                        ps[:, 0:8], x1Tb[:, kc, t * 128:(t + 1) * 128], wif[:, kc, :], start=(kc == 0), stop=(kc == 7)),
                        reads=[wif_r, x1_r[kc]], writes=[pr])
                g = gts[:, t, :]
                gr = gts_r[t]
                S.op("dve", lambda e, g=g, ps=ps: e.tensor_tensor(g[:, 12:20], ps[:, 0:8], gbias[:, :], ALU.add),
                     reads=[pr, gb_r], writes=[gr])
                S.op("act", lambda e, g=g: e.activation(g[:, 20:24], g[:, 16:20], AF.Exp, scale=-1.0), reads=[gr], writes=[gr])
                S.op("act", lambda e, g=g: e.activation(g[:, 16:20], g[:, 20:24], AF.Ln, bias=1.0), reads=[gr], writes=[gr])
                S.op("dve", lambda e, g=g: e.tensor_scalar_mul(g[:, 16:20], g[:, 16:20], -1.0), reads=[gr], writes=[gr])
                ps2, pr2 = psr.next()
                S.op("pe", lambda e, ps2=ps2, g=g: e.matmul(ps2[:, 0:4], tri, g[:, 16:20], start=True, stop=True),
                     reads=[gr, cst_r], writes=[pr2])
                S.op("pe", lambda e, ps2=ps2, g=g: e.matmul(ps2[:, 4:8], ones, g[:, 16:20], start=True, stop=True),
                     reads=[gr, cst_r], writes=[pr2])
                S.op("dve", lambda e, g=g, ps2=ps2: e.tensor_tensor(g[:, 20:24], g[:, 12:16], ps2[:, 0:4], ALU.subtract),
                     reads=[gr, pr2], writes=[gr])
                S.op("act", lambda e, g=g: e.activation(g[:, 0:4], g[:, 20:24], AF.Exp), reads=[gr], writes=[gr])
                S.op("act", lambda e, g=g, ps2=ps2: e.activation(g[:, 4:8], ps2[:, 0:4], AF.Exp, scale=-1.0),
                     reads=[gr, pr2], writes=[gr])
                S.op("act", lambda e, g=g, ps2=ps2: e.activation(g[:, 8:12], ps2[:, 4:8], AF.Exp), reads=[gr, pr2], writes=[gr])
            for h in range(4):
                wb, wr = wv.next()
                S.dma("pool", wb[:, :, :], Wv[:, :, 2048 + h * 256:2048 + (h + 1) * 256], writes=[wr])
                for t in range(NT):
                    ps, pr = psr.next()
                    for kc in range(8):
                        S.op("pe", lambda e, ps=ps, kc=kc, t=t, wb=wb: e.matmul(
                            ps[:, 0:256], x1Tb[:, kc, t * 128:(t + 1) * 128], wb[:, kc, :], start=(kc == 0), stop=(kc == 7)),
                            reads=[wr, x1_r[kc]], writes=[pr])
                    S.op("act", lambda e, ps=ps, t=t, h=h: e.copy(vt[:, t, h, 0:256], ps[:, 0:256]),
                         reads=[pr], writes=[vt_r[t][h]])
            for t in range(NT):
                tc_ = slice(t * 128, (t + 1) * 128)
                g = gts[:, t, :]
                gr = gts_r[t]
                for h in range(4):
                    qc = [h * 2, h * 2 + 1]
                    kc_ = [8 + h * 2, 8 + h * 2 + 1]
                    ps, pr = psr.next()
                    for i in range(2):
                        S.op("pe", lambda e, ps=ps, i=i, kc_=kc_, qc=qc, tc_=tc_: e.matmul(
                            ps[:, 0:128], qkT[:, kc_[i], tc_], qkT[:, qc[i], tc_], start=(i == 0), stop=(i == 1)),
                            reads=[qk_r[kc_[i]], qk_r[qc[i]]], writes=[pr])
                    sm, smrr = smr.next()
                    S.op("dve", lambda e, sm=sm, ps=ps, h=h, g=g: e.scalar_tensor_tensor(
                        sm[:, :], ps[:, 0:128], g[:, h:h + 1], tri, ALU.mult, ALU.mult),
                        reads=[pr, gr, cst_r], writes=[smrr])
                    po, por = psr.next()
                    S.op("pe", lambda e, po=po, sm=sm, t=t, h=h: e.matmul(
                        po[:, 0:258], sm[:, :], vt[:, t, h, :], start=True, stop=False),
                        reads=[smrr, vt_r[t][h]], writes=[por])
                    for i in range(2):
                        S.op("pe", lambda e, po=po, i=i, h=h, qc=qc, tc_=tc_: e.matmul(
                            po[:, 0:258], qkT[:, qc[i], tc_], Cb[:, h, i, :], start=False, stop=(i == 1)),
                            reads=[qk_r[qc[i]], Cb_r[h]], writes=[por])
                    s6, s6r = sm6.next()
                    S.op("act", lambda e, s6=s6, po=po: e.activation(
                        s6[:, 0:1], po[:, 256:257], AF.Abs, scale=1.0 / 16.0), reads=[por], writes=[s6r])
                    S.op("dve", lambda e, s6=s6, g=g, h=h: e.tensor_tensor(s6[:, 0:1], s6[:, 0:1], g[:, 4 + h:5 + h], ALU.max),
                         reads=[s6r, gr], writes=[s6r])
                    S.op("dve", lambda e, s6=s6: e.reciprocal(s6[:, 1:2], s6[:, 0:1]), reads=[s6r], writes=[s6r])
                    hb, hbr = hh.next()
                    S.op("dve", lambda e, hb=hb, po=po, s6=s6: e.tensor_scalar(
                        hb[:, :], po[:, 0:256], s6[:, 1:2], 1.0 / 16.0, ALU.mult, ALU.mult), reads=[por, s6r], writes=[hbr])
                    S.op("dve", lambda e, hb=hb, s6=s6: e.bn_stats(s6[:, 2:8], hb[:, :]), reads=[hbr], writes=[s6r])
                    S.op("dve", lambda e, s6=s6: e.bn_aggr(s6[:, 0:2], s6[:, 2:8]), reads=[s6r], writes=[s6r])
                    S.op("act", lambda e, s6=s6: e.activation(s6[:, 2:3], s6[:, 1:2], AF.Sqrt, bias=epsln[:, 0:1]),
                         reads=[s6r, eps_r], writes=[s6r])
                    S.op("dve", lambda e, s6=s6: e.reciprocal(s6[:, 3:4], s6[:, 2:3]), reads=[s6r], writes=[s6r])
                    hn, hnr = hnb.next()
                    S.op("dve", lambda e, hn=hn, hb=hb, s6=s6: e.tensor_scalar(
                        hn[:, :], hb[:, :], s6[:, 0:1], s6[:, 3:4], ALU.subtract, ALU.mult), reads=[hbr, s6r], writes=[hnr])
                    for i in range(2):
                        pt, ptr = psr.next()
                        S.op("pe", lambda e, pt=pt, hn=hn, i=i: e.matmul(
                            pt[:, 0:128], hn[:, i * 128:(i + 1) * 128], identb, start=True, stop=True),
                            reads=[hnr, cstb_r], writes=[ptr])
                        S.op("dve", lambda e, pt=pt, h=h, i=i, tc_=tc_: e.scalar_tensor_tensor(
                            hmTb[:, h * 2 + i, tc_], pt[:, 0:128], pvc("mng", h * 2 + i), sigmo[:, h * 2 + i, tc_],
                            ALU.mult, ALU.mult), reads=[ptr, pv_r, sigmo_r[h * 2 + i]], writes=[hm_r[h * 2 + i]])
                    kk_, kkr = ktk.next()
                    for i in range(2):
                        pt, ptr = psr.next()
                        S.op("pe", lambda e, pt=pt, i=i, kc_=kc_, tc_=tc_: e.matmul(
                            pt[:, 0:128], qkT[:, kc_[i], tc_], identb, start=True, stop=True),
                            reads=[qk_r[kc_[i]], cstb_r], writes=[ptr])
                        S.op("act", lambda e, pt=pt, kk_=kk_, i=i, g=g, h=h: e.activation(
                            kk_[:, i * 128:(i + 1) * 128], pt[:, 0:128], AF.Identity, scale=g[:, h:h + 1]),
                            reads=[ptr, gr], writes=[kkr])
                    for i in range(2):
                        pc, pcr = psr.next()
                        S.op("pe", lambda e, pc=pc, kk_=kk_, i=i, t=t, h=h: e.matmul(
                            pc[:, 0:258], kk_[:, i * 128:(i + 1) * 128], vt[:, t, h, :], start=True, stop=True),
                            reads=[kkr, vt_r[t][h]], writes=[pcr])
                        ct, ctr = ctmp.next()
                        S.op("dve", lambda e, ct=ct, h=h, i=i, g=g: e.tensor_scalar_mul(ct[:, :], Cst[:, h, i, :], g[:, 8 + h:9 + h]),
                             reads=[C_r[h], gr], writes=[ctr])
                        S.op("dve", lambda e, ct=ct, pc=pc, h=h, i=i, g=g: e.scalar_tensor_tensor(
                            Cst[:, h, i, :], pc[:, 0:258], g[:, 8 + h:9 + h], ct[:, :], ALU.mult, ALU.add),
                            reads=[pcr, ctr, gr], writes=[C_r[h]])
                        S.op("act", lambda e, h=h, i=i: e.copy(Cb[:, h, i, :], Cst[:, h, i, :]), reads=[C_r[h]], writes=[Cb_r[h]])

        NCH = TB // 64
        rcar = sb([128, 26, 1]); rcar_r = [Res() for _ in range(26)]
        S.op("dve", lambda e: e.memset(rcar[:, :, :], 0.0), writes=rcar_r)
        rwork = cwork
        lowT = sb([128, 2, TB], BF16); low_r = [Res(), Res()]
        rtmp = Ring([(aT[:, 2 * i:2 * i + 2, :].rearrange("p a b -> p (a b)").bitcast(F32), Res()) for i in range(10)])
        ARbd = sb([128, NCH, 256], BF16); Bbd = sb([128, NCH, 128], BF16); Kbd = sb([128, NCH, 128], BF16)
        Vbd = sb([128, NCH, 128], BF16); Ynbd = sb([128, 128], BF16)
        bd_r = Res(); ynbd_r = Res()
        for tns in (ARbd, Bbd, Kbd, Vbd):
            S.op("dve", lambda e, tns=tns: e.memset(tns[:, :, :], 0.0), writes=[bd_r])
        S.op("dve", lambda e: e.memset(Ynbd[:, :], 0.0), writes=[ynbd_r])
        gam = sb([128, NCH]); gam_r = Res()
        Hst = sb([128, 8, 64]); H_r = [Res() for _ in range(8)]
        Hb = sb([128, 8, 64], BF16); Hb_r = [Res() for _ in range(8)]
        S.op("dve", lambda e: e.memset(Hst[:, :, :], 0.0), writes=H_r)
        S.op("dve", lambda e: e.memset(Hb[:, :, :], 0.0), writes=Hb_r)
        vst = sb([128, NCH, 64], BF16); vst_r = Res()
        btk = sb([128, NCH, 256], BF16); btk_r = Res()
        nm = ring(4, [128, 256], BF16)
        nsq = ring(11, [128, 128], BF16)
        ub = ring(4, [128, 64], BF16)
        uf = ring(4, [128, 64])
        gn6 = ring(4, [128, 8])
        htmp = ring(2, [128, 64])
        maskAR = cst[:, C_MUS:C_MUS + 256]; mask_r = cst_r
        mls = cst[:, C_MLS:C_MLS + 128]

        def v3(ap):
            return ap.rearrange("p (n l) -> p n l", l=64)

        def shifted(ci, ps, pr):
            wk, wkr = rwork.next()
            S.op("act", lambda e, wk=wk: e.copy(wk[:, 0:1], rcar[:, ci, :]), reads=[rcar_r[ci]], writes=[wkr])
            S.op("act", lambda e, wk=wk, ps=ps: e.copy(wk[:, 1:1 + TB], ps[:, :]), reads=[pr], writes=[wkr])
            S.op("act", lambda e, wk=wk: e.copy(rcar[:, ci, :], wk[:, TB:TB + 1]), reads=[wkr], writes=[rcar_r[ci]])
            ta, tar = rtmp.next()
            S.op("dve", lambda e, wk=wk, ta=ta: e.tensor_scalar_mul(ta[:, :], wk[:, 1:1 + TB], pvc("omu", ci)),
                 reads=[wkr, pv_r], writes=[tar])
            S.op("dve", lambda e, wk=wk, ta=ta: e.scalar_tensor_tensor(
                ta[:, :], wk[:, 0:TB], pvc("mu", ci), ta[:, :], ALU.mult, ALU.add), reads=[wkr, pv_r, tar], writes=[tar])
            return ta, tar

        krw = int(os.environ.get("KRW", "9"))

        def rwkv(blk):
            ps, pr = projT(win, RC0 + 24 * 128, x1Tb, x1_r)
            ta, tar = shifted(24, ps, pr)
            S.op("act", lambda e, ta=ta: e.activation(lowT[0:64, 0, :], ta[0:64, :], AF.Tanh), reads=[tar], writes=[low_r[0]])
            S.op("act", lambda e, ta=ta: e.copy(lowT[64:128, 0, :], ta[64:128, :]), reads=[tar], writes=[low_r[0]])
            ps, pr = projT(win, RC0 + 25 * 128, x1Tb, x1_r)
            ta, tar = shifted(25, ps, pr)
            S.op("act", lambda e, ta=ta: e.activation(lowT[:, 1, :], ta[:, :], AF.Sigmoid), reads=[tar], writes=[low_r[1]])
            for p in range(8):
                cs = slice(p * 128, (p + 1) * 128)
                ps, pr = projT(win, RC0 + p * 128, x1Tb, x1_r)
                r_, r_r = shifted(p, ps, pr)
                ps, pr = projT(win, RC0 + 1024 + p * 128, x1Tb, x1_r)
                k_, k_r = shifted(8 + p, ps, pr)
                ps, pr = projT(win, RC0 + 2048 + p * 128, x1Tb, x1_r)
                v_, v_r = shifted(16 + p, ps, pr)
                pw, pwr = psr.next()
                S.op("pe", lambda e, pw=pw, cs=cs: e.matmul(pw[:, :], rw2a2[0:64, cs], lowT[0:64, 0, :], start=True, stop=True),
                     reads=[rw_r, low_r[0]], writes=[pwr])
                lw, lwr = rtmp.next()
                S.op("act", lambda e, lw=lw, pw=pw, p=p: e.activation(lw[:, :], pw[:, :], AF.Sigmoid, bias=pvc("w0", p)),
                     reads=[pwr, pv_r], writes=[lwr])
                S.op("dve", lambda e, lw=lw: e.tensor_scalar_mul(lw[:, :], lw[:, :], -float(np.exp(-0.5))), reads=[lwr], writes=[lwr])
                pa, par = psr.next()
                S.op("pe", lambda e, pa=pa, cs=cs: e.matmul(pa[:, :], rw2a2[64:128, cs], lowT[64:128, 0, :], start=True, stop=True),
                     reads=[rw_r, low_r[0]], writes=[par])
                a_, a_r = rtmp.next()
                S.op("act", lambda e, a_=a_, pa=pa, p=p: e.activation(a_[:, :], pa[:, :], AF.Sigmoid, bias=pvc("a0", p)),
                     reads=[par, pv_r], writes=[a_r])
                pg, pgr = psr.next()
                S.op("pe", lambda e, pg=pg, cs=cs: e.matmul(pg[:, :], rg2[:, cs], lowT[:, 1, :], start=True, stop=True),
                     reads=[rw_r, low_r[1]], writes=[pgr])
                g_, g_r = rtmp.next()
                S.op("act", lambda e, g_=g_, pg=pg: e.copy(g_[:, :], pg[:, :]), reads=[pgr], writes=[g_r])
                kk, kkr = rtmp.next()
                S.op("dve", lambda e, kk=kk, k_=k_, p=p: e.tensor_scalar_mul(kk[:, :], k_[:, :], pvc("kk", p)),
                     reads=[k_r, pv_r], writes=[kkr])
                sq, sqr = tB.next()
                S.op("act", lambda e, sq=sq, kk=kk: e.activation(sq[:, :], kk[:, :], AF.Square), reads=[kkr], writes=[sqr])
                pq, pqr = psr.next()
                S.op("pe", lambda e, pq=pq, sq=sq: e.matmul(pq[:, :], bob, sq[:, :], start=True, stop=True),
                     reads=[cstb_r, sqr], writes=[pqr])
                t1, t1r = rtmp.next()
                S.op("act", lambda e, t1=t1, pq=pq: e.activation(t1[:, :], pq[:, :], AF.Sqrt), reads=[pqr], writes=[t1r])
                S.op("dve", lambda e, t1=t1: e.tensor_scalar_max(t1[:, :], t1[:, :], 1e-12), reads=[t1r], writes=[t1r])
                S.op("dve", lambda e, t1=t1: e.reciprocal(t1[:, :], t1[:, :]), reads=[t1r], writes=[t1r])
                S.op("dve", lambda e, t1=t1, kk=kk: e.tensor_tensor(kk[:, :], kk[:, :], t1[:, :], ALU.mult),
                     reads=[t1r, kkr], writes=[kkr])
                S.op("dve", lambda e, t1=t1, a_=a_, p=p: e.tensor_scalar(t1[:, :], a_[:, :], 1.0, pvc("ka", p), ALU.subtract, ALU.mult),
                     reads=[a_r, pv_r], writes=[t1r])
                S.op("dve", lambda e, t1=t1, k_=k_: e.scalar_tensor_tensor(k_[:, :], t1[:, :], 1.0, k_[:, :], ALU.add, ALU.mult),
                     reads=[t1r, k_r], writes=[k_r])
                t2, t2r = tB.next()
                S.op("dve", lambda e, t2=t2, r_=r_, k_=k_, p=p: e.scalar_tensor_tensor(
                    t2[:, :], r_[:, :], pvc("rrk", p), k_[:, :], ALU.mult, ALU.mult), reads=[r_r, k_r, pv_r], writes=[t2r])
                pb, pbr = psr.next()
                S.op("pe", lambda e, pb=pb, t2=t2: e.matmul(pb[:, :], bob, t2[:, :], start=True, stop=True),
                     reads=[cstb_r, t2r], writes=[pbr])
                bon, bonr = rtmp.next()
                S.op("dve", lambda e, bon=bon, pb=pb, v_=v_: e.tensor_tensor(bon[:, :], pb[:, :], v_[:, :], ALU.mult),
                     reads=[pbr, v_r], writes=[bonr])
                cl, clr = rtmp.next()
                S.op("dve", lambda e, cl=cl, lw=lw: e.tensor_tensor_scan(cl[:, :], rst, lw[:, :], 0.0, ALU.mult, ALU.add),
                     reads=[lwr, cstb_r], writes=[clr])
                e1, e1r = tA.next()
                S.op("act", lambda e, e1=e1, cl=cl: e.activation(e1[:, :], cl[:, :], AF.Exp), reads=[clr], writes=[e1r])
                S.op("act", lambda e, e1=e1: e.copy(gam[:, :], v3(e1[:, :])[:, :, 63]), reads=[e1r], writes=[gam_r])
                for hf in range(2):
                    hs = slice(hf * 64, hf * 64 + 64)
                    S.op("dve", lambda e, hs=hs, hf=hf, r_=r_, e1=e1: e.tensor_tensor(
                        ARbd[hs, :, 128 + hf * 64:128 + hf * 64 + 64], v3(r_[hs, :]), v3(e1[hs, :]), ALU.mult),
                        reads=[r_r, e1r], writes=[bd_r])
                e2, e2r = tA.next()
                S.op("act", lambda e, e2=e2, cl=cl: e.activation(e2[:, :], cl[:, :], AF.Exp, scale=-1.0), reads=[clr], writes=[e2r])
                S.op("dve", lambda e, t1=t1, kk=kk, a_=a_: e.tensor_tensor(t1[:, :], kk[:, :], a_[:, :], ALU.mult),
                     reads=[kkr, a_r], writes=[t1r])
                for hf in range(2):
                    hs = slice(hf * 64, hf * 64 + 64)
                    S.op("dve", lambda e, hs=hs, hf=hf, t1=t1, e2=e2: e.tensor_tensor(
                        Bbd[hs, :, hf * 64:hf * 64 + 64], v3(t1[hs, :]), v3(e2[hs, :]), ALU.mult), reads=[t1r, e2r], writes=[bd_r])
                    S.op("dve", lambda e, hs=hs, hf=hf, k_=k_, e2=e2: e.tensor_tensor(
                        Kbd[hs, :, hf * 64:hf * 64 + 64], v3(k_[hs, :]), v3(e2[hs, :]), ALU.mult), reads=[k_r, e2r], writes=[bd_r])
                    S.op("act", lambda e, hs=hs, hf=hf, v_=v_: e.copy(Vbd[hs, :, hf * 64:hf * 64 + 64], v3(v_[hs, :])),
                         reads=[v_r], writes=[bd_r])
                S.op("dve", lambda e, cl=cl, lw=lw: e.tensor_tensor(cl[:, :], cl[:, :], lw[:, :], ALU.subtract),
                     reads=[clr, lwr], writes=[clr])
                e3, e3r = tA.next()
                S.op("act", lambda e, e3=e3, cl=cl: e.activation(e3[:, :], cl[:, :], AF.Exp), reads=[clr], writes=[e3r])
                for hf in range(2):
                    hs = slice(hf * 64, hf * 64 + 64)
                    S.op("dve", lambda e, hs=hs, hf=hf, kk=kk, e3=e3: e.scalar_tensor_tensor(
                        ARbd[hs, :, hf * 64:hf * 64 + 64], v3(kk[hs, :]), -1.0, v3(e3[hs, :]), ALU.mult, ALU.mult),
                        reads=[kkr, e3r], writes=[bd_r])
                if krw <= 1:
                    S.op("dve", lambda e, p=p: e.memset(yrTb[:, p, :], 0.0), writes=[yr_r[p]])
                    continue
                pv_, pvr_ = psr.next()
                for n in range(NCH):
                    S.op("pe", lambda e, n=n, pv_=pv_: e.matmul(pv_[:, n * 64:(n + 1) * 64], Vbd[:, n, :], istb, start=True, stop=True),
                         reads=[bd_r, cstb_r], writes=[pvr_])
                S.op("act", lambda e, pv_=pv_: e.copy(vst[:, :, :], v3(pv_[:, :])), reads=[pvr_], writes=[vst_r])
                for n0 in range(0, NCH, 2):
                    pt, ptr = psr.next()
                    for n in (n0, n0 + 1):
                        o = (n - n0) * 256
                        S.op("pe", lambda e, n=n, pt=pt, o=o: e.matmul(pt[:, o:o + 128], Bbd[:, n, :], identb, start=True, stop=True),
                             reads=[bd_r, cstb_r], writes=[ptr])
                        S.op("pe", lambda e, n=n, pt=pt, o=o: e.matmul(pt[:, o + 128:o + 256], Kbd[:, n, :], identb, start=True, stop=True),
                             reads=[bd_r, cstb_r], writes=[ptr])
                    S.op("act", lambda e, n0=n0, pt=pt: e.copy(btk[:, n0:n0 + 2, :], pt[:, :].rearrange("p (n l) -> p n l", l=256)),
                         reads=[ptr], writes=[btk_r])
                if krw <= 2:
                    S.op("dve", lambda e, p=p: e.memset(yrTb[:, p, :], 0.0), writes=[yr_r[p]])
                    continue
                pyo, pyor = pyo_bank
                for n in range(NCH):
                    pA, pAr = psr.next()
                    S.op("pe", lambda e, pA=pA, n=n: e.matmul(pA[:, 0:256], Bbd[:, n, :], ARbd[:, n, :], start=True, stop=True),
                         reads=[bd_r], writes=[pAr])
                    S.op("pe", lambda e, pA=pA, n=n: e.matmul(pA[:, 256:512], Kbd[:, n, :], ARbd[:, n, :], start=True, stop=True),
                         reads=[bd_r], writes=[pAr])
                    mA, mAr = nm.next()
                    mK, mKr = nm.next()
                    S.op("dve", lambda e, mA=mA, pA=pA: e.tensor_tensor(mA[:, :], pA[:, 0:256], maskAR[:, :], ALU.mult),
                         reads=[pAr, mask_r], writes=[mAr])
                    S.op("dve", lambda e, mK=mK, pA=pA: e.tensor_tensor(mK[:, :], pA[:, 256:512], maskAR[:, :], ALU.mult),
                         reads=[pAr, mask_r], writes=[mKr])
                    pT, pTr = psr.next()
                    S.op("pe", lambda e, pT=pT, n=n: e.matmul(pT[:, 0:128], ARbd[:, n, 0:128], Bbd[:, n, :], start=True, stop=True),
                         reads=[bd_r], writes=[pTr])
                    nT, nTr = nsq.next()
                    S.op("dve", lambda e, nT=nT, pT=pT: e.tensor_tensor(nT[:, :], pT[:, 0:128], mls, ALU.mult),
                         reads=[pTr, cst_r], writes=[nTr])
                    if krw <= 3:
                        continue
                    pws = [(mA[:, 0:128], mAr)]
                    pwT = (nT[:, :], nTr)
                    for j in range(1, 6):
                        cur, curr = pws[-1]
                        p2, p2r = psr.next()
                        S.op("pe", lambda e, p2=p2, cur=cur, ct=pwT[0]: e.matmul(p2[:, 0:128], ct, cur, start=True, stop=True),
                             reads=[curr, pwT[1]], writes=[p2r])
                        if j < 5:
                            S.op("pe", lambda e, p2=p2, cur=cur, ct=pwT[0]: e.matmul(p2[:, 128:256], cur, ct, start=True, stop=True),
                                 reads=[curr, pwT[1]], writes=[p2r])
                        nx, nxr = nsq.next()
                        S.op("act", lambda e, nx=nx, p2=p2: e.copy(nx[:, :], p2[:, 0:128]), reads=[p2r], writes=[nxr])
                        if j < 5:
                            nxT, nxTr = nsq.next()
                            S.op("dve", lambda e, nxT=nxT, p2=p2: e.tensor_copy(nxT[:, :], p2[:, 128:256]), reads=[p2r], writes=[nxTr])
                            pwT = (nxT[:, :], nxTr)
                        pws.append((nx[:, :], nxr))
                    if krw <= 4:
                        continue
                    pw_, pw_r = psr.next()
                    S.op("pe", lambda e, pw_=pw_, n=n, p=p: e.matmul(pw_[:, 0:64], ARbd[:, n, 0:128], Hb[:, p, :], start=True, stop=False),
                         reads=[bd_r, Hb_r[p]], writes=[pw_r])
                    S.op("pe", lambda e, pw_=pw_, n=n, mK=mK: e.matmul(pw_[:, 0:64], mK[:, 0:128], vst[:, n, :], start=False, stop=True),
                         reads=[mKr, vst_r], writes=[pw_r])
                    u_f, ufr = uf.next()
                    u_b, ubr = ub.next()
                    S.op("act", lambda e, u_f=u_f, pw_=pw_: e.copy(u_f[:, :], pw_[:, 0:64]), reads=[pw_r], writes=[ufr])
                    S.op("dve", lambda e, u_b=u_b, pw_=pw_: e.tensor_copy(u_b[:, :], pw_[:, 0:64]), reads=[pw_r], writes=[ubr])
                    for j in range(6):
                        pu, pur_ = psr.next()
                        S.op("pe", lambda e, pu=pu, j=j, u_b=u_b, pws=pws: e.matmul(pu[:, 0:64], pws[j][0], u_b[:, :], start=True, stop=True),
                             reads=[pws[j][1], ubr], writes=[pur_])
                        u_f2, ufr2 = uf.next()
                        u_b2, ubr2 = ub.next()
                        S.op("dve", lambda e, u_f2=u_f2, u_f=u_f, pu=pu: e.tensor_tensor(u_f2[:, :], pu[:, 0:64], u_f[:, :], ALU.add),
                             reads=[pur_, ufr], writes=[ufr2])
                        S.op("act", lambda e, u_b2=u_b2, u_f2=u_f2: e.copy(u_b2[:, :], u_f2[:, :]), reads=[ufr2], writes=[ubr2])
                        u_f, ufr, u_b, ubr = u_f2, ufr2, u_b2, ubr2
                    if krw <= 5:
                        continue
                    py, pyr = psr.next()
                    S.op("pe", lambda e, py=py, n=n, p=p: e.matmul(py[:, 0:64], ARbd[:, n, 128:256], Hb[:, p, :], start=True, stop=False),
                         reads=[bd_r, Hb_r[p]], writes=[pyr])
                    S.op("pe", lambda e, py=py, mA=mA, u_b=u_b: e.matmul(py[:, 0:64], mA[:, 128:256], u_b[:, :], start=False, stop=False),
                         reads=[mAr, ubr], writes=[pyr])
                    S.op("pe", lambda e, py=py, mK=mK, n=n: e.matmul(py[:, 0:64], mK[:, 128:256], vst[:, n, :], start=False, stop=True),
                         reads=[mKr, vst_r], writes=[pyr])
                    g6, g6r = gn6.next()
                    S.op("dve", lambda e, g6=g6, py=py: e.bn_stats(g6[:, 2:8], py[:, 0:64]), reads=[pyr], writes=[g6r])
                    S.op("dve", lambda e, g6=g6: e.bn_aggr(g6[:, 0:2], g6[:, 2:8]), reads=[g6r], writes=[g6r])
                    S.op("act", lambda e, g6=g6: e.activation(g6[:, 2:3], g6[:, 1:2], AF.Sqrt, bias=epsgn[:, 0:1]),
                         reads=[g6r, eps_r], writes=[g6r])
                    S.op("dve", lambda e, g6=g6: e.reciprocal(g6[:, 3:4], g6[:, 2:3]), reads=[g6r], writes=[g6r])
                    for hf in range(2):
                        hs = slice(hf * 64, hf * 64 + 64)
                        S.op("dve", lambda e, hs=hs, hf=hf, py=py, g6=g6: e.tensor_scalar(
                            Ynbd[hs, hf * 64:hf * 64 + 64], py[hs, 0:64], g6[hs, 0:1], g6[hs, 3:4], ALU.subtract, ALU.mult),
                            reads=[pyr, g6r], writes=[ynbd_r])
                    S.op("pe", lambda e, n=n, pyo=pyo: e.matmul(pyo[:, n * 64:(n + 1) * 64], Ynbd[:, :], istb, start=True, stop=True),
                         reads=[ynbd_r, cstb_r], writes=[pyor])
                    if krw <= 6:
                        continue
                    ph, phr = psr.next()
                    S.op("pe", lambda e, ph=ph, n=n, u_b=u_b: e.matmul(ph[:, 0:64], btk[:, n, 0:128], u_b[:, :], start=True, stop=False),
                         reads=[btk_r, ubr], writes=[phr])
                    S.op("pe", lambda e, ph=ph, n=n: e.matmul(ph[:, 0:64], btk[:, n, 128:256], vst[:, n, :], start=False, stop=True),
                         reads=[btk_r, vst_r], writes=[phr])
                    ht, htr = htmp.next()
                    S.op("dve", lambda e, ht=ht, p=p, n=n: e.tensor_scalar_mul(ht[:, :], Hst[:, p, :], gam[:, n:n + 1]),
                         reads=[H_r[p], gam_r], writes=[htr])
                    S.op("dve", lambda e, ht=ht, ph=ph, p=p, n=n: e.scalar_tensor_tensor(
                        Hst[:, p, :], ph[:, 0:64], gam[:, n:n + 1], ht[:, :], ALU.mult, ALU.add),
                        reads=[phr, htr, gam_r], writes=[H_r[p]])
                    S.op("act", lambda e, p=p: e.copy(Hb[:, p, :], Hst[:, p, :]), reads=[H_r[p]], writes=[Hb_r[p]])
                if krw <= 5:
                    S.op("dve", lambda e, p=p: e.memset(yrTb[:, p, :], 0.0), writes=[yr_r[p]])
                    continue
                S.op("dve", lambda e, t1=t1, pyo=pyo, p=p: e.tensor_scalar(
                    t1[:, :], pyo[:, :], pvc("gng", p), pvc("gnb", p), ALU.mult, ALU.add), reads=[pyor, pv_r], writes=[t1r])
                S.op("dve", lambda e, t1=t1, bon=bon: e.tensor_tensor(t1[:, :], t1[:, :], bon[:, :], ALU.add),
                     reads=[t1r, bonr], writes=[t1r])
                S.op("dve", lambda e, t1=t1, g_=g_, p=p: e.tensor_tensor(yrTb[:, p, :], t1[:, :], g_[:, :], ALU.mult),
                     reads=[t1r, g_r], writes=[yr_r[p]])

        def merge_out(blk):
            for c in range(8):
                ps, pr = projT(win, 7432 + c * 128, x1Tb, x1_r)
                S.op("act", lambda e, ps=ps, c=c: e.activation(sga[:, c, :], ps[:, :], AF.Sigmoid), reads=[pr], writes=[sga_r[c]])
                ps, pr = projT(win, 8456 + c * 128, x1Tb, x1_r)
                S.op("act", lambda e, ps=ps, c=c: e.activation(sgb[:, c, :], ps[:, :], AF.Sigmoid), reads=[pr], writes=[sgb_r[c]])
            for c in range(8):
                psa, par = projT(wa_d, c * 128, hmTb, hm_r)
                psb, pbr = projT(wb_d, c * 128, yrTb, yr_r)
                ta, tar = tA.next()
                S.op("dve", lambda e, ta=ta, psa=psa, c=c: e.tensor_tensor(ta[:, :], psa[:, :], sga[:, c, :], ALU.mult),
                     reads=[par, sga_r[c]], writes=[tar])
                tb, tbr = tA.next()
                S.op("dve", lambda e, tb=tb, psb=psb, c=c: e.tensor_tensor(tb[:, :], psb[:, :], sgb[:, c, :], ALU.mult),
                     reads=[pbr, sgb_r[c]], writes=[tbr])
                S.op("dve", lambda e, ta=ta, tb=tb, c=c: e.tensor_tensor(mgTb[:, c, :], ta[:, :], tb[:, :], ALU.add),
                     reads=[tar, tbr], writes=[mg_r[c]])
            for c in range(8):
                ps, pr = projT(wo_d, c * 128, mgTb, mg_r)
                S.op("dve", lambda e, ps=ps, c=c: e.scalar_tensor_tensor(
                    zT[:, c, :], x1T[:, c, :], ALPHA, ps[:, :], ALU.mult, ALU.add), reads=[pr, x1_r[c]], writes=[zT_r[c]])
            layer_norm("ln2_g", "ln2_b", epsln, x2T, x2Tb, x2_r)

        stage = int(os.environ.get("KSTAGE", "9"))
        for blk in range(nblk):
            load_xT(blk)
            ffn_ln(xT, xTb, xT_r, w1g, w1u, w1d, "ln1_g", "ln1_b", x1T, x1Tb, x1_r)
            if stage == 1:
                store_T(blk, x1T, x1_r)
                continue
            skip = os.environ.get("KSKIP", "")
            if "mlstm" in skip:
                S.op("dve", lambda e: e.memset(hmTb[:, :, :], 0.0), writes=hm_r)
            else:
                mlstm(blk)
            if "rwkv" in skip:
                S.op("dve", lambda e: e.memset(yrTb[:, :, :], 0.0), writes=yr_r)
            else:
                rwkv(blk)
            merge_out(blk)
            if stage == 2:
                store_T(blk, x2T, x2_r)
                continue
            ffn_ln(x2T, x2Tb, x2_r, w2g, w2u, w2d, "ln3_g", "ln3_b", x3T, x3Tb, x3_r)
            store_T(blk, x3T, x3_r)

        S.op("sp", lambda e: e.nop(), reads=[out_r])
        S.emit()
    return nc


def _consts():
    c = np.zeros((128, NCST), np.float32)
    i = np.arange(128)
    c[:, C_ID:C_ID + 128] = np.eye(128)
    c[:, C_OD:C_OD + 128] = 1.0 / 1024.0
    c[:, C_TRI:C_TRI + 128] = (i[:, None] <= i[None, :])
    c[:, C_ONE:C_ONE + 128] = 1.0
    c[:, C_MUS:C_MUS + 128] = (i[:, None] < i[None, :])
    c[:, C_MLS:C_MLS + 128] = (i[:, None] > i[None, :])
    c[:, C_IST:C_IST + 64] = np.concatenate([np.eye(64), np.eye(64)], axis=0)
    c[:, C_BO:C_BO + 128] = ((i[:, None] // 64) == (i[None, :] // 64))
    return c


def _fm(v):
    v = np.asarray(v, np.float32).reshape(-1, 128)
    return np.ascontiguousarray(v.T)


def kernel(**inp):
    nblk = int(os.environ.get("KNBLK", str(SEQ // TB)))
    x = np.asarray(inp["x"], np.float32)
    pv = np.zeros((128, NPV), np.float32)

    def put(name, arr):
        a = _fm(arr)
        pv[:, PV[name]:PV[name] + a.shape[1]] = a

    for n in ("ln1_g", "ln1_b", "ln2_g", "ln2_b", "ln3_g", "ln3_b"):
        put(n, inp[n][0])
    cw = inp["m_conv_w"][0]
    for j in range(4):
        put("cw%d" % j, cw[j])
    put("cb", inp["m_conv_b"][0])
    put("mng", inp["m_norm_g"][0])
    mu = inp["r_mu"][0]
    put("mu", mu)
    put("omu", 1.0 - mu)
    put("w0", inp["r_w0"][0]); put("a0", inp["r_a0"][0]); put("kk", inp["r_k_k"][0]); put("ka", inp["r_k_a"][0])
    put("rrk", inp["r_r_k"][0].reshape(-1)); put("gng", inp["r_gn_g"][0]); put("gnb", inp["r_gn_b"][0])
    gbias = np.zeros((128, 8), np.float32)
    gbias[:, 0:4] = inp["m_i_bias"][0][None, :]
    gbias[:, 4:8] = inp["m_f_bias"][0][None, :]
    shared = {"cst": _consts(), "pv": pv, "gbias": gbias,
              "rst": np.ascontiguousarray(np.broadcast_to((np.arange(512)[None, :] % 64 != 0), (128, 512)).astype(np.float32))}
    for n in ("ffn1_w_gate", "ffn1_w_up", "ffn1_w_down", "ffn2_w_gate", "ffn2_w_up", "ffn2_w_down",
              "w_in", "w_branch_a", "w_branch_b", "w_out", "r_w2", "r_a2", "r_g2"):
        shared[n] = np.ascontiguousarray(inp[n][0], dtype=np.float32)
    in_maps = []
    for c in range(NCORES):
        m = dict(shared)
        m["x"] = np.ascontiguousarray(x[c % 2])
        in_maps.append(m)
    nc = build(nblk)
    ncr = int(os.environ.get("KCORES", str(NCORES)))
    c0 = int(os.environ.get("KCORE0", "0"))
    res = run_bass_kernel_spmd(nc, in_maps[:ncr], core_ids=list(range(c0, c0 + ncr)))
    out = np.stack([np.asarray(res.results[b % ncr]["out"]) for b in range(2)], axis=0)
    return out.reshape(2, SEQ, D).astype(np.float32)
```

```python
import contextlib
import os
import numpy as np
import concourse.bass as bass
import concourse.mybir as mybir
from concourse.bass_utils import run_bass_kernel_spmd

F32 = mybir.dt.float32
BF16 = mybir.dt.bfloat16
AF = mybir.ActivationFunctionType
ALU = mybir.AluOpType

D = 1024
DFF = 2816
NJ = DFF // 128
SEQ = 8192
TB = 512
ALPHA = 2.0 ** 0.25
LN_EPS = 1e-5
GN_EPS = 64e-5
NCORES = 8
WCOLS = 9480
RC0 = 4104
SAME_ENGINE_WAITS = os.environ.get("KSEW", "act,dve,pool,sp").split(",")


class Res:
    __slots__ = ("w", "r", "excl")

    def __init__(self, excl=False):
        self.w = None
        self.r = {}
        self.excl = excl


class Sched:
    NDS = 40

    def __init__(self, nc, stack):
        self.nc = nc
        self.E = {"pe": nc.tensor, "act": nc.scalar, "dve": nc.vector, "pool": nc.gpsimd, "sp": nc.sync}
        self.q = {k: [] for k in self.E}
        self.cnt = {k: 0 for k in self.E}
        self.esem = {k: stack.enter_context(nc.semaphore("s_" + k)) for k in self.E}
        self.dsem = [stack.enter_context(nc.semaphore("d%d" % i)) for i in range(self.NDS)]
        self.dval = [0] * self.NDS
        self.dnext = {"sp": 0, "pool": 0}
        self.dpool = {"sp": list(range(0, 8)), "pool": list(range(8, self.NDS))}
        self.waited = {k: {} for k in self.E}

    def _deps(self, eng, reads, writes, extra=()):
        deps = {}

        def add(t):
            if t is None:
                return
            k = (t[0], t[1])
            if deps.get(k, 0) < t[2]:
                deps[k] = t[2]

        for r in reads:
            add(r.w)
        for w in writes:
            add(w.w)
            for k, v in w.r.items():
                add((k[0], k[1], v))
        for t in extra:
            add(t)
        waits = []
        for k, v in deps.items():
            if k[0] == "e" and k[1] == eng and (eng == "pe" or eng not in SAME_ENGINE_WAITS):
                continue
            if self.waited[eng].get(k, 0) >= v:
                continue
            self.waited[eng][k] = v
            waits.append((k, v))
        return waits

    def _mark(self, tok, reads, writes):
        k = (tok[0], tok[1])
        for r in reads:
            if r.r.get(k, 0) < tok[2]:
                r.r[k] = tok[2]
        for w in writes:
            w.w = tok
            w.r = {}

    def op(self, eng, fn, reads=(), writes=()):
        ex = [r for r in reads if r.excl]
        if ex:
            writes = list(writes) + ex
        waits = self._deps(eng, reads, writes)
        self.cnt[eng] += 1
        tok = ("e", eng, self.cnt[eng])
        self.q[eng].append((waits, fn, None))
        self._mark(tok, reads, writes)
        return tok

    def dma(self, qeng, out, in_, reads=(), writes=()):
        pool = self.dpool[qeng]
        j = pool[self.dnext[qeng]]
        self.dnext[qeng] = (self.dnext[qeng] + 1) % len(pool)
        prev = ("d", j, self.dval[j]) if self.dval[j] else None
        waits = self._deps(qeng, reads, writes, extra=(prev,) if prev else ())
        self.dval[j] += 16
        tok = ("d", j, self.dval[j])
        self.q[qeng].append((waits, (out, in_), j))
        self._mark(tok, reads, writes)
        return tok

    def emit(self):
        nc = self.nc
        with nc.Block() as block:
            def run(kind):
                def body(eng):
                    for waits, fn, dj in self.q[kind]:
                        for k, v in waits:
                            sem = self.esem[k[1]] if k[0] == "e" else self.dsem[k[1]]
                            eng.wait_ge(sem, v)
                        if dj is None:
                            fn(eng).then_inc(self.esem[kind], 1)
                        else:
                            eng.dma_start(out=fn[0], in_=fn[1]).then_inc(self.dsem[dj], 16)
                return body
            block.tensor(run("pe"))
            block.scalar(run("act"))
            block.vector(run("dve"))
            block.gpsimd(run("pool"))
            block.sync(run("sp"))


class Ring:
    def __init__(self, items):
        self.items = items
        self.i = 0

    def next(self):
        it = self.items[self.i]
        self.i = (self.i + 1) % len(self.items)
        return it


PV = {}
_o = 0
for _n, _w in [("ln1_g", 8), ("ln1_b", 8), ("ln2_g", 8), ("ln2_b", 8), ("ln3_g", 8), ("ln3_b", 8),
               ("cw0", 16), ("cw1", 16), ("cw2", 16), ("cw3", 16), ("cb", 16), ("mng", 8),
               ("mu", 26), ("omu", 26), ("w0", 8), ("a0", 8), ("kk", 8), ("ka", 8), ("rrk", 8),
               ("gng", 8), ("gnb", 8)]:
    PV[_n] = _o
    _o += _w
NPV = _o
C_ID, C_OD, C_MUS, C_TRI, C_ONE, C_MLS, C_IST, C_BO, NCST = 0, 128, 256, 384, 512, 640, 768, 832, 960


def build(nblk):
    nc = bass.Bass("TRN2", target_bir_lowering=False)

    def din(name, shape):
        return nc.dram_tensor(name, list(shape), F32, kind="ExternalInput").ap()

    x_d = din("x", [SEQ, D])
    cst_d = din("cst", [128, NCST])
    pv_d = din("pv", [128, NPV])
    gb_d = din("gbias", [128, 8])
    rst_d = din("rst", [128, 512])
    w1g = din("ffn1_w_gate", [D, DFF]); w1u = din("ffn1_w_up", [D, DFF]); w1d = din("ffn1_w_down", [DFF, D])
    w2g = din("ffn2_w_gate", [D, DFF]); w2u = din("ffn2_w_up", [D, DFF]); w2d = din("ffn2_w_down", [DFF, D])
    win = din("w_in", [D, WCOLS])
    wa_d = din("w_branch_a", [D, D]); wb_d = din("w_branch_b", [D, D]); wo_d = din("w_out", [D, D])
    rw2_d = din("r_w2", [64, D]); ra2_d = din("r_a2", [64, D]); rg2_d = din("r_g2", [128, D])
    out_d = nc.dram_tensor("out", [SEQ, D], F32, kind="ExternalOutput").ap()

    with contextlib.ExitStack() as st:
        S = Sched(nc, st)
        _n = [0]

        def sb(shape, dt=F32):
            _n[0] += 1
            return st.enter_context(nc.sbuf_tensor("sb%d" % _n[0], list(shape), dt))

        def ring(n, shape, dt=F32):
            return Ring([(sb(shape, dt), Res()) for _ in range(n)])

        banks = [st.enter_context(nc.psum_tensor("ps%d" % i, [128, 512], F32)) for i in range(8)]
        psr = Ring([(banks[i], Res(True)) for i in range(7)])
        pyo_bank = (banks[7], Res(True))

        cst = sb([128, NCST]); cst_r = Res()
        cstb = sb([128, NCST], BF16); cstb_r = Res()
        pv = sb([128, NPV]); pv_r = Res()
        gbias = sb([128, 8]); gb_r = Res()
        S.dma("sp", cst[:, :], cst_d[:, :], writes=[cst_r])
        S.dma("sp", pv[:, :], pv_d[:, :], writes=[pv_r])
        S.dma("sp", gbias[:, :], gb_d[:, :], writes=[gb_r])
        S.op("act", lambda e: e.copy(cstb[:, :], cst[:, :]), reads=[cst_r], writes=[cstb_r])
        ident = cst[:, C_ID:C_ID + 128]
        identb = cstb[:, C_ID:C_ID + 128]
        onesdb = cstb[:, C_OD:C_OD + 128]
        tri = cst[:, C_TRI:C_TRI + 128]
        ones = cst[:, C_ONE:C_ONE + 128]
        istb = cstb[:, C_IST:C_IST + 64]
        bob = cstb[:, C_BO:C_BO + 128]
        rstb = sb([128, 512], BF16)
        S.dma("pool", rstb[:, :], rst_d[:, :], writes=[cstb_r])
        rst = rstb[:, :]
        epsln = sb([128, 1]); eps4 = sb([128, 1]); epsgn = sb([128, 1]); eps_r = Res()
        S.op("dve", lambda e: e.memset(epsln[:, :], LN_EPS), writes=[eps_r])
        S.op("dve", lambda e: e.memset(eps4[:, :], 4 * LN_EPS), writes=[eps_r])
        S.op("dve", lambda e: e.memset(epsgn[:, :], GN_EPS), writes=[eps_r])

        def pvc(name, i=0):
            c = PV[name] + i
            return pv[:, c:c + 1]

        rw2a2 = sb([128, D], BF16); rw_r = Res()
        rg2 = sb([128, D], BF16)
        S.dma("pool", rw2a2[0:64, :], rw2_d[:, :], writes=[rw_r])
        S.dma("pool", rw2a2[64:128, :], ra2_d[:, :], writes=[rw_r])
        S.dma("pool", rg2[:, :], rg2_d[:, :], writes=[rw_r])

        xT = sb([128, 8, TB]); xTb = sb([128, 8, TB], BF16); xT_r = [Res() for _ in range(8)]
        x1T = sb([128, 8, TB]); x1Tb = sb([128, 8, TB], BF16); x1_r = [Res() for _ in range(8)]
        x2T = xT; x2Tb = xTb; x2_r = xT_r
        x3T = xT; x3Tb = xTb; x3_r = xT_r
        aT = sb([128, NJ, TB], BF16); aT_r = [Res() for _ in range(NJ)]
        zT = sb([128, 8, TB]); zT_r = [Res() for _ in range(8)]
        xtok = ring(1, [128, D])
        otok = xtok
        wgu = ring(2, [128, 2, 8, 128], BF16)
        wdb = ring(2, [128, 512], BF16)
        wpj = ring(2, [128, 8, 128], BF16)
        tA = ring(3, [128, TB])
        tB = ring(2, [128, TB], BF16)
        mean_r = Res(); rstd_r = Res()
        out_r = Res()

        def load_xT(blk):
            for t in range(TB // 128):
                xt, xr = xtok.next()
                r0 = blk * TB + t * 128
                S.dma("sp", xt[:, :], x_d[r0:r0 + 128, :], writes=[xr])
                for half in range(2):
                    ps, pr = psr.next()
                    for q in range(4):
                        kc = half * 4 + q
                        S.op("pe", lambda e, ps=ps, xt=xt, kc=kc, q=q: e.matmul(
                            ps[:, q * 128:(q + 1) * 128], xt[:, kc * 128:(kc + 1) * 128], ident,
                            start=True, stop=True), reads=[xr, cst_r], writes=[pr])
                    psv = ps[:, :].rearrange("p (q n) -> p q n", q=4)
                    S.op("act", lambda e, psv=psv, half=half, t=t: e.copy(
                        xT[:, half * 4:half * 4 + 4, t * 128:(t + 1) * 128], psv),
                        reads=[pr], writes=xT_r[half * 4:half * 4 + 4])
                    S.op("dve", lambda e, psv=psv, half=half, t=t: e.tensor_copy(
                        xTb[:, half * 4:half * 4 + 4, t * 128:(t + 1) * 128], psv),
                        reads=[pr], writes=xT_r[half * 4:half * 4 + 4])

        def layer_norm(gname, bname, eps_t, outT, outTb, out_rs):
            psm, pmr = psr.next()
            pss, ssr = psr.next()
            for dc in range(8):
                tb, tr = tB.next()
                S.op("act", lambda e, tb=tb, dc=dc: e.copy(tb[:, :], zT[:, dc, :]), reads=[zT_r[dc]], writes=[tr])
                S.op("pe", lambda e, tb=tb, dc=dc: e.matmul(psm[:, :], onesdb, tb[:, :], start=(dc == 0), stop=(dc == 7)),
                     reads=[cstb_r, tr], writes=[pmr])
                tb2, tr2 = tB.next()
                S.op("act", lambda e, tb2=tb2, dc=dc: e.activation(tb2[:, :], zT[:, dc, :], AF.Square),
                     reads=[zT_r[dc]], writes=[tr2])
                S.op("pe", lambda e, tb2=tb2, dc=dc: e.matmul(pss[:, :], onesdb, tb2[:, :], start=(dc == 0), stop=(dc == 7)),
                     reads=[cstb_r, tr2], writes=[ssr])
            S.op("act", lambda e: e.copy(mean_sb[:, :], psm[:, :]), reads=[pmr], writes=[mean_r])
            t1, r1 = tA.next()
            S.op("dve", lambda e, t1=t1: e.tensor_tensor(t1[:, :], mean_sb[:, :], mean_sb[:, :], ALU.mult),
                 reads=[mean_r], writes=[r1])
            t2, r2 = tA.next()
            S.op("dve", lambda e, t1=t1, t2=t2: e.tensor_tensor(t2[:, :], pss[:, :], t1[:, :], ALU.subtract),
                 reads=[ssr, r1], writes=[r2])
            S.op("dve", lambda e, t2=t2: e.tensor_scalar_max(t2[:, :], t2[:, :], 0.0), reads=[r2], writes=[r2])
            t3, r3 = tA.next()
            S.op("act", lambda e, t2=t2, t3=t3: e.activation(t3[:, :], t2[:, :], AF.Sqrt, bias=eps_t[:, 0:1]),
                 reads=[r2, eps_r], writes=[r3])
            S.op("dve", lambda e, t3=t3: e.reciprocal(rstd_sb[:, :], t3[:, :]), reads=[r3], writes=[rstd_r])
            for dc in range(8):
                ta, tar = tA.next()
                S.op("dve", lambda e, ta=ta, dc=dc: e.tensor_tensor(ta[:, :], zT[:, dc, :], mean_sb[:, :], ALU.subtract),
                     reads=[zT_r[dc], mean_r], writes=[tar])
                S.op("dve", lambda e, ta=ta: e.tensor_tensor(ta[:, :], ta[:, :], rstd_sb[:, :], ALU.mult),
                     reads=[tar, rstd_r], writes=[tar])
                S.op("act", lambda e, ta=ta, dc=dc: e.activation(
                    outT[:, dc, :], ta[:, :], AF.Identity, scale=pvc(gname, dc), bias=pvc(bname, dc)),
                    reads=[tar, pv_r], writes=[out_rs[dc]])
                S.op("act", lambda e, ta=ta, dc=dc: e.activation(
                    outTb[:, dc, :], ta[:, :], AF.Identity, scale=pvc(gname, dc), bias=pvc(bname, dc)),
                    reads=[tar, pv_r], writes=[out_rs[dc]])

        def ffn_ln(inT, inTb, in_rs, Wg, Wu, Wd, gname, bname, outT, outTb, out_rs):
            Wg_v = Wg.rearrange("(kc p) f -> p kc f", p=128)
            Wu_v = Wu.rearrange("(kc p) f -> p kc f", p=128)
            Wd_v = Wd.rearrange("(j p) d -> p j d", p=128)
            for j in range(NJ):
                wb, wr = wgu.next()
                S.dma("pool", wb[:, 0], Wg_v[:, :, j * 128:(j + 1) * 128], writes=[wr])
                S.dma("pool", wb[:, 1], Wu_v[:, :, j * 128:(j + 1) * 128], writes=[wr])
                psg, pgr = psr.next()
                psu, pur = psr.next()
                for gu, (ps, prr) in enumerate(((psg, pgr), (psu, pur))):
                    for kc in range(8):
                        S.op("pe", lambda e, ps=ps, wb=wb, kc=kc, gu=gu: e.matmul(
                            ps[:, :], wb[:, gu, kc, :], inTb[:, kc, :], start=(kc == 0), stop=(kc == 7)),
                            reads=[wr, in_rs[kc]], writes=[prr])
                tb, tr = tA.next()
                S.op("act", lambda e, tb=tb, ps=psg: e.activation(tb[:, :], ps[:, :], AF.Silu), reads=[pgr], writes=[tr])
                S.op("dve", lambda e, tb=tb, ps=psu, j=j: e.tensor_tensor(aT[:, j, :], tb[:, :], ps[:, :], ALU.mult),
                     reads=[tr, pur], writes=[aT_r[j]])
            for half in range(2):
                pss_ = [psr.next() for _ in range(4)]
                for j in range(NJ):
                    wb, wr = wdb.next()
                    S.dma("pool", wb[:, :], Wd_v[:, j, half * 512:(half + 1) * 512], writes=[wr])
                    for q in range(4):
                        ps, pr = pss_[q]
                        S.op("pe", lambda e, ps=ps, wb=wb, j=j, q=q: e.matmul(
                            ps[:, :], wb[:, q * 128:(q + 1) * 128], aT[:, j, :], start=(j == 0), stop=(j == NJ - 1)),
                            reads=[wr, aT_r[j]], writes=[pr])
                for q in range(4):
                    dc = half * 4 + q
                    ps, pr = pss_[q]
                    S.op("dve", lambda e, ps=ps, dc=dc: e.scalar_tensor_tensor(
                        zT[:, dc, :], inT[:, dc, :], 2.0 * ALPHA, ps[:, :], ALU.mult, ALU.add),
                        reads=[pr, in_rs[dc]], writes=[zT_r[dc]])
            layer_norm(gname, bname, eps4, outT, outTb, out_rs)

        def store_T(blk, srcT, src_rs):
            for t in range(TB // 128):
                ot, orr = otok.next()
                for half in range(2):
                    ps, pr = psr.next()
                    for q in range(4):
                        kc = half * 4 + q
                        S.op("pe", lambda e, ps=ps, kc=kc, q=q, t=t: e.matmul(
                            ps[:, q * 128:(q + 1) * 128], srcT[:, kc, t * 128:(t + 1) * 128], ident,
                            start=True, stop=True), reads=[src_rs[kc], cst_r], writes=[pr])
                    S.op("act", lambda e, ps=ps, ot=ot, half=half: e.copy(ot[:, half * 512:(half + 1) * 512], ps[:, :]),
                         reads=[pr], writes=[orr])
                r0 = blk * TB + t * 128
                S.dma("sp", out_d[r0:r0 + 128, :], ot[:, :], reads=[orr], writes=[out_r])

        def projT(Wd_ap, col0, inTb, in_rs, ncols=128):
            wb, wr = wpj.next()
            Wv = Wd_ap.rearrange("(kc p) f -> p kc f", p=128)
            S.dma("pool", wb[:, :, 0:ncols], Wv[:, :, col0:col0 + ncols], writes=[wr])
            ps, pr = psr.next()
            for kc in range(8):
                S.op("pe", lambda e, ps=ps, wb=wb, kc=kc: e.matmul(
                    ps[0:ncols, :], wb[:, kc, 0:ncols], inTb[:, kc, :], start=(kc == 0), stop=(kc == 7)),
                    reads=[wr, in_rs[kc]], writes=[pr])
            return ps, pr

        NT = TB // 128
        carry = sb([128, 16, 3]); carry_r = [Res() for _ in range(16)]
        S.op("dve", lambda e: e.memset(carry[:, :, :], 0.0), writes=carry_r)
        cwork = ring(2, [128, 3 + TB])
        mean_sb = cwork.items[0][0][:, 0:TB]; rstd_sb = cwork.items[1][0][:, 0:TB]
        qkT = zT[:, :, :].rearrange("p a b -> p (a b)").bitcast(BF16).rearrange("p (c n) -> p c n", n=TB)
        qk_r = [zT_r[c // 2] for c in range(16)]
        sigmo = sb([128, 8, TB], BF16); sigmo_r = [Res() for _ in range(8)]
        sga = sigmo; sga_r = sigmo_r
        sgb = qkT[:, 8:16, :]; sgb_r = qk_r[8:16]
        vt = sb([128, NT, 4, 258], BF16); vt_r = [[Res() for _ in range(4)] for _ in range(NT)]
        S.op("dve", lambda e: e.memset(vt[:, :, :, :], 1.0), writes=[r for rr in vt_r for r in rr])
        wv = ring(1, [128, 8, 256], BF16)
        wif = sb([128, 8, 8], BF16); wif_r = Res()
        S.dma("pool", wif[:, :, :], win.rearrange("(kc p) f -> p kc f", p=128)[:, :, 4096:4104], writes=[wif_r])
        gts = sb([128, NT, 24]); gts_r = [Res() for _ in range(NT)]
        Cst = sb([128, 4, 2, 258]); C_r = [Res() for _ in range(4)]
        Cb = sb([128, 4, 2, 258], BF16); Cb_r = [Res() for _ in range(4)]
        S.op("dve", lambda e: e.memset(Cst[:, :, :, :], 0.0), writes=C_r)
        S.op("dve", lambda e: e.memset(Cb[:, :, :, :], 0.0), writes=Cb_r)
        hmTb = sb([128, 8, TB], BF16); hm_r = [Res() for _ in range(8)]
        yrTb = sb([128, 8, TB], BF16); yr_r = [Res() for _ in range(8)]
        mgTb = sigmo; mg_r = sigmo_r
        smr = ring(2, [128, 128], BF16)
        ktk = ring(1, [128, 256], BF16)
        hh = ring(1, [128, 256])
        hnb = ring(1, [128, 256], BF16)
        sm6 = ring(4, [128, 8])
        ctmp = ring(1, [128, 258])

        def mlstm(blk):
            Wv = win.rearrange("(kc p) f -> p kc f", p=128)
            for c in range(16):
                ps, pr = projT(win, c * 128, x1Tb, x1_r)
                wk, wkr = cwork.next()
                S.op("act", lambda e, wk=wk, c=c: e.copy(wk[:, 0:3], carry[:, c, :]), reads=[carry_r[c]], writes=[wkr])
                S.op("act", lambda e, wk=wk, ps=ps: e.copy(wk[:, 3:3 + TB], ps[:, :]), reads=[pr], writes=[wkr])
                S.op("act", lambda e, wk=wk, c=c: e.copy(carry[:, c, :], wk[:, TB:TB + 3]), reads=[wkr], writes=[carry_r[c]])
                ta, tar = tA.next()
                S.op("dve", lambda e, wk=wk, ta=ta, c=c: e.tensor_scalar(
                    ta[:, :], wk[:, 0:TB], pvc("cw0", c), pvc("cb", c), ALU.mult, ALU.add),
                    reads=[wkr, pv_r], writes=[tar])
                for j in (1, 2, 3):
                    S.op("dve", lambda e, wk=wk, ta=ta, c=c, j=j: e.scalar_tensor_tensor(
                        ta[:, :], wk[:, j:j + TB], pvc("cw%d" % j, c), ta[:, :], ALU.mult, ALU.add),
                        reads=[wkr, pv_r, tar], writes=[tar])
                S.op("act", lambda e, ta=ta, c=c: e.activation(qkT[:, c, :], ta[:, :], AF.Silu),
                     reads=[tar], writes=[qk_r[c]])
            for c in range(8):
                ps, pr = projT(win, 3072 + c * 128, x1Tb, x1_r)
                S.op("act", lambda e, ps=ps, c=c: e.activation(sigmo[:, c, :], ps[:, :], AF.Sigmoid),
                     reads=[pr], writes=[sigmo_r[c]])
            for t in range(NT):
                ps, pr = psr.next()
                for kc in range(8):
                    S.op("pe", lambda e, ps=ps, kc=kc, t=t: e.matmul(
                        ps[:, 0:8], x1Tb[:, kc, t * 128:(t + 1) * 128], wif[:, kc, :], start=(kc == 0), stop=(kc == 7)),
                        reads=[wif_r, x1_r[kc]], writes=[pr])
                g = gts[:, t, :]
                gr = gts_r[t]
                S.op("dve", lambda e, g=g, ps=ps: e.tensor_tensor(g[:, 12:20], ps[:, 0:8], gbias[:, :], ALU.add),
                     reads=[pr, gb_r], writes=[gr])
                S.op("act", lambda e, g=g: e.activation(g[:, 20:24], g[:, 16:20], AF.Exp, scale=-1.0), reads=[gr], writes=[gr])
                S.op("act", lambda e, g=g: e.activation(g[:, 16:20], g[:, 20:24], AF.Ln, bias=1.0), reads=[gr], writes=[gr])
                S.op("dve", lambda e, g=g: e.tensor_scalar_mul(g[:, 16:20], g[:, 16:20], -1.0), reads=[gr], writes=[gr])
                ps2, pr2 = psr.next()
                S.op("pe", lambda e, ps2=ps2, g=g: e.matmul(ps2[:, 0:4], tri, g[:, 16:20], start=True, stop=True),
                     reads=[gr, cst_r], writes=[pr2])
                S.op("pe", lambda e, ps2=ps2, g=g: e.matmul(ps2[:, 4:8], ones, g[:, 16:20], start=True, stop=True),
                     reads=[gr, cst_r], writes=[pr2])
                S.op("dve", lambda e, g=g, ps2=ps2: e.tensor_tensor(g[:, 20:24], g[:, 12:16], ps2[:, 0:4], ALU.subtract),
                     reads=[gr, pr2], writes=[gr])
                S.op("act", lambda e, g=g: e.activation(g[:, 0:4], g[:, 20:24], AF.Exp), reads=[gr], writes=[gr])
                S.op("act", lambda e, g=g, ps2=ps2: e.activation(g[:, 4:8], ps2[:, 0:4], AF.Exp, scale=-1.0),
                     reads=[gr, pr2], writes=[gr])
                S.op("act", lambda e, g=g, ps2=ps2: e.activation(g[:, 8:12], ps2[:, 4:8], AF.Exp), reads=[gr, pr2], writes=[gr])
            for h in range(4):
                wb, wr = wv.next()
                S.dma("pool", wb[:, :, :], Wv[:, :, 2048 + h * 256:2048 + (h + 1) * 256], writes=[wr])
                for t in range(NT):
                    ps, pr = psr.next()
                    for kc in range(8):
                        S.op("pe", lambda e, ps=ps, kc=kc, t=t, wb=wb: e.matmul(
                            ps[:, 0:256], x1Tb[:, kc, t * 128:(t + 1) * 128], wb[:, kc, :], start=(kc == 0), stop=(kc == 7)),
                            reads=[wr, x1_r[kc]], writes=[pr])
                    S.op("act", lambda e, ps=ps, t=t, h=h: e.copy(vt[:, t, h, 0:256], ps[:, 0:256]),
                         reads=[pr], writes=[vt_r[t][h]])
            for t in range(NT):
                tc_ = slice(t * 128, (t + 1) * 128)
                g = gts[:, t, :]
                gr = gts_r[t]
                for h in range(4):
                    qc = [h * 2, h * 2 + 1]
                    kc_ = [8 + h * 2, 8 + h * 2 + 1]
                    ps, pr = psr.next()
                    for i in range(2):
                        S.op("pe", lambda e, ps=ps, i=i, kc_=kc_, qc=qc, tc_=tc_: e.matmul(
                            ps[:, 0:128], qkT[:, kc_[i], tc_], qkT[:, qc[i], tc_], start=(i == 0), stop=(i == 1)),
                            reads=[qk_r[kc_[i]], qk_r[qc[i]]], writes=[pr])
                    sm, smrr = smr.next()
                    S.op("dve", lambda e, sm=sm, ps=ps, h=h, g=g: e.scalar_tensor_tensor(
                        sm[:, :], ps[:, 0:128], g[:, h:h + 1], tri, ALU.mult, ALU.mult),
                        reads=[pr, gr, cst_r], writes=[smrr])
                    po, por = psr.next()
                    S.op("pe", lambda e, po=po, sm=sm, t=t, h=h: e.matmul(
                        po[:, 0:258], sm[:, :], vt[:, t, h, :], start=True, stop=False),
                        reads=[smrr, vt_r[t][h]], writes=[por])
                    for i in range(2):
                        S.op("pe", lambda e, po=po, i=i, h=h, qc=qc, tc_=tc_: e.matmul(
                            po[:, 0:258], qkT[:, qc[i], tc_], Cb[:, h, i, :], start=False, stop=(i == 1)),
                            reads=[qk_r[qc[i]], Cb_r[h]], writes=[por])
                    s6, s6r = sm6.next()
                    S.op("act", lambda e, s6=s6, po=po: e.activation(
                        s6[:, 0:1], po[:, 256:257], AF.Abs, scale=1.0 / 16.0), reads=[por], writes=[s6r])
                    S.op("dve", lambda e, s6=s6, g=g, h=h: e.tensor_tensor(s6[:, 0:1], s6[:, 0:1], g[:, 4 + h:5 + h], ALU.max),
                         reads=[s6r, gr], writes=[s6r])
                    S.op("dve", lambda e, s6=s6: e.reciprocal(s6[:, 1:2], s6[:, 0:1]), reads=[s6r], writes=[s6r])
                    hb, hbr = hh.next()
                    S.op("dve", lambda e, hb=hb, po=po, s6=s6: e.tensor_scalar(
                        hb[:, :], po[:, 0:256], s6[:, 1:2], 1.0 / 16.0, ALU.mult, ALU.mult), reads=[por, s6r], writes=[hbr])
                    S.op("dve", lambda e, hb=hb, s6=s6: e.bn_stats(s6[:, 2:8], hb[:, :]), reads=[hbr], writes=[s6r])
                    S.op("dve", lambda e, s6=s6: e.bn_aggr(s6[:, 0:2], s6[:, 2:8]), reads=[s6r], writes=[s6r])
                    S.op("act", lambda e, s6=s6: e.activation(s6[:, 2:3], s6[:, 1:2], AF.Sqrt, bias=epsln[:, 0:1]),
                         reads=[s6r, eps_r], writes=[s6r])
                    S.op("dve", lambda e, s6=s6: e.reciprocal(s6[:, 3:4], s6[:, 2:3]), reads=[s6r], writes=[s6r])
                    hn, hnr = hnb.next()
                    S.op("dve", lambda e, hn=hn, hb=hb, s6=s6: e.tensor_scalar(
                        hn[:, :], hb[:, :], s6[:, 0:1], s6[:, 3:4], ALU.subtract, ALU.mult), reads=[hbr, s6r], writes=[hnr])
                    for i in range(2):
                        pt, ptr = psr.next()
                        S.op("pe", lambda e, pt=pt, hn=hn, i=i: e.matmul(
                            pt[:, 0:128], hn[:, i * 128:(i + 1) * 128], identb, start=True, stop=True),
                            reads=[hnr, cstb_r], writes=[ptr])
                        S.op("dve", lambda e, pt=pt, h=h, i=i, tc_=tc_: e.scalar_tensor_tensor(
                            hmTb[:, h * 2 + i, tc_], pt[:, 0:128], pvc("mng", h * 2 + i), sigmo[:, h * 2 + i, tc_],
                            ALU.mult, ALU.mult), reads=[ptr, pv_r, sigmo_r[h * 2 + i]], writes=[hm_r[h * 2 + i]])
                    kk_, kkr = ktk.next()
                    for i in range(2):
                        pt, ptr = psr.next()
                        S.op("pe", lambda e, pt=pt, i=i, kc_=kc_, tc_=tc_: e.matmul(
                            pt[:, 0:128], qkT[:, kc_[i], tc_], identb, start=True, stop=True),
                            reads=[qk_r[kc_[i]], cstb_r], writes=[ptr])
                        S.op("act", lambda e, pt=pt, kk_=kk_, i=i, g=g, h=h: e.activation(
                            kk_[:, i * 128:(i + 1) * 128], pt[:, 0:128], AF.Identity, scale=g[:, h:h + 1]),
                            reads=[ptr, gr], writes=[kkr])
                    for i in range(2):
                        pc, pcr = psr.next()
                        S.op("pe", lambda e, pc=pc, kk_=kk_, i=i, t=t, h=h: e.matmul(
                            pc[:, 0:258], kk_[:, i * 128:(i + 1) * 128], vt[:, t, h, :], start=True, stop=True),
                            reads=[kkr, vt_r[t][h]], writes=[pcr])
                        ct, ctr = ctmp.next()
                        S.op("dve", lambda e, ct=ct, h=h, i=i, g=g: e.tensor_scalar_mul(ct[:, :], Cst[:, h, i, :], g[:, 8 + h:9 + h]),
                             reads=[C_r[h], gr], writes=[ctr])
                        S.op("dve", lambda e, ct=ct, pc=pc, h=h, i=i, g=g: e.scalar_tensor_tensor(
                            Cst[:, h, i, :], pc[:, 0:258], g[:, 8 + h:9 + h], ct[:, :], ALU.mult, ALU.add),
                            reads=[pcr, ctr, gr], writes=[C_r[h]])
                        S.op("act", lambda e, h=h, i=i: e.copy(Cb[:, h, i, :], Cst[:, h, i, :]), reads=[C_r[h]], writes=[Cb_r[h]])

        NCH = TB // 64
        rcar = sb([128, 26, 1]); rcar_r = [Res() for _ in range(26)]
        S.op("dve", lambda e: e.memset(rcar[:, :, :], 0.0), writes=rcar_r)
        rwork = cwork
        lowT = sb([128, 2, TB], BF16); low_r = [Res(), Res()]
        rtmp = Ring([(aT[:, 2 * i:2 * i + 2, :].rearrange("p a b -> p (a b)").bitcast(F32), Res()) for i in range(10)])
        ARbd = sb([128, NCH, 256], BF16); Bbd = sb([128, NCH, 128], BF16); Kbd = sb([128, NCH, 128], BF16)
        Vbd = sb([128, NCH, 128], BF16); Ynbd = sb([128, 128], BF16)
        bd_r = Res(); ynbd_r = Res()
        for tns in (ARbd, Bbd, Kbd, Vbd):
            S.op("dve", lambda e, tns=tns: e.memset(tns[:, :, :], 0.0), writes=[bd_r])
        S.op("dve", lambda e: e.memset(Ynbd[:, :], 0.0), writes=[ynbd_r])
        gam = sb([128, NCH]); gam_r = Res()
        Hst = sb([128, 8, 64]); H_r = [Res() for _ in range(8)]
        Hb = sb([128, 8, 64], BF16); Hb_r = [Res() for _ in range(8)]
        S.op("dve", lambda e: e.memset(Hst[:, :, :], 0.0), writes=H_r)
        S.op("dve", lambda e: e.memset(Hb[:, :, :], 0.0), writes=Hb_r)
        vst = sb([128, NCH, 64], BF16); vst_r = Res()
        btk = sb([128, NCH, 256], BF16); btk_r = Res()
        nm = ring(4, [128, 256], BF16)
        nsq = ring(11, [128, 128], BF16)
        ub = ring(4, [128, 64], BF16)
        uf = ring(4, [128, 64])
        gn6 = ring(4, [128, 8])
        htmp = ring(2, [128, 64])
        maskAR = cst[:, C_MUS:C_MUS + 256]; mask_r = cst_r
        mls = cst[:, C_MLS:C_MLS + 128]

        xTv = xT[:, :, :].rearrange("p a b -> p (a b)").bitcast(BF16)
        mAK = xTv[:, 0:NCH * 512].rearrange("p (n c) -> p n c", c=512)
        Xs = xTv[:, 4096:4096 + NCH * 128].rearrange("p (n c) -> p n c", c=128)
        xTbv = xTb[:, :, :].rearrange("p a b -> p (a b)")
        PW = [[xTbv[:, (b * 8 + n) * 128:(b * 8 + n + 1) * 128] for n in range(NCH)] for b in range(2)]
        PWT = [[xTbv[:, 2048 + (b * 8 + n) * 128:2048 + (b * 8 + n + 1) * 128] for n in range(NCH)] for b in range(2)]
        mA_r = [Res() for _ in range(NCH)]; mK_r = [Res() for _ in range(NCH)]; X_r = [Res() for _ in range(NCH)]
        PW_r = [[Res() for _ in range(NCH)] for _ in range(2)]; PWT_r = [[Res() for _ in range(NCH)] for _ in range(2)]

        def v3(ap):
            return ap.rearrange("p (n l) -> p n l", l=64)

        def shifted(ci, ps, pr):
            wk, wkr = rwork.next()
            S.op("act", lambda e, wk=wk: e.copy(wk[:, 0:1], rcar[:, ci, :]), reads=[rcar_r[ci]], writes=[wkr])
            S.op("act", lambda e, wk=wk, ps=ps: e.copy(wk[:, 1:1 + TB], ps[:, :]), reads=[pr], writes=[wkr])
            S.op("act", lambda e, wk=wk: e.copy(rcar[:, ci, :], wk[:, TB:TB + 1]), reads=[wkr], writes=[rcar_r[ci]])
            ta, tar = rtmp.next()
            S.op("dve", lambda e, wk=wk, ta=ta: e.tensor_scalar_mul(ta[:, :], wk[:, 1:1 + TB], pvc("omu", ci)),
                 reads=[wkr, pv_r], writes=[tar])
            S.op("dve", lambda e, wk=wk, ta=ta: e.scalar_tensor_tensor(
                ta[:, :], wk[:, 0:TB], pvc("mu", ci), ta[:, :], ALU.mult, ALU.add), reads=[wkr, pv_r, tar], writes=[tar])
            return ta, tar

        krw = int(os.environ.get("KRW", "9"))

        def rwkv(blk):
            ps, pr = projT(win, RC0 + 24 * 128, x1Tb, x1_r)
            ta, tar = shifted(24, ps, pr)
            S.op("act", lambda e, ta=ta: e.activation(lowT[0:64, 0, :], ta[0:64, :], AF.Tanh), reads=[tar], writes=[low_r[0]])
            S.op("act", lambda e, ta=ta: e.copy(lowT[64:128, 0, :], ta[64:128, :]), reads=[tar], writes=[low_r[0]])
            ps, pr = projT(win, RC0 + 25 * 128, x1Tb, x1_r)
            ta, tar = shifted(25, ps, pr)
            S.op("act", lambda e, ta=ta: e.activation(lowT[:, 1, :], ta[:, :], AF.Sigmoid), reads=[tar], writes=[low_r[1]])
            for p in range(8):
                cs = slice(p * 128, (p + 1) * 128)
                ps, pr = projT(win, RC0 + p * 128, x1Tb, x1_r)
                r_, r_r = shifted(p, ps, pr)
                ps, pr = projT(win, RC0 + 1024 + p * 128, x1Tb, x1_r)
                k_, k_r = shifted(8 + p, ps, pr)
                ps, pr = projT(win, RC0 + 2048 + p * 128, x1Tb, x1_r)
                v_, v_r = shifted(16 + p, ps, pr)
                pw, pwr = psr.next()
                S.op("pe", lambda e, pw=pw, cs=cs: e.matmul(pw[:, :], rw2a2[0:64, cs], lowT[0:64, 0, :], start=True, stop=True),
                     reads=[rw_r, low_r[0]], writes=[pwr])
                lw, lwr = rtmp.next()
                S.op("act", lambda e, lw=lw, pw=pw, p=p: e.activation(lw[:, :], pw[:, :], AF.Sigmoid, bias=pvc("w0", p)),
                     reads=[pwr, pv_r], writes=[lwr])
                S.op("dve", lambda e, lw=lw: e.tensor_scalar_mul(lw[:, :], lw[:, :], -float(np.exp(-0.5))), reads=[lwr], writes=[lwr])
                pa, par = psr.next()
                S.op("pe", lambda e, pa=pa, cs=cs: e.matmul(pa[:, :], rw2a2[64:128, cs], lowT[64:128, 0, :], start=True, stop=True),
                     reads=[rw_r, low_r[0]], writes=[par])
                a_, a_r = rtmp.next()
                S.op("act", lambda e, a_=a_, pa=pa, p=p: e.activation(a_[:, :], pa[:, :], AF.Sigmoid, bias=pvc("a0", p)),
                     reads=[par, pv_r], writes=[a_r])
                pg, pgr = psr.next()
                S.op("pe", lambda e, pg=pg, cs=cs: e.matmul(pg[:, :], rg2[:, cs], lowT[:, 1, :], start=True, stop=True),
                     reads=[rw_r, low_r[1]], writes=[pgr])
                g_, g_r = rtmp.next()
                S.op("act", lambda e, g_=g_, pg=pg: e.copy(g_[:, :], pg[:, :]), reads=[pgr], writes=[g_r])
                kk, kkr = rtmp.next()
                S.op("dve", lambda e, kk=kk, k_=k_, p=p: e.tensor_scalar_mul(kk[:, :], k_[:, :], pvc("kk", p)),
                     reads=[k_r, pv_r], writes=[kkr])
                sq, sqr = tB.next()
                S.op("act", lambda e, sq=sq, kk=kk: e.activation(sq[:, :], kk[:, :], AF.Square), reads=[kkr], writes=[sqr])
                pq, pqr = psr.next()
                S.op("pe", lambda e, pq=pq, sq=sq: e.matmul(pq[:, :], bob, sq[:, :], start=True, stop=True),
                     reads=[cstb_r, sqr], writes=[pqr])
                t1, t1r = rtmp.next()
                S.op("act", lambda e, t1=t1, pq=pq: e.activation(t1[:, :], pq[:, :], AF.Sqrt), reads=[pqr], writes=[t1r])
                S.op("dve", lambda e, t1=t1: e.tensor_scalar_max(t1[:, :], t1[:, :], 1e-12), reads=[t1r], writes=[t1r])
                S.op("dve", lambda e, t1=t1: e.reciprocal(t1[:, :], t1[:, :]), reads=[t1r], writes=[t1r])
                S.op("dve", lambda e, t1=t1, kk=kk: e.tensor_tensor(kk[:, :], kk[:, :], t1[:, :], ALU.mult),
                     reads=[t1r, kkr], writes=[kkr])
                S.op("dve", lambda e, t1=t1, a_=a_, p=p: e.tensor_scalar(t1[:, :], a_[:, :], 1.0, pvc("ka", p), ALU.subtract, ALU.mult),
                     reads=[a_r, pv_r], writes=[t1r])
                S.op("dve", lambda e, t1=t1, k_=k_: e.scalar_tensor_tensor(k_[:, :], t1[:, :], 1.0, k_[:, :], ALU.add, ALU.mult),
                     reads=[t1r, k_r], writes=[k_r])
                t2, t2r = tB.next()
                S.op("dve", lambda e, t2=t2, r_=r_, k_=k_, p=p: e.scalar_tensor_tensor(
                    t2[:, :], r_[:, :], pvc("rrk", p), k_[:, :], ALU.mult, ALU.mult), reads=[r_r, k_r, pv_r], writes=[t2r])
                pb, pbr = psr.next()
                S.op("pe", lambda e, pb=pb, t2=t2: e.matmul(pb[:, :], bob, t2[:, :], start=True, stop=True),
                     reads=[cstb_r, t2r], writes=[pbr])
                bon, bonr = rtmp.next()
                S.op("dve", lambda e, bon=bon, pb=pb, v_=v_: e.tensor_tensor(bon[:, :], pb[:, :], v_[:, :], ALU.mult),
                     reads=[pbr, v_r], writes=[bonr])
                cl, clr = rtmp.next()
                S.op("dve", lambda e, cl=cl, lw=lw: e.tensor_tensor_scan(cl[:, :], rst, lw[:, :], 0.0, ALU.mult, ALU.add),
                     reads=[lwr, cstb_r], writes=[clr])
                e1, e1r = tA.next()
                S.op("act", lambda e, e1=e1, cl=cl: e.activation(e1[:, :], cl[:, :], AF.Exp), reads=[clr], writes=[e1r])
                S.op("act", lambda e, e1=e1: e.copy(gam[:, :], v3(e1[:, :])[:, :, 63]), reads=[e1r], writes=[gam_r])
                for hf in range(2):
                    hs = slice(hf * 64, hf * 64 + 64)
                    S.op("dve", lambda e, hs=hs, hf=hf, r_=r_, e1=e1: e.tensor_tensor(
                        ARbd[hs, :, 128 + hf * 64:128 + hf * 64 + 64], v3(r_[hs, :]), v3(e1[hs, :]), ALU.mult),
                        reads=[r_r, e1r], writes=[bd_r])
                e2, e2r = tA.next()
                S.op("act", lambda e, e2=e2, cl=cl: e.activation(e2[:, :], cl[:, :], AF.Exp, scale=-1.0), reads=[clr], writes=[e2r])
                S.op("dve", lambda e, t1=t1, kk=kk, a_=a_: e.tensor_tensor(t1[:, :], kk[:, :], a_[:, :], ALU.mult),
                     reads=[kkr, a_r], writes=[t1r])
                for hf in range(2):
                    hs = slice(hf * 64, hf * 64 + 64)
                    S.op("dve", lambda e, hs=hs, hf=hf, t1=t1, e2=e2: e.tensor_tensor(
                        Bbd[hs, :, hf * 64:hf * 64 + 64], v3(t1[hs, :]), v3(e2[hs, :]), ALU.mult), reads=[t1r, e2r], writes=[bd_r])
                    S.op("dve", lambda e, hs=hs, hf=hf, k_=k_, e2=e2: e.tensor_tensor(
                        Kbd[hs, :, hf * 64:hf * 64 + 64], v3(k_[hs, :]), v3(e2[hs, :]), ALU.mult), reads=[k_r, e2r], writes=[bd_r])
                    S.op("act", lambda e, hs=hs, hf=hf, v_=v_: e.copy(Vbd[hs, :, hf * 64:hf * 64 + 64], v3(v_[hs, :])),
                         reads=[v_r], writes=[bd_r])
                S.op("dve", lambda e, cl=cl, lw=lw: e.tensor_tensor(cl[:, :], cl[:, :], lw[:, :], ALU.subtract),
                     reads=[clr, lwr], writes=[clr])
                e3, e3r = tA.next()
                S.op("act", lambda e, e3=e3, cl=cl: e.activation(e3[:, :], cl[:, :], AF.Exp), reads=[clr], writes=[e3r])
                for hf in range(2):
                    hs = slice(hf * 64, hf * 64 + 64)
                    S.op("dve", lambda e, hs=hs, hf=hf, kk=kk, e3=e3: e.scalar_tensor_tensor(
                        ARbd[hs, :, hf * 64:hf * 64 + 64], v3(kk[hs, :]), -1.0, v3(e3[hs, :]), ALU.mult, ALU.mult),
                        reads=[kkr, e3r], writes=[bd_r])
                if krw <= 1:
                    S.op("dve", lambda e, p=p: e.memset(yrTb[:, p, :], 0.0), writes=[yr_r[p]])
                    continue
                pv_, pvr_ = psr.next()
                for n in range(NCH):
                    S.op("pe", lambda e, n=n, pv_=pv_: e.matmul(pv_[:, n * 64:(n + 1) * 64], Vbd[:, n, :], istb, start=True, stop=True),
                         reads=[bd_r, cstb_r], writes=[pvr_])
                S.op("act", lambda e, pv_=pv_: e.copy(vst[:, :, :], v3(pv_[:, :])), reads=[pvr_], writes=[vst_r])
                for n0 in range(0, NCH, 2):
                    pt, ptr = psr.next()
                    for n in (n0, n0 + 1):
                        o = (n - n0) * 256
                        S.op("pe", lambda e, n=n, pt=pt, o=o: e.matmul(pt[:, o:o + 128], Bbd[:, n, :], identb, start=True, stop=True),
                             reads=[bd_r, cstb_r], writes=[ptr])
                        S.op("pe", lambda e, n=n, pt=pt, o=o: e.matmul(pt[:, o + 128:o + 256], Kbd[:, n, :], identb, start=True, stop=True),
                             reads=[bd_r, cstb_r], writes=[ptr])
                    S.op("act", lambda e, n0=n0, pt=pt: e.copy(btk[:, n0:n0 + 2, :], pt[:, :].rearrange("p (n l) -> p n l", l=256)),
                         reads=[ptr], writes=[btk_r])
                if krw <= 2:
                    S.op("dve", lambda e, p=p: e.memset(yrTb[:, p, :], 0.0), writes=[yr_r[p]])
                    continue
                pyo, pyor = pyo_bank
                for g0 in range(0, NCH, 4):
                    G = list(range(g0, min(g0 + 4, NCH)))
                    pAs = {}
                    for n in G:
                        pA, pAr = psr.next()
                        pAs[n] = (pA, pAr)
                        S.op("pe", lambda e, pA=pA, n=n: e.matmul(pA[:, 0:256], Bbd[:, n, :], ARbd[:, n, :], start=True, stop=True),
                             reads=[bd_r], writes=[pAr])
                        S.op("pe", lambda e, pA=pA, n=n: e.matmul(pA[:, 256:512], Kbd[:, n, :], ARbd[:, n, :], start=True, stop=True),
                             reads=[bd_r], writes=[pAr])
                    for n in G:
                        pA, pAr = pAs[n]
                        S.op("dve", lambda e, pA=pA, n=n: e.tensor_tensor(mAK[:, n, 0:256], pA[:, 0:256], maskAR[:, :], ALU.mult),
                             reads=[pAr, mask_r], writes=[mA_r[n]])
                        S.op("dve", lambda e, pA=pA, n=n: e.tensor_tensor(mAK[:, n, 256:512], pA[:, 256:512], maskAR[:, :], ALU.mult),
                             reads=[pAr, mask_r], writes=[mK_r[n]])
                    pTs = {}
                    for n in G:
                        pT, pTr = psr.next()
                        pTs[n] = (pT, pTr)
                        S.op("pe", lambda e, pT=pT, n=n: e.matmul(pT[:, 0:128], ARbd[:, n, 0:128], Bbd[:, n, :], start=True, stop=True),
                             reads=[bd_r], writes=[pTr])
                    for n in G:
                        pT, pTr = pTs[n]
                        S.op("dve", lambda e, pT=pT, n=n: e.tensor_tensor(PWT[0][n], pT[:, 0:128], mls, ALU.mult),
                             reads=[pTr, cst_r], writes=[PWT_r[0][n]])
                        S.op("dve", lambda e, n=n: e.tensor_tensor(Xs[:, n, :], mAK[:, n, 0:128], identb, ALU.add),
                             reads=[mA_r[n], cstb_r], writes=[X_r[n]])
                    for j in range(1, 6):
                        b0, b1 = (j - 1) % 2, j % 2
                        p2s = {}
                        for n in G:
                            cur = mAK[:, n, 0:128] if j == 1 else PW[b0][n]
                            curr = mA_r[n] if j == 1 else PW_r[b0][n]
                            ct, ctr_ = PWT[b0][n], PWT_r[b0][n]
                            p2, p2r = psr.next()
                            p2s[n] = (p2, p2r)
                            if j < 5:
                                S.op("pe", lambda e, p2=p2, cur=cur, ct=ct: e.matmul(p2[:, 0:128], ct, cur, start=True, stop=True),
                                     reads=[curr, ctr_], writes=[p2r])
                            S.op("pe", lambda e, p2=p2, cur=cur, ct=ct: e.matmul(p2[:, 128:256], cur, ct, start=True, stop=True),
                                 reads=[curr, ctr_], writes=[p2r])
                        for n in G:
                            p2, p2r = p2s[n]
                            S.op("dve", lambda e, p2=p2, n=n, b1=b1: e.tensor_copy(PWT[b1][n], p2[:, 128:256]),
                                 reads=[p2r], writes=[PWT_r[b1][n]])
                            if j < 5:
                                S.op("act", lambda e, p2=p2, n=n, b1=b1: e.copy(PW[b1][n], p2[:, 0:128]),
                                     reads=[p2r], writes=[PW_r[b1][n]])
                        pxs = {}
                        for n in G:
                            px, pxr = psr.next()
                            pxs[n] = (px, pxr)
                            S.op("pe", lambda e, px=px, n=n, b1=b1: e.matmul(px[:, 0:128], PWT[b1][n], Xs[:, n, :], start=True, stop=True),
                                 reads=[PWT_r[b1][n], X_r[n]], writes=[pxr])
                        for n in G:
                            px, pxr = pxs[n]
                            S.op("dve", lambda e, px=px, n=n: e.tensor_tensor(Xs[:, n, :], px[:, 0:128], Xs[:, n, :], ALU.add),
                                 reads=[pxr, X_r[n]], writes=[X_r[n]])
                for n in range(NCH):
                    pw_, pw_r = psr.next()
                    S.op("pe", lambda e, pw_=pw_, n=n, p=p: e.matmul(pw_[:, 0:64], ARbd[:, n, 0:128], Hb[:, p, :], start=True, stop=False),
                         reads=[bd_r, Hb_r[p]], writes=[pw_r])
                    S.op("pe", lambda e, pw_=pw_, n=n: e.matmul(pw_[:, 0:64], mAK[:, n, 256:384], vst[:, n, :], start=False, stop=True),
                         reads=[mK_r[n], vst_r], writes=[pw_r])
                    w_b, wbr = ub.next()
                    S.op("act", lambda e, w_b=w_b, pw_=pw_: e.copy(w_b[:, :], pw_[:, 0:64]), reads=[pw_r], writes=[wbr])
                    pu, pur_ = psr.next()
                    S.op("pe", lambda e, pu=pu, n=n, w_b=w_b: e.matmul(pu[:, 0:64], Xs[:, n, :], w_b[:, :], start=True, stop=True),
                         reads=[X_r[n], wbr], writes=[pur_])
                    u_b, ubr = ub.next()
                    S.op("dve", lambda e, u_b=u_b, pu=pu: e.tensor_copy(u_b[:, :], pu[:, 0:64]), reads=[pur_], writes=[ubr])
                    ph, phr = psr.next()
                    S.op("pe", lambda e, ph=ph, n=n, u_b=u_b: e.matmul(ph[:, 0:64], btk[:, n, 0:128], u_b[:, :], start=True, stop=False),
                         reads=[btk_r, ubr], writes=[phr])
                    S.op("pe", lambda e, ph=ph, n=n: e.matmul(ph[:, 0:64], btk[:, n, 128:256], vst[:, n, :], start=False, stop=True),
                         reads=[btk_r, vst_r], writes=[phr])
                    py, pyr = psr.next()
                    S.op("pe", lambda e, py=py, n=n, p=p: e.matmul(py[:, 0:64], ARbd[:, n, 128:256], Hb[:, p, :], start=True, stop=False),
                         reads=[bd_r, Hb_r[p]], writes=[pyr])
                    S.op("pe", lambda e, py=py, n=n, u_b=u_b: e.matmul(py[:, 0:64], mAK[:, n, 128:256], u_b[:, :], start=False, stop=False),
                         reads=[mA_r[n], ubr], writes=[pyr])
                    S.op("pe", lambda e, py=py, n=n: e.matmul(py[:, 0:64], mAK[:, n, 384:512], vst[:, n, :], start=False, stop=True),
                         reads=[mK_r[n], vst_r], writes=[pyr])
                    ht, htr = htmp.next()
                    S.op("dve", lambda e, ht=ht, ph=ph, p=p: e.tensor_tensor(ht[:, :], ph[:, 0:64], Hst[:, p, :], ALU.add),
                         reads=[phr, H_r[p]], writes=[htr])
                    S.op("act", lambda e, ht=ht, p=p, n=n: e.activation(Hb[:, p, :], ht[:, :], AF.Identity, scale=gam[:, n:n + 1]),
                         reads=[htr, gam_r], writes=[Hb_r[p]])
                    S.op("act", lambda e, ht=ht, p=p, n=n: e.activation(Hst[:, p, :], ht[:, :], AF.Identity, scale=gam[:, n:n + 1]),
                         reads=[htr, gam_r], writes=[H_r[p]])
                    g6, g6r = gn6.next()
                    S.op("dve", lambda e, g6=g6, py=py: e.bn_stats(g6[:, 2:8], py[:, 0:64]), reads=[pyr], writes=[g6r])
                    S.op("dve", lambda e, g6=g6: e.bn_aggr(g6[:, 0:2], g6[:, 2:8]), reads=[g6r], writes=[g6r])
                    S.op("act", lambda e, g6=g6: e.activation(g6[:, 2:3], g6[:, 1:2], AF.Sqrt, bias=epsgn[:, 0:1]),
                         reads=[g6r, eps_r], writes=[g6r])
                    S.op("dve", lambda e, g6=g6: e.reciprocal(g6[:, 3:4], g6[:, 2:3]), reads=[g6r], writes=[g6r])
                    for hf in range(2):
                        hs = slice(hf * 64, hf * 64 + 64)
                        S.op("dve", lambda e, hs=hs, hf=hf, py=py, g6=g6: e.tensor_scalar(
                            Ynbd[hs, hf * 64:hf * 64 + 64], py[hs, 0:64], g6[hs, 0:1], g6[hs, 3:4], ALU.subtract, ALU.mult),
                            reads=[pyr, g6r], writes=[ynbd_r])
                    S.op("pe", lambda e, n=n, pyo=pyo: e.matmul(pyo[:, n * 64:(n + 1) * 64], Ynbd[:, :], istb, start=True, stop=True),
                         reads=[ynbd_r, cstb_r], writes=[pyor])
                if krw <= 5:
                    S.op("dve", lambda e, p=p: e.memset(yrTb[:, p, :], 0.0), writes=[yr_r[p]])
                    continue
                S.op("dve", lambda e, t1=t1, pyo=pyo, p=p: e.tensor_scalar(
                    t1[:, :], pyo[:, :], pvc("gng", p), pvc("gnb", p), ALU.mult, ALU.add), reads=[pyor, pv_r], writes=[t1r])
                S.op("dve", lambda e, t1=t1, bon=bon: e.tensor_tensor(t1[:, :], t1[:, :], bon[:, :], ALU.add),
                     reads=[t1r, bonr], writes=[t1r])
                S.op("dve", lambda e, t1=t1, g_=g_, p=p: e.tensor_tensor(yrTb[:, p, :], t1[:, :], g_[:, :], ALU.mult),
                     reads=[t1r, g_r], writes=[yr_r[p]])

        def merge_out(blk):
            for c in range(8):
                ps, pr = projT(win, 7432 + c * 128, x1Tb, x1_r)
                S.op("act", lambda e, ps=ps, c=c: e.activation(sga[:, c, :], ps[:, :], AF.Sigmoid), reads=[pr], writes=[sga_r[c]])
                ps, pr = projT(win, 8456 + c * 128, x1Tb, x1_r)
                S.op("act", lambda e, ps=ps, c=c: e.activation(sgb[:, c, :], ps[:, :], AF.Sigmoid), reads=[pr], writes=[sgb_r[c]])
            for c in range(8):
                psa, par = projT(wa_d, c * 128, hmTb, hm_r)
                psb, pbr = projT(wb_d, c * 128, yrTb, yr_r)
                ta, tar = tA.next()
                S.op("dve", lambda e, ta=ta, psa=psa, c=c: e.tensor_tensor(ta[:, :], psa[:, :], sga[:, c, :], ALU.mult),
                     reads=[par, sga_r[c]], writes=[tar])
                tb, tbr = tA.next()
                S.op("dve", lambda e, tb=tb, psb=psb, c=c: e.tensor_tensor(tb[:, :], psb[:, :], sgb[:, c, :], ALU.mult),
                     reads=[pbr, sgb_r[c]], writes=[tbr])
                S.op("dve", lambda e, ta=ta, tb=tb, c=c: e.tensor_tensor(mgTb[:, c, :], ta[:, :], tb[:, :], ALU.add),
                     reads=[tar, tbr], writes=[mg_r[c]])
            for c in range(8):
                ps, pr = projT(wo_d, c * 128, mgTb, mg_r)
                S.op("dve", lambda e, ps=ps, c=c: e.scalar_tensor_tensor(
                    zT[:, c, :], x1T[:, c, :], ALPHA, ps[:, :], ALU.mult, ALU.add), reads=[pr, x1_r[c]], writes=[zT_r[c]])
            layer_norm("ln2_g", "ln2_b", epsln, x2T, x2Tb, x2_r)

        stage = int(os.environ.get("KSTAGE", "9"))
        for blk in range(nblk):
            load_xT(blk)
            ffn_ln(xT, xTb, xT_r, w1g, w1u, w1d, "ln1_g", "ln1_b", x1T, x1Tb, x1_r)
            if stage == 1:
                store_T(blk, x1T, x1_r)
                continue
            skip = os.environ.get("KSKIP", "")
            if "mlstm" in skip:
                S.op("dve", lambda e: e.memset(hmTb[:, :, :], 0.0), writes=hm_r)
            else:
                mlstm(blk)
            if "rwkv" in skip:
                S.op("dve", lambda e: e.memset(yrTb[:, :, :], 0.0), writes=yr_r)
            else:
                rwkv(blk)
            merge_out(blk)
            if stage == 2:
                store_T(blk, x2T, x2_r)
                continue
            ffn_ln(x2T, x2Tb, x2_r, w2g, w2u, w2d, "ln3_g", "ln3_b", x3T, x3Tb, x3_r)
            store_T(blk, x3T, x3_r)

        S.op("sp", lambda e: e.nop(), reads=[out_r])
        S.emit()
    return nc


def _consts():
    c = np.zeros((128, NCST), np.float32)
    i = np.arange(128)
    c[:, C_ID:C_ID + 128] = np.eye(128)
    c[:, C_OD:C_OD + 128] = 1.0 / 1024.0
    c[:, C_TRI:C_TRI + 128] = (i[:, None] <= i[None, :])
    c[:, C_ONE:C_ONE + 128] = 1.0
    c[:, C_MUS:C_MUS + 128] = (i[:, None] < i[None, :])
    c[:, C_MLS:C_MLS + 128] = (i[:, None] > i[None, :])
    c[:, C_IST:C_IST + 64] = np.concatenate([np.eye(64), np.eye(64)], axis=0)
    c[:, C_BO:C_BO + 128] = ((i[:, None] // 64) == (i[None, :] // 64))
    return c


def _fm(v):
    v = np.asarray(v, np.float32).reshape(-1, 128)
    return np.ascontiguousarray(v.T)


def kernel(**inp):
    nblk = int(os.environ.get("KNBLK", str(SEQ // TB)))
    x = np.asarray(inp["x"], np.float32)
    pv = np.zeros((128, NPV), np.float32)

    def put(name, arr):
        a = _fm(arr)
        pv[:, PV[name]:PV[name] + a.shape[1]] = a

    for n in ("ln1_g", "ln1_b", "ln2_g", "ln2_b", "ln3_g", "ln3_b"):
        put(n, inp[n][0])
    cw = inp["m_conv_w"][0]
    for j in range(4):
        put("cw%d" % j, cw[j])
    put("cb", inp["m_conv_b"][0])
    put("mng", inp["m_norm_g"][0])
    mu = inp["r_mu"][0]
    put("mu", mu)
    put("omu", 1.0 - mu)
    put("w0", inp["r_w0"][0]); put("a0", inp["r_a0"][0]); put("kk", inp["r_k_k"][0]); put("ka", inp["r_k_a"][0])
    put("rrk", inp["r_r_k"][0].reshape(-1)); put("gng", inp["r_gn_g"][0]); put("gnb", inp["r_gn_b"][0])
    gbias = np.zeros((128, 8), np.float32)
    gbias[:, 0:4] = inp["m_i_bias"][0][None, :]
    gbias[:, 4:8] = inp["m_f_bias"][0][None, :]
    shared = {"cst": _consts(), "pv": pv, "gbias": gbias,
              "rst": np.ascontiguousarray(np.broadcast_to((np.arange(512)[None, :] % 64 != 0), (128, 512)).astype(np.float32))}
    for n in ("ffn1_w_gate", "ffn1_w_up", "ffn1_w_down", "ffn2_w_gate", "ffn2_w_up", "ffn2_w_down",
              "w_in", "w_branch_a", "w_branch_b", "w_out", "r_w2", "r_a2", "r_g2"):
        shared[n] = np.ascontiguousarray(inp[n][0], dtype=np.float32)
    in_maps = []
    for c in range(NCORES):
        m = dict(shared)
        m["x"] = np.ascontiguousarray(x[c % 2])
        in_maps.append(m)
    nc = build(nblk)
    ncr = int(os.environ.get("KCORES", str(NCORES)))
    c0 = int(os.environ.get("KCORE0", "0"))
    res = run_bass_kernel_spmd(nc, in_maps[:ncr], core_ids=list(range(c0, c0 + ncr)))
    out = np.stack([np.asarray(res.results[b % ncr]["out"]) for b in range(2)], axis=0)
    return out.reshape(2, SEQ, D).astype(np.float32)
```

```python
import contextlib
import os
import numpy as np
import concourse.bass as bass
import concourse.mybir as mybir
from concourse.bass_utils import run_bass_kernel_spmd

F32 = mybir.dt.float32
BF16 = mybir.dt.bfloat16
AF = mybir.ActivationFunctionType
ALU = mybir.AluOpType

D = 1024
DFF = 2816
NJ = DFF // 128
SEQ = 8192
TB = 512
ALPHA = 2.0 ** 0.25
LN_EPS = 1e-5
GN_EPS = 64e-5
NCORES = 8
WCOLS = 9480
RC0 = 4104
SAME_ENGINE_WAITS = os.environ.get("KSEW", "act,dve,pool,sp").split(",")


class Res:
    __slots__ = ("w", "r", "excl")

    def __init__(self, excl=False):
        self.w = None
        self.r = {}
        self.excl = excl


class Sched:
    NDS = 40

    def __init__(self, nc, stack):
        self.nc = nc
        self.E = {"pe": nc.tensor, "act": nc.scalar, "dve": nc.vector, "pool": nc.gpsimd, "sp": nc.sync}
        self.q = {k: [] for k in self.E}
        self.cnt = {k: 0 for k in self.E}
        self.esem = {k: stack.enter_context(nc.semaphore("s_" + k)) for k in self.E}
        self.dsem = [stack.enter_context(nc.semaphore("d%d" % i)) for i in range(self.NDS)]
        self.dval = [0] * self.NDS
        self.dnext = {"sp": 0, "pool": 0}
        self.dpool = {"sp": list(range(0, 8)), "pool": list(range(8, self.NDS))}
        self.waited = {k: {} for k in self.E}

    def _deps(self, eng, reads, writes, extra=()):
        deps = {}

        def add(t):
            if t is None:
                return
            k = (t[0], t[1])
            if deps.get(k, 0) < t[2]:
                deps[k] = t[2]

        for r in reads:
            add(r.w)
        for w in writes:
            add(w.w)
            for k, v in w.r.items():
                add((k[0], k[1], v))
        for t in extra:
            add(t)
        waits = []
        for k, v in deps.items():
            if k[0] == "e" and k[1] == eng and (eng == "pe" or eng not in SAME_ENGINE_WAITS):
                continue
            if self.waited[eng].get(k, 0) >= v:
                continue
            self.waited[eng][k] = v
            waits.append((k, v))
        return waits

    def _mark(self, tok, reads, writes):
        k = (tok[0], tok[1])
        for r in reads:
            if r.r.get(k, 0) < tok[2]:
                r.r[k] = tok[2]
        for w in writes:
            w.w = tok
            w.r = {}

    def op(self, eng, fn, reads=(), writes=()):
        ex = [r for r in reads if r.excl]
        if ex:
            writes = list(writes) + ex
        waits = self._deps(eng, reads, writes)
        self.cnt[eng] += 1
        tok = ("e", eng, self.cnt[eng])
        self.q[eng].append((waits, fn, None))
        self._mark(tok, reads, writes)
        return tok

    def dma(self, qeng, out, in_, reads=(), writes=()):
        pool = self.dpool[qeng]
        j = pool[self.dnext[qeng]]
        self.dnext[qeng] = (self.dnext[qeng] + 1) % len(pool)
        prev = ("d", j, self.dval[j]) if self.dval[j] else None
        waits = self._deps(qeng, reads, writes, extra=(prev,) if prev else ())
        self.dval[j] += 16
        tok = ("d", j, self.dval[j])
        self.q[qeng].append((waits, (out, in_), j))
        self._mark(tok, reads, writes)
        return tok

    def emit(self):
        nc = self.nc
        with nc.Block() as block:
            def run(kind):
                def body(eng):
                    for waits, fn, dj in self.q[kind]:
                        for k, v in waits:
                            sem = self.esem[k[1]] if k[0] == "e" else self.dsem[k[1]]
                            eng.wait_ge(sem, v)
                        if dj is None:
                            fn(eng).then_inc(self.esem[kind], 1)
                        else:
                            eng.dma_start(out=fn[0], in_=fn[1]).then_inc(self.dsem[dj], 16)
                return body
            block.tensor(run("pe"))
            block.scalar(run("act"))
            block.vector(run("dve"))
            block.gpsimd(run("pool"))
            block.sync(run("sp"))


class Ring:
    def __init__(self, items):
        self.items = items
        self.i = 0

    def next(self):
        it = self.items[self.i]
        self.i = (self.i + 1) % len(self.items)
        return it


PV = {}
_o = 0
for _n, _w in [("ln1_g", 8), ("ln1_b", 8), ("ln2_g", 8), ("ln2_b", 8), ("ln3_g", 8), ("ln3_b", 8),
               ("cw0", 16), ("cw1", 16), ("cw2", 16), ("cw3", 16), ("cb", 16), ("mng", 8),
               ("mu", 26), ("omu", 26), ("w0", 8), ("a0", 8), ("kk", 8), ("ka", 8), ("rrk", 8),
               ("gng", 8), ("gnb", 8)]:
    PV[_n] = _o
    _o += _w
NPV = _o
C_ID, C_OD, C_MUS, C_TRI, C_ONE, C_MLS, C_IST, C_BO, NCST = 0, 128, 256, 384, 512, 640, 768, 832, 960


def build(nblk):
    nc = bass.Bass("TRN2", target_bir_lowering=False)

    def din(name, shape):
        return nc.dram_tensor(name, list(shape), F32, kind="ExternalInput").ap()

    x_d = din("x", [SEQ, D])
    cst_d = din("cst", [128, NCST])
    pv_d = din("pv", [128, NPV])
    gb_d = din("gbias", [128, 8])
    rst_d = din("rst", [128, 512])
    w1g = din("ffn1_w_gate", [D, DFF]); w1u = din("ffn1_w_up", [D, DFF]); w1d = din("ffn1_w_down", [DFF, D])
    w2g = din("ffn2_w_gate", [D, DFF]); w2u = din("ffn2_w_up", [D, DFF]); w2d = din("ffn2_w_down", [DFF, D])
    win = din("w_in", [D, WCOLS])
    wa_d = din("w_branch_a", [D, D]); wb_d = din("w_branch_b", [D, D]); wo_d = din("w_out", [D, D])
    rw2_d = din("r_w2", [64, D]); ra2_d = din("r_a2", [64, D]); rg2_d = din("r_g2", [128, D])
    out_d = nc.dram_tensor("out", [SEQ, D], F32, kind="ExternalOutput").ap()

    with contextlib.ExitStack() as st:
        S = Sched(nc, st)
        _n = [0]

        def sb(shape, dt=F32):
            _n[0] += 1
            return st.enter_context(nc.sbuf_tensor("sb%d" % _n[0], list(shape), dt))

        def ring(n, shape, dt=F32):
            return Ring([(sb(shape, dt), Res()) for _ in range(n)])

        banks = [st.enter_context(nc.psum_tensor("ps%d" % i, [128, 512], F32)) for i in range(8)]
        psr = Ring([(banks[i], Res(True)) for i in range(7)])
        pyo_bank = (banks[7], Res(True))

        cst = sb([128, NCST]); cst_r = Res()
        cstb = sb([128, NCST], BF16); cstb_r = Res()
        pv = sb([128, NPV]); pv_r = Res()
        gbias = sb([128, 8]); gb_r = Res()
        S.dma("sp", cst[:, :], cst_d[:, :], writes=[cst_r])
        S.dma("sp", pv[:, :], pv_d[:, :], writes=[pv_r])
        S.dma("sp", gbias[:, :], gb_d[:, :], writes=[gb_r])
        S.op("act", lambda e: e.copy(cstb[:, :], cst[:, :]), reads=[cst_r], writes=[cstb_r])
        ident = cst[:, C_ID:C_ID + 128]
        identb = cstb[:, C_ID:C_ID + 128]
        onesdb = cstb[:, C_OD:C_OD + 128]
        tri = cst[:, C_TRI:C_TRI + 128]
        ones = cst[:, C_ONE:C_ONE + 128]
        istb = cstb[:, C_IST:C_IST + 64]
        bob = cstb[:, C_BO:C_BO + 128]
        rstb = sb([128, 512], BF16)
        S.dma("pool", rstb[:, :], rst_d[:, :], writes=[cstb_r])
        rst = rstb[:, :]
        epsln = sb([128, 1]); eps4 = sb([128, 1]); epsgn = sb([128, 1]); eps_r = Res()
        S.op("dve", lambda e: e.memset(epsln[:, :], LN_EPS), writes=[eps_r])
        S.op("dve", lambda e: e.memset(eps4[:, :], 4 * LN_EPS), writes=[eps_r])
        S.op("dve", lambda e: e.memset(epsgn[:, :], GN_EPS), writes=[eps_r])

        def pvc(name, i=0):
            c = PV[name] + i
            return pv[:, c:c + 1]

        rw2a2 = sb([128, D], BF16); rw_r = Res()
        rg2 = sb([128, D], BF16)
        S.dma("pool", rw2a2[0:64, :], rw2_d[:, :], writes=[rw_r])
        S.dma("pool", rw2a2[64:128, :], ra2_d[:, :], writes=[rw_r])
        S.dma("pool", rg2[:, :], rg2_d[:, :], writes=[rw_r])

        xT = sb([128, 8, TB]); xTb = sb([128, 8, TB], BF16); xT_r = [Res() for _ in range(8)]
        x1T = sb([128, 8, TB]); x1Tb = sb([128, 8, TB], BF16); x1_r = [Res() for _ in range(8)]
        x2T = xT; x2Tb = xTb; x2_r = xT_r
        x3T = xT; x3Tb = xTb; x3_r = xT_r
        aT = sb([128, NJ, TB], BF16); aT_r = [Res() for _ in range(NJ)]
        zT = sb([128, 8, TB]); zT_r = [Res() for _ in range(8)]
        xtok = ring(1, [128, D])
        otok = xtok
        wgu = ring(2, [128, 2, 8, 128], BF16)
        wdb = ring(2, [128, 512], BF16)
        wpj = ring(2, [128, 8, 128], BF16)
        tA = ring(3, [128, TB])
        tB = ring(2, [128, TB], BF16)
        mean_r = Res(); rstd_r = Res()
        out_r = Res()

        def load_xT(blk):
            for t in range(TB // 128):
                xt, xr = xtok.next()
                r0 = blk * TB + t * 128
                S.dma("sp", xt[:, :], x_d[r0:r0 + 128, :], writes=[xr])
                for half in range(2):
                    ps, pr = psr.next()
                    for q in range(4):
                        kc = half * 4 + q
                        S.op("pe", lambda e, ps=ps, xt=xt, kc=kc, q=q: e.matmul(
                            ps[:, q * 128:(q + 1) * 128], xt[:, kc * 128:(kc + 1) * 128], ident,
                            start=True, stop=True), reads=[xr, cst_r], writes=[pr])
                    psv = ps[:, :].rearrange("p (q n) -> p q n", q=4)
                    S.op("act", lambda e, psv=psv, half=half, t=t: e.copy(
                        xT[:, half * 4:half * 4 + 4, t * 128:(t + 1) * 128], psv),
                        reads=[pr], writes=xT_r[half * 4:half * 4 + 4])
                    S.op("dve", lambda e, psv=psv, half=half, t=t: e.tensor_copy(
                        xTb[:, half * 4:half * 4 + 4, t * 128:(t + 1) * 128], psv),
                        reads=[pr], writes=xT_r[half * 4:half * 4 + 4])

        def layer_norm(gname, bname, eps_t, outT, outTb, out_rs):
            psm, pmr = psr.next()
            pss, ssr = psr.next()
            for dc in range(8):
                tb, tr = tB.next()
                S.op("act", lambda e, tb=tb, dc=dc: e.copy(tb[:, :], zT[:, dc, :]), reads=[zT_r[dc]], writes=[tr])
                S.op("pe", lambda e, tb=tb, dc=dc: e.matmul(psm[:, :], onesdb, tb[:, :], start=(dc == 0), stop=(dc == 7)),
                     reads=[cstb_r, tr], writes=[pmr])
                tb2, tr2 = tB.next()
                S.op("act", lambda e, tb2=tb2, dc=dc: e.activation(tb2[:, :], zT[:, dc, :], AF.Square),
                     reads=[zT_r[dc]], writes=[tr2])
                S.op("pe", lambda e, tb2=tb2, dc=dc: e.matmul(pss[:, :], onesdb, tb2[:, :], start=(dc == 0), stop=(dc == 7)),
                     reads=[cstb_r, tr2], writes=[ssr])
            S.op("act", lambda e: e.copy(mean_sb[:, :], psm[:, :]), reads=[pmr], writes=[mean_r])
            t1, r1 = tA.next()
            S.op("dve", lambda e, t1=t1: e.tensor_tensor(t1[:, :], mean_sb[:, :], mean_sb[:, :], ALU.mult),
                 reads=[mean_r], writes=[r1])
            t2, r2 = tA.next()
            S.op("dve", lambda e, t1=t1, t2=t2: e.tensor_tensor(t2[:, :], pss[:, :], t1[:, :], ALU.subtract),
                 reads=[ssr, r1], writes=[r2])
            S.op("dve", lambda e, t2=t2: e.tensor_scalar_max(t2[:, :], t2[:, :], 0.0), reads=[r2], writes=[r2])
            t3, r3 = tA.next()
            S.op("act", lambda e, t2=t2, t3=t3: e.activation(t3[:, :], t2[:, :], AF.Sqrt, bias=eps_t[:, 0:1]),
                 reads=[r2, eps_r], writes=[r3])
            S.op("dve", lambda e, t3=t3: e.reciprocal(rstd_sb[:, :], t3[:, :]), reads=[r3], writes=[rstd_r])
            for dc in range(8):
                ta, tar = tA.next()
                S.op("dve", lambda e, ta=ta, dc=dc: e.tensor_tensor(ta[:, :], zT[:, dc, :], mean_sb[:, :], ALU.subtract),
                     reads=[zT_r[dc], mean_r], writes=[tar])
                S.op("dve", lambda e, ta=ta: e.tensor_tensor(ta[:, :], ta[:, :], rstd_sb[:, :], ALU.mult),
                     reads=[tar, rstd_r], writes=[tar])
                S.op("act", lambda e, ta=ta, dc=dc: e.activation(
                    outT[:, dc, :], ta[:, :], AF.Identity, scale=pvc(gname, dc), bias=pvc(bname, dc)),
                    reads=[tar, pv_r], writes=[out_rs[dc]])
                S.op("act", lambda e, ta=ta, dc=dc: e.activation(
                    outTb[:, dc, :], ta[:, :], AF.Identity, scale=pvc(gname, dc), bias=pvc(bname, dc)),
                    reads=[tar, pv_r], writes=[out_rs[dc]])

        def ffn_ln(inT, inTb, in_rs, Wg, Wu, Wd, gname, bname, outT, outTb, out_rs):
            Wg_v = Wg.rearrange("(kc p) f -> p kc f", p=128)
            Wu_v = Wu.rearrange("(kc p) f -> p kc f", p=128)
            Wd_v = Wd.rearrange("(j p) d -> p j d", p=128)
            for j in range(NJ):
                wb, wr = wgu.next()
                S.dma("pool", wb[:, 0], Wg_v[:, :, j * 128:(j + 1) * 128], writes=[wr])
                S.dma("pool", wb[:, 1], Wu_v[:, :, j * 128:(j + 1) * 128], writes=[wr])
                psg, pgr = psr.next()
                psu, pur = psr.next()
                for gu, (ps, prr) in enumerate(((psg, pgr), (psu, pur))):
                    for kc in range(8):
                        S.op("pe", lambda e, ps=ps, wb=wb, kc=kc, gu=gu: e.matmul(
                            ps[:, :], wb[:, gu, kc, :], inTb[:, kc, :], start=(kc == 0), stop=(kc == 7)),
                            reads=[wr, in_rs[kc]], writes=[prr])
                tb, tr = tA.next()
                S.op("act", lambda e, tb=tb, ps=psg: e.activation(tb[:, :], ps[:, :], AF.Silu), reads=[pgr], writes=[tr])
                S.op("dve", lambda e, tb=tb, ps=psu, j=j: e.tensor_tensor(aT[:, j, :], tb[:, :], ps[:, :], ALU.mult),
                     reads=[tr, pur], writes=[aT_r[j]])
            for half in range(2):
                pss_ = [psr.next() for _ in range(4)]
                for j in range(NJ):
                    wb, wr = wdb.next()
                    S.dma("pool", wb[:, :], Wd_v[:, j, half * 512:(half + 1) * 512], writes=[wr])
                    for q in range(4):
                        ps, pr = pss_[q]
                        S.op("pe", lambda e, ps=ps, wb=wb, j=j, q=q: e.matmul(
                            ps[:, :], wb[:, q * 128:(q + 1) * 128], aT[:, j, :], start=(j == 0), stop=(j == NJ - 1)),
                            reads=[wr, aT_r[j]], writes=[pr])
                for q in range(4):
                    dc = half * 4 + q
                    ps, pr = pss_[q]
                    S.op("dve", lambda e, ps=ps, dc=dc: e.scalar_tensor_tensor(
                        zT[:, dc, :], inT[:, dc, :], 2.0 * ALPHA, ps[:, :], ALU.mult, ALU.add),
                        reads=[pr, in_rs[dc]], writes=[zT_r[dc]])
            layer_norm(gname, bname, eps4, outT, outTb, out_rs)

        def store_T(blk, srcT, src_rs):
            for t in range(TB // 128):
                ot, orr = otok.next()
                for half in range(2):
                    ps, pr = psr.next()
                    for q in range(4):
                        kc = half * 4 + q
                        S.op("pe", lambda e, ps=ps, kc=kc, q=q, t=t: e.matmul(
                            ps[:, q * 128:(q + 1) * 128], srcT[:, kc, t * 128:(t + 1) * 128], ident,
                            start=True, stop=True), reads=[src_rs[kc], cst_r], writes=[pr])
                    S.op("act", lambda e, ps=ps, ot=ot, half=half: e.copy(ot[:, half * 512:(half + 1) * 512], ps[:, :]),
                         reads=[pr], writes=[orr])
                r0 = blk * TB + t * 128
                S.dma("sp", out_d[r0:r0 + 128, :], ot[:, :], reads=[orr], writes=[out_r])

        def projT(Wd_ap, col0, inTb, in_rs, ncols=128):
            wb, wr = wpj.next()
            Wv = Wd_ap.rearrange("(kc p) f -> p kc f", p=128)
            S.dma("pool", wb[:, :, 0:ncols], Wv[:, :, col0:col0 + ncols], writes=[wr])
            ps, pr = psr.next()
            for kc in range(8):
                S.op("pe", lambda e, ps=ps, wb=wb, kc=kc: e.matmul(
                    ps[0:ncols, :], wb[:, kc, 0:ncols], inTb[:, kc, :], start=(kc == 0), stop=(kc == 7)),
                    reads=[wr, in_rs[kc]], writes=[pr])
            return ps, pr

        NT = TB // 128
        carry = sb([128, 16, 3]); carry_r = [Res() for _ in range(16)]
        S.op("dve", lambda e: e.memset(carry[:, :, :], 0.0), writes=carry_r)
        cwork = ring(2, [128, 3 + TB])
        mean_sb = cwork.items[0][0][:, 0:TB]; rstd_sb = cwork.items[1][0][:, 0:TB]
        qkT = zT[:, :, :].rearrange("p a b -> p (a b)").bitcast(BF16).rearrange("p (c n) -> p c n", n=TB)
        qk_r = [zT_r[c // 2] for c in range(16)]
        sigmo = sb([128, 8, TB], BF16); sigmo_r = [Res() for _ in range(8)]
        sga = sigmo; sga_r = sigmo_r
        sgb = qkT[:, 8:16, :]; sgb_r = qk_r[8:16]
        vt = sb([128, NT, 4, 258], BF16); vt_r = [[Res() for _ in range(4)] for _ in range(NT)]
        S.op("dve", lambda e: e.memset(vt[:, :, :, :], 1.0), writes=[r for rr in vt_r for r in rr])
        wv = ring(1, [128, 8, 256], BF16)
        wif = sb([128, 8, 8], BF16); wif_r = Res()
        S.dma("pool", wif[:, :, :], win.rearrange("(kc p) f -> p kc f", p=128)[:, :, 4096:4104], writes=[wif_r])
        gts = sb([128, NT, 24]); gts_r = [Res() for _ in range(NT)]
        Cst = sb([128, 4, 2, 258]); C_r = [Res() for _ in range(4)]
        Cb = sb([128, 4, 2, 258], BF16); Cb_r = [Res() for _ in range(4)]
        S.op("dve", lambda e: e.memset(Cst[:, :, :, :], 0.0), writes=C_r)
        S.op("dve", lambda e: e.memset(Cb[:, :, :, :], 0.0), writes=Cb_r)
        hmTb = sb([128, 8, TB], BF16); hm_r = [Res() for _ in range(8)]
        yrTb = sb([128, 8, TB], BF16); yr_r = [Res() for _ in range(8)]
        mgTb = sigmo; mg_r = sigmo_r
        smr = ring(4, [128, 128], BF16)
        ktk = ring(4, [128, 256], BF16)
        sm6 = ring(4, [128, 8])

        def mlstm(blk):
            Wv = win.rearrange("(kc p) f -> p kc f", p=128)
            for c in range(16):
                ps, pr = projT(win, c * 128, x1Tb, x1_r)
                wk, wkr = cwork.next()
                S.op("act", lambda e, wk=wk, c=c: e.copy(wk[:, 0:3], carry[:, c, :]), reads=[carry_r[c]], writes=[wkr])
                S.op("act", lambda e, wk=wk, ps=ps: e.copy(wk[:, 3:3 + TB], ps[:, :]), reads=[pr], writes=[wkr])
                S.op("act", lambda e, wk=wk, c=c: e.copy(carry[:, c, :], wk[:, TB:TB + 3]), reads=[wkr], writes=[carry_r[c]])
                ta, tar = tA.next()
                S.op("dve", lambda e, wk=wk, ta=ta, c=c: e.tensor_scalar(
                    ta[:, :], wk[:, 0:TB], pvc("cw0", c), pvc("cb", c), ALU.mult, ALU.add),
                    reads=[wkr, pv_r], writes=[tar])
                for j in (1, 2, 3):
                    S.op("dve", lambda e, wk=wk, ta=ta, c=c, j=j: e.scalar_tensor_tensor(
                        ta[:, :], wk[:, j:j + TB], pvc("cw%d" % j, c), ta[:, :], ALU.mult, ALU.add),
                        reads=[wkr, pv_r, tar], writes=[tar])
                S.op("act", lambda e, ta=ta, c=c: e.activation(qkT[:, c, :], ta[:, :], AF.Silu),
                     reads=[tar], writes=[qk_r[c]])
            for c in range(8):
                ps, pr = projT(win, 3072 + c * 128, x1Tb, x1_r)
                S.op("act", lambda e, ps=ps, c=c: e.activation(sigmo[:, c, :], ps[:, :], AF.Sigmoid),
                     reads=[pr], writes=[sigmo_r[c]])
            for t in range(NT):
                ps, pr = psr.next()
                for kc in range(8):
                    S.op("pe", lambda e, ps=ps, kc=kc, t=t: e.matmul(
                        ps[:, 0:8], x1Tb[:, kc, t * 128:(t + 1) * 128], wif[:, kc, :], start=(kc == 0), stop=(kc == 7)),
                        reads=[wif_r, x1_r[kc]], writes=[pr])
                g = gts[:, t, :]
                gr = gts_r[t]
                S.op("dve", lambda e, g=g, ps=ps: e.tensor_tensor(g[:, 12:20], ps[:, 0:8], gbias[:, :], ALU.add),
                     reads=[pr, gb_r], writes=[gr])
                S.op("act", lambda e, g=g: e.activation(g[:, 20:24], g[:, 16:20], AF.Exp, scale=-1.0), reads=[gr], writes=[gr])
                S.op("act", lambda e, g=g: e.activation(g[:, 16:20], g[:, 20:24], AF.Ln, bias=1.0), reads=[gr], writes=[gr])
                S.op("dve", lambda e, g=g: e.tensor_scalar_mul(g[:, 16:20], g[:, 16:20], -1.0), reads=[gr], writes=[gr])
                ps2, pr2 = psr.next()
                S.op("pe", lambda e, ps2=ps2, g=g: e.matmul(ps2[:, 0:4], tri, g[:, 16:20], start=True, stop=True),
                     reads=[gr, cst_r], writes=[pr2])
                S.op("pe", lambda e, ps2=ps2, g=g: e.matmul(ps2[:, 4:8], ones, g[:, 16:20], start=True, stop=True),
                     reads=[gr, cst_r], writes=[pr2])
                S.op("dve", lambda e, g=g, ps2=ps2: e.tensor_tensor(g[:, 20:24], g[:, 12:16], ps2[:, 0:4], ALU.subtract),
                     reads=[gr, pr2], writes=[gr])
                S.op("act", lambda e, g=g: e.activation(g[:, 0:4], g[:, 20:24], AF.Exp), reads=[gr], writes=[gr])
                S.op("act", lambda e, g=g, ps2=ps2: e.activation(g[:, 4:8], ps2[:, 0:4], AF.Exp, scale=-1.0),
                     reads=[gr, pr2], writes=[gr])
                S.op("act", lambda e, g=g, ps2=ps2: e.activation(g[:, 8:12], ps2[:, 4:8], AF.Exp), reads=[gr, pr2], writes=[gr])
            for h in range(4):
                wb, wr = wv.next()
                S.dma("pool", wb[:, :, :], Wv[:, :, 2048 + h * 256:2048 + (h + 1) * 256], writes=[wr])
                for t in range(NT):
                    ps, pr = psr.next()
                    for kc in range(8):
                        S.op("pe", lambda e, ps=ps, kc=kc, t=t, wb=wb: e.matmul(
                            ps[:, 0:256], x1Tb[:, kc, t * 128:(t + 1) * 128], wb[:, kc, :], start=(kc == 0), stop=(kc == 7)),
                            reads=[wr, x1_r[kc]], writes=[pr])
                    S.op("act", lambda e, ps=ps, t=t, h=h: e.copy(vt[:, t, h, 0:256], ps[:, 0:256]),
                         reads=[pr], writes=[vt_r[t][h]])
            for t in range(NT):
                tc_ = slice(t * 128, (t + 1) * 128)
                g = gts[:, t, :]
                gr = gts_r[t]
                def head_chain(t, h, tc_, g, gr):
                    qc = [h * 2, h * 2 + 1]
                    kc_ = [8 + h * 2, 8 + h * 2 + 1]
                    ps, pr = psr.next()
                    for i in range(2):
                        S.op("pe", lambda e, ps=ps, i=i, kc_=kc_, qc=qc, tc_=tc_: e.matmul(
                            ps[:, 0:128], qkT[:, kc_[i], tc_], qkT[:, qc[i], tc_], start=(i == 0), stop=(i == 1)),
                            reads=[qk_r[kc_[i]], qk_r[qc[i]]], writes=[pr])
                        yield
                    sm, smrr = smr.next()
                    S.op("dve", lambda e, sm=sm, ps=ps, h=h, g=g: e.scalar_tensor_tensor(
                        sm[:, :], ps[:, 0:128], g[:, h:h + 1], tri, ALU.mult, ALU.mult),
                        reads=[pr, gr, cst_r], writes=[smrr])
                    yield
                    po, por = psr.next()
                    S.op("pe", lambda e, po=po, sm=sm, t=t, h=h: e.matmul(
                        po[:, 0:258], sm[:, :], vt[:, t, h, :], start=True, stop=False),
                        reads=[smrr, vt_r[t][h]], writes=[por])
                    yield
                    for i in range(2):
                        S.op("pe", lambda e, po=po, i=i, h=h, qc=qc, tc_=tc_: e.matmul(
                            po[:, 0:258], qkT[:, qc[i], tc_], Cb[:, h, i, :], start=False, stop=(i == 1)),
                            reads=[qk_r[qc[i]], Cb_r[h]], writes=[por])
                        yield
                    s6, s6r = sm6.next()
                    S.op("act", lambda e, s6=s6, po=po: e.activation(
                        s6[:, 0:1], po[:, 256:257], AF.Abs, scale=1.0 / 16.0), reads=[por], writes=[s6r])
                    yield
                    S.op("dve", lambda e, s6=s6, g=g, h=h: e.tensor_tensor(s6[:, 0:1], s6[:, 0:1], g[:, 4 + h:5 + h], ALU.max),
                         reads=[s6r, gr], writes=[s6r])
                    yield
                    S.op("dve", lambda e, s6=s6: e.reciprocal(s6[:, 1:2], s6[:, 0:1]), reads=[s6r], writes=[s6r])
                    yield
                    hb, hbr = hh.next()
                    S.op("dve", lambda e, hb=hb, po=po, s6=s6: e.tensor_scalar(
                        hb[:, :], po[:, 0:256], s6[:, 1:2], 1.0 / 16.0, ALU.mult, ALU.mult), reads=[por, s6r], writes=[hbr])
                    yield
                    S.op("dve", lambda e, hb=hb, s6=s6: e.bn_stats(s6[:, 2:8], hb[:, :]), reads=[hbr], writes=[s6r])
                    yield
                    S.op("dve", lambda e, s6=s6: e.bn_aggr(s6[:, 0:2], s6[:, 2:8]), reads=[s6r], writes=[s6r])
                    yield
                    S.op("act", lambda e, s6=s6: e.activation(s6[:, 2:3], s6[:, 1:2], AF.Sqrt, bias=epsln[:, 0:1]),
                         reads=[s6r, eps_r], writes=[s6r])
                    yield
                    S.op("dve", lambda e, s6=s6: e.reciprocal(s6[:, 3:4], s6[:, 2:3]), reads=[s6r], writes=[s6r])
                    yield
                    hn, hnr = hnb.next()
                    S.op("dve", lambda e, hn=hn, hb=hb, s6=s6: e.tensor_scalar(
                        hn[:, :], hb[:, :], s6[:, 0:1], s6[:, 3:4], ALU.subtract, ALU.mult), reads=[hbr, s6r], writes=[hnr])
                    yield
                    for i in range(2):
                        pt, ptr = psr.next()
                        S.op("pe", lambda e, pt=pt, hn=hn, i=i: e.matmul(
                            pt[:, 0:128], hn[:, i * 128:(i + 1) * 128], identb, start=True, stop=True),
                            reads=[hnr, cstb_r], writes=[ptr])
                        yield
                        S.op("dve", lambda e, pt=pt, h=h, i=i, tc_=tc_: e.scalar_tensor_tensor(
                            hmTb[:, h * 2 + i, tc_], pt[:, 0:128], pvc("mng", h * 2 + i), sigmo[:, h * 2 + i, tc_],
                            ALU.mult, ALU.mult), reads=[ptr, pv_r, sigmo_r[h * 2 + i]], writes=[hm_r[h * 2 + i]])
                        yield
                    kk_, kkr = ktk.next()
                    for i in range(2):
                        pt, ptr = psr.next()
                        S.op("pe", lambda e, pt=pt, i=i, kc_=kc_, tc_=tc_: e.matmul(
                            pt[:, 0:128], qkT[:, kc_[i], tc_], identb, start=True, stop=True),
                            reads=[qk_r[kc_[i]], cstb_r], writes=[ptr])
                        yield
                        S.op("act", lambda e, pt=pt, kk_=kk_, i=i, g=g, h=h: e.activation(
                            kk_[:, i * 128:(i + 1) * 128], pt[:, 0:128], AF.Identity, scale=g[:, h:h + 1]),
                            reads=[ptr, gr], writes=[kkr])
                        yield
                    for i in range(2):
                        pc, pcr = psr.next()
                        S.op("pe", lambda e, pc=pc, kk_=kk_, i=i, t=t, h=h: e.matmul(
                            pc[:, 0:258], kk_[:, i * 128:(i + 1) * 128], vt[:, t, h, :], start=True, stop=True),
                            reads=[kkr, vt_r[t][h]], writes=[pcr])
                        yield
                        S.op("dve", lambda e, h=h, i=i, g=g: e.tensor_scalar_mul(Cst[:, h, i, :], Cst[:, h, i, :], g[:, 8 + h:9 + h]),
                             reads=[C_r[h], gr], writes=[C_r[h]])
                        yield
                        S.op("dve", lambda e, pc=pc, h=h, i=i, g=g: e.scalar_tensor_tensor(
                            Cst[:, h, i, :], pc[:, 0:258], g[:, 8 + h:9 + h], Cst[:, h, i, :], ALU.mult, ALU.add),
                            reads=[pcr, gr, C_r[h]], writes=[C_r[h]])
                        yield
                        S.op("act", lambda e, h=h, i=i: e.copy(Cb[:, h, i, :], Cst[:, h, i, :]), reads=[C_r[h]], writes=[Cb_r[h]])
                        yield

                gens = [head_chain(t, h, tc_, g, gr) for h in range(4)]
                while gens:
                    for gg in list(gens):
                        try:
                            next(gg)
                        except StopIteration:
                            gens.remove(gg)

        NCH = TB // 64
        rcar = sb([128, 26, 1]); rcar_r = [Res() for _ in range(26)]
        S.op("dve", lambda e: e.memset(rcar[:, :, :], 0.0), writes=rcar_r)
        rwork = cwork
        lowT = sb([128, 2, TB], BF16); low_r = [Res(), Res()]
        rtmp = Ring([(aT[:, 2 * i:2 * i + 2, :].rearrange("p a b -> p (a b)").bitcast(F32), Res()) for i in range(10)])
        ARbd = sb([128, NCH, 256], BF16); Bbd = sb([128, NCH, 128], BF16); Kbd = sb([128, NCH, 128], BF16)
        Vbd = sb([128, NCH, 128], BF16); Ynbd = sb([128, 128], BF16)
        bd_r = Res(); ynbd_r = Res()
        for tns in (ARbd, Bbd, Kbd, Vbd):
            S.op("dve", lambda e, tns=tns: e.memset(tns[:, :, :], 0.0), writes=[bd_r])
        S.op("dve", lambda e: e.memset(Ynbd[:, :], 0.0), writes=[ynbd_r])
        gam = sb([128, NCH]); gam_r = Res()
        Hst = sb([128, 8, 64]); H_r = [Res() for _ in range(8)]
        Hb = sb([128, 8, 64], BF16); Hb_r = [Res() for _ in range(8)]
        S.op("dve", lambda e: e.memset(Hst[:, :, :], 0.0), writes=H_r)
        S.op("dve", lambda e: e.memset(Hb[:, :, :], 0.0), writes=Hb_r)
        vst = sb([128, NCH, 64], BF16); vst_r = Res()
        btk = sb([128, NCH, 256], BF16); btk_r = Res()
        nm = ring(4, [128, 256], BF16)
        nsq = ring(11, [128, 128], BF16)
        ub = ring(4, [128, 64], BF16)
        uf = ring(4, [128, 64])
        gn6 = ring(4, [128, 8])
        htmp = ring(2, [128, 64])
        maskAR = cst[:, C_MUS:C_MUS + 256]; mask_r = cst_r
        mls = cst[:, C_MLS:C_MLS + 128]

        xTv = xT[:, :, :].rearrange("p a b -> p (a b)").bitcast(BF16)
        mAK = xTv[:, 0:NCH * 512].rearrange("p (n c) -> p n c", c=512)
        Xs = xTv[:, 4096:4096 + NCH * 128].rearrange("p (n c) -> p n c", c=128)
        xTbv = xTb[:, :, :].rearrange("p a b -> p (a b)")
        PW = [[xTbv[:, (b * 8 + n) * 128:(b * 8 + n + 1) * 128] for n in range(NCH)] for b in range(2)]
        PWT = [[xTbv[:, 2048 + (b * 8 + n) * 128:2048 + (b * 8 + n + 1) * 128] for n in range(NCH)] for b in range(2)]
        hh = Ring([(xTv[:, 5120 + i * 512:5120 + (i + 1) * 512].bitcast(F32), Res()) for i in range(4)])
        hnb = Ring([(xTv[:, 7168 + i * 256:7168 + (i + 1) * 256], Res()) for i in range(4)])
        mA_r = [Res() for _ in range(NCH)]; mK_r = [Res() for _ in range(NCH)]; X_r = [Res() for _ in range(NCH)]
        PW_r = [[Res() for _ in range(NCH)] for _ in range(2)]; PWT_r = [[Res() for _ in range(NCH)] for _ in range(2)]

        def v3(ap):
            return ap.rearrange("p (n l) -> p n l", l=64)

        def shifted(ci, ps, pr):
            wk, wkr = rwork.next()
            S.op("act", lambda e, wk=wk: e.copy(wk[:, 0:1], rcar[:, ci, :]), reads=[rcar_r[ci]], writes=[wkr])
            S.op("act", lambda e, wk=wk, ps=ps: e.copy(wk[:, 1:1 + TB], ps[:, :]), reads=[pr], writes=[wkr])
            S.op("act", lambda e, wk=wk: e.copy(rcar[:, ci, :], wk[:, TB:TB + 1]), reads=[wkr], writes=[rcar_r[ci]])
            ta, tar = rtmp.next()
            S.op("dve", lambda e, wk=wk, ta=ta: e.tensor_scalar_mul(ta[:, :], wk[:, 1:1 + TB], pvc("omu", ci)),
                 reads=[wkr, pv_r], writes=[tar])
            S.op("dve", lambda e, wk=wk, ta=ta: e.scalar_tensor_tensor(
                ta[:, :], wk[:, 0:TB], pvc("mu", ci), ta[:, :], ALU.mult, ALU.add), reads=[wkr, pv_r, tar], writes=[tar])
            return ta, tar

        krw = int(os.environ.get("KRW", "9"))

        def rwkv(blk):
            ps, pr = projT(win, RC0 + 24 * 128, x1Tb, x1_r)
            ta, tar = shifted(24, ps, pr)
            S.op("act", lambda e, ta=ta: e.activation(lowT[0:64, 0, :], ta[0:64, :], AF.Tanh), reads=[tar], writes=[low_r[0]])
            S.op("act", lambda e, ta=ta: e.copy(lowT[64:128, 0, :], ta[64:128, :]), reads=[tar], writes=[low_r[0]])
            ps, pr = projT(win, RC0 + 25 * 128, x1Tb, x1_r)
            ta, tar = shifted(25, ps, pr)
            S.op("act", lambda e, ta=ta: e.activation(lowT[:, 1, :], ta[:, :], AF.Sigmoid), reads=[tar], writes=[low_r[1]])
            for p in range(8):
                cs = slice(p * 128, (p + 1) * 128)
                ps, pr = projT(win, RC0 + p * 128, x1Tb, x1_r)
                r_, r_r = shifted(p, ps, pr)
                ps, pr = projT(win, RC0 + 1024 + p * 128, x1Tb, x1_r)
                k_, k_r = shifted(8 + p, ps, pr)
                ps, pr = projT(win, RC0 + 2048 + p * 128, x1Tb, x1_r)
                v_, v_r = shifted(16 + p, ps, pr)
                pw, pwr = psr.next()
                S.op("pe", lambda e, pw=pw, cs=cs: e.matmul(pw[:, :], rw2a2[0:64, cs], lowT[0:64, 0, :], start=True, stop=True),
                     reads=[rw_r, low_r[0]], writes=[pwr])
                lw, lwr = rtmp.next()
                S.op("act", lambda e, lw=lw, pw=pw, p=p: e.activation(lw[:, :], pw[:, :], AF.Sigmoid, bias=pvc("w0", p)),
                     reads=[pwr, pv_r], writes=[lwr])
                S.op("dve", lambda e, lw=lw: e.tensor_scalar_mul(lw[:, :], lw[:, :], -float(np.exp(-0.5))), reads=[lwr], writes=[lwr])
                pa, par = psr.next()
                S.op("pe", lambda e, pa=pa, cs=cs: e.matmul(pa[:, :], rw2a2[64:128, cs], lowT[64:128, 0, :], start=True, stop=True),
                     reads=[rw_r, low_r[0]], writes=[par])
                a_, a_r = rtmp.next()
                S.op("act", lambda e, a_=a_, pa=pa, p=p: e.activation(a_[:, :], pa[:, :], AF.Sigmoid, bias=pvc("a0", p)),
                     reads=[par, pv_r], writes=[a_r])
                pg, pgr = psr.next()
                S.op("pe", lambda e, pg=pg, cs=cs: e.matmul(pg[:, :], rg2[:, cs], lowT[:, 1, :], start=True, stop=True),
                     reads=[rw_r, low_r[1]], writes=[pgr])
                g_, g_r = rtmp.next()
                S.op("act", lambda e, g_=g_, pg=pg: e.copy(g_[:, :], pg[:, :]), reads=[pgr], writes=[g_r])
                kk, kkr = rtmp.next()
                S.op("dve", lambda e, kk=kk, k_=k_, p=p: e.tensor_scalar_mul(kk[:, :], k_[:, :], pvc("kk", p)),
                     reads=[k_r, pv_r], writes=[kkr])
                sq, sqr = tB.next()
                S.op("act", lambda e, sq=sq, kk=kk: e.activation(sq[:, :], kk[:, :], AF.Square), reads=[kkr], writes=[sqr])
                pq, pqr = psr.next()
                S.op("pe", lambda e, pq=pq, sq=sq: e.matmul(pq[:, :], bob, sq[:, :], start=True, stop=True),
                     reads=[cstb_r, sqr], writes=[pqr])
                t1, t1r = rtmp.next()
                S.op("act", lambda e, t1=t1, pq=pq: e.activation(t1[:, :], pq[:, :], AF.Sqrt), reads=[pqr], writes=[t1r])
                S.op("dve", lambda e, t1=t1: e.tensor_scalar_max(t1[:, :], t1[:, :], 1e-12), reads=[t1r], writes=[t1r])
                S.op("dve", lambda e, t1=t1: e.reciprocal(t1[:, :], t1[:, :]), reads=[t1r], writes=[t1r])
                S.op("dve", lambda e, t1=t1, kk=kk: e.tensor_tensor(kk[:, :], kk[:, :], t1[:, :], ALU.mult),
                     reads=[t1r, kkr], writes=[kkr])
                S.op("dve", lambda e, t1=t1, a_=a_, p=p: e.tensor_scalar(t1[:, :], a_[:, :], 1.0, pvc("ka", p), ALU.subtract, ALU.mult),
                     reads=[a_r, pv_r], writes=[t1r])
                S.op("dve", lambda e, t1=t1, k_=k_: e.scalar_tensor_tensor(k_[:, :], t1[:, :], 1.0, k_[:, :], ALU.add, ALU.mult),
                     reads=[t1r, k_r], writes=[k_r])
                t2, t2r = tB.next()
                S.op("dve", lambda e, t2=t2, r_=r_, k_=k_, p=p: e.scalar_tensor_tensor(
                    t2[:, :], r_[:, :], pvc("rrk", p), k_[:, :], ALU.mult, ALU.mult), reads=[r_r, k_r, pv_r], writes=[t2r])
                pb, pbr = psr.next()
                S.op("pe", lambda e, pb=pb, t2=t2: e.matmul(pb[:, :], bob, t2[:, :], start=True, stop=True),
                     reads=[cstb_r, t2r], writes=[pbr])
                bon, bonr = rtmp.next()
                S.op("dve", lambda e, bon=bon, pb=pb, v_=v_: e.tensor_tensor(bon[:, :], pb[:, :], v_[:, :], ALU.mult),
                     reads=[pbr, v_r], writes=[bonr])
                cl, clr = rtmp.next()
                S.op("dve", lambda e, cl=cl, lw=lw: e.tensor_tensor_scan(cl[:, :], rst, lw[:, :], 0.0, ALU.mult, ALU.add),
                     reads=[lwr, cstb_r], writes=[clr])
                e1, e1r = tA.next()
                S.op("act", lambda e, e1=e1, cl=cl: e.activation(e1[:, :], cl[:, :], AF.Exp), reads=[clr], writes=[e1r])
                S.op("act", lambda e, e1=e1: e.copy(gam[:, :], v3(e1[:, :])[:, :, 63]), reads=[e1r], writes=[gam_r])
                for hf in range(2):
                    hs = slice(hf * 64, hf * 64 + 64)
                    S.op("dve", lambda e, hs=hs, hf=hf, r_=r_, e1=e1: e.tensor_tensor(
                        ARbd[hs, :, 128 + hf * 64:128 + hf * 64 + 64], v3(r_[hs, :]), v3(e1[hs, :]), ALU.mult),
                        reads=[r_r, e1r], writes=[bd_r])
                e2, e2r = tA.next()
                S.op("act", lambda e, e2=e2, cl=cl: e.activation(e2[:, :], cl[:, :], AF.Exp, scale=-1.0), reads=[clr], writes=[e2r])
                S.op("dve", lambda e, t1=t1, kk=kk, a_=a_: e.tensor_tensor(t1[:, :], kk[:, :], a_[:, :], ALU.mult),
                     reads=[kkr, a_r], writes=[t1r])
                for hf in range(2):
                    hs = slice(hf * 64, hf * 64 + 64)
                    S.op("dve", lambda e, hs=hs, hf=hf, t1=t1, e2=e2: e.tensor_tensor(
                        Bbd[hs, :, hf * 64:hf * 64 + 64], v3(t1[hs, :]), v3(e2[hs, :]), ALU.mult), reads=[t1r, e2r], writes=[bd_r])
                    S.op("dve", lambda e, hs=hs, hf=hf, k_=k_, e2=e2: e.tensor_tensor(
                        Kbd[hs, :, hf * 64:hf * 64 + 64], v3(k_[hs, :]), v3(e2[hs, :]), ALU.mult), reads=[k_r, e2r], writes=[bd_r])
                    S.op("act", lambda e, hs=hs, hf=hf, v_=v_: e.copy(Vbd[hs, :, hf * 64:hf * 64 + 64], v3(v_[hs, :])),
                         reads=[v_r], writes=[bd_r])
                S.op("dve", lambda e, cl=cl, lw=lw: e.tensor_tensor(cl[:, :], cl[:, :], lw[:, :], ALU.subtract),
                     reads=[clr, lwr], writes=[clr])
                e3, e3r = tA.next()
                S.op("act", lambda e, e3=e3, cl=cl: e.activation(e3[:, :], cl[:, :], AF.Exp), reads=[clr], writes=[e3r])
                for hf in range(2):
                    hs = slice(hf * 64, hf * 64 + 64)
                    S.op("dve", lambda e, hs=hs, hf=hf, kk=kk, e3=e3: e.scalar_tensor_tensor(
                        ARbd[hs, :, hf * 64:hf * 64 + 64], v3(kk[hs, :]), -1.0, v3(e3[hs, :]), ALU.mult, ALU.mult),
                        reads=[kkr, e3r], writes=[bd_r])
                if krw <= 1:
                    S.op("dve", lambda e, p=p: e.memset(yrTb[:, p, :], 0.0), writes=[yr_r[p]])
                    continue
                pv_, pvr_ = psr.next()
                for n in range(NCH):
                    S.op("pe", lambda e, n=n, pv_=pv_: e.matmul(pv_[:, n * 64:(n + 1) * 64], Vbd[:, n, :], istb, start=True, stop=True),
                         reads=[bd_r, cstb_r], writes=[pvr_])
                S.op("act", lambda e, pv_=pv_: e.copy(vst[:, :, :], v3(pv_[:, :])), reads=[pvr_], writes=[vst_r])
                for n0 in range(0, NCH, 2):
                    pt, ptr = psr.next()
                    for n in (n0, n0 + 1):
                        o = (n - n0) * 256
                        S.op("pe", lambda e, n=n, pt=pt, o=o: e.matmul(pt[:, o:o + 128], Bbd[:, n, :], identb, start=True, stop=True),
                             reads=[bd_r, cstb_r], writes=[ptr])
                        S.op("pe", lambda e, n=n, pt=pt, o=o: e.matmul(pt[:, o + 128:o + 256], Kbd[:, n, :], identb, start=True, stop=True),
                             reads=[bd_r, cstb_r], writes=[ptr])
                    S.op("act", lambda e, n0=n0, pt=pt: e.copy(btk[:, n0:n0 + 2, :], pt[:, :].rearrange("p (n l) -> p n l", l=256)),
                         reads=[ptr], writes=[btk_r])
                if krw <= 2:
                    S.op("dve", lambda e, p=p: e.memset(yrTb[:, p, :], 0.0), writes=[yr_r[p]])
                    continue
                pyo, pyor = pyo_bank
                for g0 in range(0, NCH, 4):
                    G = list(range(g0, min(g0 + 4, NCH)))
                    pAs = {}
                    for n in G:
                        pA, pAr = psr.next()
                        pAs[n] = (pA, pAr)
                        S.op("pe", lambda e, pA=pA, n=n: e.matmul(pA[:, 0:256], Bbd[:, n, :], ARbd[:, n, :], start=True, stop=True),
                             reads=[bd_r], writes=[pAr])
                        S.op("pe", lambda e, pA=pA, n=n: e.matmul(pA[:, 256:512], Kbd[:, n, :], ARbd[:, n, :], start=True, stop=True),
                             reads=[bd_r], writes=[pAr])
                    for n in G:
                        pA, pAr = pAs[n]
                        S.op("dve", lambda e, pA=pA, n=n: e.tensor_tensor(mAK[:, n, 0:256], pA[:, 0:256], maskAR[:, :], ALU.mult),
                             reads=[pAr, mask_r], writes=[mA_r[n]])
                        S.op("dve", lambda e, pA=pA, n=n: e.tensor_tensor(mAK[:, n, 256:512], pA[:, 256:512], maskAR[:, :], ALU.mult),
                             reads=[pAr, mask_r], writes=[mK_r[n]])
                    pTs = {}
                    for n in G:
                        pT, pTr = psr.next()
                        pTs[n] = (pT, pTr)
                        S.op("pe", lambda e, pT=pT, n=n: e.matmul(pT[:, 0:128], ARbd[:, n, 0:128], Bbd[:, n, :], start=True, stop=True),
                             reads=[bd_r], writes=[pTr])
                    for n in G:
                        pT, pTr = pTs[n]
                        S.op("dve", lambda e, pT=pT, n=n: e.tensor_tensor(PWT[0][n], pT[:, 0:128], mls, ALU.mult),
                             reads=[pTr, cst_r], writes=[PWT_r[0][n]])
                        S.op("dve", lambda e, n=n: e.tensor_tensor(Xs[:, n, :], mAK[:, n, 0:128], identb, ALU.add),
                             reads=[mA_r[n], cstb_r], writes=[X_r[n]])
                    for j in range(1, 6):
                        b0, b1 = (j - 1) % 2, j % 2
                        p2s = {}
                        for n in G:
                            cur = mAK[:, n, 0:128] if j == 1 else PW[b0][n]
                            curr = mA_r[n] if j == 1 else PW_r[b0][n]
                            ct, ctr_ = PWT[b0][n], PWT_r[b0][n]
                            p2, p2r = psr.next()
                            p2s[n] = (p2, p2r)
                            if j < 5:
                                S.op("pe", lambda e, p2=p2, cur=cur, ct=ct: e.matmul(p2[:, 0:128], ct, cur, start=True, stop=True),
                                     reads=[curr, ctr_], writes=[p2r])
                            S.op("pe", lambda e, p2=p2, cur=cur, ct=ct: e.matmul(p2[:, 128:256], cur, ct, start=True, stop=True),
                                 reads=[curr, ctr_], writes=[p2r])
                        for n in G:
                            p2, p2r = p2s[n]
                            S.op("dve", lambda e, p2=p2, n=n, b1=b1: e.tensor_copy(PWT[b1][n], p2[:, 128:256]),
                                 reads=[p2r], writes=[PWT_r[b1][n]])
                            if j < 5:
                                S.op("act", lambda e, p2=p2, n=n, b1=b1: e.copy(PW[b1][n], p2[:, 0:128]),
                                     reads=[p2r], writes=[PW_r[b1][n]])
                        pxs = {}
                        for n in G:
                            px, pxr = psr.next()
                            pxs[n] = (px, pxr)
                            S.op("pe", lambda e, px=px, n=n, b1=b1: e.matmul(px[:, 0:128], PWT[b1][n], Xs[:, n, :], start=True, stop=True),
                                 reads=[PWT_r[b1][n], X_r[n]], writes=[pxr])
                        for n in G:
                            px, pxr = pxs[n]
                            S.op("dve", lambda e, px=px, n=n: e.tensor_tensor(Xs[:, n, :], px[:, 0:128], Xs[:, n, :], ALU.add),
                                 reads=[pxr, X_r[n]], writes=[X_r[n]])
                for n in range(NCH):
                    pw_, pw_r = psr.next()
                    S.op("pe", lambda e, pw_=pw_, n=n, p=p: e.matmul(pw_[:, 0:64], ARbd[:, n, 0:128], Hb[:, p, :], start=True, stop=False),
                         reads=[bd_r, Hb_r[p]], writes=[pw_r])
                    S.op("pe", lambda e, pw_=pw_, n=n: e.matmul(pw_[:, 0:64], mAK[:, n, 256:384], vst[:, n, :], start=False, stop=True),
                         reads=[mK_r[n], vst_r], writes=[pw_r])
                    w_b, wbr = ub.next()
                    S.op("act", lambda e, w_b=w_b, pw_=pw_: e.copy(w_b[:, :], pw_[:, 0:64]), reads=[pw_r], writes=[wbr])
                    pu, pur_ = psr.next()
                    S.op("pe", lambda e, pu=pu, n=n, w_b=w_b: e.matmul(pu[:, 0:64], Xs[:, n, :], w_b[:, :], start=True, stop=True),
                         reads=[X_r[n], wbr], writes=[pur_])
                    u_b, ubr = ub.next()
                    S.op("dve", lambda e, u_b=u_b, pu=pu: e.tensor_copy(u_b[:, :], pu[:, 0:64]), reads=[pur_], writes=[ubr])
                    ph, phr = psr.next()
                    S.op("pe", lambda e, ph=ph, n=n, u_b=u_b: e.matmul(ph[:, 0:64], btk[:, n, 0:128], u_b[:, :], start=True, stop=False),
                         reads=[btk_r, ubr], writes=[phr])
                    S.op("pe", lambda e, ph=ph, n=n: e.matmul(ph[:, 0:64], btk[:, n, 128:256], vst[:, n, :], start=False, stop=True),
                         reads=[btk_r, vst_r], writes=[phr])
                    py, pyr = psr.next()
                    S.op("pe", lambda e, py=py, n=n, p=p: e.matmul(py[:, 0:64], ARbd[:, n, 128:256], Hb[:, p, :], start=True, stop=False),
                         reads=[bd_r, Hb_r[p]], writes=[pyr])
                    S.op("pe", lambda e, py=py, n=n, u_b=u_b: e.matmul(py[:, 0:64], mAK[:, n, 128:256], u_b[:, :], start=False, stop=False),
                         reads=[mA_r[n], ubr], writes=[pyr])
                    S.op("pe", lambda e, py=py, n=n: e.matmul(py[:, 0:64], mAK[:, n, 384:512], vst[:, n, :], start=False, stop=True),
                         reads=[mK_r[n], vst_r], writes=[pyr])
                    ht, htr = htmp.next()
                    S.op("dve", lambda e, ht=ht, ph=ph, p=p: e.tensor_tensor(ht[:, :], ph[:, 0:64], Hst[:, p, :], ALU.add),
                         reads=[phr, H_r[p]], writes=[htr])
                    S.op("act", lambda e, ht=ht, p=p, n=n: e.activation(Hb[:, p, :], ht[:, :], AF.Identity, scale=gam[:, n:n + 1]),
                         reads=[htr, gam_r], writes=[Hb_r[p]])
                    S.op("act", lambda e, ht=ht, p=p, n=n: e.activation(Hst[:, p, :], ht[:, :], AF.Identity, scale=gam[:, n:n + 1]),
                         reads=[htr, gam_r], writes=[H_r[p]])
                    g6, g6r = gn6.next()
                    S.op("dve", lambda e, g6=g6, py=py: e.bn_stats(g6[:, 2:8], py[:, 0:64]), reads=[pyr], writes=[g6r])
                    S.op("dve", lambda e, g6=g6: e.bn_aggr(g6[:, 0:2], g6[:, 2:8]), reads=[g6r], writes=[g6r])
                    S.op("act", lambda e, g6=g6: e.activation(g6[:, 2:3], g6[:, 1:2], AF.Sqrt, bias=epsgn[:, 0:1]),
                         reads=[g6r, eps_r], writes=[g6r])
                    S.op("dve", lambda e, g6=g6: e.reciprocal(g6[:, 3:4], g6[:, 2:3]), reads=[g6r], writes=[g6r])
                    for hf in range(2):
                        hs = slice(hf * 64, hf * 64 + 64)
                        S.op("dve", lambda e, hs=hs, hf=hf, py=py, g6=g6: e.tensor_scalar(
                            Ynbd[hs, hf * 64:hf * 64 + 64], py[hs, 0:64], g6[hs, 0:1], g6[hs, 3:4], ALU.subtract, ALU.mult),
                            reads=[pyr, g6r], writes=[ynbd_r])
                    S.op("pe", lambda e, n=n, pyo=pyo: e.matmul(pyo[:, n * 64:(n + 1) * 64], Ynbd[:, :], istb, start=True, stop=True),
                         reads=[ynbd_r, cstb_r], writes=[pyor])
                if krw <= 5:
                    S.op("dve", lambda e, p=p: e.memset(yrTb[:, p, :], 0.0), writes=[yr_r[p]])
                    continue
                S.op("dve", lambda e, t1=t1, pyo=pyo, p=p: e.tensor_scalar(
                    t1[:, :], pyo[:, :], pvc("gng", p), pvc("gnb", p), ALU.mult, ALU.add), reads=[pyor, pv_r], writes=[t1r])
                S.op("dve", lambda e, t1=t1, bon=bon: e.tensor_tensor(t1[:, :], t1[:, :], bon[:, :], ALU.add),
                     reads=[t1r, bonr], writes=[t1r])
                S.op("dve", lambda e, t1=t1, g_=g_, p=p: e.tensor_tensor(yrTb[:, p, :], t1[:, :], g_[:, :], ALU.mult),
                     reads=[t1r, g_r], writes=[yr_r[p]])

        def merge_out(blk):
            for c in range(8):
                ps, pr = projT(win, 7432 + c * 128, x1Tb, x1_r)
                S.op("act", lambda e, ps=ps, c=c: e.activation(sga[:, c, :], ps[:, :], AF.Sigmoid), reads=[pr], writes=[sga_r[c]])
                ps, pr = projT(win, 8456 + c * 128, x1Tb, x1_r)
                S.op("act", lambda e, ps=ps, c=c: e.activation(sgb[:, c, :], ps[:, :], AF.Sigmoid), reads=[pr], writes=[sgb_r[c]])
            for c in range(8):
                psa, par = projT(wa_d, c * 128, hmTb, hm_r)
                psb, pbr = projT(wb_d, c * 128, yrTb, yr_r)
                ta, tar = tA.next()
                S.op("dve", lambda e, ta=ta, psa=psa, c=c: e.tensor_tensor(ta[:, :], psa[:, :], sga[:, c, :], ALU.mult),
                     reads=[par, sga_r[c]], writes=[tar])
                tb, tbr = tA.next()
                S.op("dve", lambda e, tb=tb, psb=psb, c=c: e.tensor_tensor(tb[:, :], psb[:, :], sgb[:, c, :], ALU.mult),
                     reads=[pbr, sgb_r[c]], writes=[tbr])
                S.op("dve", lambda e, ta=ta, tb=tb, c=c: e.tensor_tensor(mgTb[:, c, :], ta[:, :], tb[:, :], ALU.add),
                     reads=[tar, tbr], writes=[mg_r[c]])
            for c in range(8):
                ps, pr = projT(wo_d, c * 128, mgTb, mg_r)
                S.op("dve", lambda e, ps=ps, c=c: e.scalar_tensor_tensor(
                    zT[:, c, :], x1T[:, c, :], ALPHA, ps[:, :], ALU.mult, ALU.add), reads=[pr, x1_r[c]], writes=[zT_r[c]])
            layer_norm("ln2_g", "ln2_b", epsln, x2T, x2Tb, x2_r)

        stage = int(os.environ.get("KSTAGE", "9"))
        for blk in range(nblk):
            load_xT(blk)
            ffn_ln(xT, xTb, xT_r, w1g, w1u, w1d, "ln1_g", "ln1_b", x1T, x1Tb, x1_r)
            if stage == 1:
                store_T(blk, x1T, x1_r)
                continue
            skip = os.environ.get("KSKIP", "")
            if "mlstm" in skip:
                S.op("dve", lambda e: e.memset(hmTb[:, :, :], 0.0), writes=hm_r)
            else:
                mlstm(blk)
            if "rwkv" in skip:
                S.op("dve", lambda e: e.memset(yrTb[:, :, :], 0.0), writes=yr_r)
            else:
                rwkv(blk)
            merge_out(blk)
            if stage == 2:
                store_T(blk, x2T, x2_r)
                continue
            ffn_ln(x2T, x2Tb, x2_r, w2g, w2u, w2d, "ln3_g", "ln3_b", x3T, x3Tb, x3_r)
            store_T(blk, x3T, x3_r)

        S.op("sp", lambda e: e.nop(), reads=[out_r])
        S.emit()
    return nc


def _consts():
    c = np.zeros((128, NCST), np.float32)
    i = np.arange(128)
    c[:, C_ID:C_ID + 128] = np.eye(128)
    c[:, C_OD:C_OD + 128] = 1.0 / 1024.0
    c[:, C_TRI:C_TRI + 128] = (i[:, None] <= i[None, :])
    c[:, C_ONE:C_ONE + 128] = 1.0
    c[:, C_MUS:C_MUS + 128] = (i[:, None] < i[None, :])
    c[:, C_MLS:C_MLS + 128] = (i[:, None] > i[None, :])
    c[:, C_IST:C_IST + 64] = np.concatenate([np.eye(64), np.eye(64)], axis=0)
    c[:, C_BO:C_BO + 128] = ((i[:, None] // 64) == (i[None, :] // 64))
    return c


def _fm(v):
    v = np.asarray(v, np.float32).reshape(-1, 128)
    return np.ascontiguousarray(v.T)


def kernel(**inp):
    nblk = int(os.environ.get("KNBLK", str(SEQ // TB)))
    x = np.asarray(inp["x"], np.float32)
    pv = np.zeros((128, NPV), np.float32)

    def put(name, arr):
        a = _fm(arr)
        pv[:, PV[name]:PV[name] + a.shape[1]] = a

    for n in ("ln1_g", "ln1_b", "ln2_g", "ln2_b", "ln3_g", "ln3_b"):
        put(n, inp[n][0])
    cw = inp["m_conv_w"][0]
    for j in range(4):
        put("cw%d" % j, cw[j])
    put("cb", inp["m_conv_b"][0])
    put("mng", inp["m_norm_g"][0])
    mu = inp["r_mu"][0]
    put("mu", mu)
    put("omu", 1.0 - mu)
    put("w0", inp["r_w0"][0]); put("a0", inp["r_a0"][0]); put("kk", inp["r_k_k"][0]); put("ka", inp["r_k_a"][0])
    put("rrk", inp["r_r_k"][0].reshape(-1)); put("gng", inp["r_gn_g"][0]); put("gnb", inp["r_gn_b"][0])
    gbias = np.zeros((128, 8), np.float32)
    gbias[:, 0:4] = inp["m_i_bias"][0][None, :]
    gbias[:, 4:8] = inp["m_f_bias"][0][None, :]
    shared = {"cst": _consts(), "pv": pv, "gbias": gbias,
              "rst": np.ascontiguousarray(np.broadcast_to((np.arange(512)[None, :] % 64 != 0), (128, 512)).astype(np.float32))}
    for n in ("ffn1_w_gate", "ffn1_w_up", "ffn1_w_down", "ffn2_w_gate", "ffn2_w_up", "ffn2_w_down",
              "w_in", "w_branch_a", "w_branch_b", "w_out", "r_w2", "r_a2", "r_g2"):
        shared[n] = np.ascontiguousarray(inp[n][0], dtype=np.float32)
    in_maps = []
    for c in range(NCORES):
        m = dict(shared)
        m["x"] = np.ascontiguousarray(x[c % 2])
        in_maps.append(m)
    nc = build(nblk)
    ncr = int(os.environ.get("KCORES", str(NCORES)))
    c0 = int(os.environ.get("KCORE0", "0"))
    res = run_bass_kernel_spmd(nc, in_maps[:ncr], core_ids=list(range(c0, c0 + ncr)))
    out = np.stack([np.asarray(res.results[b % ncr]["out"]) for b in range(2)], axis=0)
    return out.reshape(2, SEQ, D).astype(np.float32)
```

```python
import contextlib
import os
import numpy as np
import concourse.bass as bass
import concourse.mybir as mybir
from concourse.bass_utils import run_bass_kernel_spmd

F32 = mybir.dt.float32
BF16 = mybir.dt.bfloat16
AF = mybir.ActivationFunctionType
ALU = mybir.AluOpType

D = 1024
DFF = 2816
NJ = DFF // 128
SEQ = 8192
TB = 512
ALPHA = 2.0 ** 0.25
LN_EPS = 1e-5
GN_EPS = 64e-5
NCORES = 8
WCOLS = 9480
RC0 = 4104
SAME_ENGINE_WAITS = os.environ.get("KSEW", "act,dve,pool,sp").split(",")


class Res:
    __slots__ = ("w", "r", "excl")

    def __init__(self, excl=False):
        self.w = None
        self.r = {}
        self.excl = excl


class Sched:
    NDS = 24

    def __init__(self, nc, stack):
        self.nc = nc
        self.E = {"pe": nc.tensor, "act": nc.scalar, "dve": nc.vector, "pool": nc.gpsimd, "sp": nc.sync}
        self.q = {k: [] for k in self.E}
        self.cnt = {k: 0 for k in self.E}
        self.esem = {k: stack.enter_context(nc.semaphore("s_" + k)) for k in self.E}
        self.dsem = [stack.enter_context(nc.semaphore("d%d" % i)) for i in range(self.NDS)]
        self.dval = [0] * self.NDS
        self.dnext = {"sp": 0, "pool": 0}
        self.dpool = {"sp": list(range(0, 8)), "pool": list(range(8, self.NDS))}
        self.waited = {k: {} for k in self.E}
        self.csem = stack.enter_context(nc.semaphore("cc"))
        self.cval = 0

    def _deps(self, eng, reads, writes, extra=()):
        deps = {}

        def add(t):
            if t is None:
                return
            k = (t[0], t[1])
            if deps.get(k, 0) < t[2]:
                deps[k] = t[2]

        for r in reads:
            add(r.w)
        for w in writes:
            add(w.w)
            for k, v in w.r.items():
                add((k[0], k[1], v))
        for t in extra:
            add(t)
        waits = []
        for k, v in deps.items():
            if k[0] == "e" and k[1] == eng and (eng == "pe" or eng not in SAME_ENGINE_WAITS):
                continue
            if self.waited[eng].get(k, 0) >= v:
                continue
            self.waited[eng][k] = v
            waits.append((k, v))
        return waits

    def _mark(self, tok, reads, writes):
        k = (tok[0], tok[1])
        for r in reads:
            if r.r.get(k, 0) < tok[2]:
                r.r[k] = tok[2]
        for w in writes:
            w.w = tok
            w.r = {}

    def op(self, eng, fn, reads=(), writes=()):
        ex = [r for r in reads if r.excl]
        if ex:
            writes = list(writes) + ex
        waits = self._deps(eng, reads, writes)
        self.cnt[eng] += 1
        tok = ("e", eng, self.cnt[eng])
        self.q[eng].append((waits, fn, None))
        self._mark(tok, reads, writes)
        return tok

    def dma(self, qeng, out, in_, reads=(), writes=()):
        pool = self.dpool[qeng]
        j = pool[self.dnext[qeng]]
        self.dnext[qeng] = (self.dnext[qeng] + 1) % len(pool)
        prev = ("d", j, self.dval[j]) if self.dval[j] else None
        extra = [prev] if prev else []
        waits = self._deps(qeng, reads, writes, extra=extra)
        self.dval[j] += 16
        tok = ("d", j, self.dval[j])
        self.q[qeng].append((waits, (out, in_), j))
        self._mark(tok, reads, writes)
        return tok

    def cc(self, kind, ins, outs, groups, reads=(), writes=()):
        waits = self._deps("pool", reads, writes)
        self.cval += 1
        tok = ("c", 0, self.cval)
        self.q["pool"].append((waits, (kind, ins, outs, groups), "cc"))
        self._mark(tok, reads, writes)
        return tok

    def emit(self):
        nc = self.nc
        with nc.Block() as block:
            def run(kind):
                def body(eng):
                    for waits, fn, dj in self.q[kind]:
                        for k, v in waits:
                            sem = self.esem[k[1]] if k[0] == "e" else (self.csem if k[0] == "c" else self.dsem[k[1]])
                            eng.wait_ge(sem, v)
                        if dj is None:
                            fn(eng).then_inc(self.esem[kind], 1)
                        elif dj == "cc":
                            eng.collective_compute(fn[0], ALU.bypass, replica_groups=fn[3], ins=[fn[1]], outs=[fn[2]]).then_inc(self.csem, 1)
                        else:
                            eng.dma_start(out=fn[0], in_=fn[1]).then_inc(self.dsem[dj], 16)
                return body
            block.tensor(run("pe"))
            block.scalar(run("act"))
            block.vector(run("dve"))
            block.gpsimd(run("pool"))
            block.sync(run("sp"))


class Ring:
    def __init__(self, items):
        self.items = items
        self.i = 0

    def next(self):
        it = self.items[self.i]
        self.i = (self.i + 1) % len(self.items)
        return it


PV = {}
_o = 0
for _n, _w in [("ln1_g", 8), ("ln1_b", 8), ("ln2_g", 8), ("ln2_b", 8), ("ln3_g", 8), ("ln3_b", 8),
               ("cw0", 4), ("cw1", 4), ("cw2", 4), ("cw3", 4), ("cb", 4), ("mng", 2),
               ("mu", 8), ("omu", 8), ("w0", 2), ("a0", 2), ("kk", 2), ("ka", 2), ("rrk", 2),
               ("gng", 2), ("gnb", 2)]:
    PV[_n] = _o
    _o += _w
NPV = _o
C_ID, C_OD, C_MUS, C_TRI, C_ONE, C_MLS, C_IST, C_BO, NCST = 0, 128, 256, 384, 512, 640, 768, 832, 960


def build(nbl, groups):
    nc = bass.Bass("TRN2", target_bir_lowering=False)

    def din(name, shape):
        return nc.dram_tensor(name, list(shape), F32, kind="ExternalInput").ap()

    NBT = 4 * nbl
    x_d = din("x", [nbl * TB, D])
    cst_d = din("cst", [128, NCST])
    pv_d = din("pv", [128, NPV])
    gb_d = din("gbias", [128, 8])
    rst_d = din("rst", [128, 512])
    w1g = din("ffn1_w_gate", [D, DFF]); w1u = din("ffn1_w_up", [D, DFF]); w1d = din("ffn1_w_down", [DFF, D])
    w2g = din("ffn2_w_gate", [D, DFF]); w2u = din("ffn2_w_up", [D, DFF]); w2d = din("ffn2_w_down", [DFF, D])
    win = din("w_loc", [D, 2056])
    wif_d = din("w_if", [D, 8])
    wgate = din("w_gate", [D, 2048])
    oh_d = din("oh", [128, 4])
    wa_d = din("w_branch_a", [D, D]); wb_d = din("w_branch_b", [D, D]); wo_d = din("w_out", [D, D])
    rw2_d = din("r_w2", [64, 256]); ra2_d = din("r_a2", [64, 256]); rg2_d = din("r_g2", [128, 256])
    out_d = nc.dram_tensor("out", [nbl * TB, D], F32, kind="ExternalOutput").ap()
    x1f_loc = nc.dram_tensor("x1f_loc", [nbl * 128, 8 * TB], F32).ap()
    x1b_loc = [[nc.dram_tensor("x1bl_%d_%d" % (k, h), [128, 4 * TB], BF16).ap() for h in range(2)] for k in range(nbl)]
    x1b_all = [[nc.dram_tensor("x1ba_%d_%d" % (k, h), [4 * 128, 4 * TB], BF16).ap() for h in range(2)] for k in range(nbl)]
    hy_loc = [nc.dram_tensor("hyl_%d" % k, [128, 4 * TB], BF16).ap() for k in range(NBT)]
    hy_all = [nc.dram_tensor("hya_%d" % k, [4 * 128, 4 * TB], BF16).ap() for k in range(NBT)]

    with contextlib.ExitStack() as st:
        S = Sched(nc, st)
        _n = [0]

        def sb(shape, dt=F32):
            _n[0] += 1
            return st.enter_context(nc.sbuf_tensor("sb%d" % _n[0], list(shape), dt))

        def ring(n, shape, dt=F32):
            return Ring([(sb(shape, dt), Res()) for _ in range(n)])

        banks = [st.enter_context(nc.psum_tensor("ps%d" % i, [128, 512], F32)) for i in range(8)]
        psr = Ring([(banks[i], Res(True)) for i in range(7)])
        pyo_bank = (banks[7], Res(True))

        cst = sb([128, NCST]); cst_r = Res()
        cstb = sb([128, NCST], BF16); cstb_r = Res()
        pv = sb([128, NPV]); pv_r = Res()
        gbias = sb([128, 8]); gb_r = Res()
        S.dma("sp", cst[:, :], cst_d[:, :], writes=[cst_r])
        S.dma("sp", pv[:, :], pv_d[:, :], writes=[pv_r])
        S.dma("sp", gbias[:, :], gb_d[:, :], writes=[gb_r])
        S.op("act", lambda e: e.copy(cstb[:, :], cst[:, :]), reads=[cst_r], writes=[cstb_r])
        ident = cst[:, C_ID:C_ID + 128]
        identb = cstb[:, C_ID:C_ID + 128]
        onesdb = cstb[:, C_OD:C_OD + 128]
        tri = cst[:, C_TRI:C_TRI + 128]
        ones = cst[:, C_ONE:C_ONE + 128]
        istb = cstb[:, C_IST:C_IST + 64]
        bob = cstb[:, C_BO:C_BO + 128]
        rstb = sb([128, 512], BF16)
        S.dma("pool", rstb[:, :], rst_d[:, :], writes=[cstb_r])
        rst = rstb[:, :]
        epsln = sb([128, 1]); eps4 = sb([128, 1]); epsgn = sb([128, 1]); eps_r = Res()
        S.op("dve", lambda e: e.memset(epsln[:, :], LN_EPS), writes=[eps_r])
        S.op("dve", lambda e: e.memset(eps4[:, :], 4 * LN_EPS), writes=[eps_r])
        S.op("dve", lambda e: e.memset(epsgn[:, :], GN_EPS), writes=[eps_r])

        def pvc(name, i=0):
            c = PV[name] + i
            return pv[:, c:c + 1]

        rw2a2 = sb([128, 256], BF16); rw_r = Res()
        rg2 = sb([128, 256], BF16)
        S.dma("pool", rw2a2[0:64, :], rw2_d[:, :], writes=[rw_r])
        S.dma("pool", rw2a2[64:128, :], ra2_d[:, :], writes=[rw_r])
        S.dma("pool", rg2[:, :], rg2_d[:, :], writes=[rw_r])

        xT = sb([128, 8, TB]); xTb = sb([128, 8, TB], BF16); xT_r = [Res() for _ in range(8)]
        x1T = sb([128, 8, TB]); x1Tb = sb([128, 8, TB], BF16); x1_r = [Res() for _ in range(8)]
        x2T = xT; x2Tb = xTb; x2_r = xT_r
        x3T = xT; x3Tb = xTb; x3_r = xT_r
        aT = sb([128, NJ, TB], BF16); aT_r = [Res() for _ in range(NJ)]
        zT = sb([128, 8, TB]); zT_r = [Res() for _ in range(8)]
        xtok = ring(1, [128, D])
        otok = xtok
        wgu = ring(2, [128, 2, 8, 128], BF16)
        wdb = ring(2, [128, 512], BF16)
        wpj = ring(2, [128, 8, 128], BF16)
        tA = ring(3, [128, TB])
        tB = ring(2, [128, TB], BF16)
        mean_r = Res(); rstd_r = Res()
        out_r = Res()

        def load_xT(blk):
            for t in range(TB // 128):
                xt, xr = xtok.next()
                r0 = blk * TB + t * 128
                S.dma("sp", xt[:, :], x_d[r0:r0 + 128, :], writes=[xr])
                for half in range(2):
                    ps, pr = psr.next()
                    for q in range(4):
                        kc = half * 4 + q
                        S.op("pe", lambda e, ps=ps, xt=xt, kc=kc, q=q: e.matmul(
                            ps[:, q * 128:(q + 1) * 128], xt[:, kc * 128:(kc + 1) * 128], ident,
                            start=True, stop=True), reads=[xr, cst_r], writes=[pr])
                    psv = ps[:, :].rearrange("p (q n) -> p q n", q=4)
                    S.op("act", lambda e, psv=psv, half=half, t=t: e.copy(
                        xT[:, half * 4:half * 4 + 4, t * 128:(t + 1) * 128], psv),
                        reads=[pr], writes=xT_r[half * 4:half * 4 + 4])
                    S.op("dve", lambda e, psv=psv, half=half, t=t: e.tensor_copy(
                        xTb[:, half * 4:half * 4 + 4, t * 128:(t + 1) * 128], psv),
                        reads=[pr], writes=xT_r[half * 4:half * 4 + 4])

        def layer_norm(gname, bname, eps_t, outT, outTb, out_rs):
            psm, pmr = psr.next()
            pss, ssr = psr.next()
            for dc in range(8):
                tb, tr = tB.next()
                S.op("act", lambda e, tb=tb, dc=dc: e.copy(tb[:, :], zT[:, dc, :]), reads=[zT_r[dc]], writes=[tr])
                S.op("pe", lambda e, tb=tb, dc=dc: e.matmul(psm[:, :], onesdb, tb[:, :], start=(dc == 0), stop=(dc == 7)),
                     reads=[cstb_r, tr], writes=[pmr])
                tb2, tr2 = tB.next()
                S.op("act", lambda e, tb2=tb2, dc=dc: e.activation(tb2[:, :], zT[:, dc, :], AF.Square),
                     reads=[zT_r[dc]], writes=[tr2])
                S.op("pe", lambda e, tb2=tb2, dc=dc: e.matmul(pss[:, :], onesdb, tb2[:, :], start=(dc == 0), stop=(dc == 7)),
                     reads=[cstb_r, tr2], writes=[ssr])
            S.op("act", lambda e: e.copy(mean_sb[:, :], psm[:, :]), reads=[pmr], writes=[mean_r])
            t1, r1 = tA.next()
            S.op("dve", lambda e, t1=t1: e.tensor_tensor(t1[:, :], mean_sb[:, :], mean_sb[:, :], ALU.mult),
                 reads=[mean_r], writes=[r1])
            t2, r2 = tA.next()
            S.op("dve", lambda e, t1=t1, t2=t2: e.tensor_tensor(t2[:, :], pss[:, :], t1[:, :], ALU.subtract),
                 reads=[ssr, r1], writes=[r2])
            S.op("dve", lambda e, t2=t2: e.tensor_scalar_max(t2[:, :], t2[:, :], 0.0), reads=[r2], writes=[r2])
            t3, r3 = tA.next()
            S.op("act", lambda e, t2=t2, t3=t3: e.activation(t3[:, :], t2[:, :], AF.Sqrt, bias=eps_t[:, 0:1]),
                 reads=[r2, eps_r], writes=[r3])
            S.op("dve", lambda e, t3=t3: e.reciprocal(rstd_sb[:, :], t3[:, :]), reads=[r3], writes=[rstd_r])
            for dc in range(8):
                ta, tar = tA.next()
                S.op("dve", lambda e, ta=ta, dc=dc: e.tensor_tensor(ta[:, :], zT[:, dc, :], mean_sb[:, :], ALU.subtract),
                     reads=[zT_r[dc], mean_r], writes=[tar])
                S.op("dve", lambda e, ta=ta: e.tensor_tensor(ta[:, :], ta[:, :], rstd_sb[:, :], ALU.mult),
                     reads=[tar, rstd_r], writes=[tar])
                S.op("act", lambda e, ta=ta, dc=dc: e.activation(
                    outT[:, dc, :], ta[:, :], AF.Identity, scale=pvc(gname, dc), bias=pvc(bname, dc)),
                    reads=[tar, pv_r], writes=[out_rs[dc]])
                S.op("act", lambda e, ta=ta, dc=dc: e.activation(
                    outTb[:, dc, :], ta[:, :], AF.Identity, scale=pvc(gname, dc), bias=pvc(bname, dc)),
                    reads=[tar, pv_r], writes=[out_rs[dc]])

        def ffn_ln(inT, inTb, in_rs, Wg, Wu, Wd, gname, bname, outT, outTb, out_rs):
            Wg_v = Wg.rearrange("(kc p) f -> p kc f", p=128)
            Wu_v = Wu.rearrange("(kc p) f -> p kc f", p=128)
            Wd_v = Wd.rearrange("(j p) d -> p j d", p=128)
            for j in range(NJ):
                wb, wr = wgu.next()
                S.dma("pool", wb[:, 0], Wg_v[:, :, j * 128:(j + 1) * 128], writes=[wr])
                S.dma("pool", wb[:, 1], Wu_v[:, :, j * 128:(j + 1) * 128], writes=[wr])
                psg, pgr = psr.next()
                psu, pur = psr.next()
                for gu, (ps, prr) in enumerate(((psg, pgr), (psu, pur))):
                    for kc in range(8):
                        S.op("pe", lambda e, ps=ps, wb=wb, kc=kc, gu=gu: e.matmul(
                            ps[:, :], wb[:, gu, kc, :], inTb[:, kc, :], start=(kc == 0), stop=(kc == 7)),
                            reads=[wr, in_rs[kc]], writes=[prr])
                tb, tr = tA.next()
                S.op("act", lambda e, tb=tb, ps=psg: e.activation(tb[:, :], ps[:, :], AF.Silu), reads=[pgr], writes=[tr])
                S.op("dve", lambda e, tb=tb, ps=psu, j=j: e.tensor_tensor(aT[:, j, :], tb[:, :], ps[:, :], ALU.mult),
                     reads=[tr, pur], writes=[aT_r[j]])
            for half in range(2):
                pss_ = [psr.next() for _ in range(4)]
                for j in range(NJ):
                    wb, wr = wdb.next()
                    S.dma("pool", wb[:, :], Wd_v[:, j, half * 512:(half + 1) * 512], writes=[wr])
                    for q in range(4):
                        ps, pr = pss_[q]
                        S.op("pe", lambda e, ps=ps, wb=wb, j=j, q=q: e.matmul(
                            ps[:, :], wb[:, q * 128:(q + 1) * 128], aT[:, j, :], start=(j == 0), stop=(j == NJ - 1)),
                            reads=[wr, aT_r[j]], writes=[pr])
                for q in range(4):
                    dc = half * 4 + q
                    ps, pr = pss_[q]
                    S.op("dve", lambda e, ps=ps, dc=dc: e.scalar_tensor_tensor(
                        zT[:, dc, :], inT[:, dc, :], 2.0 * ALPHA, ps[:, :], ALU.mult, ALU.add),
                        reads=[pr, in_rs[dc]], writes=[zT_r[dc]])
            layer_norm(gname, bname, eps4, outT, outTb, out_rs)

        def store_T(blk, srcT, src_rs):
            for t in range(TB // 128):
                ot, orr = otok.next()
                for half in range(2):
                    ps, pr = psr.next()
                    for q in range(4):
                        kc = half * 4 + q
                        S.op("pe", lambda e, ps=ps, kc=kc, q=q, t=t: e.matmul(
                            ps[:, q * 128:(q + 1) * 128], srcT[:, kc, t * 128:(t + 1) * 128], ident,
                            start=True, stop=True), reads=[src_rs[kc], cst_r], writes=[pr])
                    S.op("act", lambda e, ps=ps, ot=ot, half=half: e.copy(ot[:, half * 512:(half + 1) * 512], ps[:, :]),
                         reads=[pr], writes=[orr])
                r0 = blk * TB + t * 128
                S.dma("sp", out_d[r0:r0 + 128, :], ot[:, :], reads=[orr], writes=[out_r])

        def projT(Wd_ap, col0, inTb, in_rs, ncols=128):
            wb, wr = wpj.next()
            Wv = Wd_ap.rearrange("(kc p) f -> p kc f", p=128)
            S.dma("pool", wb[:, :, 0:ncols], Wv[:, :, col0:col0 + ncols], writes=[wr])
            ps, pr = psr.next()
            for kc in range(8):
                src = inTb(kc) if callable(inTb) else inTb[:, kc, :]
                S.op("pe", lambda e, ps=ps, wb=wb, kc=kc, src=src: e.matmul(
                    ps[0:ncols, :], wb[:, kc, 0:ncols], src, start=(kc == 0), stop=(kc == 7)),
                    reads=[wr, in_rs[kc]], writes=[pr])
            return ps, pr

        NT = TB // 128
        carry = sb([128, 4, 3]); carry_r = [Res() for _ in range(4)]
        S.op("dve", lambda e: e.memset(carry[:, :, :], 0.0), writes=carry_r)
        cwork = ring(2, [128, 3 + TB])
        mean_sb = cwork.items[0][0][:, 0:TB]; rstd_sb = cwork.items[1][0][:, 0:TB]
        qkT = zT[:, :, :].rearrange("p a b -> p (a b)").bitcast(BF16).rearrange("p (c n) -> p c n", n=TB)
        qk_r = [zT_r[c // 2] for c in range(16)]
        sigmo = sb([128, 2, TB], BF16); sigmo_r = [Res() for _ in range(2)]
        sga = sb([128, 8, TB], BF16); sga_r = [Res() for _ in range(8)]
        sgb = sb([128, 8, TB], BF16); sgb_r = [Res() for _ in range(8)]
        vt = sb([128, NT, 1, 258], BF16); vt_r = [[Res() for _ in range(1)] for _ in range(NT)]
        S.op("dve", lambda e: e.memset(vt[:, :, :, :], 1.0), writes=[r for rr in vt_r for r in rr])
        wv = ring(1, [128, 8, 256], BF16)
        wif = sb([128, 8, 8], BF16); wif_r = Res()
        S.dma("pool", wif[:, :, :], wif_d.rearrange("(kc p) f -> p kc f", p=128), writes=[wif_r])
        gts = sb([128, NT, 24]); gts_r = [Res() for _ in range(NT)]
        Cst = sb([128, 1, 2, 258]); C_r = [Res() for _ in range(1)]
        Cb = sb([128, 1, 2, 258], BF16); Cb_r = [Res() for _ in range(1)]
        S.op("dve", lambda e: e.memset(Cst[:, :, :, :], 0.0), writes=C_r)
        S.op("dve", lambda e: e.memset(Cb[:, :, :, :], 0.0), writes=Cb_r)
        hyT = sb([128, 4, TB], BF16); hm_r = [Res() for _ in range(2)]
        hmTb = hyT[:, 0:2, :]
        HYf = sb([128, 4, 4, TB], BF16); hyf_r = [Res() for _ in range(4)]
        yrTb = hyT[:, 2:4, :]; yr_r = [Res() for _ in range(2)]
        mgTb = sga; mg_r = sga_r
        smr = ring(4, [128, 128], BF16)
        ktk = ring(4, [128, 256], BF16)
        sm6 = ring(4, [128, 8])

        def mlstm(blk):
            Wv = win.rearrange("(kc p) f -> p kc f", p=128)
            for c in range(4):
                ps, pr = projT(win, c * 128, x1Tb, x1_r)
                wk, wkr = cwork.next()
                S.op("act", lambda e, wk=wk, c=c: e.copy(wk[:, 0:3], carry[:, c, :]), reads=[carry_r[c]], writes=[wkr])
                S.op("act", lambda e, wk=wk, ps=ps: e.copy(wk[:, 3:3 + TB], ps[:, :]), reads=[pr], writes=[wkr])
                S.op("act", lambda e, wk=wk, c=c: e.copy(carry[:, c, :], wk[:, TB:TB + 3]), reads=[wkr], writes=[carry_r[c]])
                ta, tar = tA.next()
                S.op("dve", lambda e, wk=wk, ta=ta, c=c: e.tensor_scalar(
                    ta[:, :], wk[:, 0:TB], pvc("cw0", c), pvc("cb", c), ALU.mult, ALU.add),
                    reads=[wkr, pv_r], writes=[tar])
                for j in (1, 2, 3):
                    S.op("dve", lambda e, wk=wk, ta=ta, c=c, j=j: e.scalar_tensor_tensor(
                        ta[:, :], wk[:, j:j + TB], pvc("cw%d" % j, c), ta[:, :], ALU.mult, ALU.add),
                        reads=[wkr, pv_r, tar], writes=[tar])
                S.op("act", lambda e, ta=ta, c=c: e.activation(qkT[:, c, :], ta[:, :], AF.Silu),
                     reads=[tar], writes=[qk_r[c]])
            for c in range(2):
                ps, pr = projT(win, 768 + c * 128, x1Tb, x1_r)
                S.op("act", lambda e, ps=ps, c=c: e.activation(sigmo[:, c, :], ps[:, :], AF.Sigmoid),
                     reads=[pr], writes=[sigmo_r[c]])
            for t in range(NT):
                ps, pr = psr.next()
                for kc in range(8):
                    S.op("pe", lambda e, ps=ps, kc=kc, t=t: e.matmul(
                        ps[:, 0:8], x1Tb[:, kc, t * 128:(t + 1) * 128], wif[:, kc, :], start=(kc == 0), stop=(kc == 7)),
                        reads=[wif_r, x1_r[kc]], writes=[pr])
                g = gts[:, t, :]
                gr = gts_r[t]
                S.op("dve", lambda e, g=g, ps=ps: e.tensor_tensor(g[:, 12:20], ps[:, 0:8], gbias[:, :], ALU.add),
                     reads=[pr, gb_r], writes=[gr])
                S.op("act", lambda e, g=g: e.activation(g[:, 20:24], g[:, 16:20], AF.Exp, scale=-1.0), reads=[gr], writes=[gr])
                S.op("act", lambda e, g=g: e.activation(g[:, 16:20], g[:, 20:24], AF.Ln, bias=1.0), reads=[gr], writes=[gr])
                S.op("dve", lambda e, g=g: e.tensor_scalar_mul(g[:, 16:20], g[:, 16:20], -1.0), reads=[gr], writes=[gr])
                ps2, pr2 = psr.next()
                S.op("pe", lambda e, ps2=ps2, g=g: e.matmul(ps2[:, 0:4], tri, g[:, 16:20], start=True, stop=True),
                     reads=[gr, cst_r], writes=[pr2])
                S.op("pe", lambda e, ps2=ps2, g=g: e.matmul(ps2[:, 4:8], ones, g[:, 16:20], start=True, stop=True),
                     reads=[gr, cst_r], writes=[pr2])
                S.op("dve", lambda e, g=g, ps2=ps2: e.tensor_tensor(g[:, 20:24], g[:, 12:16], ps2[:, 0:4], ALU.subtract),
                     reads=[gr, pr2], writes=[gr])
                S.op("act", lambda e, g=g: e.activation(g[:, 0:4], g[:, 20:24], AF.Exp), reads=[gr], writes=[gr])
                S.op("act", lambda e, g=g, ps2=ps2: e.activation(g[:, 4:8], ps2[:, 0:4], AF.Exp, scale=-1.0),
                     reads=[gr, pr2], writes=[gr])
                S.op("act", lambda e, g=g, ps2=ps2: e.activation(g[:, 8:12], ps2[:, 4:8], AF.Exp), reads=[gr, pr2], writes=[gr])
            for h in range(1):
                wb, wr = wv.next()
                S.dma("pool", wb[:, :, :], Wv[:, :, 512:768], writes=[wr])
                for t in range(NT):
                    ps, pr = psr.next()
                    for kc in range(8):
                        S.op("pe", lambda e, ps=ps, kc=kc, t=t, wb=wb: e.matmul(
                            ps[:, 0:256], x1Tb[:, kc, t * 128:(t + 1) * 128], wb[:, kc, :], start=(kc == 0), stop=(kc == 7)),
                            reads=[wr, x1_r[kc]], writes=[pr])
                    S.op("act", lambda e, ps=ps, t=t, h=h: e.copy(vt[:, t, h, 0:256], ps[:, 0:256]),
                         reads=[pr], writes=[vt_r[t][h]])
            for t in range(NT):
                tc_ = slice(t * 128, (t + 1) * 128)
                g = gts[:, t, :]
                gr = gts_r[t]
                def head_chain(t, h, tc_, g, gr):
                    qc = [h * 2, h * 2 + 1]
                    kc_ = [2 + h * 2, 2 + h * 2 + 1]
                    ps, pr = psr.next()
                    for i in range(2):
                        S.op("pe", lambda e, ps=ps, i=i, kc_=kc_, qc=qc, tc_=tc_: e.matmul(
                            ps[:, 0:128], qkT[:, kc_[i], tc_], qkT[:, qc[i], tc_], start=(i == 0), stop=(i == 1)),
                            reads=[qk_r[kc_[i]], qk_r[qc[i]]], writes=[pr])
                        yield
                    sm, smrr = smr.next()
                    S.op("dve", lambda e, sm=sm, ps=ps, h=h, g=g: e.scalar_tensor_tensor(
                        sm[:, :], ps[:, 0:128], g[:, h:h + 1], tri, ALU.mult, ALU.mult),
                        reads=[pr, gr, cst_r], writes=[smrr])
                    yield
                    po, por = psr.next()
                    S.op("pe", lambda e, po=po, sm=sm, t=t, h=h: e.matmul(
                        po[:, 0:258], sm[:, :], vt[:, t, h, :], start=True, stop=False),
                        reads=[smrr, vt_r[t][h]], writes=[por])
                    yield
                    for i in range(2):
                        S.op("pe", lambda e, po=po, i=i, h=h, qc=qc, tc_=tc_: e.matmul(
                            po[:, 0:258], qkT[:, qc[i], tc_], Cb[:, h, i, :], start=False, stop=(i == 1)),
                            reads=[qk_r[qc[i]], Cb_r[h]], writes=[por])
                        yield
                    s6, s6r = sm6.next()
                    S.op("act", lambda e, s6=s6, po=po: e.activation(
                        s6[:, 0:1], po[:, 256:257], AF.Abs, scale=1.0 / 16.0), reads=[por], writes=[s6r])
                    yield
                    S.op("dve", lambda e, s6=s6, g=g, h=h: e.tensor_tensor(s6[:, 0:1], s6[:, 0:1], g[:, 4 + h:5 + h], ALU.max),
                         reads=[s6r, gr], writes=[s6r])
                    yield
                    S.op("dve", lambda e, s6=s6: e.reciprocal(s6[:, 1:2], s6[:, 0:1]), reads=[s6r], writes=[s6r])
                    yield
                    hb, hbr = hh.next()
                    S.op("dve", lambda e, hb=hb, po=po, s6=s6: e.tensor_scalar(
                        hb[:, :], po[:, 0:256], s6[:, 1:2], 1.0 / 16.0, ALU.mult, ALU.mult), reads=[por, s6r], writes=[hbr])
                    yield
                    S.op("dve", lambda e, hb=hb, s6=s6: e.bn_stats(s6[:, 2:8], hb[:, :]), reads=[hbr], writes=[s6r])
                    yield
                    S.op("dve", lambda e, s6=s6: e.bn_aggr(s6[:, 0:2], s6[:, 2:8]), reads=[s6r], writes=[s6r])
                    yield
                    S.op("act", lambda e, s6=s6: e.activation(s6[:, 2:3], s6[:, 1:2], AF.Sqrt, bias=epsln[:, 0:1]),
                         reads=[s6r, eps_r], writes=[s6r])
                    yield
                    S.op("dve", lambda e, s6=s6: e.reciprocal(s6[:, 3:4], s6[:, 2:3]), reads=[s6r], writes=[s6r])
                    yield
                    hn, hnr = hnb.next()
                    S.op("dve", lambda e, hn=hn, hb=hb, s6=s6: e.tensor_scalar(
                        hn[:, :], hb[:, :], s6[:, 0:1], s6[:, 3:4], ALU.subtract, ALU.mult), reads=[hbr, s6r], writes=[hnr])
                    yield
                    for i in range(2):
                        pt, ptr = psr.next()
                        S.op("pe", lambda e, pt=pt, hn=hn, i=i: e.matmul(
                            pt[:, 0:128], hn[:, i * 128:(i + 1) * 128], identb, start=True, stop=True),
                            reads=[hnr, cstb_r], writes=[ptr])
                        yield
                        S.op("dve", lambda e, pt=pt, h=h, i=i, tc_=tc_: e.scalar_tensor_tensor(
                            hmTb[:, h * 2 + i, tc_], pt[:, 0:128], pvc("mng", h * 2 + i), sigmo[:, h * 2 + i, tc_],
                            ALU.mult, ALU.mult), reads=[ptr, pv_r, sigmo_r[h * 2 + i]], writes=[hm_r[h * 2 + i]])
                        yield
                    kk_, kkr = ktk.next()
                    for i in range(2):
                        pt, ptr = psr.next()
                        S.op("pe", lambda e, pt=pt, i=i, kc_=kc_, tc_=tc_: e.matmul(
                            pt[:, 0:128], qkT[:, kc_[i], tc_], identb, start=True, stop=True),
                            reads=[qk_r[kc_[i]], cstb_r], writes=[ptr])
                        yield
                        S.op("act", lambda e, pt=pt, kk_=kk_, i=i, g=g, h=h: e.activation(
                            kk_[:, i * 128:(i + 1) * 128], pt[:, 0:128], AF.Identity, scale=g[:, h:h + 1]),
                            reads=[ptr, gr], writes=[kkr])
                        yield
                    for i in range(2):
                        pc, pcr = psr.next()
                        S.op("pe", lambda e, pc=pc, kk_=kk_, i=i, t=t, h=h: e.matmul(
                            pc[:, 0:258], kk_[:, i * 128:(i + 1) * 128], vt[:, t, h, :], start=True, stop=True),
                            reads=[kkr, vt_r[t][h]], writes=[pcr])
                        yield
                        S.op("dve", lambda e, h=h, i=i, g=g: e.tensor_scalar_mul(Cst[:, h, i, :], Cst[:, h, i, :], g[:, 8 + h:9 + h]),
                             reads=[C_r[h], gr], writes=[C_r[h]])
                        yield
                        S.op("dve", lambda e, pc=pc, h=h, i=i, g=g: e.scalar_tensor_tensor(
                            Cst[:, h, i, :], pc[:, 0:258], g[:, 8 + h:9 + h], Cst[:, h, i, :], ALU.mult, ALU.add),
                            reads=[pcr, gr, C_r[h]], writes=[C_r[h]])
                        yield
                        S.op("act", lambda e, h=h, i=i: e.copy(Cb[:, h, i, :], Cst[:, h, i, :]), reads=[C_r[h]], writes=[Cb_r[h]])
                        yield

                gens = [head_chain(t, h, tc_, g, gr) for h in range(1)]
                while gens:
                    for gg in list(gens):
                        try:
                            next(gg)
                        except StopIteration:
                            gens.remove(gg)

        NCH = TB // 64
        rcar = sb([128, 8, 1]); rcar_r = [Res() for _ in range(8)]
        S.op("dve", lambda e: e.memset(rcar[:, :, :], 0.0), writes=rcar_r)
        rwork = cwork
        lowT = sb([128, 2, TB], BF16); low_r = [Res(), Res()]
        rtmp = Ring([(aT[:, 2 * i:2 * i + 2, :].rearrange("p a b -> p (a b)").bitcast(F32), Res()) for i in range(10)])
        ARbd = sb([128, NCH, 256], BF16); Bbd = sb([128, NCH, 128], BF16); Kbd = sb([128, NCH, 128], BF16)
        Vbd = sb([128, NCH, 128], BF16); Ynbd = sb([128, 128], BF16)
        bd_r = Res(); ynbd_r = Res()
        for tns in (ARbd, Bbd, Kbd, Vbd):
            S.op("dve", lambda e, tns=tns: e.memset(tns[:, :, :], 0.0), writes=[bd_r])
        S.op("dve", lambda e: e.memset(Ynbd[:, :], 0.0), writes=[ynbd_r])
        gam = sb([128, NCH]); gam_r = Res()
        Hst = sb([128, 2, 64]); H_r = [Res() for _ in range(2)]
        Hb = sb([128, 2, 64], BF16); Hb_r = [Res() for _ in range(2)]
        S.op("dve", lambda e: e.memset(Hst[:, :, :], 0.0), writes=H_r)
        S.op("dve", lambda e: e.memset(Hb[:, :, :], 0.0), writes=Hb_r)
        vst = sb([128, NCH, 64], BF16); vst_r = Res()
        btk = sb([128, NCH, 256], BF16); btk_r = Res()
        nm = ring(4, [128, 256], BF16)
        nsq = ring(11, [128, 128], BF16)
        ub = ring(4, [128, 64], BF16)
        uf = ring(4, [128, 64])
        gn6 = ring(4, [128, 8])
        htmp = ring(2, [128, 64])
        maskAR = cst[:, C_MUS:C_MUS + 256]; mask_r = cst_r
        mls = cst[:, C_MLS:C_MLS + 128]

        xTv = xT[:, :, :].rearrange("p a b -> p (a b)").bitcast(BF16)
        mAK = xTv[:, 0:NCH * 512].rearrange("p (n c) -> p n c", c=512)
        Xs = xTv[:, 4096:4096 + NCH * 128].rearrange("p (n c) -> p n c", c=128)
        xTbv = xTb[:, :, :].rearrange("p a b -> p (a b)")
        PW = [[xTbv[:, (b * 8 + n) * 128:(b * 8 + n + 1) * 128] for n in range(NCH)] for b in range(2)]
        PWT = [[xTbv[:, 2048 + (b * 8 + n) * 128:2048 + (b * 8 + n + 1) * 128] for n in range(NCH)] for b in range(2)]
        hh = Ring([(xTv[:, 5120 + i * 512:5120 + (i + 1) * 512].bitcast(F32), Res()) for i in range(4)])
        hnb = Ring([(xTv[:, 7168 + i * 256:7168 + (i + 1) * 256], Res()) for i in range(4)])
        mA_r = [Res() for _ in range(NCH)]; mK_r = [Res() for _ in range(NCH)]; X_r = [Res() for _ in range(NCH)]
        PW_r = [[Res() for _ in range(NCH)] for _ in range(2)]; PWT_r = [[Res() for _ in range(NCH)] for _ in range(2)]

        def v3(ap):
            return ap.rearrange("p (n l) -> p n l", l=64)

        def shifted(ci, ps, pr):
            wk, wkr = rwork.next()
            S.op("act", lambda e, wk=wk: e.copy(wk[:, 0:1], rcar[:, ci, :]), reads=[rcar_r[ci]], writes=[wkr])
            S.op("act", lambda e, wk=wk, ps=ps: e.copy(wk[:, 1:1 + TB], ps[:, :]), reads=[pr], writes=[wkr])
            S.op("act", lambda e, wk=wk: e.copy(rcar[:, ci, :], wk[:, TB:TB + 1]), reads=[wkr], writes=[rcar_r[ci]])
            ta, tar = rtmp.next()
            S.op("dve", lambda e, wk=wk, ta=ta: e.tensor_scalar_mul(ta[:, :], wk[:, 1:1 + TB], pvc("omu", ci)),
                 reads=[wkr, pv_r], writes=[tar])
            S.op("dve", lambda e, wk=wk, ta=ta: e.scalar_tensor_tensor(
                ta[:, :], wk[:, 0:TB], pvc("mu", ci), ta[:, :], ALU.mult, ALU.add), reads=[wkr, pv_r, tar], writes=[tar])
            return ta, tar

        krw = int(os.environ.get("KRW", "9"))

        def rwkv(blk):
            ps, pr = projT(win, 1800, x1Tb, x1_r)
            ta, tar = shifted(6, ps, pr)
            S.op("act", lambda e, ta=ta: e.activation(lowT[0:64, 0, :], ta[0:64, :], AF.Tanh), reads=[tar], writes=[low_r[0]])
            S.op("act", lambda e, ta=ta: e.copy(lowT[64:128, 0, :], ta[64:128, :]), reads=[tar], writes=[low_r[0]])
            ps, pr = projT(win, 1928, x1Tb, x1_r)
            ta, tar = shifted(7, ps, pr)
            S.op("act", lambda e, ta=ta: e.activation(lowT[:, 1, :], ta[:, :], AF.Sigmoid), reads=[tar], writes=[low_r[1]])
            for p in range(2):
                cs = slice(p * 128, (p + 1) * 128)
                ps, pr = projT(win, 1032 + p * 128, x1Tb, x1_r)
                r_, r_r = shifted(p, ps, pr)
                ps, pr = projT(win, 1288 + p * 128, x1Tb, x1_r)
                k_, k_r = shifted(2 + p, ps, pr)
                ps, pr = projT(win, 1544 + p * 128, x1Tb, x1_r)
                v_, v_r = shifted(4 + p, ps, pr)
                pw, pwr = psr.next()
                S.op("pe", lambda e, pw=pw, cs=cs: e.matmul(pw[:, :], rw2a2[0:64, cs], lowT[0:64, 0, :], start=True, stop=True),
                     reads=[rw_r, low_r[0]], writes=[pwr])
                lw, lwr = rtmp.next()
                S.op("act", lambda e, lw=lw, pw=pw, p=p: e.activation(lw[:, :], pw[:, :], AF.Sigmoid, bias=pvc("w0", p)),
                     reads=[pwr, pv_r], writes=[lwr])
                S.op("dve", lambda e, lw=lw: e.tensor_scalar_mul(lw[:, :], lw[:, :], -float(np.exp(-0.5))), reads=[lwr], writes=[lwr])
                pa, par = psr.next()
                S.op("pe", lambda e, pa=pa, cs=cs: e.matmul(pa[:, :], rw2a2[64:128, cs], lowT[64:128, 0, :], start=True, stop=True),
                     reads=[rw_r, low_r[0]], writes=[par])
                a_, a_r = rtmp.next()
                S.op("act", lambda e, a_=a_, pa=pa, p=p: e.activation(a_[:, :], pa[:, :], AF.Sigmoid, bias=pvc("a0", p)),
                     reads=[par, pv_r], writes=[a_r])
                pg, pgr = psr.next()
                S.op("pe", lambda e, pg=pg, cs=cs: e.matmul(pg[:, :], rg2[:, cs], lowT[:, 1, :], start=True, stop=True),
                     reads=[rw_r, low_r[1]], writes=[pgr])
                g_, g_r = rtmp.next()
                S.op("act", lambda e, g_=g_, pg=pg: e.copy(g_[:, :], pg[:, :]), reads=[pgr], writes=[g_r])
                kk, kkr = rtmp.next()
                S.op("dve", lambda e, kk=kk, k_=k_, p=p: e.tensor_scalar_mul(kk[:, :], k_[:, :], pvc("kk", p)),
                     reads=[k_r, pv_r], writes=[kkr])
                sq, sqr = tB.next()
                S.op("act", lambda e, sq=sq, kk=kk: e.activation(sq[:, :], kk[:, :], AF.Square), reads=[kkr], writes=[sqr])
                pq, pqr = psr.next()
                S.op("pe", lambda e, pq=pq, sq=sq: e.matmul(pq[:, :], bob, sq[:, :], start=True, stop=True),
                     reads=[cstb_r, sqr], writes=[pqr])
                t1, t1r = rtmp.next()
                S.op("act", lambda e, t1=t1, pq=pq: e.activation(t1[:, :], pq[:, :], AF.Sqrt), reads=[pqr], writes=[t1r])
                S.op("dve", lambda e, t1=t1: e.tensor_scalar_max(t1[:, :], t1[:, :], 1e-12), reads=[t1r], writes=[t1r])
                S.op("dve", lambda e, t1=t1: e.reciprocal(t1[:, :], t1[:, :]), reads=[t1r], writes=[t1r])
                S.op("dve", lambda e, t1=t1, kk=kk: e.tensor_tensor(kk[:, :], kk[:, :], t1[:, :], ALU.mult),
                     reads=[t1r, kkr], writes=[kkr])
                S.op("dve", lambda e, t1=t1, a_=a_, p=p: e.tensor_scalar(t1[:, :], a_[:, :], 1.0, pvc("ka", p), ALU.subtract, ALU.mult),
                     reads=[a_r, pv_r], writes=[t1r])
                S.op("dve", lambda e, t1=t1, k_=k_: e.scalar_tensor_tensor(k_[:, :], t1[:, :], 1.0, k_[:, :], ALU.add, ALU.mult),
                     reads=[t1r, k_r], writes=[k_r])
                t2, t2r = tB.next()
                S.op("dve", lambda e, t2=t2, r_=r_, k_=k_, p=p: e.scalar_tensor_tensor(
                    t2[:, :], r_[:, :], pvc("rrk", p), k_[:, :], ALU.mult, ALU.mult), reads=[r_r, k_r, pv_r], writes=[t2r])
                pb, pbr = psr.next()
                S.op("pe", lambda e, pb=pb, t2=t2: e.matmul(pb[:, :], bob, t2[:, :], start=True, stop=True),
                     reads=[cstb_r, t2r], writes=[pbr])
                bon, bonr = rtmp.next()
                S.op("dve", lambda e, bon=bon, pb=pb, v_=v_: e.tensor_tensor(bon[:, :], pb[:, :], v_[:, :], ALU.mult),
                     reads=[pbr, v_r], writes=[bonr])
                cl, clr = rtmp.next()
                S.op("dve", lambda e, cl=cl, lw=lw: e.tensor_tensor_scan(cl[:, :], rst, lw[:, :], 0.0, ALU.mult, ALU.add),
                     reads=[lwr, cstb_r], writes=[clr])
                e1, e1r = tA.next()
                S.op("act", lambda e, e1=e1, cl=cl: e.activation(e1[:, :], cl[:, :], AF.Exp), reads=[clr], writes=[e1r])
                S.op("act", lambda e, e1=e1: e.copy(gam[:, :], v3(e1[:, :])[:, :, 63]), reads=[e1r], writes=[gam_r])
                for hf in range(2):
                    hs = slice(hf * 64, hf * 64 + 64)
                    S.op("dve", lambda e, hs=hs, hf=hf, r_=r_, e1=e1: e.tensor_tensor(
                        ARbd[hs, :, 128 + hf * 64:128 + hf * 64 + 64], v3(r_[hs, :]), v3(e1[hs, :]), ALU.mult),
                        reads=[r_r, e1r], writes=[bd_r])
                e2, e2r = tA.next()
                S.op("act", lambda e, e2=e2, cl=cl: e.activation(e2[:, :], cl[:, :], AF.Exp, scale=-1.0), reads=[clr], writes=[e2r])
                S.op("dve", lambda e, t1=t1, kk=kk, a_=a_: e.tensor_tensor(t1[:, :], kk[:, :], a_[:, :], ALU.mult),
                     reads=[kkr, a_r], writes=[t1r])
                for hf in range(2):
                    hs = slice(hf * 64, hf * 64 + 64)
                    S.op("dve", lambda e, hs=hs, hf=hf, t1=t1, e2=e2: e.tensor_tensor(
                        Bbd[hs, :, hf * 64:hf * 64 + 64], v3(t1[hs, :]), v3(e2[hs, :]), ALU.mult), reads=[t1r, e2r], writes=[bd_r])
                    S.op("dve", lambda e, hs=hs, hf=hf, k_=k_, e2=e2: e.tensor_tensor(
                        Kbd[hs, :, hf * 64:hf * 64 + 64], v3(k_[hs, :]), v3(e2[hs, :]), ALU.mult), reads=[k_r, e2r], writes=[bd_r])
                    S.op("act", lambda e, hs=hs, hf=hf, v_=v_: e.copy(Vbd[hs, :, hf * 64:hf * 64 + 64], v3(v_[hs, :])),
                         reads=[v_r], writes=[bd_r])
                S.op("dve", lambda e, cl=cl, lw=lw: e.tensor_tensor(cl[:, :], cl[:, :], lw[:, :], ALU.subtract),
                     reads=[clr, lwr], writes=[clr])
                e3, e3r = tA.next()
                S.op("act", lambda e, e3=e3, cl=cl: e.activation(e3[:, :], cl[:, :], AF.Exp), reads=[clr], writes=[e3r])
                for hf in range(2):
                    hs = slice(hf * 64, hf * 64 + 64)
                    S.op("dve", lambda e, hs=hs, hf=hf, kk=kk, e3=e3: e.scalar_tensor_tensor(
                        ARbd[hs, :, hf * 64:hf * 64 + 64], v3(kk[hs, :]), -1.0, v3(e3[hs, :]), ALU.mult, ALU.mult),
                        reads=[kkr, e3r], writes=[bd_r])
                if krw <= 1:
                    S.op("dve", lambda e, p=p: e.memset(yrTb[:, p, :], 0.0), writes=[yr_r[p]])
                    continue
                pv_, pvr_ = psr.next()
                for n in range(NCH):
                    S.op("pe", lambda e, n=n, pv_=pv_: e.matmul(pv_[:, n * 64:(n + 1) * 64], Vbd[:, n, :], istb, start=True, stop=True),
                         reads=[bd_r, cstb_r], writes=[pvr_])
                S.op("act", lambda e, pv_=pv_: e.copy(vst[:, :, :], v3(pv_[:, :])), reads=[pvr_], writes=[vst_r])
                for n0 in range(0, NCH, 2):
                    pt, ptr = psr.next()
                    for n in (n0, n0 + 1):
                        o = (n - n0) * 256
                        S.op("pe", lambda e, n=n, pt=pt, o=o: e.matmul(pt[:, o:o + 128], Bbd[:, n, :], identb, start=True, stop=True),
                             reads=[bd_r, cstb_r], writes=[ptr])
                        S.op("pe", lambda e, n=n, pt=pt, o=o: e.matmul(pt[:, o + 128:o + 256], Kbd[:, n, :], identb, start=True, stop=True),
                             reads=[bd_r, cstb_r], writes=[ptr])
                    S.op("act", lambda e, n0=n0, pt=pt: e.copy(btk[:, n0:n0 + 2, :], pt[:, :].rearrange("p (n l) -> p n l", l=256)),
                         reads=[ptr], writes=[btk_r])
                if krw <= 2:
                    S.op("dve", lambda e, p=p: e.memset(yrTb[:, p, :], 0.0), writes=[yr_r[p]])
                    continue
                pyo, pyor = pyo_bank
                for g0 in range(0, NCH, 4):
                    G = list(range(g0, min(g0 + 4, NCH)))
                    pAs = {}
                    for n in G:
                        pA, pAr = psr.next()
                        pAs[n] = (pA, pAr)
                        S.op("pe", lambda e, pA=pA, n=n: e.matmul(pA[:, 0:256], Bbd[:, n, :], ARbd[:, n, :], start=True, stop=True),
                             reads=[bd_r], writes=[pAr])
                        S.op("pe", lambda e, pA=pA, n=n: e.matmul(pA[:, 256:512], Kbd[:, n, :], ARbd[:, n, :], start=True, stop=True),
                             reads=[bd_r], writes=[pAr])
                    for n in G:
                        pA, pAr = pAs[n]
                        S.op("dve", lambda e, pA=pA, n=n: e.tensor_tensor(mAK[:, n, 0:256], pA[:, 0:256], maskAR[:, :], ALU.mult),
                             reads=[pAr, mask_r], writes=[mA_r[n]])
                        S.op("dve", lambda e, pA=pA, n=n: e.tensor_tensor(mAK[:, n, 256:512], pA[:, 256:512], maskAR[:, :], ALU.mult),
                             reads=[pAr, mask_r], writes=[mK_r[n]])
                    pTs = {}
                    for n in G:
                        pT, pTr = psr.next()
                        pTs[n] = (pT, pTr)
                        S.op("pe", lambda e, pT=pT, n=n: e.matmul(pT[:, 0:128], ARbd[:, n, 0:128], Bbd[:, n, :], start=True, stop=True),
                             reads=[bd_r], writes=[pTr])
                    for n in G:
                        pT, pTr = pTs[n]
                        S.op("dve", lambda e, pT=pT, n=n: e.tensor_tensor(PWT[0][n], pT[:, 0:128], mls, ALU.mult),
                             reads=[pTr, cst_r], writes=[PWT_r[0][n]])
                        S.op("dve", lambda e, n=n: e.tensor_tensor(Xs[:, n, :], mAK[:, n, 0:128], identb, ALU.add),
                             reads=[mA_r[n], cstb_r], writes=[X_r[n]])
                    for j in range(1, 6):
                        b0, b1 = (j - 1) % 2, j % 2
                        p2s = {}
                        for n in G:
                            cur = mAK[:, n, 0:128] if j == 1 else PW[b0][n]
                            curr = mA_r[n] if j == 1 else PW_r[b0][n]
                            ct, ctr_ = PWT[b0][n], PWT_r[b0][n]
                            p2, p2r = psr.next()
                            p2s[n] = (p2, p2r)
                            if j < 5:
                                S.op("pe", lambda e, p2=p2, cur=cur, ct=ct: e.matmul(p2[:, 0:128], ct, cur, start=True, stop=True),
                                     reads=[curr, ctr_], writes=[p2r])
                            S.op("pe", lambda e, p2=p2, cur=cur, ct=ct: e.matmul(p2[:, 128:256], cur, ct, start=True, stop=True),
                                 reads=[curr, ctr_], writes=[p2r])
                        for n in G:
                            p2, p2r = p2s[n]
                            S.op("dve", lambda e, p2=p2, n=n, b1=b1: e.tensor_copy(PWT[b1][n], p2[:, 128:256]),
                                 reads=[p2r], writes=[PWT_r[b1][n]])
                            if j < 5:
                                S.op("act", lambda e, p2=p2, n=n, b1=b1: e.copy(PW[b1][n], p2[:, 0:128]),
                                     reads=[p2r], writes=[PW_r[b1][n]])
                        pxs = {}
                        for n in G:
                            px, pxr = psr.next()
                            pxs[n] = (px, pxr)
                            S.op("pe", lambda e, px=px, n=n, b1=b1: e.matmul(px[:, 0:128], PWT[b1][n], Xs[:, n, :], start=True, stop=True),
                                 reads=[PWT_r[b1][n], X_r[n]], writes=[pxr])
                        for n in G:
                            px, pxr = pxs[n]
                            S.op("dve", lambda e, px=px, n=n: e.tensor_tensor(Xs[:, n, :], px[:, 0:128], Xs[:, n, :], ALU.add),
                                 reads=[pxr, X_r[n]], writes=[X_r[n]])
                for n in range(NCH):
                    pw_, pw_r = psr.next()
                    S.op("pe", lambda e, pw_=pw_, n=n, p=p: e.matmul(pw_[:, 0:64], ARbd[:, n, 0:128], Hb[:, p, :], start=True, stop=False),
                         reads=[bd_r, Hb_r[p]], writes=[pw_r])
                    S.op("pe", lambda e, pw_=pw_, n=n: e.matmul(pw_[:, 0:64], mAK[:, n, 256:384], vst[:, n, :], start=False, stop=True),
                         reads=[mK_r[n], vst_r], writes=[pw_r])
                    w_b, wbr = ub.next()
                    S.op("act", lambda e, w_b=w_b, pw_=pw_: e.copy(w_b[:, :], pw_[:, 0:64]), reads=[pw_r], writes=[wbr])
                    pu, pur_ = psr.next()
                    S.op("pe", lambda e, pu=pu, n=n, w_b=w_b: e.matmul(pu[:, 0:64], Xs[:, n, :], w_b[:, :], start=True, stop=True),
                         reads=[X_r[n], wbr], writes=[pur_])
                    u_b, ubr = ub.next()
                    S.op("dve", lambda e, u_b=u_b, pu=pu: e.tensor_copy(u_b[:, :], pu[:, 0:64]), reads=[pur_], writes=[ubr])
                    ph, phr = psr.next()
                    S.op("pe", lambda e, ph=ph, n=n, u_b=u_b: e.matmul(ph[:, 0:64], btk[:, n, 0:128], u_b[:, :], start=True, stop=False),
                         reads=[btk_r, ubr], writes=[phr])
                    S.op("pe", lambda e, ph=ph, n=n: e.matmul(ph[:, 0:64], btk[:, n, 128:256], vst[:, n, :], start=False, stop=True),
                         reads=[btk_r, vst_r], writes=[phr])
                    py, pyr = psr.next()
                    S.op("pe", lambda e, py=py, n=n, p=p: e.matmul(py[:, 0:64], ARbd[:, n, 128:256], Hb[:, p, :], start=True, stop=False),
                         reads=[bd_r, Hb_r[p]], writes=[pyr])
                    S.op("pe", lambda e, py=py, n=n, u_b=u_b: e.matmul(py[:, 0:64], mAK[:, n, 128:256], u_b[:, :], start=False, stop=False),
                         reads=[mA_r[n], ubr], writes=[pyr])
                    S.op("pe", lambda e, py=py, n=n: e.matmul(py[:, 0:64], mAK[:, n, 384:512], vst[:, n, :], start=False, stop=True),
                         reads=[mK_r[n], vst_r], writes=[pyr])
                    ht, htr = htmp.next()
                    S.op("dve", lambda e, ht=ht, ph=ph, p=p: e.tensor_tensor(ht[:, :], ph[:, 0:64], Hst[:, p, :], ALU.add),
                         reads=[phr, H_r[p]], writes=[htr])
                    S.op("act", lambda e, ht=ht, p=p, n=n: e.activation(Hb[:, p, :], ht[:, :], AF.Identity, scale=gam[:, n:n + 1]),
                         reads=[htr, gam_r], writes=[Hb_r[p]])
                    S.op("act", lambda e, ht=ht, p=p, n=n: e.activation(Hst[:, p, :], ht[:, :], AF.Identity, scale=gam[:, n:n + 1]),
                         reads=[htr, gam_r], writes=[H_r[p]])
                    g6, g6r = gn6.next()
                    S.op("dve", lambda e, g6=g6, py=py: e.bn_stats(g6[:, 2:8], py[:, 0:64]), reads=[pyr], writes=[g6r])
                    S.op("dve", lambda e, g6=g6: e.bn_aggr(g6[:, 0:2], g6[:, 2:8]), reads=[g6r], writes=[g6r])
                    S.op("act", lambda e, g6=g6: e.activation(g6[:, 2:3], g6[:, 1:2], AF.Sqrt, bias=epsgn[:, 0:1]),
                         reads=[g6r, eps_r], writes=[g6r])
                    S.op("dve", lambda e, g6=g6: e.reciprocal(g6[:, 3:4], g6[:, 2:3]), reads=[g6r], writes=[g6r])
                    for hf in range(2):
                        hs = slice(hf * 64, hf * 64 + 64)
                        S.op("dve", lambda e, hs=hs, hf=hf, py=py, g6=g6: e.tensor_scalar(
                            Ynbd[hs, hf * 64:hf * 64 + 64], py[hs, 0:64], g6[hs, 0:1], g6[hs, 3:4], ALU.subtract, ALU.mult),
                            reads=[pyr, g6r], writes=[ynbd_r])
                    S.op("pe", lambda e, n=n, pyo=pyo: e.matmul(pyo[:, n * 64:(n + 1) * 64], Ynbd[:, :], istb, start=True, stop=True),
                         reads=[ynbd_r, cstb_r], writes=[pyor])
                if krw <= 5:
                    S.op("dve", lambda e, p=p: e.memset(yrTb[:, p, :], 0.0), writes=[yr_r[p]])
                    continue
                S.op("dve", lambda e, t1=t1, pyo=pyo, p=p: e.tensor_scalar(
                    t1[:, :], pyo[:, :], pvc("gng", p), pvc("gnb", p), ALU.mult, ALU.add), reads=[pyor, pv_r], writes=[t1r])
                S.op("dve", lambda e, t1=t1, bon=bon: e.tensor_tensor(t1[:, :], t1[:, :], bon[:, :], ALU.add),
                     reads=[t1r, bonr], writes=[t1r])
                S.op("dve", lambda e, t1=t1, g_=g_, p=p: e.tensor_tensor(yrTb[:, p, :], t1[:, :], g_[:, :], ALU.mult),
                     reads=[t1r, g_r], writes=[yr_r[p]])

        def merge_out(blk):
            for c in range(8):
                ps, pr = projT(wgate, c * 128, x1Tb, x1_r)
                S.op("act", lambda e, ps=ps, c=c: e.activation(sga[:, c, :], ps[:, :], AF.Sigmoid), reads=[pr], writes=[sga_r[c]])
                ps, pr = projT(wgate, 1024 + c * 128, x1Tb, x1_r)
                S.op("act", lambda e, ps=ps, c=c: e.activation(sgb[:, c, :], ps[:, :], AF.Sigmoid), reads=[pr], writes=[sgb_r[c]])
            for c in range(8):
                psa, par = projT(wa_d, c * 128, lambda kc: HYf[:, kc // 2, kc % 2, :], [hyf_r[kc // 2] for kc in range(8)])
                psb, pbr = projT(wb_d, c * 128, lambda kc: HYf[:, kc // 2, 2 + kc % 2, :], [hyf_r[kc // 2] for kc in range(8)])
                ta, tar = tA.next()
                S.op("dve", lambda e, ta=ta, psa=psa, c=c: e.tensor_tensor(ta[:, :], psa[:, :], sga[:, c, :], ALU.mult),
                     reads=[par, sga_r[c]], writes=[tar])
                tb, tbr = tA.next()
                S.op("dve", lambda e, tb=tb, psb=psb, c=c: e.tensor_tensor(tb[:, :], psb[:, :], sgb[:, c, :], ALU.mult),
                     reads=[pbr, sgb_r[c]], writes=[tbr])
                S.op("dve", lambda e, ta=ta, tb=tb, c=c: e.tensor_tensor(mgTb[:, c, :], ta[:, :], tb[:, :], ALU.add),
                     reads=[tar, tbr], writes=[mg_r[c]])
            for c in range(8):
                ps, pr = projT(wo_d, c * 128, mgTb, mg_r)
                S.op("dve", lambda e, ps=ps, c=c: e.scalar_tensor_tensor(
                    zT[:, c, :], x1T[:, c, :], ALPHA, ps[:, :], ALU.mult, ALU.add), reads=[pr, x1_r[c]], writes=[zT_r[c]])
            layer_norm("ln2_g", "ln2_b", epsln, x2T, x2Tb, x2_r)

        x1f_r = Res()
        x1bl_r = [[Res(), Res()] for _ in range(nbl)]; x1ba_r = [[Res(), Res()] for _ in range(nbl)]
        hyl_r = [Res() for _ in range(NBT)]; hya_r = [Res() for _ in range(NBT)]
        oh = sb([128, 4]); oh_r = Res()
        S.dma("sp", oh[:, :], oh_d[:, :], writes=[oh_r])

        def half(t, h):
            return t[:, 4 * h:4 * h + 4, :].rearrange("p a b -> p (a b)")

        for blk in range(nbl):
            load_xT(blk)
            ffn_ln(xT, xTb, xT_r, w1g, w1u, w1d, "ln1_g", "ln1_b", x1T, x1Tb, x1_r)
            rows = slice(blk * 128, (blk + 1) * 128)
            S.dma("sp", x1f_loc[rows, :], x1T[:, :, :].rearrange("p a b -> p (a b)"), reads=x1_r, writes=[x1f_r])
            for h in range(2):
                S.dma("sp", x1b_loc[blk][h][:, :], half(x1Tb, h), reads=x1_r, writes=[x1bl_r[blk][h]])
                S.cc("AllGather", x1b_loc[blk][h][:, :], x1b_all[blk][h][:, :], groups,
                     reads=[x1bl_r[blk][h]], writes=[x1ba_r[blk][h]])
        for gb in range(NBT):
            i, blk = gb // nbl, gb % nbl
            for h in range(2):
                S.dma("sp", half(x1Tb, h), x1b_all[blk][h][i * 128:(i + 1) * 128, :], reads=[x1ba_r[blk][h]],
                      writes=x1_r[4 * h:4 * h + 4])
            mlstm(gb)
            rwkv(gb)
            S.dma("sp", hy_loc[gb][:, :], hyT[:, :, :].rearrange("p a b -> p (a b)"), reads=hm_r + yr_r, writes=[hyl_r[gb]])
            S.cc("AllGather", hy_loc[gb][:, :], hy_all[gb][:, :], groups, reads=[hyl_r[gb]], writes=[hya_r[gb]])
        for blk in range(nbl):
            rows = slice(blk * 128, (blk + 1) * 128)
            S.dma("sp", x1T[:, :, :].rearrange("p a b -> p (a b)"), x1f_loc[rows, :], reads=[x1f_r], writes=x1_r)
            for h in range(2):
                S.dma("sp", half(x1Tb, h), x1b_loc[blk][h][:, :], reads=[x1bl_r[blk][h]], writes=x1_r[4 * h:4 * h + 4])
            for i in range(4):
                dst = HYf[:, i, :, :].rearrange("p a b -> p (a b)")
                for j in range(4):
                    k = j * nbl + blk
                    stg = sgb[:, 4 * (j % 2):4 * (j % 2) + 4, :].rearrange("p a b -> p (a b)")
                    stg_r = sgb_r[4 * (j % 2):4 * (j % 2) + 4]
                    S.dma("sp", stg, hy_all[k][i * 128:(i + 1) * 128, :], reads=[hya_r[k]], writes=stg_r)
                    if j == 0:
                        S.op("dve", lambda e, dst=dst, stg=stg, j=j: e.tensor_scalar_mul(dst, stg, oh[:, j:j + 1]),
                             reads=stg_r + [oh_r], writes=[hyf_r[i]])
                    else:
                        S.op("dve", lambda e, dst=dst, stg=stg, j=j: e.scalar_tensor_tensor(
                            dst, stg, oh[:, j:j + 1], dst, ALU.mult, ALU.add),
                            reads=stg_r + [oh_r, hyf_r[i]], writes=[hyf_r[i]])
            merge_out(blk)
            ffn_ln(x2T, x2Tb, x2_r, w2g, w2u, w2d, "ln3_g", "ln3_b", x3T, x3Tb, x3_r)
            store_T(blk, x3T, x3_r)

        S.op("sp", lambda e: e.nop(), reads=[out_r])
        S.emit()
    return nc


def _consts():
    c = np.zeros((128, NCST), np.float32)
    i = np.arange(128)
    c[:, C_ID:C_ID + 128] = np.eye(128)
    c[:, C_OD:C_OD + 128] = 1.0 / 1024.0
    c[:, C_TRI:C_TRI + 128] = (i[:, None] <= i[None, :])
    c[:, C_ONE:C_ONE + 128] = 1.0
    c[:, C_MUS:C_MUS + 128] = (i[:, None] < i[None, :])
    c[:, C_MLS:C_MLS + 128] = (i[:, None] > i[None, :])
    c[:, C_IST:C_IST + 64] = np.concatenate([np.eye(64), np.eye(64)], axis=0)
    c[:, C_BO:C_BO + 128] = ((i[:, None] // 64) == (i[None, :] // 64))
    return c


def _fm(v):
    v = np.asarray(v, np.float32).reshape(-1, 128)
    return np.ascontiguousarray(v.T)


def kernel(**inp):
    nbl = int(os.environ.get("KNBL", "4"))
    ngrp = int(os.environ.get("KNGRP", "2"))
    x = np.asarray(inp["x"], np.float32)
    W = np.asarray(inp["w_in"][0], np.float32)
    gates = np.ascontiguousarray(W[:, 7432:9480])
    cw = inp["m_conv_w"][0]; cbv = inp["m_conv_b"][0]
    mu = inp["r_mu"][0]
    in_maps = []
    for c in range(4 * ngrp):
        b, r = c // 4, c % 4
        pv = np.zeros((128, NPV), np.float32)

        def put(name, arr):
            a = _fm(arr)
            pv[:, PV[name]:PV[name] + a.shape[1]] = a

        for n in ("ln1_g", "ln1_b", "ln2_g", "ln2_b", "ln3_g", "ln3_b"):
            put(n, inp[n][0])
        qs = slice(r * 256, (r + 1) * 256)
        ks = slice(1024 + r * 256, 1024 + (r + 1) * 256)
        for j in range(4):
            put("cw%d" % j, np.concatenate([cw[j][qs], cw[j][ks]]))
        put("cb", np.concatenate([cbv[qs], cbv[ks]]))
        put("mng", inp["m_norm_g"][0][qs])
        ps_ = slice(r * 256, (r + 1) * 256)
        mul = np.concatenate([mu[0:1024][ps_], mu[1024:2048][ps_], mu[2048:3072][ps_], mu[3072:3328]])
        put("mu", mul)
        put("omu", 1.0 - mul)
        put("w0", inp["r_w0"][0][ps_]); put("a0", inp["r_a0"][0][ps_]); put("kk", inp["r_k_k"][0][ps_])
        put("ka", inp["r_k_a"][0][ps_]); put("rrk", inp["r_r_k"][0].reshape(-1)[ps_])
        put("gng", inp["r_gn_g"][0][ps_]); put("gnb", inp["r_gn_b"][0][ps_])
        wl = np.concatenate([W[:, qs], W[:, ks], W[:, 2048 + r * 256:2048 + (r + 1) * 256],
                             W[:, 3072 + r * 256:3072 + (r + 1) * 256], np.zeros((D, 8), np.float32),
                             W[:, RC0 + r * 256:RC0 + (r + 1) * 256],
                             W[:, RC0 + 1024 + r * 256:RC0 + 1024 + (r + 1) * 256],
                             W[:, RC0 + 2048 + r * 256:RC0 + 2048 + (r + 1) * 256],
                             W[:, RC0 + 3072:RC0 + 3328]], axis=1)
        assert wl.shape[1] == 2056
        wif = np.zeros((D, 8), np.float32)
        wif[:, 0] = W[:, 4096 + r]
        wif[:, 4] = W[:, 4100 + r]
        gbias = np.zeros((128, 8), np.float32)
        gbias[:, 0] = inp["m_i_bias"][0][r]
        gbias[:, 4] = inp["m_f_bias"][0][r]
        gbias[:, 5:8] = 30.0
        m = {"cst": _consts(), "pv": pv, "gbias": gbias,
             "rst": np.ascontiguousarray(np.broadcast_to((np.arange(512)[None, :] % 64 != 0), (128, 512)).astype(np.float32)),
             "w_loc": np.ascontiguousarray(wl), "w_if": wif, "w_gate": gates,
             "r_w2": np.ascontiguousarray(inp["r_w2"][0][:, ps_]), "r_a2": np.ascontiguousarray(inp["r_a2"][0][:, ps_]),
             "r_g2": np.ascontiguousarray(inp["r_g2"][0][:, ps_]),
             "x": np.ascontiguousarray(x[b, r * nbl * TB:(r + 1) * nbl * TB]),
             "oh": np.ascontiguousarray(np.broadcast_to(np.eye(4, dtype=np.float32)[r][None, :], (128, 4)))}
        for n in ("ffn1_w_gate", "ffn1_w_up", "ffn1_w_down", "ffn2_w_gate", "ffn2_w_up", "ffn2_w_down",
                  "w_branch_a", "w_branch_b", "w_out"):
            m[n] = np.ascontiguousarray(inp[n][0], dtype=np.float32)
        in_maps.append(m)
    groups = [list(range(4 * g, 4 * g + 4)) for g in range(ngrp)]
    nc = build(nbl, groups)
    res = run_bass_kernel_spmd(nc, in_maps, core_ids=list(range(4 * ngrp)))
    out = np.zeros((ngrp, 4 * nbl * TB, D), np.float32)
    for c in range(4 * ngrp):
        out[c // 4, (c % 4) * nbl * TB:(c % 4 + 1) * nbl * TB] = np.asarray(res.results[c]["out"])
    return out
```

```python
import contextlib
import os
import numpy as np
import concourse.bass as bass
import concourse.mybir as mybir
from concourse.bass_utils import run_bass_kernel_spmd

F32 = mybir.dt.float32
BF16 = mybir.dt.bfloat16
AF = mybir.ActivationFunctionType
ALU = mybir.AluOpType

D = 1024
DFF = 2816
NJ = DFF // 128
SEQ = 8192
TB = 512
ALPHA = 2.0 ** 0.25
LN_EPS = 1e-5
GN_EPS = 64e-5
NCORES = 8
WCOLS = 9480
RC0 = 4104
SAME_ENGINE_WAITS = os.environ.get("KSEW", "act,dve,pool,sp").split(",")


class Res:
    __slots__ = ("w", "r", "excl")

    def __init__(self, excl=False):
        self.w = None
        self.r = {}
        self.excl = excl


class Sched:
    NDS = 24

    def __init__(self, nc, stack):
        self.nc = nc
        self.E = {"pe": nc.tensor, "act": nc.scalar, "dve": nc.vector, "pool": nc.gpsimd, "sp": nc.sync}
        self.q = {k: [] for k in self.E}
        self.cnt = {k: 0 for k in self.E}
        self.esem = {k: stack.enter_context(nc.semaphore("s_" + k)) for k in self.E}
        self.dsem = [stack.enter_context(nc.semaphore("d%d" % i)) for i in range(self.NDS)]
        self.dval = [0] * self.NDS
        self.dnext = {"sp": 0, "pool": 0}
        self.dpool = {"sp": list(range(0, 8)), "pool": list(range(8, self.NDS))}
        self.waited = {k: {} for k in self.E}
        self.csem = stack.enter_context(nc.semaphore("cc"))
        self.cval = 0

    def _deps(self, eng, reads, writes, extra=()):
        deps = {}

        def add(t):
            if t is None:
                return
            k = (t[0], t[1])
            if deps.get(k, 0) < t[2]:
                deps[k] = t[2]

        for r in reads:
            add(r.w)
        for w in writes:
            add(w.w)
            for k, v in w.r.items():
                add((k[0], k[1], v))
        for t in extra:
            add(t)
        waits = []
        for k, v in deps.items():
            if k[0] == "e" and k[1] == eng and (eng == "pe" or eng not in SAME_ENGINE_WAITS):
                continue
            if self.waited[eng].get(k, 0) >= v:
                continue
            self.waited[eng][k] = v
            waits.append((k, v))
        return waits

    def _mark(self, tok, reads, writes):
        k = (tok[0], tok[1])
        for r in reads:
            if r.r.get(k, 0) < tok[2]:
                r.r[k] = tok[2]
        for w in writes:
            w.w = tok
            w.r = {}

    def op(self, eng, fn, reads=(), writes=()):
        ex = [r for r in reads if r.excl]
        if ex:
            writes = list(writes) + ex
        waits = self._deps(eng, reads, writes)
        self.cnt[eng] += 1
        tok = ("e", eng, self.cnt[eng])
        self.q[eng].append((waits, fn, None))
        self._mark(tok, reads, writes)
        return tok

    def dma(self, qeng, out, in_, reads=(), writes=()):
        pool = self.dpool[qeng]
        j = pool[self.dnext[qeng]]
        self.dnext[qeng] = (self.dnext[qeng] + 1) % len(pool)
        prev = ("d", j, self.dval[j]) if self.dval[j] else None
        extra = [prev] if prev else []
        waits = self._deps(qeng, reads, writes, extra=extra)
        self.dval[j] += 16
        tok = ("d", j, self.dval[j])
        self.q[qeng].append((waits, (out, in_), j))
        self._mark(tok, reads, writes)
        return tok

    def cc(self, kind, ins, outs, groups, reads=(), writes=()):
        waits = self._deps("pool", reads, writes)
        self.cval += 1
        tok = ("c", 0, self.cval)
        self.q["pool"].append((waits, (kind, ins, outs, groups), "cc"))
        self._mark(tok, reads, writes)
        return tok

    def emit(self):
        nc = self.nc
        with nc.Block() as block:
            def run(kind):
                def body(eng):
                    for waits, fn, dj in self.q[kind]:
                        for k, v in waits:
                            sem = self.esem[k[1]] if k[0] == "e" else (self.csem if k[0] == "c" else self.dsem[k[1]])
                            eng.wait_ge(sem, v)
                        if dj is None:
                            fn(eng).then_inc(self.esem[kind], 1)
                        elif dj == "cc":
                            eng.collective_compute(fn[0], ALU.bypass, replica_groups=fn[3], ins=[fn[1]], outs=[fn[2]]).then_inc(self.csem, 1)
                        else:
                            eng.dma_start(out=fn[0], in_=fn[1]).then_inc(self.dsem[dj], 16)
                return body
            block.tensor(run("pe"))
            block.scalar(run("act"))
            block.vector(run("dve"))
            block.gpsimd(run("pool"))
            block.sync(run("sp"))


class Ring:
    def __init__(self, items):
        self.items = items
        self.i = 0

    def next(self):
        it = self.items[self.i]
        self.i = (self.i + 1) % len(self.items)
        return it


PV = {}
_o = 0
for _n, _w in [("ln1_g", 8), ("ln1_b", 8), ("ln2_g", 8), ("ln2_b", 8), ("ln3_g", 8), ("ln3_b", 8),
               ("cw0", 4), ("cw1", 4), ("cw2", 4), ("cw3", 4), ("cb", 4), ("mng", 2),
               ("mu", 8), ("omu", 8), ("w0", 2), ("a0", 2), ("kk", 2), ("ka", 2), ("rrk", 2),
               ("gng", 2), ("gnb", 2)]:
    PV[_n] = _o
    _o += _w
NPV = _o
C_ID, C_OD, C_MUS, C_TRI, C_ONE, C_MLS, C_IST, C_BO, NCST = 0, 128, 256, 384, 512, 640, 768, 832, 960


def build(nbl, groups):
    nc = bass.Bass("TRN2", target_bir_lowering=False)

    def din(name, shape):
        return nc.dram_tensor(name, list(shape), F32, kind="ExternalInput").ap()

    NBT = 4 * nbl
    x_d = din("x", [nbl * TB, D])
    cst_d = din("cst", [128, NCST])
    pv_d = din("pv", [128, NPV])
    gb_d = din("gbias", [128, 8])
    rst_d = din("rst", [128, 512])
    w1g = din("ffn1_w_gate", [D, DFF]); w1u = din("ffn1_w_up", [D, DFF]); w1d = din("ffn1_w_down", [DFF, D])
    w2g = din("ffn2_w_gate", [D, DFF]); w2u = din("ffn2_w_up", [D, DFF]); w2d = din("ffn2_w_down", [DFF, D])
    win = din("w_loc", [D, 2056])
    wif_d = din("w_if", [D, 8])
    wgate = din("w_gate", [D, 2048])
    oh_d = din("oh", [128, 4])
    wa_d = din("w_branch_a", [D, D]); wb_d = din("w_branch_b", [D, D]); wo_d = din("w_out", [D, D])
    rw2_d = din("r_w2", [64, 256]); ra2_d = din("r_a2", [64, 256]); rg2_d = din("r_g2", [128, 256])
    out_d = nc.dram_tensor("out", [nbl * TB, D], F32, kind="ExternalOutput").ap()
    x1f_loc = nc.dram_tensor("x1f_loc", [nbl * 128, 8 * TB], F32).ap()
    x1b_loc = [[nc.dram_tensor("x1bl_%d_%d" % (k, h), [128, 4 * TB], BF16).ap() for h in range(2)] for k in range(nbl)]
    x1b_all = [[nc.dram_tensor("x1ba_%d_%d" % (k, h), [4 * 128, 4 * TB], BF16).ap() for h in range(2)] for k in range(nbl)]
    hy_loc = [nc.dram_tensor("hyl_%d" % k, [128, 4 * TB], BF16).ap() for k in range(NBT)]
    hy_all = [nc.dram_tensor("hya_%d" % k, [4 * 128, 4 * TB], BF16).ap() for k in range(NBT)]

    with contextlib.ExitStack() as st:
        S = Sched(nc, st)
        _n = [0]

        def sb(shape, dt=F32):
            _n[0] += 1
            return st.enter_context(nc.sbuf_tensor("sb%d" % _n[0], list(shape), dt))

        def ring(n, shape, dt=F32):
            return Ring([(sb(shape, dt), Res()) for _ in range(n)])

        banks = [st.enter_context(nc.psum_tensor("ps%d" % i, [128, 512], F32)) for i in range(8)]
        psr = Ring([(banks[i], Res(True)) for i in range(7)])
        pyo_bank = (banks[7], Res(True))

        cst = sb([128, NCST]); cst_r = Res()
        cstb = sb([128, NCST], BF16); cstb_r = Res()
        pv = sb([128, NPV]); pv_r = Res()
        gbias = sb([128, 8]); gb_r = Res()
        S.dma("sp", cst[:, :], cst_d[:, :], writes=[cst_r])
        S.dma("sp", pv[:, :], pv_d[:, :], writes=[pv_r])
        S.dma("sp", gbias[:, :], gb_d[:, :], writes=[gb_r])
        S.op("act", lambda e: e.copy(cstb[:, :], cst[:, :]), reads=[cst_r], writes=[cstb_r])
        ident = cst[:, C_ID:C_ID + 128]
        identb = cstb[:, C_ID:C_ID + 128]
        onesdb = cstb[:, C_OD:C_OD + 128]
        tri = cst[:, C_TRI:C_TRI + 128]
        ones = cst[:, C_ONE:C_ONE + 128]
        istb = cstb[:, C_IST:C_IST + 64]
        bob = cstb[:, C_BO:C_BO + 128]
        rstb = sb([128, 512], BF16)
        S.dma("pool", rstb[:, :], rst_d[:, :], writes=[cstb_r])
        rst = rstb[:, :]
        epsln = sb([128, 1]); eps4 = sb([128, 1]); epsgn = sb([128, 1]); eps_r = Res()
        S.op("dve", lambda e: e.memset(epsln[:, :], LN_EPS), writes=[eps_r])
        S.op("dve", lambda e: e.memset(eps4[:, :], 4 * LN_EPS), writes=[eps_r])
        S.op("dve", lambda e: e.memset(epsgn[:, :], GN_EPS), writes=[eps_r])

        def pvc(name, i=0):
            c = PV[name] + i
            return pv[:, c:c + 1]

        rw2a2 = sb([128, 256], BF16); rw_r = Res()
        rg2 = sb([128, 256], BF16)
        S.dma("pool", rw2a2[0:64, :], rw2_d[:, :], writes=[rw_r])
        S.dma("pool", rw2a2[64:128, :], ra2_d[:, :], writes=[rw_r])
        S.dma("pool", rg2[:, :], rg2_d[:, :], writes=[rw_r])

        xT = sb([128, 8, TB]); xTb = sb([128, 8, TB], BF16); xT_r = [Res() for _ in range(8)]
        x1T = sb([128, 8, TB]); x1Tb = sb([128, 8, TB], BF16); x1_r = [Res() for _ in range(8)]
        x2T = xT; x2Tb = xTb; x2_r = xT_r
        x3T = xT; x3Tb = xTb; x3_r = xT_r
        aT = sb([128, NJ, TB], BF16); aT_r = [Res() for _ in range(NJ)]
        zT = sb([128, 8, TB]); zT_r = [Res() for _ in range(8)]
        xtok = ring(1, [128, D])
        otok = xtok
        wgu = ring(2, [128, 2, 8, 128], BF16)
        wdb = ring(2, [128, 512], BF16)
        wpj = ring(2, [128, 8, 128], BF16)
        tA = ring(3, [128, TB])
        tB = ring(2, [128, TB], BF16)
        mean_r = Res(); rstd_r = Res()
        out_r = Res()

        def load_xT(blk):
            for t in range(TB // 128):
                xt, xr = xtok.next()
                r0 = blk * TB + t * 128
                S.dma("sp", xt[:, :], x_d[r0:r0 + 128, :], writes=[xr])
                for half in range(2):
                    ps, pr = psr.next()
                    for q in range(4):
                        kc = half * 4 + q
                        S.op("pe", lambda e, ps=ps, xt=xt, kc=kc, q=q: e.matmul(
                            ps[:, q * 128:(q + 1) * 128], xt[:, kc * 128:(kc + 1) * 128], ident,
                            start=True, stop=True), reads=[xr, cst_r], writes=[pr])
                    psv = ps[:, :].rearrange("p (q n) -> p q n", q=4)
                    S.op("act", lambda e, psv=psv, half=half, t=t: e.copy(
                        xT[:, half * 4:half * 4 + 4, t * 128:(t + 1) * 128], psv),
                        reads=[pr], writes=xT_r[half * 4:half * 4 + 4])
                    S.op("dve", lambda e, psv=psv, half=half, t=t: e.tensor_copy(
                        xTb[:, half * 4:half * 4 + 4, t * 128:(t + 1) * 128], psv),
                        reads=[pr], writes=xT_r[half * 4:half * 4 + 4])

        def layer_norm(gname, bname, eps_t, outT, outTb, out_rs):
            psm, pmr = psr.next()
            pss, ssr = psr.next()
            for dc in range(8):
                tb, tr = tB.next()
                S.op("act", lambda e, tb=tb, dc=dc: e.copy(tb[:, :], zT[:, dc, :]), reads=[zT_r[dc]], writes=[tr])
                S.op("pe", lambda e, tb=tb, dc=dc: e.matmul(psm[:, :], onesdb, tb[:, :], start=(dc == 0), stop=(dc == 7)),
                     reads=[cstb_r, tr], writes=[pmr])
                tb2, tr2 = tB.next()
                S.op("act", lambda e, tb2=tb2, dc=dc: e.activation(tb2[:, :], zT[:, dc, :], AF.Square),
                     reads=[zT_r[dc]], writes=[tr2])
                S.op("pe", lambda e, tb2=tb2, dc=dc: e.matmul(pss[:, :], onesdb, tb2[:, :], start=(dc == 0), stop=(dc == 7)),
                     reads=[cstb_r, tr2], writes=[ssr])
            S.op("act", lambda e: e.copy(mean_sb[:, :], psm[:, :]), reads=[pmr], writes=[mean_r])
            t1, r1 = tA.next()
            S.op("dve", lambda e, t1=t1: e.tensor_tensor(t1[:, :], mean_sb[:, :], mean_sb[:, :], ALU.mult),
                 reads=[mean_r], writes=[r1])
            t2, r2 = tA.next()
            S.op("dve", lambda e, t1=t1, t2=t2: e.tensor_tensor(t2[:, :], pss[:, :], t1[:, :], ALU.subtract),
                 reads=[ssr, r1], writes=[r2])
            S.op("dve", lambda e, t2=t2: e.tensor_scalar_max(t2[:, :], t2[:, :], 0.0), reads=[r2], writes=[r2])
            t3, r3 = tA.next()
            S.op("act", lambda e, t2=t2, t3=t3: e.activation(t3[:, :], t2[:, :], AF.Sqrt, bias=eps_t[:, 0:1]),
                 reads=[r2, eps_r], writes=[r3])
            S.op("dve", lambda e, t3=t3: e.reciprocal(rstd_sb[:, :], t3[:, :]), reads=[r3], writes=[rstd_r])
            for dc in range(8):
                ta, tar = tA.next()
                S.op("dve", lambda e, ta=ta, dc=dc: e.tensor_tensor(ta[:, :], zT[:, dc, :], mean_sb[:, :], ALU.subtract),
                     reads=[zT_r[dc], mean_r], writes=[tar])
                S.op("dve", lambda e, ta=ta: e.tensor_tensor(ta[:, :], ta[:, :], rstd_sb[:, :], ALU.mult),
                     reads=[tar, rstd_r], writes=[tar])
                S.op("act", lambda e, ta=ta, dc=dc: e.activation(
                    outT[:, dc, :], ta[:, :], AF.Identity, scale=pvc(gname, dc), bias=pvc(bname, dc)),
                    reads=[tar, pv_r], writes=[out_rs[dc]])
                S.op("act", lambda e, ta=ta, dc=dc: e.activation(
                    outTb[:, dc, :], ta[:, :], AF.Identity, scale=pvc(gname, dc), bias=pvc(bname, dc)),
                    reads=[tar, pv_r], writes=[out_rs[dc]])

        def ffn_ln(inT, inTb, in_rs, Wg, Wu, Wd, gname, bname, outT, outTb, out_rs):
            Wg_v = Wg.rearrange("(kc p) f -> p kc f", p=128)
            Wu_v = Wu.rearrange("(kc p) f -> p kc f", p=128)
            Wd_v = Wd.rearrange("(j p) d -> p j d", p=128)
            for j in range(NJ):
                wb, wr = wgu.next()
                S.dma("pool", wb[:, 0], Wg_v[:, :, j * 128:(j + 1) * 128], writes=[wr])
                S.dma("pool", wb[:, 1], Wu_v[:, :, j * 128:(j + 1) * 128], writes=[wr])
                psg, pgr = psr.next()
                psu, pur = psr.next()
                for gu, (ps, prr) in enumerate(((psg, pgr), (psu, pur))):
                    for kc in range(8):
                        S.op("pe", lambda e, ps=ps, wb=wb, kc=kc, gu=gu: e.matmul(
                            ps[:, :], wb[:, gu, kc, :], inTb[:, kc, :], start=(kc == 0), stop=(kc == 7)),
                            reads=[wr, in_rs[kc]], writes=[prr])
                tb, tr = tA.next()
                S.op("act", lambda e, tb=tb, ps=psg: e.activation(tb[:, :], ps[:, :], AF.Silu), reads=[pgr], writes=[tr])
                S.op("dve", lambda e, tb=tb, ps=psu, j=j: e.tensor_tensor(aT[:, j, :], tb[:, :], ps[:, :], ALU.mult),
                     reads=[tr, pur], writes=[aT_r[j]])
            for half in range(2):
                pss_ = [psr.next() for _ in range(4)]
                for j in range(NJ):
                    wb, wr = wdb.next()
                    S.dma("pool", wb[:, :], Wd_v[:, j, half * 512:(half + 1) * 512], writes=[wr])
                    for q in range(4):
                        ps, pr = pss_[q]
                        S.op("pe", lambda e, ps=ps, wb=wb, j=j, q=q: e.matmul(
                            ps[:, :], wb[:, q * 128:(q + 1) * 128], aT[:, j, :], start=(j == 0), stop=(j == NJ - 1)),
                            reads=[wr, aT_r[j]], writes=[pr])
                for q in range(4):
                    dc = half * 4 + q
                    ps, pr = pss_[q]
                    S.op("dve", lambda e, ps=ps, dc=dc: e.scalar_tensor_tensor(
                        zT[:, dc, :], inT[:, dc, :], 2.0 * ALPHA, ps[:, :], ALU.mult, ALU.add),
                        reads=[pr, in_rs[dc]], writes=[zT_r[dc]])
            layer_norm(gname, bname, eps4, outT, outTb, out_rs)

        def store_T(blk, srcT, src_rs):
            for t in range(TB // 128):
                ot, orr = otok.next()
                for half in range(2):
                    ps, pr = psr.next()
                    for q in range(4):
                        kc = half * 4 + q
                        S.op("pe", lambda e, ps=ps, kc=kc, q=q, t=t: e.matmul(
                            ps[:, q * 128:(q + 1) * 128], srcT[:, kc, t * 128:(t + 1) * 128], ident,
                            start=True, stop=True), reads=[src_rs[kc], cst_r], writes=[pr])
                    S.op("act", lambda e, ps=ps, ot=ot, half=half: e.copy(ot[:, half * 512:(half + 1) * 512], ps[:, :]),
                         reads=[pr], writes=[orr])
                r0 = blk * TB + t * 128
                S.dma("sp", out_d[r0:r0 + 128, :], ot[:, :], reads=[orr], writes=[out_r])

        def projT(Wd_ap, col0, inTb, in_rs, ncols=128, wring=None, pring=None):
            wb, wr = (wring or wpj).next()
            Wv = Wd_ap.rearrange("(kc p) f -> p kc f", p=128)
            S.dma("pool", wb[:, :, 0:ncols], Wv[:, :, col0:col0 + ncols], writes=[wr])
            ps, pr = (pring or psr).next()
            for kc in range(8):
                src = inTb(kc) if callable(inTb) else inTb[:, kc, :]
                S.op("pe", lambda e, ps=ps, wb=wb, kc=kc, src=src: e.matmul(
                    ps[0:ncols, :], wb[:, kc, 0:ncols], src, start=(kc == 0), stop=(kc == 7)),
                    reads=[wr, in_rs[kc]], writes=[pr])
            return ps, pr

        NT = TB // 128
        carry = sb([128, 4, 3]); carry_r = [Res() for _ in range(4)]
        S.op("dve", lambda e: e.memset(carry[:, :, :], 0.0), writes=carry_r)
        cwork = ring(2, [128, 3 + TB])
        mean_sb = cwork.items[0][0][:, 0:TB]; rstd_sb = cwork.items[1][0][:, 0:TB]
        qkT = zT[:, :, :].rearrange("p a b -> p (a b)").bitcast(BF16).rearrange("p (c n) -> p c n", n=TB)
        qk_r = [zT_r[c // 2] for c in range(16)]
        sigmo = sb([128, 2, TB], BF16); sigmo_r = [Res() for _ in range(2)]
        sga = sb([128, 8, TB], BF16); sga_r = [Res() for _ in range(8)]
        sgb = sb([128, 8, TB], BF16); sgb_r = [Res() for _ in range(8)]
        vt = sb([128, NT, 1, 258], BF16); vt_r = [[Res() for _ in range(1)] for _ in range(NT)]
        S.op("dve", lambda e: e.memset(vt[:, :, :, :], 1.0), writes=[r for rr in vt_r for r in rr])
        wv = ring(1, [128, 8, 256], BF16)
        wif = sb([128, 8, 8], BF16); wif_r = Res()
        S.dma("pool", wif[:, :, :], wif_d.rearrange("(kc p) f -> p kc f", p=128), writes=[wif_r])
        gts = sb([128, NT, 24]); gts_r = [Res() for _ in range(NT)]
        Cst = sb([128, 1, 2, 258]); C_r = [Res() for _ in range(1)]
        Cb = sb([128, 1, 2, 258], BF16); Cb_r = [Res() for _ in range(1)]
        S.op("dve", lambda e: e.memset(Cst[:, :, :, :], 0.0), writes=C_r)
        S.op("dve", lambda e: e.memset(Cb[:, :, :, :], 0.0), writes=Cb_r)
        hyT = sb([128, 4, TB], BF16); hm_r = [Res() for _ in range(2)]
        hmTb = hyT[:, 0:2, :]
        HYf = sb([128, 4, 4, TB], BF16); hyf_r = [Res() for _ in range(4)]
        yrTb = hyT[:, 2:4, :]; yr_r = [Res() for _ in range(2)]
        mgTb = sga; mg_r = sga_r
        smr = ring(4, [128, 128], BF16)
        ktk = ring(4, [128, 256], BF16)
        sm6 = ring(4, [128, 8])

        def mlstm(blk):
            Wv = win.rearrange("(kc p) f -> p kc f", p=128)
            for c in range(4):
                ps, pr = projT(win, c * 128, x1Tb, x1_r, wring=wpjM, pring=psrM)
                wk, wkr = cwork.next()
                S.op("act", lambda e, wk=wk, c=c: e.copy(wk[:, 0:3], carry[:, c, :]), reads=[carry_r[c]], writes=[wkr])
                yield
                S.op("act", lambda e, wk=wk, ps=ps: e.copy(wk[:, 3:3 + TB], ps[:, :]), reads=[pr], writes=[wkr])
                yield
                S.op("act", lambda e, wk=wk, c=c: e.copy(carry[:, c, :], wk[:, TB:TB + 3]), reads=[wkr], writes=[carry_r[c]])
                yield
                ta, tar = tAM.next()
                S.op("dve", lambda e, wk=wk, ta=ta, c=c: e.tensor_scalar(
                    ta[:, :], wk[:, 0:TB], pvc("cw0", c), pvc("cb", c), ALU.mult, ALU.add),
                    reads=[wkr, pv_r], writes=[tar])
                yield
                for j in (1, 2, 3):
                    S.op("dve", lambda e, wk=wk, ta=ta, c=c, j=j: e.scalar_tensor_tensor(
                        ta[:, :], wk[:, j:j + TB], pvc("cw%d" % j, c), ta[:, :], ALU.mult, ALU.add),
                        reads=[wkr, pv_r, tar], writes=[tar])
                    yield
                S.op("act", lambda e, ta=ta, c=c: e.activation(qkT[:, c, :], ta[:, :], AF.Silu),
                     reads=[tar], writes=[qk_r[c]])
                yield
            for c in range(2):
                ps, pr = projT(win, 768 + c * 128, x1Tb, x1_r, wring=wpjM, pring=psrM)
                S.op("act", lambda e, ps=ps, c=c: e.activation(sigmo[:, c, :], ps[:, :], AF.Sigmoid),
                     reads=[pr], writes=[sigmo_r[c]])
                yield
            for t in range(NT):
                ps, pr = psrM.next()
                for kc in range(8):
                    S.op("pe", lambda e, ps=ps, kc=kc, t=t: e.matmul(
                        ps[:, 0:8], x1Tb[:, kc, t * 128:(t + 1) * 128], wif[:, kc, :], start=(kc == 0), stop=(kc == 7)),
                        reads=[wif_r, x1_r[kc]], writes=[pr])
                    yield
                g = gts[:, t, :]
                gr = gts_r[t]
                S.op("dve", lambda e, g=g, ps=ps: e.tensor_tensor(g[:, 12:20], ps[:, 0:8], gbias[:, :], ALU.add),
                     reads=[pr, gb_r], writes=[gr])
                yield
                S.op("act", lambda e, g=g: e.activation(g[:, 20:24], g[:, 16:20], AF.Exp, scale=-1.0), reads=[gr], writes=[gr])
                yield
                S.op("act", lambda e, g=g: e.activation(g[:, 16:20], g[:, 20:24], AF.Ln, bias=1.0), reads=[gr], writes=[gr])
                yield
                S.op("dve", lambda e, g=g: e.tensor_scalar_mul(g[:, 16:20], g[:, 16:20], -1.0), reads=[gr], writes=[gr])
                yield
                ps2, pr2 = psrM.next()
                S.op("pe", lambda e, ps2=ps2, g=g: e.matmul(ps2[:, 0:4], tri, g[:, 16:20], start=True, stop=True),
                     reads=[gr, cst_r], writes=[pr2])
                yield
                S.op("pe", lambda e, ps2=ps2, g=g: e.matmul(ps2[:, 4:8], ones, g[:, 16:20], start=True, stop=True),
                     reads=[gr, cst_r], writes=[pr2])
                yield
                S.op("dve", lambda e, g=g, ps2=ps2: e.tensor_tensor(g[:, 20:24], g[:, 12:16], ps2[:, 0:4], ALU.subtract),
                     reads=[gr, pr2], writes=[gr])
                yield
                S.op("act", lambda e, g=g: e.activation(g[:, 0:4], g[:, 20:24], AF.Exp), reads=[gr], writes=[gr])
                yield
                S.op("act", lambda e, g=g, ps2=ps2: e.activation(g[:, 4:8], ps2[:, 0:4], AF.Exp, scale=-1.0),
                     reads=[gr, pr2], writes=[gr])
                yield
                S.op("act", lambda e, g=g, ps2=ps2: e.activation(g[:, 8:12], ps2[:, 4:8], AF.Exp), reads=[gr, pr2], writes=[gr])
                yield
            for h in range(1):
                wb, wr = wv.next()
                S.dma("pool", wb[:, :, :], Wv[:, :, 512:768], writes=[wr])
                yield
                for t in range(NT):
                    ps, pr = psrM.next()
                    for kc in range(8):
                        S.op("pe", lambda e, ps=ps, kc=kc, t=t, wb=wb: e.matmul(
                            ps[:, 0:256], x1Tb[:, kc, t * 128:(t + 1) * 128], wb[:, kc, :], start=(kc == 0), stop=(kc == 7)),
                            reads=[wr, x1_r[kc]], writes=[pr])
                        yield
                    S.op("act", lambda e, ps=ps, t=t, h=h: e.copy(vt[:, t, h, 0:256], ps[:, 0:256]),
                         reads=[pr], writes=[vt_r[t][h]])
                    yield
            for t in range(NT):
                tc_ = slice(t * 128, (t + 1) * 128)
                g = gts[:, t, :]
                gr = gts_r[t]
                def head_chain(t, h, tc_, g, gr):
                    qc = [h * 2, h * 2 + 1]
                    kc_ = [2 + h * 2, 2 + h * 2 + 1]
                    ps, pr = psrM.next()
                    for i in range(2):
                        S.op("pe", lambda e, ps=ps, i=i, kc_=kc_, qc=qc, tc_=tc_: e.matmul(
                            ps[:, 0:128], qkT[:, kc_[i], tc_], qkT[:, qc[i], tc_], start=(i == 0), stop=(i == 1)),
                            reads=[qk_r[kc_[i]], qk_r[qc[i]]], writes=[pr])
                        yield
                    sm, smrr = smr.next()
                    S.op("dve", lambda e, sm=sm, ps=ps, h=h, g=g: e.scalar_tensor_tensor(
                        sm[:, :], ps[:, 0:128], g[:, h:h + 1], tri, ALU.mult, ALU.mult),
                        reads=[pr, gr, cst_r], writes=[smrr])
                    yield
                    po, por = psrM.next()
                    S.op("pe", lambda e, po=po, sm=sm, t=t, h=h: e.matmul(
                        po[:, 0:258], sm[:, :], vt[:, t, h, :], start=True, stop=False),
                        reads=[smrr, vt_r[t][h]], writes=[por])
                    yield
                    for i in range(2):
                        S.op("pe", lambda e, po=po, i=i, h=h, qc=qc, tc_=tc_: e.matmul(
                            po[:, 0:258], qkT[:, qc[i], tc_], Cb[:, h, i, :], start=False, stop=(i == 1)),
                            reads=[qk_r[qc[i]], Cb_r[h]], writes=[por])
                        yield
                    s6, s6r = sm6.next()
                    S.op("act", lambda e, s6=s6, po=po: e.activation(
                        s6[:, 0:1], po[:, 256:257], AF.Abs, scale=1.0 / 16.0), reads=[por], writes=[s6r])
                    yield
                    S.op("dve", lambda e, s6=s6, g=g, h=h: e.tensor_tensor(s6[:, 0:1], s6[:, 0:1], g[:, 4 + h:5 + h], ALU.max),
                         reads=[s6r, gr], writes=[s6r])
                    yield
                    S.op("dve", lambda e, s6=s6: e.reciprocal(s6[:, 1:2], s6[:, 0:1]), reads=[s6r], writes=[s6r])
                    yield
                    hb, hbr = hh.next()
                    S.op("dve", lambda e, hb=hb, po=po, s6=s6: e.tensor_scalar(
                        hb[:, :], po[:, 0:256], s6[:, 1:2], 1.0 / 16.0, ALU.mult, ALU.mult), reads=[por, s6r], writes=[hbr])
                    yield
                    S.op("dve", lambda e, hb=hb, s6=s6: e.bn_stats(s6[:, 2:8], hb[:, :]), reads=[hbr], writes=[s6r])
                    yield
                    S.op("dve", lambda e, s6=s6: e.bn_aggr(s6[:, 0:2], s6[:, 2:8]), reads=[s6r], writes=[s6r])
                    yield
                    S.op("act", lambda e, s6=s6: e.activation(s6[:, 2:3], s6[:, 1:2], AF.Sqrt, bias=epsln[:, 0:1]),
                         reads=[s6r, eps_r], writes=[s6r])
                    yield
                    S.op("dve", lambda e, s6=s6: e.reciprocal(s6[:, 3:4], s6[:, 2:3]), reads=[s6r], writes=[s6r])
                    yield
                    hn, hnr = hnb.next()
                    S.op("dve", lambda e, hn=hn, hb=hb, s6=s6: e.tensor_scalar(
                        hn[:, :], hb[:, :], s6[:, 0:1], s6[:, 3:4], ALU.subtract, ALU.mult), reads=[hbr, s6r], writes=[hnr])
                    yield
                    for i in range(2):
                        pt, ptr = psrM.next()
                        S.op("pe", lambda e, pt=pt, hn=hn, i=i: e.matmul(
                            pt[:, 0:128], hn[:, i * 128:(i + 1) * 128], identb, start=True, stop=True),
                            reads=[hnr, cstb_r], writes=[ptr])
                        yield
                        S.op("dve", lambda e, pt=pt, h=h, i=i, tc_=tc_: e.scalar_tensor_tensor(
                            hmTb[:, h * 2 + i, tc_], pt[:, 0:128], pvc("mng", h * 2 + i), sigmo[:, h * 2 + i, tc_],
                            ALU.mult, ALU.mult), reads=[ptr, pv_r, sigmo_r[h * 2 + i]], writes=[hm_r[h * 2 + i]])
                        yield
                    kk_, kkr = ktk.next()
                    for i in range(2):
                        pt, ptr = psrM.next()
                        S.op("pe", lambda e, pt=pt, i=i, kc_=kc_, tc_=tc_: e.matmul(
                            pt[:, 0:128], qkT[:, kc_[i], tc_], identb, start=True, stop=True),
                            reads=[qk_r[kc_[i]], cstb_r], writes=[ptr])
                        yield
                        S.op("act", lambda e, pt=pt, kk_=kk_, i=i, g=g, h=h: e.activation(
                            kk_[:, i * 128:(i + 1) * 128], pt[:, 0:128], AF.Identity, scale=g[:, h:h + 1]),
                            reads=[ptr, gr], writes=[kkr])
                        yield
                    for i in range(2):
                        pc, pcr = psrM.next()
                        S.op("pe", lambda e, pc=pc, kk_=kk_, i=i, t=t, h=h: e.matmul(
                            pc[:, 0:258], kk_[:, i * 128:(i + 1) * 128], vt[:, t, h, :], start=True, stop=True),
                            reads=[kkr, vt_r[t][h]], writes=[pcr])
                        yield
                        S.op("dve", lambda e, h=h, i=i, g=g: e.tensor_scalar_mul(Cst[:, h, i, :], Cst[:, h, i, :], g[:, 8 + h:9 + h]),
                             reads=[C_r[h], gr], writes=[C_r[h]])
                        yield
                        S.op("dve", lambda e, pc=pc, h=h, i=i, g=g: e.scalar_tensor_tensor(
                            Cst[:, h, i, :], pc[:, 0:258], g[:, 8 + h:9 + h], Cst[:, h, i, :], ALU.mult, ALU.add),
                            reads=[pcr, gr, C_r[h]], writes=[C_r[h]])
                        yield
                        S.op("act", lambda e, h=h, i=i: e.copy(Cb[:, h, i, :], Cst[:, h, i, :]), reads=[C_r[h]], writes=[Cb_r[h]])
                        yield

                gens = [head_chain(t, h, tc_, g, gr) for h in range(1)]
                while gens:
                    for gg in list(gens):
                        try:
                            next(gg)
                            yield
                        except StopIteration:
                            gens.remove(gg)

        NCH = TB // 64
        rcar = sb([128, 8, 1]); rcar_r = [Res() for _ in range(8)]
        S.op("dve", lambda e: e.memset(rcar[:, :, :], 0.0), writes=rcar_r)
        psrM = Ring(psr.items[0:3]); psrR = Ring(psr.items[3:7])
        tAM = Ring([(x1T[:, i, :], Res()) for i in range(2)])
        rwork = Ring([(x1T[:, 2 + 2 * i:4 + 2 * i, :].rearrange("p a b -> p (a b)")[:, 0:1 + TB], Res()) for i in range(2)])
        wpjM = Ring([(x1T[:, 6 + i, :].bitcast(BF16).rearrange("p (k c) -> p k c", c=128), Res()) for i in range(2)])
        scr_r = [r for _, r in tAM.items + rwork.items + wpjM.items]
        lowT = sb([128, 2, TB], BF16); low_r = [Res(), Res()]
        rtmp = Ring([(aT[:, 2 * i:2 * i + 2, :].rearrange("p a b -> p (a b)").bitcast(F32), Res()) for i in range(10)])
        ARbd = sb([128, NCH, 256], BF16); Bbd = sb([128, NCH, 128], BF16); Kbd = sb([128, NCH, 128], BF16)
        Vbd = sb([128, NCH, 128], BF16); Ynbd = sb([128, 128], BF16)
        bd_r = Res(); ynbd_r = Res()
        for tns in (ARbd, Bbd, Kbd, Vbd):
            S.op("dve", lambda e, tns=tns: e.memset(tns[:, :, :], 0.0), writes=[bd_r])
        S.op("dve", lambda e: e.memset(Ynbd[:, :], 0.0), writes=[ynbd_r])
        gam = sb([128, NCH]); gam_r = Res()
        Hst = sb([128, 2, 64]); H_r = [Res() for _ in range(2)]
        Hb = sb([128, 2, 64], BF16); Hb_r = [Res() for _ in range(2)]
        S.op("dve", lambda e: e.memset(Hst[:, :, :], 0.0), writes=H_r)
        S.op("dve", lambda e: e.memset(Hb[:, :, :], 0.0), writes=Hb_r)
        vst = sb([128, NCH, 64], BF16); vst_r = Res()
        btk = sb([128, NCH, 256], BF16); btk_r = Res()
        nm = ring(4, [128, 256], BF16)
        nsq = ring(11, [128, 128], BF16)
        ub = ring(4, [128, 64], BF16)
        uf = ring(4, [128, 64])
        gn6 = ring(4, [128, 8])
        htmp = ring(2, [128, 64])
        maskAR = cst[:, C_MUS:C_MUS + 256]; mask_r = cst_r
        mls = cst[:, C_MLS:C_MLS + 128]

        xTv = xT[:, :, :].rearrange("p a b -> p (a b)").bitcast(BF16)
        mAK = xTv[:, 0:NCH * 512].rearrange("p (n c) -> p n c", c=512)
        Xs = xTv[:, 4096:4096 + NCH * 128].rearrange("p (n c) -> p n c", c=128)
        xTbv = xTb[:, :, :].rearrange("p a b -> p (a b)")
        PW = [[xTbv[:, (b * 8 + n) * 128:(b * 8 + n + 1) * 128] for n in range(NCH)] for b in range(2)]
        PWT = [[xTbv[:, 2048 + (b * 8 + n) * 128:2048 + (b * 8 + n + 1) * 128] for n in range(NCH)] for b in range(2)]
        hh = Ring([(xTv[:, 5120 + i * 512:5120 + (i + 1) * 512].bitcast(F32), Res()) for i in range(4)])
        hnb = Ring([(xTv[:, 7168 + i * 256:7168 + (i + 1) * 256], Res()) for i in range(4)])
        mA_r = [Res() for _ in range(NCH)]; mK_r = [Res() for _ in range(NCH)]; X_r = [Res() for _ in range(NCH)]
        PW_r = [[Res() for _ in range(NCH)] for _ in range(2)]; PWT_r = [[Res() for _ in range(NCH)] for _ in range(2)]

        def v3(ap):
            return ap.rearrange("p (n l) -> p n l", l=64)

        def shifted(ci, ps, pr):
            wk, wkr = rwork.next()
            S.op("act", lambda e, wk=wk: e.copy(wk[:, 0:1], rcar[:, ci, :]), reads=[rcar_r[ci]], writes=[wkr])
            S.op("act", lambda e, wk=wk, ps=ps: e.copy(wk[:, 1:1 + TB], ps[:, :]), reads=[pr], writes=[wkr])
            S.op("act", lambda e, wk=wk: e.copy(rcar[:, ci, :], wk[:, TB:TB + 1]), reads=[wkr], writes=[rcar_r[ci]])
            ta, tar = rtmp.next()
            S.op("dve", lambda e, wk=wk, ta=ta: e.tensor_scalar_mul(ta[:, :], wk[:, 1:1 + TB], pvc("omu", ci)),
                 reads=[wkr, pv_r], writes=[tar])
            S.op("dve", lambda e, wk=wk, ta=ta: e.scalar_tensor_tensor(
                ta[:, :], wk[:, 0:TB], pvc("mu", ci), ta[:, :], ALU.mult, ALU.add), reads=[wkr, pv_r, tar], writes=[tar])
            return ta, tar

        krw = int(os.environ.get("KRW", "9"))

        def rwkv(blk):
            ps, pr = projT(win, 1800, x1Tb, x1_r, pring=psrR)
            ta, tar = shifted(6, ps, pr)
            S.op("act", lambda e, ta=ta: e.activation(lowT[0:64, 0, :], ta[0:64, :], AF.Tanh), reads=[tar], writes=[low_r[0]])
            yield
            S.op("act", lambda e, ta=ta: e.copy(lowT[64:128, 0, :], ta[64:128, :]), reads=[tar], writes=[low_r[0]])
            yield
            ps, pr = projT(win, 1928, x1Tb, x1_r, pring=psrR)
            ta, tar = shifted(7, ps, pr)
            S.op("act", lambda e, ta=ta: e.activation(lowT[:, 1, :], ta[:, :], AF.Sigmoid), reads=[tar], writes=[low_r[1]])
            yield
            for p in range(2):
                cs = slice(p * 128, (p + 1) * 128)
                ps, pr = projT(win, 1032 + p * 128, x1Tb, x1_r, pring=psrR)
                r_, r_r = shifted(p, ps, pr)
                ps, pr = projT(win, 1288 + p * 128, x1Tb, x1_r, pring=psrR)
                k_, k_r = shifted(2 + p, ps, pr)
                ps, pr = projT(win, 1544 + p * 128, x1Tb, x1_r, pring=psrR)
                v_, v_r = shifted(4 + p, ps, pr)
                pw, pwr = psrR.next()
                S.op("pe", lambda e, pw=pw, cs=cs: e.matmul(pw[:, :], rw2a2[0:64, cs], lowT[0:64, 0, :], start=True, stop=True),
                     reads=[rw_r, low_r[0]], writes=[pwr])
                yield
                lw, lwr = rtmp.next()
                S.op("act", lambda e, lw=lw, pw=pw, p=p: e.activation(lw[:, :], pw[:, :], AF.Sigmoid, bias=pvc("w0", p)),
                     reads=[pwr, pv_r], writes=[lwr])
                yield
                S.op("dve", lambda e, lw=lw: e.tensor_scalar_mul(lw[:, :], lw[:, :], -float(np.exp(-0.5))), reads=[lwr], writes=[lwr])
                yield
                pa, par = psrR.next()
                S.op("pe", lambda e, pa=pa, cs=cs: e.matmul(pa[:, :], rw2a2[64:128, cs], lowT[64:128, 0, :], start=True, stop=True),
                     reads=[rw_r, low_r[0]], writes=[par])
                yield
                a_, a_r = rtmp.next()
                S.op("act", lambda e, a_=a_, pa=pa, p=p: e.activation(a_[:, :], pa[:, :], AF.Sigmoid, bias=pvc("a0", p)),
                     reads=[par, pv_r], writes=[a_r])
                yield
                pg, pgr = psrR.next()
                S.op("pe", lambda e, pg=pg, cs=cs: e.matmul(pg[:, :], rg2[:, cs], lowT[:, 1, :], start=True, stop=True),
                     reads=[rw_r, low_r[1]], writes=[pgr])
                yield
                g_, g_r = rtmp.next()
                S.op("act", lambda e, g_=g_, pg=pg: e.copy(g_[:, :], pg[:, :]), reads=[pgr], writes=[g_r])
                yield
                kk, kkr = rtmp.next()
                S.op("dve", lambda e, kk=kk, k_=k_, p=p: e.tensor_scalar_mul(kk[:, :], k_[:, :], pvc("kk", p)),
                     reads=[k_r, pv_r], writes=[kkr])
                yield
                sq, sqr = tB.next()
                S.op("act", lambda e, sq=sq, kk=kk: e.activation(sq[:, :], kk[:, :], AF.Square), reads=[kkr], writes=[sqr])
                yield
                pq, pqr = psrR.next()
                S.op("pe", lambda e, pq=pq, sq=sq: e.matmul(pq[:, :], bob, sq[:, :], start=True, stop=True),
                     reads=[cstb_r, sqr], writes=[pqr])
                yield
                t1, t1r = rtmp.next()
                S.op("act", lambda e, t1=t1, pq=pq: e.activation(t1[:, :], pq[:, :], AF.Sqrt), reads=[pqr], writes=[t1r])
                yield
                S.op("dve", lambda e, t1=t1: e.tensor_scalar_max(t1[:, :], t1[:, :], 1e-12), reads=[t1r], writes=[t1r])
                yield
                S.op("dve", lambda e, t1=t1: e.reciprocal(t1[:, :], t1[:, :]), reads=[t1r], writes=[t1r])
                yield
                S.op("dve", lambda e, t1=t1, kk=kk: e.tensor_tensor(kk[:, :], kk[:, :], t1[:, :], ALU.mult),
                     reads=[t1r, kkr], writes=[kkr])
                yield
                S.op("dve", lambda e, t1=t1, a_=a_, p=p: e.tensor_scalar(t1[:, :], a_[:, :], 1.0, pvc("ka", p), ALU.subtract, ALU.mult),
                     reads=[a_r, pv_r], writes=[t1r])
                yield
                S.op("dve", lambda e, t1=t1, k_=k_: e.scalar_tensor_tensor(k_[:, :], t1[:, :], 1.0, k_[:, :], ALU.add, ALU.mult),
                     reads=[t1r, k_r], writes=[k_r])
                yield
                t2, t2r = tB.next()
                S.op("dve", lambda e, t2=t2, r_=r_, k_=k_, p=p: e.scalar_tensor_tensor(
                    t2[:, :], r_[:, :], pvc("rrk", p), k_[:, :], ALU.mult, ALU.mult), reads=[r_r, k_r, pv_r], writes=[t2r])
                yield
                pb, pbr = psrR.next()
                S.op("pe", lambda e, pb=pb, t2=t2: e.matmul(pb[:, :], bob, t2[:, :], start=True, stop=True),
                     reads=[cstb_r, t2r], writes=[pbr])
                yield
                bon, bonr = rtmp.next()
                S.op("dve", lambda e, bon=bon, pb=pb, v_=v_: e.tensor_tensor(bon[:, :], pb[:, :], v_[:, :], ALU.mult),
                     reads=[pbr, v_r], writes=[bonr])
                yield
                cl, clr = rtmp.next()
                S.op("dve", lambda e, cl=cl, lw=lw: e.tensor_tensor_scan(cl[:, :], rst, lw[:, :], 0.0, ALU.mult, ALU.add),
                     reads=[lwr, cstb_r], writes=[clr])
                yield
                e1, e1r = tA.next()
                S.op("act", lambda e, e1=e1, cl=cl: e.activation(e1[:, :], cl[:, :], AF.Exp), reads=[clr], writes=[e1r])
                yield
                S.op("act", lambda e, e1=e1: e.copy(gam[:, :], v3(e1[:, :])[:, :, 63]), reads=[e1r], writes=[gam_r])
                yield
                for hf in range(2):
                    hs = slice(hf * 64, hf * 64 + 64)
                    S.op("dve", lambda e, hs=hs, hf=hf, r_=r_, e1=e1: e.tensor_tensor(
                        ARbd[hs, :, 128 + hf * 64:128 + hf * 64 + 64], v3(r_[hs, :]), v3(e1[hs, :]), ALU.mult),
                        reads=[r_r, e1r], writes=[bd_r])
                    yield
                e2, e2r = tA.next()
                S.op("act", lambda e, e2=e2, cl=cl: e.activation(e2[:, :], cl[:, :], AF.Exp, scale=-1.0), reads=[clr], writes=[e2r])
                yield
                S.op("dve", lambda e, t1=t1, kk=kk, a_=a_: e.tensor_tensor(t1[:, :], kk[:, :], a_[:, :], ALU.mult),
                     reads=[kkr, a_r], writes=[t1r])
                yield
                for hf in range(2):
                    hs = slice(hf * 64, hf * 64 + 64)
                    S.op("dve", lambda e, hs=hs, hf=hf, t1=t1, e2=e2: e.tensor_tensor(
                        Bbd[hs, :, hf * 64:hf * 64 + 64], v3(t1[hs, :]), v3(e2[hs, :]), ALU.mult), reads=[t1r, e2r], writes=[bd_r])
                    yield
                    S.op("dve", lambda e, hs=hs, hf=hf, k_=k_, e2=e2: e.tensor_tensor(
                        Kbd[hs, :, hf * 64:hf * 64 + 64], v3(k_[hs, :]), v3(e2[hs, :]), ALU.mult), reads=[k_r, e2r], writes=[bd_r])
                    yield
                    S.op("act", lambda e, hs=hs, hf=hf, v_=v_: e.copy(Vbd[hs, :, hf * 64:hf * 64 + 64], v3(v_[hs, :])),
                         reads=[v_r], writes=[bd_r])
                    yield
                S.op("dve", lambda e, cl=cl, lw=lw: e.tensor_tensor(cl[:, :], cl[:, :], lw[:, :], ALU.subtract),
                     reads=[clr, lwr], writes=[clr])
                yield
                e3, e3r = tA.next()
                S.op("act", lambda e, e3=e3, cl=cl: e.activation(e3[:, :], cl[:, :], AF.Exp), reads=[clr], writes=[e3r])
                yield
                for hf in range(2):
                    hs = slice(hf * 64, hf * 64 + 64)
                    S.op("dve", lambda e, hs=hs, hf=hf, kk=kk, e3=e3: e.scalar_tensor_tensor(
                        ARbd[hs, :, hf * 64:hf * 64 + 64], v3(kk[hs, :]), -1.0, v3(e3[hs, :]), ALU.mult, ALU.mult),
                        reads=[kkr, e3r], writes=[bd_r])
                    yield
                if krw <= 1:
                    S.op("dve", lambda e, p=p: e.memset(yrTb[:, p, :], 0.0), writes=[yr_r[p]])
                    yield
                    continue
                pv_, pvr_ = psrR.next()
                for n in range(NCH):
                    S.op("pe", lambda e, n=n, pv_=pv_: e.matmul(pv_[:, n * 64:(n + 1) * 64], Vbd[:, n, :], istb, start=True, stop=True),
                         reads=[bd_r, cstb_r], writes=[pvr_])
                    yield
                S.op("act", lambda e, pv_=pv_: e.copy(vst[:, :, :], v3(pv_[:, :])), reads=[pvr_], writes=[vst_r])
                yield
                for n0 in range(0, NCH, 2):
                    pt, ptr = psrR.next()
                    for n in (n0, n0 + 1):
                        o = (n - n0) * 256
                        S.op("pe", lambda e, n=n, pt=pt, o=o: e.matmul(pt[:, o:o + 128], Bbd[:, n, :], identb, start=True, stop=True),
                             reads=[bd_r, cstb_r], writes=[ptr])
                        yield
                        S.op("pe", lambda e, n=n, pt=pt, o=o: e.matmul(pt[:, o + 128:o + 256], Kbd[:, n, :], identb, start=True, stop=True),
                             reads=[bd_r, cstb_r], writes=[ptr])
                        yield
                    S.op("act", lambda e, n0=n0, pt=pt: e.copy(btk[:, n0:n0 + 2, :], pt[:, :].rearrange("p (n l) -> p n l", l=256)),
                         reads=[ptr], writes=[btk_r])
                    yield
                if krw <= 2:
                    S.op("dve", lambda e, p=p: e.memset(yrTb[:, p, :], 0.0), writes=[yr_r[p]])
                    yield
                    continue
                pyo, pyor = pyo_bank
                for g0 in range(0, NCH, 4):
                    G = list(range(g0, min(g0 + 4, NCH)))
                    pAs = {}
                    for n in G:
                        pA, pAr = psrR.next()
                        pAs[n] = (pA, pAr)
                        S.op("pe", lambda e, pA=pA, n=n: e.matmul(pA[:, 0:256], Bbd[:, n, :], ARbd[:, n, :], start=True, stop=True),
                             reads=[bd_r], writes=[pAr])
                        yield
                        S.op("pe", lambda e, pA=pA, n=n: e.matmul(pA[:, 256:512], Kbd[:, n, :], ARbd[:, n, :], start=True, stop=True),
                             reads=[bd_r], writes=[pAr])
                        yield
                    for n in G:
                        pA, pAr = pAs[n]
                        S.op("dve", lambda e, pA=pA, n=n: e.tensor_tensor(mAK[:, n, 0:256], pA[:, 0:256], maskAR[:, :], ALU.mult),
                             reads=[pAr, mask_r], writes=[mA_r[n]])
                        yield
                        S.op("dve", lambda e, pA=pA, n=n: e.tensor_tensor(mAK[:, n, 256:512], pA[:, 256:512], maskAR[:, :], ALU.mult),
                             reads=[pAr, mask_r], writes=[mK_r[n]])
                        yield
                    pTs = {}
                    for n in G:
                        pT, pTr = psrR.next()
                        pTs[n] = (pT, pTr)
                        S.op("pe", lambda e, pT=pT, n=n: e.matmul(pT[:, 0:128], ARbd[:, n, 0:128], Bbd[:, n, :], start=True, stop=True),
                             reads=[bd_r], writes=[pTr])
                        yield
                    for n in G:
                        pT, pTr = pTs[n]
                        S.op("dve", lambda e, pT=pT, n=n: e.tensor_tensor(PWT[0][n], pT[:, 0:128], mls, ALU.mult),
                             reads=[pTr, cst_r], writes=[PWT_r[0][n]])
                        yield
                        S.op("dve", lambda e, n=n: e.tensor_tensor(Xs[:, n, :], mAK[:, n, 0:128], identb, ALU.add),
                             reads=[mA_r[n], cstb_r], writes=[X_r[n]])
                        yield
                    for j in range(1, 6):
                        b0, b1 = (j - 1) % 2, j % 2
                        p2s = {}
                        for n in G:
                            cur = mAK[:, n, 0:128] if j == 1 else PW[b0][n]
                            curr = mA_r[n] if j == 1 else PW_r[b0][n]
                            ct, ctr_ = PWT[b0][n], PWT_r[b0][n]
                            p2, p2r = psrR.next()
                            p2s[n] = (p2, p2r)
                            if j < 5:
                                S.op("pe", lambda e, p2=p2, cur=cur, ct=ct: e.matmul(p2[:, 0:128], ct, cur, start=True, stop=True),
                                     reads=[curr, ctr_], writes=[p2r])
                                yield
                            S.op("pe", lambda e, p2=p2, cur=cur, ct=ct: e.matmul(p2[:, 128:256], cur, ct, start=True, stop=True),
                                 reads=[curr, ctr_], writes=[p2r])
                            yield
                        for n in G:
                            p2, p2r = p2s[n]
                            S.op("dve", lambda e, p2=p2, n=n, b1=b1: e.tensor_copy(PWT[b1][n], p2[:, 128:256]),
                                 reads=[p2r], writes=[PWT_r[b1][n]])
                            yield
                            if j < 5:
                                S.op("act", lambda e, p2=p2, n=n, b1=b1: e.copy(PW[b1][n], p2[:, 0:128]),
                                     reads=[p2r], writes=[PW_r[b1][n]])
                                yield
                        pxs = {}
                        for n in G:
                            px, pxr = psrR.next()
                            pxs[n] = (px, pxr)
                            S.op("pe", lambda e, px=px, n=n, b1=b1: e.matmul(px[:, 0:128], PWT[b1][n], Xs[:, n, :], start=True, stop=True),
                                 reads=[PWT_r[b1][n], X_r[n]], writes=[pxr])
                            yield
                        for n in G:
                            px, pxr = pxs[n]
                            S.op("dve", lambda e, px=px, n=n: e.tensor_tensor(Xs[:, n, :], px[:, 0:128], Xs[:, n, :], ALU.add),
                                 reads=[pxr, X_r[n]], writes=[X_r[n]])
                            yield
                for n in range(NCH):
                    pw_, pw_r = psrR.next()
                    S.op("pe", lambda e, pw_=pw_, n=n, p=p: e.matmul(pw_[:, 0:64], ARbd[:, n, 0:128], Hb[:, p, :], start=True, stop=False),
                         reads=[bd_r, Hb_r[p]], writes=[pw_r])
                    yield
                    S.op("pe", lambda e, pw_=pw_, n=n: e.matmul(pw_[:, 0:64], mAK[:, n, 256:384], vst[:, n, :], start=False, stop=True),
                         reads=[mK_r[n], vst_r], writes=[pw_r])
                    yield
                    w_b, wbr = ub.next()
                    S.op("act", lambda e, w_b=w_b, pw_=pw_: e.copy(w_b[:, :], pw_[:, 0:64]), reads=[pw_r], writes=[wbr])
                    yield
                    pu, pur_ = psrR.next()
                    S.op("pe", lambda e, pu=pu, n=n, w_b=w_b: e.matmul(pu[:, 0:64], Xs[:, n, :], w_b[:, :], start=True, stop=True),
                         reads=[X_r[n], wbr], writes=[pur_])
                    yield
                    u_b, ubr = ub.next()
                    S.op("dve", lambda e, u_b=u_b, pu=pu: e.tensor_copy(u_b[:, :], pu[:, 0:64]), reads=[pur_], writes=[ubr])
                    yield
                    ph, phr = psrR.next()
                    S.op("pe", lambda e, ph=ph, n=n, u_b=u_b: e.matmul(ph[:, 0:64], btk[:, n, 0:128], u_b[:, :], start=True, stop=False),
                         reads=[btk_r, ubr], writes=[phr])
                    yield
                    S.op("pe", lambda e, ph=ph, n=n: e.matmul(ph[:, 0:64], btk[:, n, 128:256], vst[:, n, :], start=False, stop=True),
                         reads=[btk_r, vst_r], writes=[phr])
                    yield
                    py, pyr = psrR.next()
                    S.op("pe", lambda e, py=py, n=n, p=p: e.matmul(py[:, 0:64], ARbd[:, n, 128:256], Hb[:, p, :], start=True, stop=False),
                         reads=[bd_r, Hb_r[p]], writes=[pyr])
                    yield
                    S.op("pe", lambda e, py=py, n=n, u_b=u_b: e.matmul(py[:, 0:64], mAK[:, n, 128:256], u_b[:, :], start=False, stop=False),
                         reads=[mA_r[n], ubr], writes=[pyr])
                    yield
                    S.op("pe", lambda e, py=py, n=n: e.matmul(py[:, 0:64], mAK[:, n, 384:512], vst[:, n, :], start=False, stop=True),
                         reads=[mK_r[n], vst_r], writes=[pyr])
                    yield
                    ht, htr = htmp.next()
                    S.op("dve", lambda e, ht=ht, ph=ph, p=p: e.tensor_tensor(ht[:, :], ph[:, 0:64], Hst[:, p, :], ALU.add),
                         reads=[phr, H_r[p]], writes=[htr])
                    yield
                    S.op("act", lambda e, ht=ht, p=p, n=n: e.activation(Hb[:, p, :], ht[:, :], AF.Identity, scale=gam[:, n:n + 1]),
                         reads=[htr, gam_r], writes=[Hb_r[p]])
                    yield
                    S.op("act", lambda e, ht=ht, p=p, n=n: e.activation(Hst[:, p, :], ht[:, :], AF.Identity, scale=gam[:, n:n + 1]),
                         reads=[htr, gam_r], writes=[H_r[p]])
                    yield
                    g6, g6r = gn6.next()
                    S.op("dve", lambda e, g6=g6, py=py: e.bn_stats(g6[:, 2:8], py[:, 0:64]), reads=[pyr], writes=[g6r])
                    yield
                    S.op("dve", lambda e, g6=g6: e.bn_aggr(g6[:, 0:2], g6[:, 2:8]), reads=[g6r], writes=[g6r])
                    yield
                    S.op("act", lambda e, g6=g6: e.activation(g6[:, 2:3], g6[:, 1:2], AF.Sqrt, bias=epsgn[:, 0:1]),
                         reads=[g6r, eps_r], writes=[g6r])
                    yield
                    S.op("dve", lambda e, g6=g6: e.reciprocal(g6[:, 3:4], g6[:, 2:3]), reads=[g6r], writes=[g6r])
                    yield
                    for hf in range(2):
                        hs = slice(hf * 64, hf * 64 + 64)
                        S.op("dve", lambda e, hs=hs, hf=hf, py=py, g6=g6: e.tensor_scalar(
                            Ynbd[hs, hf * 64:hf * 64 + 64], py[hs, 0:64], g6[hs, 0:1], g6[hs, 3:4], ALU.subtract, ALU.mult),
                            reads=[pyr, g6r], writes=[ynbd_r])
                        yield
                    S.op("pe", lambda e, n=n, pyo=pyo: e.matmul(pyo[:, n * 64:(n + 1) * 64], Ynbd[:, :], istb, start=True, stop=True),
                         reads=[ynbd_r, cstb_r], writes=[pyor])
                    yield
                if krw <= 5:
                    S.op("dve", lambda e, p=p: e.memset(yrTb[:, p, :], 0.0), writes=[yr_r[p]])
                    yield
                    continue
                S.op("dve", lambda e, t1=t1, pyo=pyo, p=p: e.tensor_scalar(
                    t1[:, :], pyo[:, :], pvc("gng", p), pvc("gnb", p), ALU.mult, ALU.add), reads=[pyor, pv_r], writes=[t1r])
                yield
                S.op("dve", lambda e, t1=t1, bon=bon: e.tensor_tensor(t1[:, :], t1[:, :], bon[:, :], ALU.add),
                     reads=[t1r, bonr], writes=[t1r])
                yield
                S.op("dve", lambda e, t1=t1, g_=g_, p=p: e.tensor_tensor(yrTb[:, p, :], t1[:, :], g_[:, :], ALU.mult),
                     reads=[t1r, g_r], writes=[yr_r[p]])
                yield

        def merge_out(blk):
            for c in range(8):
                ps, pr = projT(wgate, c * 128, x1Tb, x1_r)
                S.op("act", lambda e, ps=ps, c=c: e.activation(sga[:, c, :], ps[:, :], AF.Sigmoid), reads=[pr], writes=[sga_r[c]])
                ps, pr = projT(wgate, 1024 + c * 128, x1Tb, x1_r)
                S.op("act", lambda e, ps=ps, c=c: e.activation(sgb[:, c, :], ps[:, :], AF.Sigmoid), reads=[pr], writes=[sgb_r[c]])
            for c in range(8):
                psa, par = projT(wa_d, c * 128, lambda kc: HYf[:, kc // 2, kc % 2, :], [hyf_r[kc // 2] for kc in range(8)])
                psb, pbr = projT(wb_d, c * 128, lambda kc: HYf[:, kc // 2, 2 + kc % 2, :], [hyf_r[kc // 2] for kc in range(8)])
                ta, tar = tA.next()
                S.op("dve", lambda e, ta=ta, psa=psa, c=c: e.tensor_tensor(ta[:, :], psa[:, :], sga[:, c, :], ALU.mult),
                     reads=[par, sga_r[c]], writes=[tar])
                tb, tbr = tA.next()
                S.op("dve", lambda e, tb=tb, psb=psb, c=c: e.tensor_tensor(tb[:, :], psb[:, :], sgb[:, c, :], ALU.mult),
                     reads=[pbr, sgb_r[c]], writes=[tbr])
                S.op("dve", lambda e, ta=ta, tb=tb, c=c: e.tensor_tensor(mgTb[:, c, :], ta[:, :], tb[:, :], ALU.add),
                     reads=[tar, tbr], writes=[mg_r[c]])
            for c in range(8):
                ps, pr = projT(wo_d, c * 128, mgTb, mg_r)
                S.op("dve", lambda e, ps=ps, c=c: e.scalar_tensor_tensor(
                    zT[:, c, :], x1T[:, c, :], ALPHA, ps[:, :], ALU.mult, ALU.add), reads=[pr, x1_r[c]], writes=[zT_r[c]])
            layer_norm("ln2_g", "ln2_b", epsln, x2T, x2Tb, x2_r)

        x1f_r = Res()
        x1bl_r = [[Res(), Res()] for _ in range(nbl)]; x1ba_r = [[Res(), Res()] for _ in range(nbl)]
        hyl_r = [Res() for _ in range(NBT)]; hya_r = [Res() for _ in range(NBT)]
        oh = sb([128, 4]); oh_r = Res()
        S.dma("sp", oh[:, :], oh_d[:, :], writes=[oh_r])

        def half(t, h):
            return t[:, 4 * h:4 * h + 4, :].rearrange("p a b -> p (a b)")

        for blk in range(nbl):
            load_xT(blk)
            ffn_ln(xT, xTb, xT_r, w1g, w1u, w1d, "ln1_g", "ln1_b", x1T, x1Tb, x1_r)
            rows = slice(blk * 128, (blk + 1) * 128)
            S.dma("sp", x1f_loc[rows, :], x1T[:, :, :].rearrange("p a b -> p (a b)"), reads=x1_r, writes=[x1f_r])
            for h in range(2):
                S.dma("sp", x1b_loc[blk][h][:, :], half(x1Tb, h), reads=x1_r, writes=[x1bl_r[blk][h]])
                S.cc("AllGather", x1b_loc[blk][h][:, :], x1b_all[blk][h][:, :], groups,
                     reads=[x1bl_r[blk][h]], writes=[x1ba_r[blk][h]])
        for gb in range(NBT):
            i, blk = gb // nbl, gb % nbl
            for h in range(2):
                S.dma("sp", half(x1Tb, h), x1b_all[blk][h][i * 128:(i + 1) * 128, :], reads=[x1ba_r[blk][h]],
                      writes=x1_r[4 * h:4 * h + 4])
            if gb == 0:
                S.op("dve", lambda e: e.memset(x1T[:, 0, 0:8], 0.0), writes=x1_r + scr_r)
            gens = [mlstm(gb), rwkv(gb)]
            while gens:
                for gg in list(gens):
                    try:
                        next(gg)
                    except StopIteration:
                        gens.remove(gg)
            S.dma("sp", hy_loc[gb][:, :], hyT[:, :, :].rearrange("p a b -> p (a b)"), reads=hm_r + yr_r, writes=[hyl_r[gb]])
            S.cc("AllGather", hy_loc[gb][:, :], hy_all[gb][:, :], groups, reads=[hyl_r[gb]], writes=[hya_r[gb]])
        for blk in range(nbl):
            rows = slice(blk * 128, (blk + 1) * 128)
            S.dma("sp", x1T[:, :, :].rearrange("p a b -> p (a b)"), x1f_loc[rows, :], reads=[x1f_r], writes=x1_r + scr_r)
            for h in range(2):
                S.dma("sp", half(x1Tb, h), x1b_loc[blk][h][:, :], reads=[x1bl_r[blk][h]], writes=x1_r[4 * h:4 * h + 4])
            for i in range(4):
                dst = HYf[:, i, :, :].rearrange("p a b -> p (a b)")
                for j in range(4):
                    k = j * nbl + blk
                    stg = sgb[:, 4 * (j % 2):4 * (j % 2) + 4, :].rearrange("p a b -> p (a b)")
                    stg_r = sgb_r[4 * (j % 2):4 * (j % 2) + 4]
                    S.dma("sp", stg, hy_all[k][i * 128:(i + 1) * 128, :], reads=[hya_r[k]], writes=stg_r)
                    if j == 0:
                        S.op("dve", lambda e, dst=dst, stg=stg, j=j: e.tensor_scalar_mul(dst, stg, oh[:, j:j + 1]),
                             reads=stg_r + [oh_r], writes=[hyf_r[i]])
                    else:
                        S.op("dve", lambda e, dst=dst, stg=stg, j=j: e.scalar_tensor_tensor(
                            dst, stg, oh[:, j:j + 1], dst, ALU.mult, ALU.add),
                            reads=stg_r + [oh_r, hyf_r[i]], writes=[hyf_r[i]])
            merge_out(blk)
            ffn_ln(x2T, x2Tb, x2_r, w2g, w2u, w2d, "ln3_g", "ln3_b", x3T, x3Tb, x3_r)
            store_T(blk, x3T, x3_r)

        S.op("sp", lambda e: e.nop(), reads=[out_r])
        S.emit()
    return nc


def _consts():
    c = np.zeros((128, NCST), np.float32)
    i = np.arange(128)
    c[:, C_ID:C_ID + 128] = np.eye(128)
    c[:, C_OD:C_OD + 128] = 1.0 / 1024.0
    c[:, C_TRI:C_TRI + 128] = (i[:, None] <= i[None, :])
    c[:, C_ONE:C_ONE + 128] = 1.0
    c[:, C_MUS:C_MUS + 128] = (i[:, None] < i[None, :])
    c[:, C_MLS:C_MLS + 128] = (i[:, None] > i[None, :])
    c[:, C_IST:C_IST + 64] = np.concatenate([np.eye(64), np.eye(64)], axis=0)
    c[:, C_BO:C_BO + 128] = ((i[:, None] // 64) == (i[None, :] // 64))
    return c


def _fm(v):
    v = np.asarray(v, np.float32).reshape(-1, 128)
    return np.ascontiguousarray(v.T)


def kernel(**inp):
    nbl = int(os.environ.get("KNBL", "4"))
    ngrp = int(os.environ.get("KNGRP", "2"))
    x = np.asarray(inp["x"], np.float32)
    W = np.asarray(inp["w_in"][0], np.float32)
    gates = np.ascontiguousarray(W[:, 7432:9480])
    cw = inp["m_conv_w"][0]; cbv = inp["m_conv_b"][0]
    mu = inp["r_mu"][0]
    in_maps = []
    for c in range(4 * ngrp):
        b, r = c // 4, c % 4
        pv = np.zeros((128, NPV), np.float32)

        def put(name, arr):
            a = _fm(arr)
            pv[:, PV[name]:PV[name] + a.shape[1]] = a

        for n in ("ln1_g", "ln1_b", "ln2_g", "ln2_b", "ln3_g", "ln3_b"):
            put(n, inp[n][0])
        qs = slice(r * 256, (r + 1) * 256)
        ks = slice(1024 + r * 256, 1024 + (r + 1) * 256)
        for j in range(4):
            put("cw%d" % j, np.concatenate([cw[j][qs], cw[j][ks]]))
        put("cb", np.concatenate([cbv[qs], cbv[ks]]))
        put("mng", inp["m_norm_g"][0][qs])
        ps_ = slice(r * 256, (r + 1) * 256)
        mul = np.concatenate([mu[0:1024][ps_], mu[1024:2048][ps_], mu[2048:3072][ps_], mu[3072:3328]])
        put("mu", mul)
        put("omu", 1.0 - mul)
        put("w0", inp["r_w0"][0][ps_]); put("a0", inp["r_a0"][0][ps_]); put("kk", inp["r_k_k"][0][ps_])
        put("ka", inp["r_k_a"][0][ps_]); put("rrk", inp["r_r_k"][0].reshape(-1)[ps_])
        put("gng", inp["r_gn_g"][0][ps_]); put("gnb", inp["r_gn_b"][0][ps_])
        wl = np.concatenate([W[:, qs], W[:, ks], W[:, 2048 + r * 256:2048 + (r + 1) * 256],
                             W[:, 3072 + r * 256:3072 + (r + 1) * 256], np.zeros((D, 8), np.float32),
                             W[:, RC0 + r * 256:RC0 + (r + 1) * 256],
                             W[:, RC0 + 1024 + r * 256:RC0 + 1024 + (r + 1) * 256],
                             W[:, RC0 + 2048 + r * 256:RC0 + 2048 + (r + 1) * 256],
                             W[:, RC0 + 3072:RC0 + 3328]], axis=1)
        assert wl.shape[1] == 2056
        wif = np.zeros((D, 8), np.float32)
        wif[:, 0] = W[:, 4096 + r]
        wif[:, 4] = W[:, 4100 + r]
        gbias = np.zeros((128, 8), np.float32)
        gbias[:, 0] = inp["m_i_bias"][0][r]
        gbias[:, 4] = inp["m_f_bias"][0][r]
        gbias[:, 5:8] = 30.0
        m = {"cst": _consts(), "pv": pv, "gbias": gbias,
             "rst": np.ascontiguousarray(np.broadcast_to((np.arange(512)[None, :] % 64 != 0), (128, 512)).astype(np.float32)),
             "w_loc": np.ascontiguousarray(wl), "w_if": wif, "w_gate": gates,
             "r_w2": np.ascontiguousarray(inp["r_w2"][0][:, ps_]), "r_a2": np.ascontiguousarray(inp["r_a2"][0][:, ps_]),
             "r_g2": np.ascontiguousarray(inp["r_g2"][0][:, ps_]),
             "x": np.ascontiguousarray(x[b, r * nbl * TB:(r + 1) * nbl * TB]),
             "oh": np.ascontiguousarray(np.broadcast_to(np.eye(4, dtype=np.float32)[r][None, :], (128, 4)))}
        for n in ("ffn1_w_gate", "ffn1_w_up", "ffn1_w_down", "ffn2_w_gate", "ffn2_w_up", "ffn2_w_down",
                  "w_branch_a", "w_branch_b", "w_out"):
            m[n] = np.ascontiguousarray(inp[n][0], dtype=np.float32)
        in_maps.append(m)
    groups = [list(range(4 * g, 4 * g + 4)) for g in range(ngrp)]
    nc = build(nbl, groups)
    res = run_bass_kernel_spmd(nc, in_maps, core_ids=list(range(4 * ngrp)))
    out = np.zeros((ngrp, 4 * nbl * TB, D), np.float32)
    for c in range(4 * ngrp):
        out[c // 4, (c % 4) * nbl * TB:(c % 4 + 1) * nbl * TB] = np.asarray(res.results[c]["out"])
    return out
```

```python
import contextlib
import os
import numpy as np
import concourse.bass as bass
import concourse.mybir as mybir
from concourse.bass_utils import run_bass_kernel_spmd

F32 = mybir.dt.float32
BF16 = mybir.dt.bfloat16
AF = mybir.ActivationFunctionType
ALU = mybir.AluOpType

D = 1024
DFF = 2816
NJ = DFF // 128
SEQ = 8192
TB = 512
ALPHA = 2.0 ** 0.25
LN_EPS = 1e-5
GN_EPS = 64e-5
NCORES = 8
WCOLS = 9480
RC0 = 4104
SAME_ENGINE_WAITS = os.environ.get("KSEW", "act,dve,pool,sp").split(",")


class Res:
    __slots__ = ("w", "r", "excl")

    def __init__(self, excl=False):
        self.w = None
        self.r = {}
        self.excl = excl


class Sched:
    NDS = 24

    def __init__(self, nc, stack):
        self.nc = nc
        self.E = {"pe": nc.tensor, "act": nc.scalar, "dve": nc.vector, "pool": nc.gpsimd, "sp": nc.sync}
        self.q = {k: [] for k in self.E}
        self.cnt = {k: 0 for k in self.E}
        self.esem = {k: stack.enter_context(nc.semaphore("s_" + k)) for k in self.E}
        self.dsem = [stack.enter_context(nc.semaphore("d%d" % i)) for i in range(self.NDS)]
        self.dval = [0] * self.NDS
        self.dnext = {"sp": 0, "pool": 0}
        self.dpool = {"sp": list(range(0, 8)), "pool": list(range(8, self.NDS))}
        self.waited = {k: {} for k in self.E}
        self.csem = stack.enter_context(nc.semaphore("cc"))
        self.cval = 0

    def _deps(self, eng, reads, writes, extra=()):
        deps = {}

        def add(t):
            if t is None:
                return
            k = (t[0], t[1])
            if deps.get(k, 0) < t[2]:
                deps[k] = t[2]

        for r in reads:
            add(r.w)
        for w in writes:
            add(w.w)
            for k, v in w.r.items():
                add((k[0], k[1], v))
        for t in extra:
            add(t)
        waits = []
        for k, v in deps.items():
            if k[0] == "e" and k[1] == eng and (eng == "pe" or eng not in SAME_ENGINE_WAITS):
                continue
            if self.waited[eng].get(k, 0) >= v:
                continue
            self.waited[eng][k] = v
            waits.append((k, v))
        return waits

    def _mark(self, tok, reads, writes):
        k = (tok[0], tok[1])
        for r in reads:
            if r.r.get(k, 0) < tok[2]:
                r.r[k] = tok[2]
        for w in writes:
            w.w = tok
            w.r = {}

    def op(self, eng, fn, reads=(), writes=()):
        ex = [r for r in reads if r.excl]
        if ex:
            writes = list(writes) + ex
        waits = self._deps(eng, reads, writes)
        self.cnt[eng] += 1
        tok = ("e", eng, self.cnt[eng])
        self.q[eng].append((waits, fn, None))
        self._mark(tok, reads, writes)
        return tok

    def dma(self, qeng, out, in_, reads=(), writes=()):
        pool = self.dpool[qeng]
        j = pool[self.dnext[qeng]]
        self.dnext[qeng] = (self.dnext[qeng] + 1) % len(pool)
        prev = ("d", j, self.dval[j]) if self.dval[j] else None
        extra = [prev] if prev else []
        waits = self._deps(qeng, reads, writes, extra=extra)
        self.dval[j] += 16
        tok = ("d", j, self.dval[j])
        self.q[qeng].append((waits, (out, in_), j))
        self._mark(tok, reads, writes)
        return tok

    def cc(self, kind, ins, outs, groups, reads=(), writes=()):
        waits = self._deps("pool", reads, writes)
        self.cval += 1
        tok = ("c", 0, self.cval)
        self.q["pool"].append((waits, (kind, ins, outs, groups), "cc"))
        self._mark(tok, reads, writes)
        return tok

    def emit(self):
        nc = self.nc
        with nc.Block() as block:
            def run(kind):
                def body(eng):
                    for waits, fn, dj in self.q[kind]:
                        for k, v in waits:
                            sem = self.esem[k[1]] if k[0] == "e" else (self.csem if k[0] == "c" else self.dsem[k[1]])
                            eng.wait_ge(sem, v)
                        if dj is None:
                            fn(eng).then_inc(self.esem[kind], 1)
                        elif dj == "cc":
                            eng.collective_compute(fn[0], ALU.bypass, replica_groups=fn[3], ins=[fn[1]], outs=[fn[2]]).then_inc(self.csem, 1)
                        else:
                            eng.dma_start(out=fn[0], in_=fn[1]).then_inc(self.dsem[dj], 16)
                return body
            block.tensor(run("pe"))
            block.scalar(run("act"))
            block.vector(run("dve"))
            block.gpsimd(run("pool"))
            block.sync(run("sp"))


class Ring:
    def __init__(self, items):
        self.items = items
        self.i = 0

    def next(self):
        it = self.items[self.i]
        self.i = (self.i + 1) % len(self.items)
        return it


PV = {}
_o = 0
for _n, _w in [("ln1_g", 8), ("ln1_b", 8), ("ln2_g", 8), ("ln2_b", 8), ("ln3_g", 8), ("ln3_b", 8),
               ("cw0", 4), ("cw1", 4), ("cw2", 4), ("cw3", 4), ("cb", 4), ("mng", 2),
               ("mu", 8), ("omu", 8), ("w0", 2), ("a0", 2), ("kk", 2), ("ka", 2), ("rrk", 2),
               ("gng", 2), ("gnb", 2)]:
    PV[_n] = _o
    _o += _w
NPV = _o
C_ID, C_OD, C_MUS, C_TRI, C_ONE, C_MLS, C_IST, C_BO, NCST = 0, 128, 256, 384, 512, 640, 768, 832, 960


def build(nbl, groups):
    nc = bass.Bass("TRN2", target_bir_lowering=False)

    def din(name, shape):
        return nc.dram_tensor(name, list(shape), F32, kind="ExternalInput").ap()

    NBT = 4 * nbl
    x_d = din("x", [nbl * TB, D])
    cst_d = din("cst", [128, NCST])
    pv_d = din("pv", [128, NPV])
    gb_d = din("gbias", [128, 8])
    rst_d = din("rst", [128, 512])
    w1g = din("ffn1_w_gate", [D, DFF]); w1u = din("ffn1_w_up", [D, DFF]); w1d = din("ffn1_w_down", [DFF, D])
    w2g = din("ffn2_w_gate", [D, DFF]); w2u = din("ffn2_w_up", [D, DFF]); w2d = din("ffn2_w_down", [DFF, D])
    win = din("w_loc", [D, 2056])
    wif_d = din("w_if", [D, 8])
    wgate = din("w_gate", [D, 2048])
    oh_d = din("oh", [128, 4])
    wa_d = din("w_branch_a", [D, D]); wb_d = din("w_branch_b", [D, D]); wo_d = din("w_out", [D, D])
    rw2_d = din("r_w2", [64, 256]); ra2_d = din("r_a2", [64, 256]); rg2_d = din("r_g2", [128, 256])
    out_d = nc.dram_tensor("out", [nbl * TB, D], F32, kind="ExternalOutput").ap()
    x1f_loc = nc.dram_tensor("x1f_loc", [nbl * 128, 8 * TB], F32).ap()
    x1b_loc = [[nc.dram_tensor("x1bl_%d_%d" % (k, h), [128, 4 * TB], BF16).ap() for h in range(2)] for k in range(nbl)]
    x1b_all = [[nc.dram_tensor("x1ba_%d_%d" % (k, h), [4 * 128, 4 * TB], BF16).ap() for h in range(2)] for k in range(nbl)]
    hy_loc = [nc.dram_tensor("hyl_%d" % k, [128, 4 * TB], BF16).ap() for k in range(NBT)]
    hy_all = [nc.dram_tensor("hya_%d" % k, [4 * 128, 4 * TB], BF16).ap() for k in range(NBT)]

    with contextlib.ExitStack() as st:
        S = Sched(nc, st)
        _n = [0]

        def sb(shape, dt=F32):
            _n[0] += 1
            return st.enter_context(nc.sbuf_tensor("sb%d" % _n[0], list(shape), dt))

        def ring(n, shape, dt=F32):
            return Ring([(sb(shape, dt), Res()) for _ in range(n)])

        banks = [st.enter_context(nc.psum_tensor("ps%d" % i, [128, 512], F32)) for i in range(8)]
        psr = Ring([(banks[i], Res(True)) for i in range(7)])
        pyo_bank = (banks[7], Res(True))

        cst = sb([128, NCST]); cst_r = Res()
        cstb = sb([128, NCST], BF16); cstb_r = Res()
        pv = sb([128, NPV]); pv_r = Res()
        gbias = sb([128, 8]); gb_r = Res()
        S.dma("sp", cst[:, :], cst_d[:, :], writes=[cst_r])
        S.dma("sp", pv[:, :], pv_d[:, :], writes=[pv_r])
        S.dma("sp", gbias[:, :], gb_d[:, :], writes=[gb_r])
        S.op("act", lambda e: e.copy(cstb[:, :], cst[:, :]), reads=[cst_r], writes=[cstb_r])
        ident = cst[:, C_ID:C_ID + 128]
        identb = cstb[:, C_ID:C_ID + 128]
        onesdb = cstb[:, C_OD:C_OD + 128]
        tri = cst[:, C_TRI:C_TRI + 128]
        ones = cst[:, C_ONE:C_ONE + 128]
        istb = cstb[:, C_IST:C_IST + 64]
        bob = cstb[:, C_BO:C_BO + 128]
        rstb = sb([128, 512], BF16)
        S.dma("pool", rstb[:, :], rst_d[:, :], writes=[cstb_r])
        rst = rstb[:, :]
        epsln = sb([128, 1]); eps4 = sb([128, 1]); epsgn = sb([128, 1]); eps_r = Res()
        S.op("dve", lambda e: e.memset(epsln[:, :], LN_EPS), writes=[eps_r])
        S.op("dve", lambda e: e.memset(eps4[:, :], 4 * LN_EPS), writes=[eps_r])
        S.op("dve", lambda e: e.memset(epsgn[:, :], GN_EPS), writes=[eps_r])

        def pvc(name, i=0):
            c = PV[name] + i
            return pv[:, c:c + 1]

        rw2a2 = sb([128, 256], BF16); rw_r = Res()
        rg2 = sb([128, 256], BF16)
        S.dma("pool", rw2a2[0:64, :], rw2_d[:, :], writes=[rw_r])
        S.dma("pool", rw2a2[64:128, :], ra2_d[:, :], writes=[rw_r])
        S.dma("pool", rg2[:, :], rg2_d[:, :], writes=[rw_r])

        xT = sb([128, 8, TB]); xTb = sb([128, 8, TB], BF16); xT_r = [Res() for _ in range(8)]
        x1T = sb([128, 8, TB]); x1Tb = sb([128, 8, TB], BF16); x1_r = [Res() for _ in range(8)]
        x2T = xT; x2Tb = xTb; x2_r = xT_r
        x3T = xT; x3Tb = xTb; x3_r = xT_r
        aT = sb([128, NJ, TB], BF16); aT_r = [Res() for _ in range(NJ)]
        zT = sb([128, 8, TB]); zT_r = [Res() for _ in range(8)]
        xtok = ring(1, [128, D])
        otok = xtok
        wgu = ring(3, [128, 2, 8, 128], BF16)
        wdb = ring(4, [128, 512], BF16)
        wpj = ring(2, [128, 8, 128], BF16)
        tA = ring(3, [128, TB])
        tB = ring(2, [128, TB], BF16)
        mean_r = Res(); rstd_r = Res()
        out_r = Res()

        def load_xT(blk):
            for t in range(TB // 128):
                xt, xr = xtok.next()
                r0 = blk * TB + t * 128
                S.dma("sp", xt[:, :], x_d[r0:r0 + 128, :], writes=[xr])
                for half in range(2):
                    ps, pr = psr.next()
                    for q in range(4):
                        kc = half * 4 + q
                        S.op("pe", lambda e, ps=ps, xt=xt, kc=kc, q=q: e.matmul(
                            ps[:, q * 128:(q + 1) * 128], xt[:, kc * 128:(kc + 1) * 128], ident,
                            start=True, stop=True), reads=[xr, cst_r], writes=[pr])
                    psv = ps[:, :].rearrange("p (q n) -> p q n", q=4)
                    S.op("act", lambda e, psv=psv, half=half, t=t: e.copy(
                        xT[:, half * 4:half * 4 + 4, t * 128:(t + 1) * 128], psv),
                        reads=[pr], writes=xT_r[half * 4:half * 4 + 4])
                    S.op("dve", lambda e, psv=psv, half=half, t=t: e.tensor_copy(
                        xTb[:, half * 4:half * 4 + 4, t * 128:(t + 1) * 128], psv),
                        reads=[pr], writes=xT_r[half * 4:half * 4 + 4])

        def layer_norm(gname, bname, eps_t, outT, outTb, out_rs):
            psm, pmr = psr.next()
            pss, ssr = psr.next()
            for dc in range(8):
                tb, tr = tB.next()
                S.op("act", lambda e, tb=tb, dc=dc: e.copy(tb[:, :], zT[:, dc, :]), reads=[zT_r[dc]], writes=[tr])
                S.op("pe", lambda e, tb=tb, dc=dc: e.matmul(psm[:, :], onesdb, tb[:, :], start=(dc == 0), stop=(dc == 7)),
                     reads=[cstb_r, tr], writes=[pmr])
                tb2, tr2 = tB.next()
                S.op("act", lambda e, tb2=tb2, dc=dc: e.activation(tb2[:, :], zT[:, dc, :], AF.Square),
                     reads=[zT_r[dc]], writes=[tr2])
                S.op("pe", lambda e, tb2=tb2, dc=dc: e.matmul(pss[:, :], onesdb, tb2[:, :], start=(dc == 0), stop=(dc == 7)),
                     reads=[cstb_r, tr2], writes=[ssr])
            S.op("act", lambda e: e.copy(mean_sb[:, :], psm[:, :]), reads=[pmr], writes=[mean_r])
            t1, r1 = tA.next()
            S.op("dve", lambda e, t1=t1: e.tensor_tensor(t1[:, :], mean_sb[:, :], mean_sb[:, :], ALU.mult),
                 reads=[mean_r], writes=[r1])
            t2, r2 = tA.next()
            S.op("dve", lambda e, t1=t1, t2=t2: e.tensor_tensor(t2[:, :], pss[:, :], t1[:, :], ALU.subtract),
                 reads=[ssr, r1], writes=[r2])
            S.op("dve", lambda e, t2=t2: e.tensor_scalar_max(t2[:, :], t2[:, :], 0.0), reads=[r2], writes=[r2])
            t3, r3 = tA.next()
            S.op("act", lambda e, t2=t2, t3=t3: e.activation(t3[:, :], t2[:, :], AF.Sqrt, bias=eps_t[:, 0:1]),
                 reads=[r2, eps_r], writes=[r3])
            S.op("dve", lambda e, t3=t3: e.reciprocal(rstd_sb[:, :], t3[:, :]), reads=[r3], writes=[rstd_r])
            for dc in range(8):
                ta, tar = tA.next()
                S.op("dve", lambda e, ta=ta, dc=dc: e.tensor_tensor(ta[:, :], zT[:, dc, :], mean_sb[:, :], ALU.subtract),
                     reads=[zT_r[dc], mean_r], writes=[tar])
                S.op("dve", lambda e, ta=ta: e.tensor_tensor(ta[:, :], ta[:, :], rstd_sb[:, :], ALU.mult),
                     reads=[tar, rstd_r], writes=[tar])
                S.op("act", lambda e, ta=ta, dc=dc: e.activation(
                    outT[:, dc, :], ta[:, :], AF.Identity, scale=pvc(gname, dc), bias=pvc(bname, dc)),
                    reads=[tar, pv_r], writes=[out_rs[dc]])
                S.op("act", lambda e, ta=ta, dc=dc: e.activation(
                    outTb[:, dc, :], ta[:, :], AF.Identity, scale=pvc(gname, dc), bias=pvc(bname, dc)),
                    reads=[tar, pv_r], writes=[out_rs[dc]])

        def ffn_ln(inT, inTb, in_rs, Wg, Wu, Wd, gname, bname, outT, outTb, out_rs):
            Wg_v = Wg.rearrange("(kc p) f -> p kc f", p=128)
            Wu_v = Wu.rearrange("(kc p) f -> p kc f", p=128)
            Wd_v = Wd.rearrange("(j p) d -> p j d", p=128)
            for j in range(NJ):
                wb, wr = wgu.next()
                S.dma("pool", wb[:, 0], Wg_v[:, :, j * 128:(j + 1) * 128], writes=[wr])
                S.dma("pool", wb[:, 1], Wu_v[:, :, j * 128:(j + 1) * 128], writes=[wr])
                psg, pgr = psr.next()
                psu, pur = psr.next()
                for gu, (ps, prr) in enumerate(((psg, pgr), (psu, pur))):
                    for kc in range(8):
                        S.op("pe", lambda e, ps=ps, wb=wb, kc=kc, gu=gu: e.matmul(
                            ps[:, :], wb[:, gu, kc, :], inTb[:, kc, :], start=(kc == 0), stop=(kc == 7)),
                            reads=[wr, in_rs[kc]], writes=[prr])
                tb, tr = tA.next()
                S.op("act", lambda e, tb=tb, ps=psg: e.activation(tb[:, :], ps[:, :], AF.Silu), reads=[pgr], writes=[tr])
                S.op("dve", lambda e, tb=tb, ps=psu, j=j: e.tensor_tensor(aT[:, j, :], tb[:, :], ps[:, :], ALU.mult),
                     reads=[tr, pur], writes=[aT_r[j]])
            for half in range(2):
                pss_ = [psr.next() for _ in range(4)]
                for j in range(NJ):
                    wb, wr = wdb.next()
                    S.dma("pool", wb[:, :], Wd_v[:, j, half * 512:(half + 1) * 512], writes=[wr])
                    for q in range(4):
                        ps, pr = pss_[q]
                        S.op("pe", lambda e, ps=ps, wb=wb, j=j, q=q: e.matmul(
                            ps[:, :], wb[:, q * 128:(q + 1) * 128], aT[:, j, :], start=(j == 0), stop=(j == NJ - 1)),
                            reads=[wr, aT_r[j]], writes=[pr])
                for q in range(4):
                    dc = half * 4 + q
                    ps, pr = pss_[q]
                    S.op("dve", lambda e, ps=ps, dc=dc: e.scalar_tensor_tensor(
                        zT[:, dc, :], inT[:, dc, :], 2.0 * ALPHA, ps[:, :], ALU.mult, ALU.add),
                        reads=[pr, in_rs[dc]], writes=[zT_r[dc]])
            layer_norm(gname, bname, eps4, outT, outTb, out_rs)

        def store_T(blk, srcT, src_rs):
            for t in range(TB // 128):
                ot, orr = otok.next()
                for half in range(2):
                    ps, pr = psr.next()
                    for q in range(4):
                        kc = half * 4 + q
                        S.op("pe", lambda e, ps=ps, kc=kc, q=q, t=t: e.matmul(
                            ps[:, q * 128:(q + 1) * 128], srcT[:, kc, t * 128:(t + 1) * 128], ident,
                            start=True, stop=True), reads=[src_rs[kc], cst_r], writes=[pr])
                    S.op("act", lambda e, ps=ps, ot=ot, half=half: e.copy(ot[:, half * 512:(half + 1) * 512], ps[:, :]),
                         reads=[pr], writes=[orr])
                r0 = blk * TB + t * 128
                S.dma("sp", out_d[r0:r0 + 128, :], ot[:, :], reads=[orr], writes=[out_r])

        def projT(Wd_ap, col0, inTb, in_rs, ncols=128, wring=None, pring=None):
            wb, wr = (wring or wpj).next()
            Wv = Wd_ap.rearrange("(kc p) f -> p kc f", p=128)
            S.dma("pool", wb[:, :, 0:ncols], Wv[:, :, col0:col0 + ncols], writes=[wr])
            ps, pr = (pring or psr).next()
            for kc in range(8):
                src = inTb(kc) if callable(inTb) else inTb[:, kc, :]
                S.op("pe", lambda e, ps=ps, wb=wb, kc=kc, src=src: e.matmul(
                    ps[0:ncols, :], wb[:, kc, 0:ncols], src, start=(kc == 0), stop=(kc == 7)),
                    reads=[wr, in_rs[kc]], writes=[pr])
            return ps, pr

        NT = TB // 128
        carry = sb([128, 4, 3]); carry_r = [Res() for _ in range(4)]
        S.op("dve", lambda e: e.memset(carry[:, :, :], 0.0), writes=carry_r)
        cwork = ring(2, [128, 3 + TB])
        mean_sb = cwork.items[0][0][:, 0:TB]; rstd_sb = cwork.items[1][0][:, 0:TB]
        qkT = zT[:, :, :].rearrange("p a b -> p (a b)").bitcast(BF16).rearrange("p (c n) -> p c n", n=TB)
        qk_r = [zT_r[c // 2] for c in range(16)]
        sigmo = sb([128, 2, TB], BF16); sigmo_r = [Res() for _ in range(2)]
        sga = sb([128, 8, TB], BF16); sga_r = [Res() for _ in range(8)]
        sgb = sb([128, 8, TB], BF16); sgb_r = [Res() for _ in range(8)]
        vt = sb([128, NT, 1, 258], BF16); vt_r = [[Res() for _ in range(1)] for _ in range(NT)]
        S.op("dve", lambda e: e.memset(vt[:, :, :, :], 1.0), writes=[r for rr in vt_r for r in rr])
        wv = ring(1, [128, 8, 256], BF16)
        wif = sb([128, 8, 8], BF16); wif_r = Res()
        S.dma("pool", wif[:, :, :], wif_d.rearrange("(kc p) f -> p kc f", p=128), writes=[wif_r])
        gts = sb([128, NT, 24]); gts_r = [Res() for _ in range(NT)]
        Cst = sb([128, 1, 2, 258]); C_r = [Res() for _ in range(1)]
        Cb = sb([128, 1, 2, 258], BF16); Cb_r = [Res() for _ in range(1)]
        S.op("dve", lambda e: e.memset(Cst[:, :, :, :], 0.0), writes=C_r)
        S.op("dve", lambda e: e.memset(Cb[:, :, :, :], 0.0), writes=Cb_r)
        hyT = sb([128, 4, TB], BF16); hm_r = [Res() for _ in range(2)]
        hmTb = hyT[:, 0:2, :]
        HYf = sb([128, 4, 4, TB], BF16); hyf_r = [Res() for _ in range(4)]
        yrTb = hyT[:, 2:4, :]; yr_r = [Res() for _ in range(2)]
        mgTb = sga; mg_r = sga_r
        smr = ring(4, [128, 128], BF16)
        ktk = ring(4, [128, 256], BF16)
        sm6 = ring(4, [128, 8])

        def mlstm(blk):
            Wv = win.rearrange("(kc p) f -> p kc f", p=128)
            for c in range(4):
                ps, pr = projT(win, c * 128, x1Tb, x1_r, wring=wpjM, pring=psrM)
                wk, wkr = cwork.next()
                S.op("act", lambda e, wk=wk, c=c: e.copy(wk[:, 0:3], carry[:, c, :]), reads=[carry_r[c]], writes=[wkr])
                yield
                S.op("act", lambda e, wk=wk, ps=ps: e.copy(wk[:, 3:3 + TB], ps[:, :]), reads=[pr], writes=[wkr])
                yield
                S.op("act", lambda e, wk=wk, c=c: e.copy(carry[:, c, :], wk[:, TB:TB + 3]), reads=[wkr], writes=[carry_r[c]])
                yield
                ta, tar = tAM.next()
                S.op("dve", lambda e, wk=wk, ta=ta, c=c: e.tensor_scalar(
                    ta[:, :], wk[:, 0:TB], pvc("cw0", c), pvc("cb", c), ALU.mult, ALU.add),
                    reads=[wkr, pv_r], writes=[tar])
                yield
                for j in (1, 2, 3):
                    S.op("dve", lambda e, wk=wk, ta=ta, c=c, j=j: e.scalar_tensor_tensor(
                        ta[:, :], wk[:, j:j + TB], pvc("cw%d" % j, c), ta[:, :], ALU.mult, ALU.add),
                        reads=[wkr, pv_r, tar], writes=[tar])
                    yield
                S.op("act", lambda e, ta=ta, c=c: e.activation(qkT[:, c, :], ta[:, :], AF.Silu),
                     reads=[tar], writes=[qk_r[c]])
                yield
            for c in range(2):
                ps, pr = projT(win, 768 + c * 128, x1Tb, x1_r, wring=wpjM, pring=psrM)
                S.op("act", lambda e, ps=ps, c=c: e.activation(sigmo[:, c, :], ps[:, :], AF.Sigmoid),
                     reads=[pr], writes=[sigmo_r[c]])
                yield
            for t in range(NT):
                ps, pr = psrM.next()
                for kc in range(8):
                    S.op("pe", lambda e, ps=ps, kc=kc, t=t: e.matmul(
                        ps[:, 0:8], x1Tb[:, kc, t * 128:(t + 1) * 128], wif[:, kc, :], start=(kc == 0), stop=(kc == 7)),
                        reads=[wif_r, x1_r[kc]], writes=[pr])
                    yield
                g = gts[:, t, :]
                gr = gts_r[t]
                S.op("dve", lambda e, g=g, ps=ps: e.tensor_tensor(g[:, 12:20], ps[:, 0:8], gbias[:, :], ALU.add),
                     reads=[pr, gb_r], writes=[gr])
                yield
                S.op("act", lambda e, g=g: e.activation(g[:, 20:24], g[:, 16:20], AF.Exp, scale=-1.0), reads=[gr], writes=[gr])
                yield
                S.op("act", lambda e, g=g: e.activation(g[:, 16:20], g[:, 20:24], AF.Ln, bias=1.0), reads=[gr], writes=[gr])
                yield
                S.op("dve", lambda e, g=g: e.tensor_scalar_mul(g[:, 16:20], g[:, 16:20], -1.0), reads=[gr], writes=[gr])
                yield
                ps2, pr2 = psrM.next()
                S.op("pe", lambda e, ps2=ps2, g=g: e.matmul(ps2[:, 0:4], tri, g[:, 16:20], start=True, stop=True),
                     reads=[gr, cst_r], writes=[pr2])
                yield
                S.op("pe", lambda e, ps2=ps2, g=g: e.matmul(ps2[:, 4:8], ones, g[:, 16:20], start=True, stop=True),
                     reads=[gr, cst_r], writes=[pr2])
                yield
                S.op("dve", lambda e, g=g, ps2=ps2: e.tensor_tensor(g[:, 20:24], g[:, 12:16], ps2[:, 0:4], ALU.subtract),
                     reads=[gr, pr2], writes=[gr])
                yield
                S.op("act", lambda e, g=g: e.activation(g[:, 0:4], g[:, 20:24], AF.Exp), reads=[gr], writes=[gr])
                yield
                S.op("act", lambda e, g=g, ps2=ps2: e.activation(g[:, 4:8], ps2[:, 0:4], AF.Exp, scale=-1.0),
                     reads=[gr, pr2], writes=[gr])
                yield
                S.op("act", lambda e, g=g, ps2=ps2: e.activation(g[:, 8:12], ps2[:, 4:8], AF.Exp), reads=[gr, pr2], writes=[gr])
                yield
            for h in range(1):
                wb, wr = wv.next()
                S.dma("pool", wb[:, :, :], Wv[:, :, 512:768], writes=[wr])
                yield
                for t in range(NT):
                    ps, pr = psrM.next()
                    for kc in range(8):
                        S.op("pe", lambda e, ps=ps, kc=kc, t=t, wb=wb: e.matmul(
                            ps[:, 0:256], x1Tb[:, kc, t * 128:(t + 1) * 128], wb[:, kc, :], start=(kc == 0), stop=(kc == 7)),
                            reads=[wr, x1_r[kc]], writes=[pr])
                        yield
                    S.op("act", lambda e, ps=ps, t=t, h=h: e.copy(vt[:, t, h, 0:256], ps[:, 0:256]),
                         reads=[pr], writes=[vt_r[t][h]])
                    yield
            for t in range(NT):
                tc_ = slice(t * 128, (t + 1) * 128)
                g = gts[:, t, :]
                gr = gts_r[t]
                def head_chain(t, h, tc_, g, gr):
                    qc = [h * 2, h * 2 + 1]
                    kc_ = [2 + h * 2, 2 + h * 2 + 1]
                    ps, pr = psrM.next()
                    for i in range(2):
                        S.op("pe", lambda e, ps=ps, i=i, kc_=kc_, qc=qc, tc_=tc_: e.matmul(
                            ps[:, 0:128], qkT[:, kc_[i], tc_], qkT[:, qc[i], tc_], start=(i == 0), stop=(i == 1)),
                            reads=[qk_r[kc_[i]], qk_r[qc[i]]], writes=[pr])
                        yield
                    sm, smrr = smr.next()
                    S.op("dve", lambda e, sm=sm, ps=ps, h=h, g=g: e.scalar_tensor_tensor(
                        sm[:, :], ps[:, 0:128], g[:, h:h + 1], tri, ALU.mult, ALU.mult),
                        reads=[pr, gr, cst_r], writes=[smrr])
                    yield
                    po, por = psrM.next()
                    S.op("pe", lambda e, po=po, sm=sm, t=t, h=h: e.matmul(
                        po[:, 0:258], sm[:, :], vt[:, t, h, :], start=True, stop=False),
                        reads=[smrr, vt_r[t][h]], writes=[por])
                    yield
                    for i in range(2):
                        S.op("pe", lambda e, po=po, i=i, h=h, qc=qc, tc_=tc_: e.matmul(
                            po[:, 0:258], qkT[:, qc[i], tc_], Cb[:, h, i, :], start=False, stop=(i == 1)),
                            reads=[qk_r[qc[i]], Cb_r[h]], writes=[por])
                        yield
                    s6, s6r = sm6.next()
                    S.op("act", lambda e, s6=s6, po=po: e.activation(
                        s6[:, 0:1], po[:, 256:257], AF.Abs, scale=1.0 / 16.0), reads=[por], writes=[s6r])
                    yield
                    S.op("dve", lambda e, s6=s6, g=g, h=h: e.tensor_tensor(s6[:, 0:1], s6[:, 0:1], g[:, 4 + h:5 + h], ALU.max),
                         reads=[s6r, gr], writes=[s6r])
                    yield
                    S.op("dve", lambda e, s6=s6: e.reciprocal(s6[:, 1:2], s6[:, 0:1]), reads=[s6r], writes=[s6r])
                    yield
                    hb, hbr = hh.next()
                    S.op("dve", lambda e, hb=hb, po=po, s6=s6: e.tensor_scalar(
                        hb[:, :], po[:, 0:256], s6[:, 1:2], 1.0 / 16.0, ALU.mult, ALU.mult), reads=[por, s6r], writes=[hbr])
                    yield
                    S.op("dve", lambda e, hb=hb, s6=s6: e.bn_stats(s6[:, 2:8], hb[:, :]), reads=[hbr], writes=[s6r])
                    yield
                    S.op("dve", lambda e, s6=s6: e.bn_aggr(s6[:, 0:2], s6[:, 2:8]), reads=[s6r], writes=[s6r])
                    yield
                    S.op("act", lambda e, s6=s6: e.activation(s6[:, 2:3], s6[:, 1:2], AF.Sqrt, bias=epsln[:, 0:1]),
                         reads=[s6r, eps_r], writes=[s6r])
                    yield
                    S.op("dve", lambda e, s6=s6: e.reciprocal(s6[:, 3:4], s6[:, 2:3]), reads=[s6r], writes=[s6r])
                    yield
                    hn, hnr = hnb.next()
                    S.op("dve", lambda e, hn=hn, hb=hb, s6=s6: e.tensor_scalar(
                        hn[:, :], hb[:, :], s6[:, 0:1], s6[:, 3:4], ALU.subtract, ALU.mult), reads=[hbr, s6r], writes=[hnr])
                    yield
                    for i in range(2):
                        pt, ptr = psrM.next()
                        S.op("pe", lambda e, pt=pt, hn=hn, i=i: e.matmul(
                            pt[:, 0:128], hn[:, i * 128:(i + 1) * 128], identb, start=True, stop=True),
                            reads=[hnr, cstb_r], writes=[ptr])
                        yield
                        S.op("dve", lambda e, pt=pt, h=h, i=i, tc_=tc_: e.scalar_tensor_tensor(
                            hmTb[:, h * 2 + i, tc_], pt[:, 0:128], pvc("mng", h * 2 + i), sigmo[:, h * 2 + i, tc_],
                            ALU.mult, ALU.mult), reads=[ptr, pv_r, sigmo_r[h * 2 + i]], writes=[hm_r[h * 2 + i]])
                        yield
                    kk_, kkr = ktk.next()
                    for i in range(2):
                        pt, ptr = psrM.next()
                        S.op("pe", lambda e, pt=pt, i=i, kc_=kc_, tc_=tc_: e.matmul(
                            pt[:, 0:128], qkT[:, kc_[i], tc_], identb, start=True, stop=True),
                            reads=[qk_r[kc_[i]], cstb_r], writes=[ptr])
                        yield
                        S.op("act", lambda e, pt=pt, kk_=kk_, i=i, g=g, h=h: e.activation(
                            kk_[:, i * 128:(i + 1) * 128], pt[:, 0:128], AF.Identity, scale=g[:, h:h + 1]),
                            reads=[ptr, gr], writes=[kkr])
                        yield
                    for i in range(2):
                        pc, pcr = psrM.next()
                        S.op("pe", lambda e, pc=pc, kk_=kk_, i=i, t=t, h=h: e.matmul(
                            pc[:, 0:258], kk_[:, i * 128:(i + 1) * 128], vt[:, t, h, :], start=True, stop=True),
                            reads=[kkr, vt_r[t][h]], writes=[pcr])
                        yield
                        S.op("dve", lambda e, h=h, i=i, g=g: e.tensor_scalar_mul(Cst[:, h, i, :], Cst[:, h, i, :], g[:, 8 + h:9 + h]),
                             reads=[C_r[h], gr], writes=[C_r[h]])
                        yield
                        S.op("dve", lambda e, pc=pc, h=h, i=i, g=g: e.scalar_tensor_tensor(
                            Cst[:, h, i, :], pc[:, 0:258], g[:, 8 + h:9 + h], Cst[:, h, i, :], ALU.mult, ALU.add),
                            reads=[pcr, gr, C_r[h]], writes=[C_r[h]])
                        yield
                        S.op("act", lambda e, h=h, i=i: e.copy(Cb[:, h, i, :], Cst[:, h, i, :]), reads=[C_r[h]], writes=[Cb_r[h]])
                        yield

                gens = [head_chain(t, h, tc_, g, gr) for h in range(1)]
                while gens:
                    for gg in list(gens):
                        try:
                            next(gg)
                            yield
                        except StopIteration:
                            gens.remove(gg)

        NCH = TB // 64
        rcar = sb([128, 8, 1]); rcar_r = [Res() for _ in range(8)]
        S.op("dve", lambda e: e.memset(rcar[:, :, :], 0.0), writes=rcar_r)
        psrM = Ring(psr.items[0:3]); psrR = Ring(psr.items[3:7])
        tAM = Ring([(x1T[:, i, :], Res()) for i in range(2)])
        rwork = Ring([(x1T[:, 2 + 2 * i:4 + 2 * i, :].rearrange("p a b -> p (a b)")[:, 0:1 + TB], Res()) for i in range(2)])
        wpjM = Ring([(x1T[:, 6 + i, :].bitcast(BF16).rearrange("p (k c) -> p k c", c=128), Res()) for i in range(2)])
        scr_r = [r for _, r in tAM.items + rwork.items + wpjM.items]
        lowT = sb([128, 2, TB], BF16); low_r = [Res(), Res()]
        rtmp = Ring([(aT[:, 2 * i:2 * i + 2, :].rearrange("p a b -> p (a b)").bitcast(F32), Res()) for i in range(10)])
        ARbd = sb([128, NCH, 256], BF16); Bbd = sb([128, NCH, 128], BF16); Kbd = sb([128, NCH, 128], BF16)
        Vbd = sb([128, NCH, 128], BF16); Ynbd = sb([128, 128], BF16)
        bd_r = Res(); ynbd_r = Res()
        for tns in (ARbd, Bbd, Kbd, Vbd):
            S.op("dve", lambda e, tns=tns: e.memset(tns[:, :, :], 0.0), writes=[bd_r])
        S.op("dve", lambda e: e.memset(Ynbd[:, :], 0.0), writes=[ynbd_r])
        gam = sb([128, NCH]); gam_r = Res()
        Hst = sb([128, 2, 64]); H_r = [Res() for _ in range(2)]
        Hb = sb([128, 2, 64], BF16); Hb_r = [Res() for _ in range(2)]
        S.op("dve", lambda e: e.memset(Hst[:, :, :], 0.0), writes=H_r)
        S.op("dve", lambda e: e.memset(Hb[:, :, :], 0.0), writes=Hb_r)
        vst = sb([128, NCH, 64], BF16); vst_r = Res()
        btk = sb([128, NCH, 256], BF16); btk_r = Res()
        nm = ring(4, [128, 256], BF16)
        nsq = ring(11, [128, 128], BF16)
        ub = ring(4, [128, 64], BF16)
        uf = ring(4, [128, 64])
        gn6 = ring(4, [128, 8])
        htmp = ring(2, [128, 64])
        maskAR = cst[:, C_MUS:C_MUS + 256]; mask_r = cst_r
        mls = cst[:, C_MLS:C_MLS + 128]

        xTv = xT[:, :, :].rearrange("p a b -> p (a b)").bitcast(BF16)
        mAK = xTv[:, 0:NCH * 512].rearrange("p (n c) -> p n c", c=512)
        Xs = xTv[:, 4096:4096 + NCH * 128].rearrange("p (n c) -> p n c", c=128)
        xTbv = xTb[:, :, :].rearrange("p a b -> p (a b)")
        PW = [[xTbv[:, (b * 8 + n) * 128:(b * 8 + n + 1) * 128] for n in range(NCH)] for b in range(2)]
        PWT = [[xTbv[:, 2048 + (b * 8 + n) * 128:2048 + (b * 8 + n + 1) * 128] for n in range(NCH)] for b in range(2)]
        hh = Ring([(xTv[:, 5120 + i * 512:5120 + (i + 1) * 512].bitcast(F32), Res()) for i in range(4)])
        hnb = Ring([(xTv[:, 7168 + i * 256:7168 + (i + 1) * 256], Res()) for i in range(4)])
        mA_r = [Res() for _ in range(NCH)]; mK_r = [Res() for _ in range(NCH)]; X_r = [Res() for _ in range(NCH)]
        PW_r = [[Res() for _ in range(NCH)] for _ in range(2)]; PWT_r = [[Res() for _ in range(NCH)] for _ in range(2)]

        def v3(ap):
            return ap.rearrange("p (n l) -> p n l", l=64)

        def shifted(ci, ps, pr):
            wk, wkr = rwork.next()
            S.op("act", lambda e, wk=wk: e.copy(wk[:, 0:1], rcar[:, ci, :]), reads=[rcar_r[ci]], writes=[wkr])
            S.op("act", lambda e, wk=wk, ps=ps: e.copy(wk[:, 1:1 + TB], ps[:, :]), reads=[pr], writes=[wkr])
            S.op("act", lambda e, wk=wk: e.copy(rcar[:, ci, :], wk[:, TB:TB + 1]), reads=[wkr], writes=[rcar_r[ci]])
            ta, tar = rtmp.next()
            S.op("dve", lambda e, wk=wk, ta=ta: e.tensor_scalar_mul(ta[:, :], wk[:, 1:1 + TB], pvc("omu", ci)),
                 reads=[wkr, pv_r], writes=[tar])
            S.op("dve", lambda e, wk=wk, ta=ta: e.scalar_tensor_tensor(
                ta[:, :], wk[:, 0:TB], pvc("mu", ci), ta[:, :], ALU.mult, ALU.add), reads=[wkr, pv_r, tar], writes=[tar])
            return ta, tar

        krw = int(os.environ.get("KRW", "9"))

        def rwkv(blk):
            ps, pr = projT(win, 1800, x1Tb, x1_r, pring=psrR)
            ta, tar = shifted(6, ps, pr)
            S.op("act", lambda e, ta=ta: e.activation(lowT[0:64, 0, :], ta[0:64, :], AF.Tanh), reads=[tar], writes=[low_r[0]])
            yield
            S.op("act", lambda e, ta=ta: e.copy(lowT[64:128, 0, :], ta[64:128, :]), reads=[tar], writes=[low_r[0]])
            yield
            ps, pr = projT(win, 1928, x1Tb, x1_r, pring=psrR)
            ta, tar = shifted(7, ps, pr)
            S.op("act", lambda e, ta=ta: e.activation(lowT[:, 1, :], ta[:, :], AF.Sigmoid), reads=[tar], writes=[low_r[1]])
            yield
            for p in range(2):
                cs = slice(p * 128, (p + 1) * 128)
                ps, pr = projT(win, 1032 + p * 128, x1Tb, x1_r, pring=psrR)
                r_, r_r = shifted(p, ps, pr)
                ps, pr = projT(win, 1288 + p * 128, x1Tb, x1_r, pring=psrR)
                k_, k_r = shifted(2 + p, ps, pr)
                ps, pr = projT(win, 1544 + p * 128, x1Tb, x1_r, pring=psrR)
                v_, v_r = shifted(4 + p, ps, pr)
                pw, pwr = psrR.next()
                S.op("pe", lambda e, pw=pw, cs=cs: e.matmul(pw[:, :], rw2a2[0:64, cs], lowT[0:64, 0, :], start=True, stop=True),
                     reads=[rw_r, low_r[0]], writes=[pwr])
                yield
                lw, lwr = rtmp.next()
                S.op("act", lambda e, lw=lw, pw=pw, p=p: e.activation(lw[:, :], pw[:, :], AF.Sigmoid, bias=pvc("w0", p)),
                     reads=[pwr, pv_r], writes=[lwr])
                yield
                S.op("dve", lambda e, lw=lw: e.tensor_scalar_mul(lw[:, :], lw[:, :], -float(np.exp(-0.5))), reads=[lwr], writes=[lwr])
                yield
                pa, par = psrR.next()
                S.op("pe", lambda e, pa=pa, cs=cs: e.matmul(pa[:, :], rw2a2[64:128, cs], lowT[64:128, 0, :], start=True, stop=True),
                     reads=[rw_r, low_r[0]], writes=[par])
                yield
                a_, a_r = rtmp.next()
                S.op("act", lambda e, a_=a_, pa=pa, p=p: e.activation(a_[:, :], pa[:, :], AF.Sigmoid, bias=pvc("a0", p)),
                     reads=[par, pv_r], writes=[a_r])
                yield
                pg, pgr = psrR.next()
                S.op("pe", lambda e, pg=pg, cs=cs: e.matmul(pg[:, :], rg2[:, cs], lowT[:, 1, :], start=True, stop=True),
                     reads=[rw_r, low_r[1]], writes=[pgr])
                yield
                g_, g_r = rtmp.next()
                S.op("act", lambda e, g_=g_, pg=pg: e.copy(g_[:, :], pg[:, :]), reads=[pgr], writes=[g_r])
                yield
                kk, kkr = rtmp.next()
                S.op("dve", lambda e, kk=kk, k_=k_, p=p: e.tensor_scalar_mul(kk[:, :], k_[:, :], pvc("kk", p)),
                     reads=[k_r, pv_r], writes=[kkr])
                yield
                sq, sqr = tB.next()
                S.op("act", lambda e, sq=sq, kk=kk: e.activation(sq[:, :], kk[:, :], AF.Square), reads=[kkr], writes=[sqr])
                yield
                pq, pqr = psrR.next()
                S.op("pe", lambda e, pq=pq, sq=sq: e.matmul(pq[:, :], bob, sq[:, :], start=True, stop=True),
                     reads=[cstb_r, sqr], writes=[pqr])
                yield
                t1, t1r = rtmp.next()
                S.op("act", lambda e, t1=t1, pq=pq: e.activation(t1[:, :], pq[:, :], AF.Sqrt), reads=[pqr], writes=[t1r])
                yield
                S.op("dve", lambda e, t1=t1: e.tensor_scalar_max(t1[:, :], t1[:, :], 1e-12), reads=[t1r], writes=[t1r])
                yield
                S.op("dve", lambda e, t1=t1: e.reciprocal(t1[:, :], t1[:, :]), reads=[t1r], writes=[t1r])
                yield
                S.op("dve", lambda e, t1=t1, kk=kk: e.tensor_tensor(kk[:, :], kk[:, :], t1[:, :], ALU.mult),
                     reads=[t1r, kkr], writes=[kkr])
                yield
                S.op("dve", lambda e, t1=t1, a_=a_, p=p: e.tensor_scalar(t1[:, :], a_[:, :], 1.0, pvc("ka", p), ALU.subtract, ALU.mult),
                     reads=[a_r, pv_r], writes=[t1r])
                yield
                S.op("dve", lambda e, t1=t1, k_=k_: e.scalar_tensor_tensor(k_[:, :], t1[:, :], 1.0, k_[:, :], ALU.add, ALU.mult),
                     reads=[t1r, k_r], writes=[k_r])
                yield
                t2, t2r = tB.next()
                S.op("dve", lambda e, t2=t2, r_=r_, k_=k_, p=p: e.scalar_tensor_tensor(
                    t2[:, :], r_[:, :], pvc("rrk", p), k_[:, :], ALU.mult, ALU.mult), reads=[r_r, k_r, pv_r], writes=[t2r])
                yield
                pb, pbr = psrR.next()
                S.op("pe", lambda e, pb=pb, t2=t2: e.matmul(pb[:, :], bob, t2[:, :], start=True, stop=True),
                     reads=[cstb_r, t2r], writes=[pbr])
                yield
                bon, bonr = rtmp.next()
                S.op("dve", lambda e, bon=bon, pb=pb, v_=v_: e.tensor_tensor(bon[:, :], pb[:, :], v_[:, :], ALU.mult),
                     reads=[pbr, v_r], writes=[bonr])
                yield
                cl, clr = rtmp.next()
                S.op("dve", lambda e, cl=cl, lw=lw: e.tensor_tensor_scan(cl[:, :], rst, lw[:, :], 0.0, ALU.mult, ALU.add),
                     reads=[lwr, cstb_r], writes=[clr])
                yield
                e1, e1r = tA.next()
                S.op("act", lambda e, e1=e1, cl=cl: e.activation(e1[:, :], cl[:, :], AF.Exp), reads=[clr], writes=[e1r])
                yield
                S.op("act", lambda e, e1=e1: e.copy(gam[:, :], v3(e1[:, :])[:, :, 63]), reads=[e1r], writes=[gam_r])
                yield
                for hf in range(2):
                    hs = slice(hf * 64, hf * 64 + 64)
                    S.op("dve", lambda e, hs=hs, hf=hf, r_=r_, e1=e1: e.tensor_tensor(
                        ARbd[hs, :, 128 + hf * 64:128 + hf * 64 + 64], v3(r_[hs, :]), v3(e1[hs, :]), ALU.mult),
                        reads=[r_r, e1r], writes=[bd_r])
                    yield
                e2, e2r = tA.next()
                S.op("act", lambda e, e2=e2, cl=cl: e.activation(e2[:, :], cl[:, :], AF.Exp, scale=-1.0), reads=[clr], writes=[e2r])
                yield
                S.op("dve", lambda e, t1=t1, kk=kk, a_=a_: e.tensor_tensor(t1[:, :], kk[:, :], a_[:, :], ALU.mult),
                     reads=[kkr, a_r], writes=[t1r])
                yield
                for hf in range(2):
                    hs = slice(hf * 64, hf * 64 + 64)
                    S.op("dve", lambda e, hs=hs, hf=hf, t1=t1, e2=e2: e.tensor_tensor(
                        Bbd[hs, :, hf * 64:hf * 64 + 64], v3(t1[hs, :]), v3(e2[hs, :]), ALU.mult), reads=[t1r, e2r], writes=[bd_r])
                    yield
                    S.op("dve", lambda e, hs=hs, hf=hf, k_=k_, e2=e2: e.tensor_tensor(
                        Kbd[hs, :, hf * 64:hf * 64 + 64], v3(k_[hs, :]), v3(e2[hs, :]), ALU.mult), reads=[k_r, e2r], writes=[bd_r])
                    yield
                    S.op("act", lambda e, hs=hs, hf=hf, v_=v_: e.copy(Vbd[hs, :, hf * 64:hf * 64 + 64], v3(v_[hs, :])),
                         reads=[v_r], writes=[bd_r])
                    yield
                S.op("dve", lambda e, cl=cl, lw=lw: e.tensor_tensor(cl[:, :], cl[:, :], lw[:, :], ALU.subtract),
                     reads=[clr, lwr], writes=[clr])
                yield
                e3, e3r = tA.next()
                S.op("act", lambda e, e3=e3, cl=cl: e.activation(e3[:, :], cl[:, :], AF.Exp), reads=[clr], writes=[e3r])
                yield
                for hf in range(2):
                    hs = slice(hf * 64, hf * 64 + 64)
                    S.op("dve", lambda e, hs=hs, hf=hf, kk=kk, e3=e3: e.scalar_tensor_tensor(
                        ARbd[hs, :, hf * 64:hf * 64 + 64], v3(kk[hs, :]), -1.0, v3(e3[hs, :]), ALU.mult, ALU.mult),
                        reads=[kkr, e3r], writes=[bd_r])
                    yield
                if krw <= 1:
                    S.op("dve", lambda e, p=p: e.memset(yrTb[:, p, :], 0.0), writes=[yr_r[p]])
                    yield
                    continue
                pv_, pvr_ = psrR.next()
                for n in range(NCH):
                    S.op("pe", lambda e, n=n, pv_=pv_: e.matmul(pv_[:, n * 64:(n + 1) * 64], Vbd[:, n, :], istb, start=True, stop=True),
                         reads=[bd_r, cstb_r], writes=[pvr_])
                    yield
                S.op("act", lambda e, pv_=pv_: e.copy(vst[:, :, :], v3(pv_[:, :])), reads=[pvr_], writes=[vst_r])
                yield
                for n0 in range(0, NCH, 2):
                    pt, ptr = psrR.next()
                    for n in (n0, n0 + 1):
                        o = (n - n0) * 256
                        S.op("pe", lambda e, n=n, pt=pt, o=o: e.matmul(pt[:, o:o + 128], Bbd[:, n, :], identb, start=True, stop=True),
                             reads=[bd_r, cstb_r], writes=[ptr])
                        yield
                        S.op("pe", lambda e, n=n, pt=pt, o=o: e.matmul(pt[:, o + 128:o + 256], Kbd[:, n, :], identb, start=True, stop=True),
                             reads=[bd_r, cstb_r], writes=[ptr])
                        yield
                    S.op("act", lambda e, n0=n0, pt=pt: e.copy(btk[:, n0:n0 + 2, :], pt[:, :].rearrange("p (n l) -> p n l", l=256)),
                         reads=[ptr], writes=[btk_r])
                    yield
                if krw <= 2:
                    S.op("dve", lambda e, p=p: e.memset(yrTb[:, p, :], 0.0), writes=[yr_r[p]])
                    yield
                    continue
                pyo, pyor = pyo_bank
                for g0 in range(0, NCH, 4):
                    G = list(range(g0, min(g0 + 4, NCH)))
                    pAs = {}
                    for n in G:
                        pA, pAr = psrR.next()
                        pAs[n] = (pA, pAr)
                        S.op("pe", lambda e, pA=pA, n=n: e.matmul(pA[:, 0:256], Bbd[:, n, :], ARbd[:, n, :], start=True, stop=True),
                             reads=[bd_r], writes=[pAr])
                        yield
                        S.op("pe", lambda e, pA=pA, n=n: e.matmul(pA[:, 256:512], Kbd[:, n, :], ARbd[:, n, :], start=True, stop=True),
                             reads=[bd_r], writes=[pAr])
                        yield
                    for n in G:
                        pA, pAr = pAs[n]
                        S.op("dve", lambda e, pA=pA, n=n: e.tensor_tensor(mAK[:, n, 0:256], pA[:, 0:256], maskAR[:, :], ALU.mult),
                             reads=[pAr, mask_r], writes=[mA_r[n]])
                        yield
                        S.op("dve", lambda e, pA=pA, n=n: e.tensor_tensor(mAK[:, n, 256:512], pA[:, 256:512], maskAR[:, :], ALU.mult),
                             reads=[pAr, mask_r], writes=[mK_r[n]])
                        yield
                    pTs = {}
                    for n in G:
                        pT, pTr = psrR.next()
                        pTs[n] = (pT, pTr)
                        S.op("pe", lambda e, pT=pT, n=n: e.matmul(pT[:, 0:128], ARbd[:, n, 0:128], Bbd[:, n, :], start=True, stop=True),
                             reads=[bd_r], writes=[pTr])
                        yield
                    for n in G:
                        pT, pTr = pTs[n]
                        S.op("dve", lambda e, pT=pT, n=n: e.tensor_tensor(PWT[0][n], pT[:, 0:128], mls, ALU.mult),
                             reads=[pTr, cst_r], writes=[PWT_r[0][n]])
                        yield
                        S.op("dve", lambda e, n=n: e.tensor_tensor(Xs[:, n, :], mAK[:, n, 0:128], identb, ALU.add),
                             reads=[mA_r[n], cstb_r], writes=[X_r[n]])
                        yield
                    for j in range(1, 6):
                        b0, b1 = (j - 1) % 2, j % 2
                        p2s = {}
                        for n in G:
                            cur = mAK[:, n, 0:128] if j == 1 else PW[b0][n]
                            curr = mA_r[n] if j == 1 else PW_r[b0][n]
                            ct, ctr_ = PWT[b0][n], PWT_r[b0][n]
                            p2, p2r = psrR.next()
                            p2s[n] = (p2, p2r)
                            if j < 5:
                                S.op("pe", lambda e, p2=p2, cur=cur, ct=ct: e.matmul(p2[:, 0:128], ct, cur, start=True, stop=True),
                                     reads=[curr, ctr_], writes=[p2r])
                                yield
                            S.op("pe", lambda e, p2=p2, cur=cur, ct=ct: e.matmul(p2[:, 128:256], cur, ct, start=True, stop=True),
                                 reads=[curr, ctr_], writes=[p2r])
                            yield
                        for n in G:
                            p2, p2r = p2s[n]
                            S.op("dve", lambda e, p2=p2, n=n, b1=b1: e.tensor_copy(PWT[b1][n], p2[:, 128:256]),
                                 reads=[p2r], writes=[PWT_r[b1][n]])
                            yield
                            if j < 5:
                                S.op("act", lambda e, p2=p2, n=n, b1=b1: e.copy(PW[b1][n], p2[:, 0:128]),
                                     reads=[p2r], writes=[PW_r[b1][n]])
                                yield
                        pxs = {}
                        for n in G:
                            px, pxr = psrR.next()
                            pxs[n] = (px, pxr)
                            S.op("pe", lambda e, px=px, n=n, b1=b1: e.matmul(px[:, 0:128], PWT[b1][n], Xs[:, n, :], start=True, stop=True),
                                 reads=[PWT_r[b1][n], X_r[n]], writes=[pxr])
                            yield
                        for n in G:
                            px, pxr = pxs[n]
                            S.op("dve", lambda e, px=px, n=n: e.tensor_tensor(Xs[:, n, :], px[:, 0:128], Xs[:, n, :], ALU.add),
                                 reads=[pxr, X_r[n]], writes=[X_r[n]])
                            yield
                for n in range(NCH):
                    pw_, pw_r = psrR.next()
                    S.op("pe", lambda e, pw_=pw_, n=n, p=p: e.matmul(pw_[:, 0:64], ARbd[:, n, 0:128], Hb[:, p, :], start=True, stop=False),
                         reads=[bd_r, Hb_r[p]], writes=[pw_r])
                    yield
                    S.op("pe", lambda e, pw_=pw_, n=n: e.matmul(pw_[:, 0:64], mAK[:, n, 256:384], vst[:, n, :], start=False, stop=True),
                         reads=[mK_r[n], vst_r], writes=[pw_r])
                    yield
                    w_b, wbr = ub.next()
                    S.op("act", lambda e, w_b=w_b, pw_=pw_: e.copy(w_b[:, :], pw_[:, 0:64]), reads=[pw_r], writes=[wbr])
                    yield
                    pu, pur_ = psrR.next()
                    S.op("pe", lambda e, pu=pu, n=n, w_b=w_b: e.matmul(pu[:, 0:64], Xs[:, n, :], w_b[:, :], start=True, stop=True),
                         reads=[X_r[n], wbr], writes=[pur_])
                    yield
                    u_b, ubr = ub.next()
                    S.op("dve", lambda e, u_b=u_b, pu=pu: e.tensor_copy(u_b[:, :], pu[:, 0:64]), reads=[pur_], writes=[ubr])
                    yield
                    ph, phr = psrR.next()
                    S.op("pe", lambda e, ph=ph, n=n, u_b=u_b: e.matmul(ph[:, 0:64], btk[:, n, 0:128], u_b[:, :], start=True, stop=False),
                         reads=[btk_r, ubr], writes=[phr])
                    yield
                    S.op("pe", lambda e, ph=ph, n=n: e.matmul(ph[:, 0:64], btk[:, n, 128:256], vst[:, n, :], start=False, stop=True),
                         reads=[btk_r, vst_r], writes=[phr])
                    yield
                    py, pyr = psrR.next()
                    S.op("pe", lambda e, py=py, n=n, p=p: e.matmul(py[:, 0:64], ARbd[:, n, 128:256], Hb[:, p, :], start=True, stop=False),
                         reads=[bd_r, Hb_r[p]], writes=[pyr])
                    yield
                    S.op("pe", lambda e, py=py, n=n, u_b=u_b: e.matmul(py[:, 0:64], mAK[:, n, 128:256], u_b[:, :], start=False, stop=False),
                         reads=[mA_r[n], ubr], writes=[pyr])
                    yield
                    S.op("pe", lambda e, py=py, n=n: e.matmul(py[:, 0:64], mAK[:, n, 384:512], vst[:, n, :], start=False, stop=True),
                         reads=[mK_r[n], vst_r], writes=[pyr])
                    yield
                    ht, htr = htmp.next()
                    S.op("dve", lambda e, ht=ht, ph=ph, p=p: e.tensor_tensor(ht[:, :], ph[:, 0:64], Hst[:, p, :], ALU.add),
                         reads=[phr, H_r[p]], writes=[htr])
                    yield
                    S.op("act", lambda e, ht=ht, p=p, n=n: e.activation(Hb[:, p, :], ht[:, :], AF.Identity, scale=gam[:, n:n + 1]),
                         reads=[htr, gam_r], writes=[Hb_r[p]])
                    yield
                    S.op("act", lambda e, ht=ht, p=p, n=n: e.activation(Hst[:, p, :], ht[:, :], AF.Identity, scale=gam[:, n:n + 1]),
                         reads=[htr, gam_r], writes=[H_r[p]])
                    yield
                    g6, g6r = gn6.next()
                    S.op("dve", lambda e, g6=g6, py=py: e.bn_stats(g6[:, 2:8], py[:, 0:64]), reads=[pyr], writes=[g6r])
                    yield
                    S.op("dve", lambda e, g6=g6: e.bn_aggr(g6[:, 0:2], g6[:, 2:8]), reads=[g6r], writes=[g6r])
                    yield
                    S.op("act", lambda e, g6=g6: e.activation(g6[:, 2:3], g6[:, 1:2], AF.Sqrt, bias=epsgn[:, 0:1]),
                         reads=[g6r, eps_r], writes=[g6r])
                    yield
                    S.op("dve", lambda e, g6=g6: e.reciprocal(g6[:, 3:4], g6[:, 2:3]), reads=[g6r], writes=[g6r])
                    yield
                    for hf in range(2):
                        hs = slice(hf * 64, hf * 64 + 64)
                        S.op("dve", lambda e, hs=hs, hf=hf, py=py, g6=g6: e.tensor_scalar(
                            Ynbd[hs, hf * 64:hf * 64 + 64], py[hs, 0:64], g6[hs, 0:1], g6[hs, 3:4], ALU.subtract, ALU.mult),
                            reads=[pyr, g6r], writes=[ynbd_r])
                        yield
                    S.op("pe", lambda e, n=n, pyo=pyo: e.matmul(pyo[:, n * 64:(n + 1) * 64], Ynbd[:, :], istb, start=True, stop=True),
                         reads=[ynbd_r, cstb_r], writes=[pyor])
                    yield
                if krw <= 5:
                    S.op("dve", lambda e, p=p: e.memset(yrTb[:, p, :], 0.0), writes=[yr_r[p]])
                    yield
                    continue
                S.op("dve", lambda e, t1=t1, pyo=pyo, p=p: e.tensor_scalar(
                    t1[:, :], pyo[:, :], pvc("gng", p), pvc("gnb", p), ALU.mult, ALU.add), reads=[pyor, pv_r], writes=[t1r])
                yield
                S.op("dve", lambda e, t1=t1, bon=bon: e.tensor_tensor(t1[:, :], t1[:, :], bon[:, :], ALU.add),
                     reads=[t1r, bonr], writes=[t1r])
                yield
                S.op("dve", lambda e, t1=t1, g_=g_, p=p: e.tensor_tensor(yrTb[:, p, :], t1[:, :], g_[:, :], ALU.mult),
                     reads=[t1r, g_r], writes=[yr_r[p]])
                yield

        def merge_out(blk):
            for c in range(8):
                ps, pr = projT(wgate, c * 128, x1Tb, x1_r)
                S.op("act", lambda e, ps=ps, c=c: e.activation(sga[:, c, :], ps[:, :], AF.Sigmoid), reads=[pr], writes=[sga_r[c]])
                ps, pr = projT(wgate, 1024 + c * 128, x1Tb, x1_r)
                S.op("act", lambda e, ps=ps, c=c: e.activation(sgb[:, c, :], ps[:, :], AF.Sigmoid), reads=[pr], writes=[sgb_r[c]])
            for c in range(8):
                psa, par = projT(wa_d, c * 128, lambda kc: HYf[:, kc // 2, kc % 2, :], [hyf_r[kc // 2] for kc in range(8)])
                psb, pbr = projT(wb_d, c * 128, lambda kc: HYf[:, kc // 2, 2 + kc % 2, :], [hyf_r[kc // 2] for kc in range(8)])
                ta, tar = tA.next()
                S.op("dve", lambda e, ta=ta, psa=psa, c=c: e.tensor_tensor(ta[:, :], psa[:, :], sga[:, c, :], ALU.mult),
                     reads=[par, sga_r[c]], writes=[tar])
                tb, tbr = tA.next()
                S.op("dve", lambda e, tb=tb, psb=psb, c=c: e.tensor_tensor(tb[:, :], psb[:, :], sgb[:, c, :], ALU.mult),
                     reads=[pbr, sgb_r[c]], writes=[tbr])
                S.op("dve", lambda e, ta=ta, tb=tb, c=c: e.tensor_tensor(mgTb[:, c, :], ta[:, :], tb[:, :], ALU.add),
                     reads=[tar, tbr], writes=[mg_r[c]])
            for c in range(8):
                ps, pr = projT(wo_d, c * 128, mgTb, mg_r)
                S.op("dve", lambda e, ps=ps, c=c: e.scalar_tensor_tensor(
                    zT[:, c, :], x1T[:, c, :], ALPHA, ps[:, :], ALU.mult, ALU.add), reads=[pr, x1_r[c]], writes=[zT_r[c]])
            layer_norm("ln2_g", "ln2_b", epsln, x2T, x2Tb, x2_r)

        x1f_r = Res()
        x1bl_r = [[Res(), Res()] for _ in range(nbl)]; x1ba_r = [[Res(), Res()] for _ in range(nbl)]
        hyl_r = [Res() for _ in range(NBT)]; hya_r = [Res() for _ in range(NBT)]
        oh = sb([128, 4]); oh_r = Res()
        S.dma("sp", oh[:, :], oh_d[:, :], writes=[oh_r])

        def half(t, h):
            return t[:, 4 * h:4 * h + 4, :].rearrange("p a b -> p (a b)")

        for blk in range(nbl):
            load_xT(blk)
            ffn_ln(xT, xTb, xT_r, w1g, w1u, w1d, "ln1_g", "ln1_b", x1T, x1Tb, x1_r)
            rows = slice(blk * 128, (blk + 1) * 128)
            S.dma("sp", x1f_loc[rows, :], x1T[:, :, :].rearrange("p a b -> p (a b)"), reads=x1_r, writes=[x1f_r])
            for h in range(2):
                S.dma("sp", x1b_loc[blk][h][:, :], half(x1Tb, h), reads=x1_r, writes=[x1bl_r[blk][h]])
                S.cc("AllGather", x1b_loc[blk][h][:, :], x1b_all[blk][h][:, :], groups,
                     reads=[x1bl_r[blk][h]], writes=[x1ba_r[blk][h]])
        for gb in range(NBT):
            i, blk = gb // nbl, gb % nbl
            for h in range(2):
                S.dma("sp", half(x1Tb, h), x1b_all[blk][h][i * 128:(i + 1) * 128, :], reads=[x1ba_r[blk][h]],
                      writes=x1_r[4 * h:4 * h + 4])
            if gb == 0:
                S.op("dve", lambda e: e.memset(x1T[:, 0, 0:8], 0.0), writes=x1_r + scr_r)
            gens = [mlstm(gb), rwkv(gb)]
            while gens:
                for gg in list(gens):
                    try:
                        next(gg)
                    except StopIteration:
                        gens.remove(gg)
            S.dma("sp", hy_loc[gb][:, :], hyT[:, :, :].rearrange("p a b -> p (a b)"), reads=hm_r + yr_r, writes=[hyl_r[gb]])
            S.cc("AllGather", hy_loc[gb][:, :], hy_all[gb][:, :], groups, reads=[hyl_r[gb]], writes=[hya_r[gb]])
        for blk in range(nbl):
            rows = slice(blk * 128, (blk + 1) * 128)
            S.dma("sp", x1T[:, :, :].rearrange("p a b -> p (a b)"), x1f_loc[rows, :], reads=[x1f_r], writes=x1_r + scr_r)
            for h in range(2):
                S.dma("sp", half(x1Tb, h), x1b_loc[blk][h][:, :], reads=[x1bl_r[blk][h]], writes=x1_r[4 * h:4 * h + 4])
            for i in range(4):
                dst = HYf[:, i, :, :].rearrange("p a b -> p (a b)")
                for j in range(4):
                    k = j * nbl + blk
                    stg = sgb[:, 4 * (j % 2):4 * (j % 2) + 4, :].rearrange("p a b -> p (a b)")
                    stg_r = sgb_r[4 * (j % 2):4 * (j % 2) + 4]
                    S.dma("sp", stg, hy_all[k][i * 128:(i + 1) * 128, :], reads=[hya_r[k]], writes=stg_r)
                    if j == 0:
                        S.op("dve", lambda e, dst=dst, stg=stg, j=j: e.tensor_scalar_mul(dst, stg, oh[:, j:j + 1]),
                             reads=stg_r + [oh_r], writes=[hyf_r[i]])
                    else:
                        S.op("dve", lambda e, dst=dst, stg=stg, j=j: e.scalar_tensor_tensor(
                            dst, stg, oh[:, j:j + 1], dst, ALU.mult, ALU.add),
                            reads=stg_r + [oh_r, hyf_r[i]], writes=[hyf_r[i]])
            merge_out(blk)
            ffn_ln(x2T, x2Tb, x2_r, w2g, w2u, w2d, "ln3_g", "ln3_b", x3T, x3Tb, x3_r)
            store_T(blk, x3T, x3_r)

        S.op("sp", lambda e: e.nop(), reads=[out_r])
        S.emit()
    return nc


def _consts():
    c = np.zeros((128, NCST), np.float32)
    i = np.arange(128)
    c[:, C_ID:C_ID + 128] = np.eye(128)
    c[:, C_OD:C_OD + 128] = 1.0 / 1024.0
    c[:, C_TRI:C_TRI + 128] = (i[:, None] <= i[None, :])
    c[:, C_ONE:C_ONE + 128] = 1.0
    c[:, C_MUS:C_MUS + 128] = (i[:, None] < i[None, :])
    c[:, C_MLS:C_MLS + 128] = (i[:, None] > i[None, :])
    c[:, C_IST:C_IST + 64] = np.concatenate([np.eye(64), np.eye(64)], axis=0)
    c[:, C_BO:C_BO + 128] = ((i[:, None] // 64) == (i[None, :] // 64))
    return c


def _fm(v):
    v = np.asarray(v, np.float32).reshape(-1, 128)
    return np.ascontiguousarray(v.T)


def kernel(**inp):
    nbl = int(os.environ.get("KNBL", "4"))
    ngrp = int(os.environ.get("KNGRP", "2"))
    x = np.asarray(inp["x"], np.float32)
    W = np.asarray(inp["w_in"][0], np.float32)
    gates = np.ascontiguousarray(W[:, 7432:9480])
    cw = inp["m_conv_w"][0]; cbv = inp["m_conv_b"][0]
    mu = inp["r_mu"][0]
    in_maps = []
    for c in range(4 * ngrp):
        b, r = c // 4, c % 4
        pv = np.zeros((128, NPV), np.float32)

        def put(name, arr):
            a = _fm(arr)
            pv[:, PV[name]:PV[name] + a.shape[1]] = a

        for n in ("ln1_g", "ln1_b", "ln2_g", "ln2_b", "ln3_g", "ln3_b"):
            put(n, inp[n][0])
        qs = slice(r * 256, (r + 1) * 256)
        ks = slice(1024 + r * 256, 1024 + (r + 1) * 256)
        for j in range(4):
            put("cw%d" % j, np.concatenate([cw[j][qs], cw[j][ks]]))
        put("cb", np.concatenate([cbv[qs], cbv[ks]]))
        put("mng", inp["m_norm_g"][0][qs])
        ps_ = slice(r * 256, (r + 1) * 256)
        mul = np.concatenate([mu[0:1024][ps_], mu[1024:2048][ps_], mu[2048:3072][ps_], mu[3072:3328]])
        put("mu", mul)
        put("omu", 1.0 - mul)
        put("w0", inp["r_w0"][0][ps_]); put("a0", inp["r_a0"][0][ps_]); put("kk", inp["r_k_k"][0][ps_])
        put("ka", inp["r_k_a"][0][ps_]); put("rrk", inp["r_r_k"][0].reshape(-1)[ps_])
        put("gng", inp["r_gn_g"][0][ps_]); put("gnb", inp["r_gn_b"][0][ps_])
        wl = np.concatenate([W[:, qs], W[:, ks], W[:, 2048 + r * 256:2048 + (r + 1) * 256],
                             W[:, 3072 + r * 256:3072 + (r + 1) * 256], np.zeros((D, 8), np.float32),
                             W[:, RC0 + r * 256:RC0 + (r + 1) * 256],
                             W[:, RC0 + 1024 + r * 256:RC0 + 1024 + (r + 1) * 256],
                             W[:, RC0 + 2048 + r * 256:RC0 + 2048 + (r + 1) * 256],
                             W[:, RC0 + 3072:RC0 + 3328]], axis=1)
        assert wl.shape[1] == 2056
        wif = np.zeros((D, 8), np.float32)
        wif[:, 0] = W[:, 4096 + r]
        wif[:, 4] = W[:, 4100 + r]
        gbias = np.zeros((128, 8), np.float32)
        gbias[:, 0] = inp["m_i_bias"][0][r]
        gbias[:, 4] = inp["m_f_bias"][0][r]
        gbias[:, 5:8] = 30.0
        m = {"cst": _consts(), "pv": pv, "gbias": gbias,
             "rst": np.ascontiguousarray(np.broadcast_to((np.arange(512)[None, :] % 64 != 0), (128, 512)).astype(np.float32)),
             "w_loc": np.ascontiguousarray(wl), "w_if": wif, "w_gate": gates,
             "r_w2": np.ascontiguousarray(inp["r_w2"][0][:, ps_]), "r_a2": np.ascontiguousarray(inp["r_a2"][0][:, ps_]),
             "r_g2": np.ascontiguousarray(inp["r_g2"][0][:, ps_]),
             "x": np.ascontiguousarray(x[b, r * nbl * TB:(r + 1) * nbl * TB]),
             "oh": np.ascontiguousarray(np.broadcast_to(np.eye(4, dtype=np.float32)[r][None, :], (128, 4)))}
        for n in ("ffn1_w_gate", "ffn1_w_up", "ffn1_w_down", "ffn2_w_gate", "ffn2_w_up", "ffn2_w_down",
                  "w_branch_a", "w_branch_b", "w_out"):
            m[n] = np.ascontiguousarray(inp[n][0], dtype=np.float32)
        in_maps.append(m)
    groups = [list(range(4 * g, 4 * g + 4)) for g in range(ngrp)]
    nc = build(nbl, groups)
    res = run_bass_kernel_spmd(nc, in_maps, core_ids=list(range(4 * ngrp)))
    out = np.zeros((ngrp, 4 * nbl * TB, D), np.float32)
    for c in range(4 * ngrp):
        out[c // 4, (c % 4) * nbl * TB:(c % 4 + 1) * nbl * TB] = np.asarray(res.results[c]["out"])
    return out
```

```python
import contextlib
import os
import numpy as np
import concourse.bass as bass
import concourse.mybir as mybir
from concourse.bass_utils import run_bass_kernel_spmd

F32 = mybir.dt.float32
BF16 = mybir.dt.bfloat16
AF = mybir.ActivationFunctionType
ALU = mybir.AluOpType

D = 1024
DFF = 2816
NJ = DFF // 128
SEQ = 8192
TB = 512
ALPHA = 2.0 ** 0.25
LN_EPS = 1e-5
GN_EPS = 64e-5
NCORES = 8
WCOLS = 9480
RC0 = 4104
SAME_ENGINE_WAITS = os.environ.get("KSEW", "act,dve,pool,sp").split(",")


class Res:
    __slots__ = ("w", "r", "excl")

    def __init__(self, excl=False):
        self.w = None
        self.r = {}
        self.excl = excl


class Sched:
    NDS = 24

    def __init__(self, nc, stack):
        self.nc = nc
        self.E = {"pe": nc.tensor, "act": nc.scalar, "dve": nc.vector, "pool": nc.gpsimd, "sp": nc.sync}
        self.q = {k: [] for k in self.E}
        self.cnt = {k: 0 for k in self.E}
        self.esem = {k: stack.enter_context(nc.semaphore("s_" + k)) for k in self.E}
        self.dsem = [stack.enter_context(nc.semaphore("d%d" % i)) for i in range(self.NDS)]
        self.dval = [0] * self.NDS
        self.dnext = {"sp": 0, "pool": 0}
        self.dpool = {"sp": list(range(0, 8)), "pool": list(range(8, self.NDS))}
        self.waited = {k: {} for k in self.E}
        self.csem = stack.enter_context(nc.semaphore("cc"))
        self.cval = 0

    def _deps(self, eng, reads, writes, extra=()):
        deps = {}

        def add(t):
            if t is None:
                return
            k = (t[0], t[1])
            if deps.get(k, 0) < t[2]:
                deps[k] = t[2]

        for r in reads:
            add(r.w)
        for w in writes:
            add(w.w)
            for k, v in w.r.items():
                add((k[0], k[1], v))
        for t in extra:
            add(t)
        waits = []
        for k, v in deps.items():
            if k[0] == "e" and k[1] == eng and (eng == "pe" or eng not in SAME_ENGINE_WAITS):
                continue
            if self.waited[eng].get(k, 0) >= v:
                continue
            self.waited[eng][k] = v
            waits.append((k, v))
        return waits

    def _mark(self, tok, reads, writes):
        k = (tok[0], tok[1])
        for r in reads:
            if r.r.get(k, 0) < tok[2]:
                r.r[k] = tok[2]
        for w in writes:
            w.w = tok
            w.r = {}

    def op(self, eng, fn, reads=(), writes=()):
        ex = [r for r in reads if r.excl]
        if ex:
            writes = list(writes) + ex
        waits = self._deps(eng, reads, writes)
        self.cnt[eng] += 1
        tok = ("e", eng, self.cnt[eng])
        self.q[eng].append((waits, fn, None))
        self._mark(tok, reads, writes)
        return tok

    def dma(self, qeng, out, in_, reads=(), writes=()):
        pool = self.dpool[qeng]
        j = pool[self.dnext[qeng]]
        self.dnext[qeng] = (self.dnext[qeng] + 1) % len(pool)
        prev = ("d", j, self.dval[j]) if self.dval[j] else None
        extra = [prev] if prev else []
        waits = self._deps(qeng, reads, writes, extra=extra)
        self.dval[j] += 16
        tok = ("d", j, self.dval[j])
        self.q[qeng].append((waits, (out, in_), j))
        self._mark(tok, reads, writes)
        return tok

    def cc(self, kind, ins, outs, groups, reads=(), writes=()):
        waits = self._deps("pool", reads, writes)
        self.cval += 1
        tok = ("c", 0, self.cval)
        self.q["pool"].append((waits, (kind, ins, outs, groups), "cc"))
        self._mark(tok, reads, writes)
        return tok

    def emit(self):
        nc = self.nc
        with nc.Block() as block:
            def run(kind):
                def body(eng):
                    for waits, fn, dj in self.q[kind]:
                        for k, v in waits:
                            sem = self.esem[k[1]] if k[0] == "e" else (self.csem if k[0] == "c" else self.dsem[k[1]])
                            eng.wait_ge(sem, v)
                        if dj is None:
                            fn(eng).then_inc(self.esem[kind], 1)
                        elif dj == "cc":
                            eng.collective_compute(fn[0], ALU.bypass, replica_groups=fn[3], ins=[fn[1]], outs=[fn[2]]).then_inc(self.csem, 1)
                        else:
                            eng.dma_start(out=fn[0], in_=fn[1]).then_inc(self.dsem[dj], 16)
                return body
            block.tensor(run("pe"))
            block.scalar(run("act"))
            block.vector(run("dve"))
            block.gpsimd(run("pool"))
            block.sync(run("sp"))


class Ring:
    def __init__(self, items):
        self.items = items
        self.i = 0

    def next(self):
        it = self.items[self.i]
        self.i = (self.i + 1) % len(self.items)
        return it


PV = {}
_o = 0
for _n, _w in [("ln1_g", 8), ("ln1_b", 8), ("ln2_g", 8), ("ln2_b", 8), ("ln3_g", 8), ("ln3_b", 8),
               ("cw0", 4), ("cw1", 4), ("cw2", 4), ("cw3", 4), ("cb", 4), ("mng", 2),
               ("mu", 8), ("omu", 8), ("w0", 2), ("a0", 2), ("kk", 2), ("ka", 2), ("rrk", 2),
               ("gng", 2), ("gnb", 2)]:
    PV[_n] = _o
    _o += _w
NPV = _o
C_ID, C_OD, C_MUS, C_TRI, C_ONE, C_MLS, C_IST, C_BO, NCST = 0, 128, 256, 384, 512, 640, 768, 832, 960


def build(nbl, groups):
    nc = bass.Bass("TRN2", target_bir_lowering=False)

    def din(name, shape):
        return nc.dram_tensor(name, list(shape), F32, kind="ExternalInput").ap()

    NBT = 4 * nbl
    x_d = din("x", [nbl * TB, D])
    cst_d = din("cst", [128, NCST])
    pv_d = din("pv", [128, NPV])
    gb_d = din("gbias", [128, 8])
    rst_d = din("rst", [128, 512])
    w1g = din("ffn1_w_gate", [D, DFF]); w1u = din("ffn1_w_up", [D, DFF]); w1d = din("ffn1_w_down", [DFF, D])
    w2g = din("ffn2_w_gate", [D, DFF]); w2u = din("ffn2_w_up", [D, DFF]); w2d = din("ffn2_w_down", [DFF, D])
    win = din("w_loc", [D, 2056])
    wif_d = din("w_if", [D, 8])
    wgate = din("w_gate", [D, 2048])
    oh_d = din("oh", [128, 4])
    wa_d = din("w_branch_a", [D, D]); wb_d = din("w_branch_b", [D, D]); wo_d = din("w_out", [D, D])
    rw2_d = din("r_w2", [64, 256]); ra2_d = din("r_a2", [64, 256]); rg2_d = din("r_g2", [128, 256])
    out_d = nc.dram_tensor("out", [nbl * TB, D], F32, kind="ExternalOutput").ap()
    x1f_loc = nc.dram_tensor("x1f_loc", [nbl * 128, 8 * TB], F32).ap()
    x1b_loc = [[nc.dram_tensor("x1bl_%d_%d" % (k, h), [128, 4 * TB], BF16).ap() for h in range(2)] for k in range(nbl)]
    x1b_all = [[nc.dram_tensor("x1ba_%d_%d" % (k, h), [4 * 128, 4 * TB], BF16).ap() for h in range(2)] for k in range(nbl)]
    hy_loc = [nc.dram_tensor("hyl_%d" % k, [128, 4 * TB], BF16).ap() for k in range(NBT)]
    hy_all = [nc.dram_tensor("hya_%d" % k, [4 * 128, 4 * TB], BF16).ap() for k in range(NBT)]

    with contextlib.ExitStack() as st:
        S = Sched(nc, st)
        _n = [0]

        def sb(shape, dt=F32):
            _n[0] += 1
            return st.enter_context(nc.sbuf_tensor("sb%d" % _n[0], list(shape), dt))

        def ring(n, shape, dt=F32):
            return Ring([(sb(shape, dt), Res()) for _ in range(n)])

        banks = [st.enter_context(nc.psum_tensor("ps%d" % i, [128, 512], F32)) for i in range(8)]
        psr = Ring([(banks[i], Res(True)) for i in range(7)])
        pyo_bank = (banks[7], Res(True))

        cst = sb([128, NCST]); cst_r = Res()
        cstb = sb([128, NCST], BF16); cstb_r = Res()
        pv = sb([128, NPV]); pv_r = Res()
        gbias = sb([128, 8]); gb_r = Res()
        S.dma("sp", cst[:, :], cst_d[:, :], writes=[cst_r])
        S.dma("sp", pv[:, :], pv_d[:, :], writes=[pv_r])
        S.dma("sp", gbias[:, :], gb_d[:, :], writes=[gb_r])
        S.op("act", lambda e: e.copy(cstb[:, :], cst[:, :]), reads=[cst_r], writes=[cstb_r])
        ident = cst[:, C_ID:C_ID + 128]
        identb = cstb[:, C_ID:C_ID + 128]
        onesdb = cstb[:, C_OD:C_OD + 128]
        tri = cst[:, C_TRI:C_TRI + 128]
        ones = cst[:, C_ONE:C_ONE + 128]
        istb = cstb[:, C_IST:C_IST + 64]
        bob = cstb[:, C_BO:C_BO + 128]
        rstb = sb([128, 512], BF16)
        S.dma("pool", rstb[:, :], rst_d[:, :], writes=[cstb_r])
        rst = rstb[:, :]
        epsln = sb([128, 1]); eps4 = sb([128, 1]); epsgn = sb([128, 1]); eps_r = Res()
        S.op("dve", lambda e: e.memset(epsln[:, :], LN_EPS), writes=[eps_r])
        S.op("dve", lambda e: e.memset(eps4[:, :], 4 * LN_EPS), writes=[eps_r])
        S.op("dve", lambda e: e.memset(epsgn[:, :], GN_EPS), writes=[eps_r])

        def pvc(name, i=0):
            c = PV[name] + i
            return pv[:, c:c + 1]

        rw2a2 = sb([128, 256], BF16); rw_r = Res()
        rg2 = sb([128, 256], BF16)
        S.dma("pool", rw2a2[0:64, :], rw2_d[:, :], writes=[rw_r])
        S.dma("pool", rw2a2[64:128, :], ra2_d[:, :], writes=[rw_r])
        S.dma("pool", rg2[:, :], rg2_d[:, :], writes=[rw_r])

        xT = sb([128, 8, TB]); xTb = sb([128, 8, TB], BF16); xT_r = [Res() for _ in range(8)]
        x1T = sb([128, 8, TB]); x1Tb = sb([128, 8, TB], BF16); x1_r = [Res() for _ in range(8)]
        x2T = xT; x2Tb = xTb; x2_r = xT_r
        x3T = xT; x3Tb = xTb; x3_r = xT_r
        aT = sb([128, NJ, TB], BF16); aT_r = [Res() for _ in range(NJ)]
        zT = sb([128, 8, TB]); zT_r = [Res() for _ in range(8)]
        xtok = ring(1, [128, D])
        otok = xtok
        wgu = ring(2, [128, 2, 8, 256], BF16)
        wdb = ring(4, [128, 512], BF16)
        wpj = ring(2, [128, 8, 128], BF16)
        tA = ring(3, [128, TB])
        tB = ring(2, [128, TB], BF16)
        mean_r = Res(); rstd_r = Res()
        out_r = Res()

        def load_xT(blk):
            for t in range(TB // 128):
                xt, xr = xtok.next()
                r0 = blk * TB + t * 128
                S.dma("sp", xt[:, :], x_d[r0:r0 + 128, :], writes=[xr])
                for half in range(2):
                    ps, pr = psr.next()
                    for q in range(4):
                        kc = half * 4 + q
                        S.op("pe", lambda e, ps=ps, xt=xt, kc=kc, q=q: e.matmul(
                            ps[:, q * 128:(q + 1) * 128], xt[:, kc * 128:(kc + 1) * 128], ident,
                            start=True, stop=True), reads=[xr, cst_r], writes=[pr])
                    psv = ps[:, :].rearrange("p (q n) -> p q n", q=4)
                    S.op("act", lambda e, psv=psv, half=half, t=t: e.copy(
                        xT[:, half * 4:half * 4 + 4, t * 128:(t + 1) * 128], psv),
                        reads=[pr], writes=xT_r[half * 4:half * 4 + 4])
                    S.op("dve", lambda e, psv=psv, half=half, t=t: e.tensor_copy(
                        xTb[:, half * 4:half * 4 + 4, t * 128:(t + 1) * 128], psv),
                        reads=[pr], writes=xT_r[half * 4:half * 4 + 4])

        def layer_norm(gname, bname, eps_t, outT, outTb, out_rs):
            psm, pmr = psr.next()
            pss, ssr = psr.next()
            for dc in range(8):
                tb, tr = tB.next()
                S.op("act", lambda e, tb=tb, dc=dc: e.copy(tb[:, :], zT[:, dc, :]), reads=[zT_r[dc]], writes=[tr])
                S.op("pe", lambda e, tb=tb, dc=dc: e.matmul(psm[:, :], onesdb, tb[:, :], start=(dc == 0), stop=(dc == 7)),
                     reads=[cstb_r, tr], writes=[pmr])
                tb2, tr2 = tB.next()
                S.op("act", lambda e, tb2=tb2, dc=dc: e.activation(tb2[:, :], zT[:, dc, :], AF.Square),
                     reads=[zT_r[dc]], writes=[tr2])
                S.op("pe", lambda e, tb2=tb2, dc=dc: e.matmul(pss[:, :], onesdb, tb2[:, :], start=(dc == 0), stop=(dc == 7)),
                     reads=[cstb_r, tr2], writes=[ssr])
            S.op("act", lambda e: e.copy(mean_sb[:, :], psm[:, :]), reads=[pmr], writes=[mean_r])
            t1, r1 = tA.next()
            S.op("dve", lambda e, t1=t1: e.tensor_tensor(t1[:, :], mean_sb[:, :], mean_sb[:, :], ALU.mult),
                 reads=[mean_r], writes=[r1])
            t2, r2 = tA.next()
            S.op("dve", lambda e, t1=t1, t2=t2: e.tensor_tensor(t2[:, :], pss[:, :], t1[:, :], ALU.subtract),
                 reads=[ssr, r1], writes=[r2])
            S.op("dve", lambda e, t2=t2: e.tensor_scalar_max(t2[:, :], t2[:, :], 0.0), reads=[r2], writes=[r2])
            t3, r3 = tA.next()
            S.op("act", lambda e, t2=t2, t3=t3: e.activation(t3[:, :], t2[:, :], AF.Sqrt, bias=eps_t[:, 0:1]),
                 reads=[r2, eps_r], writes=[r3])
            S.op("dve", lambda e, t3=t3: e.reciprocal(rstd_sb[:, :], t3[:, :]), reads=[r3], writes=[rstd_r])
            for dc in range(8):
                ta, tar = tA.next()
                S.op("dve", lambda e, ta=ta, dc=dc: e.tensor_tensor(ta[:, :], zT[:, dc, :], mean_sb[:, :], ALU.subtract),
                     reads=[zT_r[dc], mean_r], writes=[tar])
                S.op("dve", lambda e, ta=ta: e.tensor_tensor(ta[:, :], ta[:, :], rstd_sb[:, :], ALU.mult),
                     reads=[tar, rstd_r], writes=[tar])
                S.op("act", lambda e, ta=ta, dc=dc: e.activation(
                    outT[:, dc, :], ta[:, :], AF.Identity, scale=pvc(gname, dc), bias=pvc(bname, dc)),
                    reads=[tar, pv_r], writes=[out_rs[dc]])
                S.op("act", lambda e, ta=ta, dc=dc: e.activation(
                    outTb[:, dc, :], ta[:, :], AF.Identity, scale=pvc(gname, dc), bias=pvc(bname, dc)),
                    reads=[tar, pv_r], writes=[out_rs[dc]])

        def ffn_ln(inT, inTb, in_rs, Wg, Wu, Wd, gname, bname, outT, outTb, out_rs):
            Wg_v = Wg.rearrange("(kc p) f -> p kc f", p=128)
            Wu_v = Wu.rearrange("(kc p) f -> p kc f", p=128)
            Wd_v = Wd.rearrange("(j p) d -> p j d", p=128)
            for j in range(NJ):
                if j % 2 == 0:
                    wb, wr = wgu.next()
                    S.dma("pool", wb[:, 0], Wg_v[:, :, j * 128:(j + 2) * 128], writes=[wr])
                    S.dma("pool", wb[:, 1], Wu_v[:, :, j * 128:(j + 2) * 128], writes=[wr])
                jo = (j % 2) * 128
                psg, pgr = psr.next()
                psu, pur = psr.next()
                for gu, (ps, prr) in enumerate(((psg, pgr), (psu, pur))):
                    for kc in range(8):
                        S.op("pe", lambda e, ps=ps, wb=wb, kc=kc, gu=gu, jo=jo: e.matmul(
                            ps[:, :], wb[:, gu, kc, jo:jo + 128], inTb[:, kc, :], start=(kc == 0), stop=(kc == 7)),
                            reads=[wr, in_rs[kc]], writes=[prr])
                tb, tr = tA.next()
                S.op("act", lambda e, tb=tb, ps=psg: e.activation(tb[:, :], ps[:, :], AF.Silu), reads=[pgr], writes=[tr])
                S.op("dve", lambda e, tb=tb, ps=psu, j=j: e.tensor_tensor(aT[:, j, :], tb[:, :], ps[:, :], ALU.mult),
                     reads=[tr, pur], writes=[aT_r[j]])
            for half in range(2):
                pss_ = [psr.next() for _ in range(4)]
                for j in range(NJ):
                    wb, wr = wdb.next()
                    S.dma("pool", wb[:, :], Wd_v[:, j, half * 512:(half + 1) * 512], writes=[wr])
                    for q in range(4):
                        ps, pr = pss_[q]
                        S.op("pe", lambda e, ps=ps, wb=wb, j=j, q=q: e.matmul(
                            ps[:, :], wb[:, q * 128:(q + 1) * 128], aT[:, j, :], start=(j == 0), stop=(j == NJ - 1)),
                            reads=[wr, aT_r[j]], writes=[pr])
                for q in range(4):
                    dc = half * 4 + q
                    ps, pr = pss_[q]
                    S.op("dve", lambda e, ps=ps, dc=dc: e.scalar_tensor_tensor(
                        zT[:, dc, :], inT[:, dc, :], 2.0 * ALPHA, ps[:, :], ALU.mult, ALU.add),
                        reads=[pr, in_rs[dc]], writes=[zT_r[dc]])
            layer_norm(gname, bname, eps4, outT, outTb, out_rs)

        def store_T(blk, srcT, src_rs):
            for t in range(TB // 128):
                ot, orr = otok.next()
                for half in range(2):
                    ps, pr = psr.next()
                    for q in range(4):
                        kc = half * 4 + q
                        S.op("pe", lambda e, ps=ps, kc=kc, q=q, t=t: e.matmul(
                            ps[:, q * 128:(q + 1) * 128], srcT[:, kc, t * 128:(t + 1) * 128], ident,
                            start=True, stop=True), reads=[src_rs[kc], cst_r], writes=[pr])
                    S.op("act", lambda e, ps=ps, ot=ot, half=half: e.copy(ot[:, half * 512:(half + 1) * 512], ps[:, :]),
                         reads=[pr], writes=[orr])
                r0 = blk * TB + t * 128
                S.dma("sp", out_d[r0:r0 + 128, :], ot[:, :], reads=[orr], writes=[out_r])

        def projT(Wd_ap, col0, inTb, in_rs, ncols=128, wring=None, pring=None):
            wb, wr = (wring or wpj).next()
            Wv = Wd_ap.rearrange("(kc p) f -> p kc f", p=128)
            S.dma("pool", wb[:, :, 0:ncols], Wv[:, :, col0:col0 + ncols], writes=[wr])
            ps, pr = (pring or psr).next()
            for kc in range(8):
                src = inTb(kc) if callable(inTb) else inTb[:, kc, :]
                S.op("pe", lambda e, ps=ps, wb=wb, kc=kc, src=src: e.matmul(
                    ps[0:ncols, :], wb[:, kc, 0:ncols], src, start=(kc == 0), stop=(kc == 7)),
                    reads=[wr, in_rs[kc]], writes=[pr])
            return ps, pr

        NT = TB // 128
        carry = sb([128, 4, 3]); carry_r = [Res() for _ in range(4)]
        S.op("dve", lambda e: e.memset(carry[:, :, :], 0.0), writes=carry_r)
        cwork = ring(2, [128, 3 + TB])
        mean_sb = cwork.items[0][0][:, 0:TB]; rstd_sb = cwork.items[1][0][:, 0:TB]
        qkT = zT[:, :, :].rearrange("p a b -> p (a b)").bitcast(BF16).rearrange("p (c n) -> p c n", n=TB)
        qk_r = [zT_r[c // 2] for c in range(16)]
        sigmo = sb([128, 2, TB], BF16); sigmo_r = [Res() for _ in range(2)]
        sga = sb([128, 8, TB], BF16); sga_r = [Res() for _ in range(8)]
        sgb = sb([128, 8, TB], BF16); sgb_r = [Res() for _ in range(8)]
        vt = sb([128, NT, 1, 258], BF16); vt_r = [[Res() for _ in range(1)] for _ in range(NT)]
        S.op("dve", lambda e: e.memset(vt[:, :, :, :], 1.0), writes=[r for rr in vt_r for r in rr])
        wv = ring(1, [128, 8, 256], BF16)
        wif = sb([128, 8, 8], BF16); wif_r = Res()
        S.dma("pool", wif[:, :, :], wif_d.rearrange("(kc p) f -> p kc f", p=128), writes=[wif_r])
        gts = sb([128, NT, 24]); gts_r = [Res() for _ in range(NT)]
        Cst = sb([128, 1, 2, 258]); C_r = [Res() for _ in range(1)]
        Cb = sb([128, 1, 2, 258], BF16); Cb_r = [Res() for _ in range(1)]
        S.op("dve", lambda e: e.memset(Cst[:, :, :, :], 0.0), writes=C_r)
        S.op("dve", lambda e: e.memset(Cb[:, :, :, :], 0.0), writes=Cb_r)
        hyT = sb([128, 4, TB], BF16); hm_r = [Res() for _ in range(2)]
        hmTb = hyT[:, 0:2, :]
        HYf = sb([128, 4, 4, TB], BF16); hyf_r = [Res() for _ in range(4)]
        yrTb = hyT[:, 2:4, :]; yr_r = [Res() for _ in range(2)]
        mgTb = sga; mg_r = sga_r
        smr = ring(4, [128, 128], BF16)
        ktk = ring(4, [128, 256], BF16)
        sm6 = ring(4, [128, 8])

        def mlstm(blk):
            Wv = win.rearrange("(kc p) f -> p kc f", p=128)
            for c in range(4):
                ps, pr = projT(win, c * 128, x1Tb, x1_r, wring=wpjM, pring=psrM)
                wk, wkr = cwork.next()
                S.op("act", lambda e, wk=wk, c=c: e.copy(wk[:, 0:3], carry[:, c, :]), reads=[carry_r[c]], writes=[wkr])
                yield
                S.op("act", lambda e, wk=wk, ps=ps: e.copy(wk[:, 3:3 + TB], ps[:, :]), reads=[pr], writes=[wkr])
                yield
                S.op("act", lambda e, wk=wk, c=c: e.copy(carry[:, c, :], wk[:, TB:TB + 3]), reads=[wkr], writes=[carry_r[c]])
                yield
                ta, tar = tAM.next()
                S.op("dve", lambda e, wk=wk, ta=ta, c=c: e.tensor_scalar(
                    ta[:, :], wk[:, 0:TB], pvc("cw0", c), pvc("cb", c), ALU.mult, ALU.add),
                    reads=[wkr, pv_r], writes=[tar])
                yield
                for j in (1, 2, 3):
                    S.op("dve", lambda e, wk=wk, ta=ta, c=c, j=j: e.scalar_tensor_tensor(
                        ta[:, :], wk[:, j:j + TB], pvc("cw%d" % j, c), ta[:, :], ALU.mult, ALU.add),
                        reads=[wkr, pv_r, tar], writes=[tar])
                    yield
                S.op("act", lambda e, ta=ta, c=c: e.activation(qkT[:, c, :], ta[:, :], AF.Silu),
                     reads=[tar], writes=[qk_r[c]])
                yield
            for c in range(2):
                ps, pr = projT(win, 768 + c * 128, x1Tb, x1_r, wring=wpjM, pring=psrM)
                S.op("act", lambda e, ps=ps, c=c: e.activation(sigmo[:, c, :], ps[:, :], AF.Sigmoid),
                     reads=[pr], writes=[sigmo_r[c]])
                yield
            for t in range(NT):
                ps, pr = psrM.next()
                for kc in range(8):
                    S.op("pe", lambda e, ps=ps, kc=kc, t=t: e.matmul(
                        ps[:, 0:8], x1Tb[:, kc, t * 128:(t + 1) * 128], wif[:, kc, :], start=(kc == 0), stop=(kc == 7)),
                        reads=[wif_r, x1_r[kc]], writes=[pr])
                    yield
                g = gts[:, t, :]
                gr = gts_r[t]
                S.op("dve", lambda e, g=g, ps=ps: e.tensor_tensor(g[:, 12:20], ps[:, 0:8], gbias[:, :], ALU.add),
                     reads=[pr, gb_r], writes=[gr])
                yield
                S.op("act", lambda e, g=g: e.activation(g[:, 20:24], g[:, 16:20], AF.Exp, scale=-1.0), reads=[gr], writes=[gr])
                yield
                S.op("act", lambda e, g=g: e.activation(g[:, 16:20], g[:, 20:24], AF.Ln, bias=1.0), reads=[gr], writes=[gr])
                yield
                S.op("dve", lambda e, g=g: e.tensor_scalar_mul(g[:, 16:20], g[:, 16:20], -1.0), reads=[gr], writes=[gr])
                yield
                ps2, pr2 = psrM.next()
                S.op("pe", lambda e, ps2=ps2, g=g: e.matmul(ps2[:, 0:4], tri, g[:, 16:20], start=True, stop=True),
                     reads=[gr, cst_r], writes=[pr2])
                yield
                S.op("pe", lambda e, ps2=ps2, g=g: e.matmul(ps2[:, 4:8], ones, g[:, 16:20], start=True, stop=True),
                     reads=[gr, cst_r], writes=[pr2])
                yield
                S.op("dve", lambda e, g=g, ps2=ps2: e.tensor_tensor(g[:, 20:24], g[:, 12:16], ps2[:, 0:4], ALU.subtract),
                     reads=[gr, pr2], writes=[gr])
                yield
                S.op("act", lambda e, g=g: e.activation(g[:, 0:4], g[:, 20:24], AF.Exp), reads=[gr], writes=[gr])
                yield
                S.op("act", lambda e, g=g, ps2=ps2: e.activation(g[:, 4:8], ps2[:, 0:4], AF.Exp, scale=-1.0),
                     reads=[gr, pr2], writes=[gr])
                yield
                S.op("act", lambda e, g=g, ps2=ps2: e.activation(g[:, 8:12], ps2[:, 4:8], AF.Exp), reads=[gr, pr2], writes=[gr])
                yield
            for h in range(1):
                wb, wr = wv.next()
                S.dma("pool", wb[:, :, :], Wv[:, :, 512:768], writes=[wr])
                yield
                for t in range(NT):
                    ps, pr = psrM.next()
                    for kc in range(8):
                        S.op("pe", lambda e, ps=ps, kc=kc, t=t, wb=wb: e.matmul(
                            ps[:, 0:256], x1Tb[:, kc, t * 128:(t + 1) * 128], wb[:, kc, :], start=(kc == 0), stop=(kc == 7)),
                            reads=[wr, x1_r[kc]], writes=[pr])
                        yield
                    S.op("act", lambda e, ps=ps, t=t, h=h: e.copy(vt[:, t, h, 0:256], ps[:, 0:256]),
                         reads=[pr], writes=[vt_r[t][h]])
                    yield
            for t in range(NT):
                tc_ = slice(t * 128, (t + 1) * 128)
                g = gts[:, t, :]
                gr = gts_r[t]
                def head_chain(t, h, tc_, g, gr):
                    qc = [h * 2, h * 2 + 1]
                    kc_ = [2 + h * 2, 2 + h * 2 + 1]
                    ps, pr = psrM.next()
                    for i in range(2):
                        S.op("pe", lambda e, ps=ps, i=i, kc_=kc_, qc=qc, tc_=tc_: e.matmul(
                            ps[:, 0:128], qkT[:, kc_[i], tc_], qkT[:, qc[i], tc_], start=(i == 0), stop=(i == 1)),
                            reads=[qk_r[kc_[i]], qk_r[qc[i]]], writes=[pr])
                        yield
                    sm, smrr = smr.next()
                    S.op("dve", lambda e, sm=sm, ps=ps, h=h, g=g: e.scalar_tensor_tensor(
                        sm[:, :], ps[:, 0:128], g[:, h:h + 1], tri, ALU.mult, ALU.mult),
                        reads=[pr, gr, cst_r], writes=[smrr])
                    yield
                    po, por = psrM.next()
                    S.op("pe", lambda e, po=po, sm=sm, t=t, h=h: e.matmul(
                        po[:, 0:258], sm[:, :], vt[:, t, h, :], start=True, stop=False),
                        reads=[smrr, vt_r[t][h]], writes=[por])
                    yield
                    for i in range(2):
                        S.op("pe", lambda e, po=po, i=i, h=h, qc=qc, tc_=tc_: e.matmul(
                            po[:, 0:258], qkT[:, qc[i], tc_], Cb[:, h, i, :], start=False, stop=(i == 1)),
                            reads=[qk_r[qc[i]], Cb_r[h]], writes=[por])
                        yield
                    s6, s6r = sm6.next()
                    S.op("act", lambda e, s6=s6, po=po: e.activation(
                        s6[:, 0:1], po[:, 256:257], AF.Abs, scale=1.0 / 16.0), reads=[por], writes=[s6r])
                    yield
                    S.op("dve", lambda e, s6=s6, g=g, h=h: e.tensor_tensor(s6[:, 0:1], s6[:, 0:1], g[:, 4 + h:5 + h], ALU.max),
                         reads=[s6r, gr], writes=[s6r])
                    yield
                    S.op("dve", lambda e, s6=s6: e.reciprocal(s6[:, 1:2], s6[:, 0:1]), reads=[s6r], writes=[s6r])
                    yield
                    hb, hbr = hh.next()
                    S.op("dve", lambda e, hb=hb, po=po, s6=s6: e.tensor_scalar(
                        hb[:, :], po[:, 0:256], s6[:, 1:2], 1.0 / 16.0, ALU.mult, ALU.mult), reads=[por, s6r], writes=[hbr])
                    yield
                    S.op("dve", lambda e, hb=hb, s6=s6: e.bn_stats(s6[:, 2:8], hb[:, :]), reads=[hbr], writes=[s6r])
                    yield
                    S.op("dve", lambda e, s6=s6: e.bn_aggr(s6[:, 0:2], s6[:, 2:8]), reads=[s6r], writes=[s6r])
                    yield
                    S.op("act", lambda e, s6=s6: e.activation(s6[:, 2:3], s6[:, 1:2], AF.Sqrt, bias=epsln[:, 0:1]),
                         reads=[s6r, eps_r], writes=[s6r])
                    yield
                    S.op("dve", lambda e, s6=s6: e.reciprocal(s6[:, 3:4], s6[:, 2:3]), reads=[s6r], writes=[s6r])
                    yield
                    hn, hnr = hnb.next()
                    S.op("dve", lambda e, hn=hn, hb=hb, s6=s6: e.tensor_scalar(
                        hn[:, :], hb[:, :], s6[:, 0:1], s6[:, 3:4], ALU.subtract, ALU.mult), reads=[hbr, s6r], writes=[hnr])
                    yield
                    for i in range(2):
                        pt, ptr = psrM.next()
                        S.op("pe", lambda e, pt=pt, hn=hn, i=i: e.matmul(
                            pt[:, 0:128], hn[:, i * 128:(i + 1) * 128], identb, start=True, stop=True),
                            reads=[hnr, cstb_r], writes=[ptr])
                        yield
                        S.op("dve", lambda e, pt=pt, h=h, i=i, tc_=tc_: e.scalar_tensor_tensor(
                            hmTb[:, h * 2 + i, tc_], pt[:, 0:128], pvc("mng", h * 2 + i), sigmo[:, h * 2 + i, tc_],
                            ALU.mult, ALU.mult), reads=[ptr, pv_r, sigmo_r[h * 2 + i]], writes=[hm_r[h * 2 + i]])
                        yield
                    kk_, kkr = ktk.next()
                    for i in range(2):
                        pt, ptr = psrM.next()
                        S.op("pe", lambda e, pt=pt, i=i, kc_=kc_, tc_=tc_: e.matmul(
                            pt[:, 0:128], qkT[:, kc_[i], tc_], identb, start=True, stop=True),
                            reads=[qk_r[kc_[i]], cstb_r], writes=[ptr])
                        yield
                        S.op("act", lambda e, pt=pt, kk_=kk_, i=i, g=g, h=h: e.activation(
                            kk_[:, i * 128:(i + 1) * 128], pt[:, 0:128], AF.Identity, scale=g[:, h:h + 1]),
                            reads=[ptr, gr], writes=[kkr])
                        yield
                    for i in range(2):
                        pc, pcr = psrM.next()
                        S.op("pe", lambda e, pc=pc, kk_=kk_, i=i, t=t, h=h: e.matmul(
                            pc[:, 0:258], kk_[:, i * 128:(i + 1) * 128], vt[:, t, h, :], start=True, stop=True),
                            reads=[kkr, vt_r[t][h]], writes=[pcr])
                        yield
                        S.op("dve", lambda e, h=h, i=i, g=g: e.tensor_scalar_mul(Cst[:, h, i, :], Cst[:, h, i, :], g[:, 8 + h:9 + h]),
                             reads=[C_r[h], gr], writes=[C_r[h]])
                        yield
                        S.op("dve", lambda e, pc=pc, h=h, i=i, g=g: e.scalar_tensor_tensor(
                            Cst[:, h, i, :], pc[:, 0:258], g[:, 8 + h:9 + h], Cst[:, h, i, :], ALU.mult, ALU.add),
                            reads=[pcr, gr, C_r[h]], writes=[C_r[h]])
                        yield
                        S.op("act", lambda e, h=h, i=i: e.copy(Cb[:, h, i, :], Cst[:, h, i, :]), reads=[C_r[h]], writes=[Cb_r[h]])
                        yield

                gens = [head_chain(t, h, tc_, g, gr) for h in range(1)]
                while gens:
                    for gg in list(gens):
                        try:
                            next(gg)
                            yield
                        except StopIteration:
                            gens.remove(gg)

        NCH = TB // 64
        rcar = sb([128, 8, 1]); rcar_r = [Res() for _ in range(8)]
        S.op("dve", lambda e: e.memset(rcar[:, :, :], 0.0), writes=rcar_r)
        psrM = Ring(psr.items[0:3]); psrR = Ring(psr.items[3:7])
        tAM = Ring([(x1T[:, i, :], Res()) for i in range(2)])
        rwork = Ring([(x1T[:, 2 + 2 * i:4 + 2 * i, :].rearrange("p a b -> p (a b)")[:, 0:1 + TB], Res()) for i in range(2)])
        wpjM = Ring([(x1T[:, 6 + i, :].bitcast(BF16).rearrange("p (k c) -> p k c", c=128), Res()) for i in range(2)])
        scr_r = [r for _, r in tAM.items + rwork.items + wpjM.items]
        lowT = sb([128, 2, TB], BF16); low_r = [Res(), Res()]
        rtmp = Ring([(aT[:, 2 * i:2 * i + 2, :].rearrange("p a b -> p (a b)").bitcast(F32), Res()) for i in range(10)])
        ARbd = sb([128, NCH, 256], BF16); Bbd = sb([128, NCH, 128], BF16); Kbd = sb([128, NCH, 128], BF16)
        Vbd = sb([128, NCH, 128], BF16); Ynbd = sb([128, 128], BF16)
        bd_r = Res(); ynbd_r = Res()
        for tns in (ARbd, Bbd, Kbd, Vbd):
            S.op("dve", lambda e, tns=tns: e.memset(tns[:, :, :], 0.0), writes=[bd_r])
        S.op("dve", lambda e: e.memset(Ynbd[:, :], 0.0), writes=[ynbd_r])
        gam = sb([128, NCH]); gam_r = Res()
        Hst = sb([128, 2, 64]); H_r = [Res() for _ in range(2)]
        Hb = sb([128, 2, 64], BF16); Hb_r = [Res() for _ in range(2)]
        S.op("dve", lambda e: e.memset(Hst[:, :, :], 0.0), writes=H_r)
        S.op("dve", lambda e: e.memset(Hb[:, :, :], 0.0), writes=Hb_r)
        vst = sb([128, NCH, 64], BF16); vst_r = Res()
        btk = sb([128, NCH, 256], BF16); btk_r = Res()
        ub = ring(4, [128, 64], BF16)
        gn6 = ring(4, [128, 8])
        htmp = ring(2, [128, 64])
        maskAR = cst[:, C_MUS:C_MUS + 256]; mask_r = cst_r
        mls = cst[:, C_MLS:C_MLS + 128]

        xTv = xT[:, :, :].rearrange("p a b -> p (a b)").bitcast(BF16)
        mAK = xTv[:, 0:NCH * 512].rearrange("p (n c) -> p n c", c=512)
        Xs = xTv[:, 4096:4096 + NCH * 128].rearrange("p (n c) -> p n c", c=128)
        xTbv = xTb[:, :, :].rearrange("p a b -> p (a b)")
        PW = [[xTbv[:, (b * 8 + n) * 128:(b * 8 + n + 1) * 128] for n in range(NCH)] for b in range(2)]
        PWT = [[xTbv[:, 2048 + (b * 8 + n) * 128:2048 + (b * 8 + n + 1) * 128] for n in range(NCH)] for b in range(2)]
        hh = Ring([(xTv[:, 5120 + i * 512:5120 + (i + 1) * 512].bitcast(F32), Res()) for i in range(4)])
        hnb = Ring([(xTv[:, 7168 + i * 256:7168 + (i + 1) * 256], Res()) for i in range(4)])
        mA_r = [Res() for _ in range(NCH)]; mK_r = [Res() for _ in range(NCH)]; X_r = [Res() for _ in range(NCH)]
        PW_r = [[Res() for _ in range(NCH)] for _ in range(2)]; PWT_r = [[Res() for _ in range(NCH)] for _ in range(2)]

        def v3(ap):
            return ap.rearrange("p (n l) -> p n l", l=64)

        def shifted(ci, ps, pr):
            wk, wkr = rwork.next()
            S.op("act", lambda e, wk=wk: e.copy(wk[:, 0:1], rcar[:, ci, :]), reads=[rcar_r[ci]], writes=[wkr])
            S.op("act", lambda e, wk=wk, ps=ps: e.copy(wk[:, 1:1 + TB], ps[:, :]), reads=[pr], writes=[wkr])
            S.op("act", lambda e, wk=wk: e.copy(rcar[:, ci, :], wk[:, TB:TB + 1]), reads=[wkr], writes=[rcar_r[ci]])
            ta, tar = rtmp.next()
            S.op("dve", lambda e, wk=wk, ta=ta: e.tensor_scalar_mul(ta[:, :], wk[:, 1:1 + TB], pvc("omu", ci)),
                 reads=[wkr, pv_r], writes=[tar])
            S.op("dve", lambda e, wk=wk, ta=ta: e.scalar_tensor_tensor(
                ta[:, :], wk[:, 0:TB], pvc("mu", ci), ta[:, :], ALU.mult, ALU.add), reads=[wkr, pv_r, tar], writes=[tar])
            return ta, tar

        krw = int(os.environ.get("KRW", "9"))

        def rwkv(blk):
            ps, pr = projT(win, 1800, x1Tb, x1_r, pring=psrR)
            ta, tar = shifted(6, ps, pr)
            S.op("act", lambda e, ta=ta: e.activation(lowT[0:64, 0, :], ta[0:64, :], AF.Tanh), reads=[tar], writes=[low_r[0]])
            yield
            S.op("act", lambda e, ta=ta: e.copy(lowT[64:128, 0, :], ta[64:128, :]), reads=[tar], writes=[low_r[0]])
            yield
            ps, pr = projT(win, 1928, x1Tb, x1_r, pring=psrR)
            ta, tar = shifted(7, ps, pr)
            S.op("act", lambda e, ta=ta: e.activation(lowT[:, 1, :], ta[:, :], AF.Sigmoid), reads=[tar], writes=[low_r[1]])
            yield
            for p in range(2):
                cs = slice(p * 128, (p + 1) * 128)
                ps, pr = projT(win, 1032 + p * 128, x1Tb, x1_r, pring=psrR)
                r_, r_r = shifted(p, ps, pr)
                ps, pr = projT(win, 1288 + p * 128, x1Tb, x1_r, pring=psrR)
                k_, k_r = shifted(2 + p, ps, pr)
                ps, pr = projT(win, 1544 + p * 128, x1Tb, x1_r, pring=psrR)
                v_, v_r = shifted(4 + p, ps, pr)
                pw, pwr = psrR.next()
                S.op("pe", lambda e, pw=pw, cs=cs: e.matmul(pw[:, :], rw2a2[0:64, cs], lowT[0:64, 0, :], start=True, stop=True),
                     reads=[rw_r, low_r[0]], writes=[pwr])
                yield
                lw, lwr = rtmp.next()
                S.op("act", lambda e, lw=lw, pw=pw, p=p: e.activation(lw[:, :], pw[:, :], AF.Sigmoid, bias=pvc("w0", p)),
                     reads=[pwr, pv_r], writes=[lwr])
                yield
                S.op("dve", lambda e, lw=lw: e.tensor_scalar_mul(lw[:, :], lw[:, :], -float(np.exp(-0.5))), reads=[lwr], writes=[lwr])
                yield
                pa, par = psrR.next()
                S.op("pe", lambda e, pa=pa, cs=cs: e.matmul(pa[:, :], rw2a2[64:128, cs], lowT[64:128, 0, :], start=True, stop=True),
                     reads=[rw_r, low_r[0]], writes=[par])
                yield
                a_, a_r = rtmp.next()
                S.op("act", lambda e, a_=a_, pa=pa, p=p: e.activation(a_[:, :], pa[:, :], AF.Sigmoid, bias=pvc("a0", p)),
                     reads=[par, pv_r], writes=[a_r])
                yield
                pg, pgr = psrR.next()
                S.op("pe", lambda e, pg=pg, cs=cs: e.matmul(pg[:, :], rg2[:, cs], lowT[:, 1, :], start=True, stop=True),
                     reads=[rw_r, low_r[1]], writes=[pgr])
                yield
                g_, g_r = rtmp.next()
                S.op("act", lambda e, g_=g_, pg=pg: e.copy(g_[:, :], pg[:, :]), reads=[pgr], writes=[g_r])
                yield
                kk, kkr = rtmp.next()
                S.op("dve", lambda e, kk=kk, k_=k_, p=p: e.tensor_scalar_mul(kk[:, :], k_[:, :], pvc("kk", p)),
                     reads=[k_r, pv_r], writes=[kkr])
                yield
                sq, sqr = tB.next()
                S.op("act", lambda e, sq=sq, kk=kk: e.activation(sq[:, :], kk[:, :], AF.Square), reads=[kkr], writes=[sqr])
                yield
                pq, pqr = psrR.next()
                S.op("pe", lambda e, pq=pq, sq=sq: e.matmul(pq[:, :], bob, sq[:, :], start=True, stop=True),
                     reads=[cstb_r, sqr], writes=[pqr])
                yield
                t1, t1r = rtmp.next()
                S.op("act", lambda e, t1=t1, pq=pq: e.activation(t1[:, :], pq[:, :], AF.Sqrt), reads=[pqr], writes=[t1r])
                yield
                S.op("dve", lambda e, t1=t1: e.tensor_scalar_max(t1[:, :], t1[:, :], 1e-12), reads=[t1r], writes=[t1r])
                yield
                S.op("dve", lambda e, t1=t1: e.reciprocal(t1[:, :], t1[:, :]), reads=[t1r], writes=[t1r])
                yield
                S.op("dve", lambda e, t1=t1, kk=kk: e.tensor_tensor(kk[:, :], kk[:, :], t1[:, :], ALU.mult),
                     reads=[t1r, kkr], writes=[kkr])
                yield
                S.op("dve", lambda e, t1=t1, a_=a_, p=p: e.tensor_scalar(t1[:, :], a_[:, :], 1.0, pvc("ka", p), ALU.subtract, ALU.mult),
                     reads=[a_r, pv_r], writes=[t1r])
                yield
                S.op("dve", lambda e, t1=t1, k_=k_: e.scalar_tensor_tensor(k_[:, :], t1[:, :], 1.0, k_[:, :], ALU.add, ALU.mult),
                     reads=[t1r, k_r], writes=[k_r])
                yield
                t2, t2r = tB.next()
                S.op("dve", lambda e, t2=t2, r_=r_, k_=k_, p=p: e.scalar_tensor_tensor(
                    t2[:, :], r_[:, :], pvc("rrk", p), k_[:, :], ALU.mult, ALU.mult), reads=[r_r, k_r, pv_r], writes=[t2r])
                yield
                pb, pbr = psrR.next()
                S.op("pe", lambda e, pb=pb, t2=t2: e.matmul(pb[:, :], bob, t2[:, :], start=True, stop=True),
                     reads=[cstb_r, t2r], writes=[pbr])
                yield
                bon, bonr = rtmp.next()
                S.op("dve", lambda e, bon=bon, pb=pb, v_=v_: e.tensor_tensor(bon[:, :], pb[:, :], v_[:, :], ALU.mult),
                     reads=[pbr, v_r], writes=[bonr])
                yield
                cl, clr = rtmp.next()
                S.op("dve", lambda e, cl=cl, lw=lw: e.tensor_tensor_scan(cl[:, :], rst, lw[:, :], 0.0, ALU.mult, ALU.add),
                     reads=[lwr, cstb_r], writes=[clr])
                yield
                e1, e1r = tA.next()
                S.op("act", lambda e, e1=e1, cl=cl: e.activation(e1[:, :], cl[:, :], AF.Exp), reads=[clr], writes=[e1r])
                yield
                S.op("act", lambda e, e1=e1: e.copy(gam[:, :], v3(e1[:, :])[:, :, 63]), reads=[e1r], writes=[gam_r])
                yield
                for hf in range(2):
                    hs = slice(hf * 64, hf * 64 + 64)
                    S.op("dve", lambda e, hs=hs, hf=hf, r_=r_, e1=e1: e.tensor_tensor(
                        ARbd[hs, :, 128 + hf * 64:128 + hf * 64 + 64], v3(r_[hs, :]), v3(e1[hs, :]), ALU.mult),
                        reads=[r_r, e1r], writes=[bd_r])
                    yield
                e2, e2r = tA.next()
                S.op("act", lambda e, e2=e2, cl=cl: e.activation(e2[:, :], cl[:, :], AF.Exp, scale=-1.0), reads=[clr], writes=[e2r])
                yield
                S.op("dve", lambda e, t1=t1, kk=kk, a_=a_: e.tensor_tensor(t1[:, :], kk[:, :], a_[:, :], ALU.mult),
                     reads=[kkr, a_r], writes=[t1r])
                yield
                for hf in range(2):
                    hs = slice(hf * 64, hf * 64 + 64)
                    S.op("dve", lambda e, hs=hs, hf=hf, t1=t1, e2=e2: e.tensor_tensor(
                        Bbd[hs, :, hf * 64:hf * 64 + 64], v3(t1[hs, :]), v3(e2[hs, :]), ALU.mult), reads=[t1r, e2r], writes=[bd_r])
                    yield
                    S.op("dve", lambda e, hs=hs, hf=hf, k_=k_, e2=e2: e.tensor_tensor(
                        Kbd[hs, :, hf * 64:hf * 64 + 64], v3(k_[hs, :]), v3(e2[hs, :]), ALU.mult), reads=[k_r, e2r], writes=[bd_r])
                    yield
                    S.op("act", lambda e, hs=hs, hf=hf, v_=v_: e.copy(Vbd[hs, :, hf * 64:hf * 64 + 64], v3(v_[hs, :])),
                         reads=[v_r], writes=[bd_r])
                    yield
                S.op("dve", lambda e, cl=cl, lw=lw: e.tensor_tensor(cl[:, :], cl[:, :], lw[:, :], ALU.subtract),
                     reads=[clr, lwr], writes=[clr])
                yield
                e3, e3r = tA.next()
                S.op("act", lambda e, e3=e3, cl=cl: e.activation(e3[:, :], cl[:, :], AF.Exp), reads=[clr], writes=[e3r])
                yield
                for hf in range(2):
                    hs = slice(hf * 64, hf * 64 + 64)
                    S.op("dve", lambda e, hs=hs, hf=hf, kk=kk, e3=e3: e.scalar_tensor_tensor(
                        ARbd[hs, :, hf * 64:hf * 64 + 64], v3(kk[hs, :]), -1.0, v3(e3[hs, :]), ALU.mult, ALU.mult),
                        reads=[kkr, e3r], writes=[bd_r])
                    yield
                if krw <= 1:
                    S.op("dve", lambda e, p=p: e.memset(yrTb[:, p, :], 0.0), writes=[yr_r[p]])
                    yield
                    continue
                pv_, pvr_ = psrR.next()
                for n in range(NCH):
                    S.op("pe", lambda e, n=n, pv_=pv_: e.matmul(pv_[:, n * 64:(n + 1) * 64], Vbd[:, n, :], istb, start=True, stop=True),
                         reads=[bd_r, cstb_r], writes=[pvr_])
                    yield
                S.op("act", lambda e, pv_=pv_: e.copy(vst[:, :, :], v3(pv_[:, :])), reads=[pvr_], writes=[vst_r])
                yield
                for n0 in range(0, NCH, 2):
                    pt, ptr = psrR.next()
                    for n in (n0, n0 + 1):
                        o = (n - n0) * 256
                        S.op("pe", lambda e, n=n, pt=pt, o=o: e.matmul(pt[:, o:o + 128], Bbd[:, n, :], identb, start=True, stop=True),
                             reads=[bd_r, cstb_r], writes=[ptr])
                        yield
                        S.op("pe", lambda e, n=n, pt=pt, o=o: e.matmul(pt[:, o + 128:o + 256], Kbd[:, n, :], identb, start=True, stop=True),
                             reads=[bd_r, cstb_r], writes=[ptr])
                        yield
                    S.op("act", lambda e, n0=n0, pt=pt: e.copy(btk[:, n0:n0 + 2, :], pt[:, :].rearrange("p (n l) -> p n l", l=256)),
                         reads=[ptr], writes=[btk_r])
                    yield
                if krw <= 2:
                    S.op("dve", lambda e, p=p: e.memset(yrTb[:, p, :], 0.0), writes=[yr_r[p]])
                    yield
                    continue
                pyo, pyor = pyo_bank
                for g0 in range(0, NCH, 4):
                    G = list(range(g0, min(g0 + 4, NCH)))
                    pAs = {}
                    for n in G:
                        pA, pAr = psrR.next()
                        pAs[n] = (pA, pAr)
                        S.op("pe", lambda e, pA=pA, n=n: e.matmul(pA[:, 0:256], Bbd[:, n, :], ARbd[:, n, :], start=True, stop=True),
                             reads=[bd_r], writes=[pAr])
                        yield
                        S.op("pe", lambda e, pA=pA, n=n: e.matmul(pA[:, 256:512], Kbd[:, n, :], ARbd[:, n, :], start=True, stop=True),
                             reads=[bd_r], writes=[pAr])
                        yield
                    for n in G:
                        pA, pAr = pAs[n]
                        S.op("dve", lambda e, pA=pA, n=n: e.tensor_tensor(mAK[:, n, 0:256], pA[:, 0:256], maskAR[:, :], ALU.mult),
                             reads=[pAr, mask_r], writes=[mA_r[n]])
                        yield
                        S.op("dve", lambda e, pA=pA, n=n: e.tensor_tensor(mAK[:, n, 256:512], pA[:, 256:512], maskAR[:, :], ALU.mult),
                             reads=[pAr, mask_r], writes=[mK_r[n]])
                        yield
                    pTs = {}
                    for n in G:
                        pT, pTr = psrR.next()
                        pTs[n] = (pT, pTr)
                        S.op("pe", lambda e, pT=pT, n=n: e.matmul(pT[:, 0:128], ARbd[:, n, 0:128], Bbd[:, n, :], start=True, stop=True),
                             reads=[bd_r], writes=[pTr])
                        yield
                    for n in G:
                        pT, pTr = pTs[n]
                        S.op("dve", lambda e, pT=pT, n=n: e.tensor_tensor(PWT[0][n], pT[:, 0:128], mls, ALU.mult),
                             reads=[pTr, cst_r], writes=[PWT_r[0][n]])
                        yield
                        S.op("dve", lambda e, n=n: e.tensor_tensor(Xs[:, n, :], mAK[:, n, 0:128], identb, ALU.add),
                             reads=[mA_r[n], cstb_r], writes=[X_r[n]])
                        yield
                    for j in range(1, 6):
                        b0, b1 = (j - 1) % 2, j % 2
                        p2s = {}
                        for n in G:
                            cur = mAK[:, n, 0:128] if j == 1 else PW[b0][n]
                            curr = mA_r[n] if j == 1 else PW_r[b0][n]
                            ct, ctr_ = PWT[b0][n], PWT_r[b0][n]
                            p2, p2r = psrR.next()
                            p2s[n] = (p2, p2r)
                            if j < 5:
                                S.op("pe", lambda e, p2=p2, cur=cur, ct=ct: e.matmul(p2[:, 0:128], ct, cur, start=True, stop=True),
                                     reads=[curr, ctr_], writes=[p2r])
                                yield
                            S.op("pe", lambda e, p2=p2, cur=cur, ct=ct: e.matmul(p2[:, 128:256], cur, ct, start=True, stop=True),
                                 reads=[curr, ctr_], writes=[p2r])
                            yield
                        for n in G:
                            p2, p2r = p2s[n]
                            S.op("dve", lambda e, p2=p2, n=n, b1=b1: e.tensor_copy(PWT[b1][n], p2[:, 128:256]),
                                 reads=[p2r], writes=[PWT_r[b1][n]])
                            yield
                            if j < 5:
                                S.op("act", lambda e, p2=p2, n=n, b1=b1: e.copy(PW[b1][n], p2[:, 0:128]),
                                     reads=[p2r], writes=[PW_r[b1][n]])
                                yield
                        pxs = {}
                        for n in G:
                            px, pxr = psrR.next()
                            pxs[n] = (px, pxr)
                            S.op("pe", lambda e, px=px, n=n, b1=b1: e.matmul(px[:, 0:128], PWT[b1][n], Xs[:, n, :], start=True, stop=True),
                                 reads=[PWT_r[b1][n], X_r[n]], writes=[pxr])
                            yield
                        for n in G:
                            px, pxr = pxs[n]
                            S.op("dve", lambda e, px=px, n=n: e.tensor_tensor(Xs[:, n, :], px[:, 0:128], Xs[:, n, :], ALU.add),
                                 reads=[pxr, X_r[n]], writes=[X_r[n]])
                            yield
                for n in range(NCH):
                    pw_, pw_r = psrR.next()
                    S.op("pe", lambda e, pw_=pw_, n=n, p=p: e.matmul(pw_[:, 0:64], ARbd[:, n, 0:128], Hb[:, p, :], start=True, stop=False),
                         reads=[bd_r, Hb_r[p]], writes=[pw_r])
                    yield
                    S.op("pe", lambda e, pw_=pw_, n=n: e.matmul(pw_[:, 0:64], mAK[:, n, 256:384], vst[:, n, :], start=False, stop=True),
                         reads=[mK_r[n], vst_r], writes=[pw_r])
                    yield
                    w_b, wbr = ub.next()
                    S.op("act", lambda e, w_b=w_b, pw_=pw_: e.copy(w_b[:, :], pw_[:, 0:64]), reads=[pw_r], writes=[wbr])
                    yield
                    pu, pur_ = psrR.next()
                    S.op("pe", lambda e, pu=pu, n=n, w_b=w_b: e.matmul(pu[:, 0:64], Xs[:, n, :], w_b[:, :], start=True, stop=True),
                         reads=[X_r[n], wbr], writes=[pur_])
                    yield
                    u_b, ubr = ub.next()
                    S.op("dve", lambda e, u_b=u_b, pu=pu: e.tensor_copy(u_b[:, :], pu[:, 0:64]), reads=[pur_], writes=[ubr])
                    yield
                    ph, phr = psrR.next()
                    S.op("pe", lambda e, ph=ph, n=n, u_b=u_b: e.matmul(ph[:, 0:64], btk[:, n, 0:128], u_b[:, :], start=True, stop=False),
                         reads=[btk_r, ubr], writes=[phr])
                    yield
                    S.op("pe", lambda e, ph=ph, n=n: e.matmul(ph[:, 0:64], btk[:, n, 128:256], vst[:, n, :], start=False, stop=True),
                         reads=[btk_r, vst_r], writes=[phr])
                    yield
                    py, pyr = psrR.next()
                    S.op("pe", lambda e, py=py, n=n, p=p: e.matmul(py[:, 0:64], ARbd[:, n, 128:256], Hb[:, p, :], start=True, stop=False),
                         reads=[bd_r, Hb_r[p]], writes=[pyr])
                    yield
                    S.op("pe", lambda e, py=py, n=n, u_b=u_b: e.matmul(py[:, 0:64], mAK[:, n, 128:256], u_b[:, :], start=False, stop=False),
                         reads=[mA_r[n], ubr], writes=[pyr])
                    yield
                    S.op("pe", lambda e, py=py, n=n: e.matmul(py[:, 0:64], mAK[:, n, 384:512], vst[:, n, :], start=False, stop=True),
                         reads=[mK_r[n], vst_r], writes=[pyr])
                    yield
                    ht, htr = htmp.next()
                    S.op("dve", lambda e, ht=ht, ph=ph, p=p: e.tensor_tensor(ht[:, :], ph[:, 0:64], Hst[:, p, :], ALU.add),
                         reads=[phr, H_r[p]], writes=[htr])
                    yield
                    S.op("act", lambda e, ht=ht, p=p, n=n: e.activation(Hb[:, p, :], ht[:, :], AF.Identity, scale=gam[:, n:n + 1]),
                         reads=[htr, gam_r], writes=[Hb_r[p]])
                    yield
                    S.op("act", lambda e, ht=ht, p=p, n=n: e.activation(Hst[:, p, :], ht[:, :], AF.Identity, scale=gam[:, n:n + 1]),
                         reads=[htr, gam_r], writes=[H_r[p]])
                    yield
                    g6, g6r = gn6.next()
                    S.op("dve", lambda e, g6=g6, py=py: e.bn_stats(g6[:, 2:8], py[:, 0:64]), reads=[pyr], writes=[g6r])
                    yield
                    S.op("dve", lambda e, g6=g6: e.bn_aggr(g6[:, 0:2], g6[:, 2:8]), reads=[g6r], writes=[g6r])
                    yield
                    S.op("act", lambda e, g6=g6: e.activation(g6[:, 2:3], g6[:, 1:2], AF.Sqrt, bias=epsgn[:, 0:1]),
                         reads=[g6r, eps_r], writes=[g6r])
                    yield
                    S.op("dve", lambda e, g6=g6: e.reciprocal(g6[:, 3:4], g6[:, 2:3]), reads=[g6r], writes=[g6r])
                    yield
                    for hf in range(2):
                        hs = slice(hf * 64, hf * 64 + 64)
                        S.op("dve", lambda e, hs=hs, hf=hf, py=py, g6=g6: e.tensor_scalar(
                            Ynbd[hs, hf * 64:hf * 64 + 64], py[hs, 0:64], g6[hs, 0:1], g6[hs, 3:4], ALU.subtract, ALU.mult),
                            reads=[pyr, g6r], writes=[ynbd_r])
                        yield
                    S.op("pe", lambda e, n=n, pyo=pyo: e.matmul(pyo[:, n * 64:(n + 1) * 64], Ynbd[:, :], istb, start=True, stop=True),
                         reads=[ynbd_r, cstb_r], writes=[pyor])
                    yield
                if krw <= 5:
                    S.op("dve", lambda e, p=p: e.memset(yrTb[:, p, :], 0.0), writes=[yr_r[p]])
                    yield
                    continue
                S.op("dve", lambda e, t1=t1, pyo=pyo, p=p: e.tensor_scalar(
                    t1[:, :], pyo[:, :], pvc("gng", p), pvc("gnb", p), ALU.mult, ALU.add), reads=[pyor, pv_r], writes=[t1r])
                yield
                S.op("dve", lambda e, t1=t1, bon=bon: e.tensor_tensor(t1[:, :], t1[:, :], bon[:, :], ALU.add),
                     reads=[t1r, bonr], writes=[t1r])
                yield
                S.op("dve", lambda e, t1=t1, g_=g_, p=p: e.tensor_tensor(yrTb[:, p, :], t1[:, :], g_[:, :], ALU.mult),
                     reads=[t1r, g_r], writes=[yr_r[p]])
                yield

        def merge_out(blk):
            for c in range(8):
                ps, pr = projT(wgate, c * 128, x1Tb, x1_r)
                S.op("act", lambda e, ps=ps, c=c: e.activation(sga[:, c, :], ps[:, :], AF.Sigmoid), reads=[pr], writes=[sga_r[c]])
                ps, pr = projT(wgate, 1024 + c * 128, x1Tb, x1_r)
                S.op("act", lambda e, ps=ps, c=c: e.activation(sgb[:, c, :], ps[:, :], AF.Sigmoid), reads=[pr], writes=[sgb_r[c]])
            for c in range(8):
                psa, par = projT(wa_d, c * 128, lambda kc: HYf[:, kc // 2, kc % 2, :], [hyf_r[kc // 2] for kc in range(8)])
                psb, pbr = projT(wb_d, c * 128, lambda kc: HYf[:, kc // 2, 2 + kc % 2, :], [hyf_r[kc // 2] for kc in range(8)])
                ta, tar = tA.next()
                S.op("dve", lambda e, ta=ta, psa=psa, c=c: e.tensor_tensor(ta[:, :], psa[:, :], sga[:, c, :], ALU.mult),
                     reads=[par, sga_r[c]], writes=[tar])
                tb, tbr = tA.next()
                S.op("dve", lambda e, tb=tb, psb=psb, c=c: e.tensor_tensor(tb[:, :], psb[:, :], sgb[:, c, :], ALU.mult),
                     reads=[pbr, sgb_r[c]], writes=[tbr])
                S.op("dve", lambda e, ta=ta, tb=tb, c=c: e.tensor_tensor(mgTb[:, c, :], ta[:, :], tb[:, :], ALU.add),
                     reads=[tar, tbr], writes=[mg_r[c]])
            for c in range(8):
                ps, pr = projT(wo_d, c * 128, mgTb, mg_r)
                S.op("dve", lambda e, ps=ps, c=c: e.scalar_tensor_tensor(
                    zT[:, c, :], x1T[:, c, :], ALPHA, ps[:, :], ALU.mult, ALU.add), reads=[pr, x1_r[c]], writes=[zT_r[c]])
            layer_norm("ln2_g", "ln2_b", epsln, x2T, x2Tb, x2_r)

        x1f_r = Res()
        x1bl_r = [[Res(), Res()] for _ in range(nbl)]; x1ba_r = [[Res(), Res()] for _ in range(nbl)]
        hyl_r = [Res() for _ in range(NBT)]; hya_r = [Res() for _ in range(NBT)]
        oh = sb([128, 4]); oh_r = Res()
        S.dma("sp", oh[:, :], oh_d[:, :], writes=[oh_r])

        def half(t, h):
            return t[:, 4 * h:4 * h + 4, :].rearrange("p a b -> p (a b)")

        for blk in range(nbl):
            load_xT(blk)
            ffn_ln(xT, xTb, xT_r, w1g, w1u, w1d, "ln1_g", "ln1_b", x1T, x1Tb, x1_r)
            rows = slice(blk * 128, (blk + 1) * 128)
            S.dma("sp", x1f_loc[rows, :], x1T[:, :, :].rearrange("p a b -> p (a b)"), reads=x1_r, writes=[x1f_r])
            for h in range(2):
                S.dma("sp", x1b_loc[blk][h][:, :], half(x1Tb, h), reads=x1_r, writes=[x1bl_r[blk][h]])
                S.cc("AllGather", x1b_loc[blk][h][:, :], x1b_all[blk][h][:, :], groups,
                     reads=[x1bl_r[blk][h]], writes=[x1ba_r[blk][h]])
        for gb in range(NBT):
            i, blk = gb // nbl, gb % nbl
            for h in range(2):
                S.dma("sp", half(x1Tb, h), x1b_all[blk][h][i * 128:(i + 1) * 128, :], reads=[x1ba_r[blk][h]],
                      writes=x1_r[4 * h:4 * h + 4])
            if gb == 0:
                S.op("dve", lambda e: e.memset(x1T[:, 0, 0:8], 0.0), writes=x1_r + scr_r)
            gens = [mlstm(gb), rwkv(gb)]
            while gens:
                for gg in list(gens):
                    try:
                        next(gg)
                    except StopIteration:
                        gens.remove(gg)
            S.dma("sp", hy_loc[gb][:, :], hyT[:, :, :].rearrange("p a b -> p (a b)"), reads=hm_r + yr_r, writes=[hyl_r[gb]])
            S.cc("AllGather", hy_loc[gb][:, :], hy_all[gb][:, :], groups, reads=[hyl_r[gb]], writes=[hya_r[gb]])
        for blk in range(nbl):
            rows = slice(blk * 128, (blk + 1) * 128)
            S.dma("sp", x1T[:, :, :].rearrange("p a b -> p (a b)"), x1f_loc[rows, :], reads=[x1f_r], writes=x1_r + scr_r)
            for h in range(2):
                S.dma("sp", half(x1Tb, h), x1b_loc[blk][h][:, :], reads=[x1bl_r[blk][h]], writes=x1_r[4 * h:4 * h + 4])
            for i in range(4):
                dst = HYf[:, i, :, :].rearrange("p a b -> p (a b)")
                for j in range(4):
                    k = j * nbl + blk
                    stg = sgb[:, 4 * (j % 2):4 * (j % 2) + 4, :].rearrange("p a b -> p (a b)")
                    stg_r = sgb_r[4 * (j % 2):4 * (j % 2) + 4]
                    S.dma("sp", stg, hy_all[k][i * 128:(i + 1) * 128, :], reads=[hya_r[k]], writes=stg_r)
                    if j == 0:
                        S.op("dve", lambda e, dst=dst, stg=stg, j=j: e.tensor_scalar_mul(dst, stg, oh[:, j:j + 1]),
                             reads=stg_r + [oh_r], writes=[hyf_r[i]])
                    else:
                        S.op("dve", lambda e, dst=dst, stg=stg, j=j: e.scalar_tensor_tensor(
                            dst, stg, oh[:, j:j + 1], dst, ALU.mult, ALU.add),
                            reads=stg_r + [oh_r, hyf_r[i]], writes=[hyf_r[i]])
            merge_out(blk)
            ffn_ln(x2T, x2Tb, x2_r, w2g, w2u, w2d, "ln3_g", "ln3_b", x3T, x3Tb, x3_r)
            store_T(blk, x3T, x3_r)

        S.op("sp", lambda e: e.nop(), reads=[out_r])
        S.emit()
    return nc


def _consts():
    c = np.zeros((128, NCST), np.float32)
    i = np.arange(128)
    c[:, C_ID:C_ID + 128] = np.eye(128)
    c[:, C_OD:C_OD + 128] = 1.0 / 1024.0
    c[:, C_TRI:C_TRI + 128] = (i[:, None] <= i[None, :])
    c[:, C_ONE:C_ONE + 128] = 1.0
    c[:, C_MUS:C_MUS + 128] = (i[:, None] < i[None, :])
    c[:, C_MLS:C_MLS + 128] = (i[:, None] > i[None, :])
    c[:, C_IST:C_IST + 64] = np.concatenate([np.eye(64), np.eye(64)], axis=0)
    c[:, C_BO:C_BO + 128] = ((i[:, None] // 64) == (i[None, :] // 64))
    return c


def _fm(v):
    v = np.asarray(v, np.float32).reshape(-1, 128)
    return np.ascontiguousarray(v.T)


def kernel(**inp):
    nbl = int(os.environ.get("KNBL", "4"))
    ngrp = int(os.environ.get("KNGRP", "2"))
    x = np.asarray(inp["x"], np.float32)
    W = np.asarray(inp["w_in"][0], np.float32)
    gates = np.ascontiguousarray(W[:, 7432:9480])
    cw = inp["m_conv_w"][0]; cbv = inp["m_conv_b"][0]
    mu = inp["r_mu"][0]
    in_maps = []
    for c in range(4 * ngrp):
        b, r = c // 4, c % 4
        pv = np.zeros((128, NPV), np.float32)

        def put(name, arr):
            a = _fm(arr)
            pv[:, PV[name]:PV[name] + a.shape[1]] = a

        for n in ("ln1_g", "ln1_b", "ln2_g", "ln2_b", "ln3_g", "ln3_b"):
            put(n, inp[n][0])
        qs = slice(r * 256, (r + 1) * 256)
        ks = slice(1024 + r * 256, 1024 + (r + 1) * 256)
        for j in range(4):
            put("cw%d" % j, np.concatenate([cw[j][qs], cw[j][ks]]))
        put("cb", np.concatenate([cbv[qs], cbv[ks]]))
        put("mng", inp["m_norm_g"][0][qs])
        ps_ = slice(r * 256, (r + 1) * 256)
        mul = np.concatenate([mu[0:1024][ps_], mu[1024:2048][ps_], mu[2048:3072][ps_], mu[3072:3328]])
        put("mu", mul)
        put("omu", 1.0 - mul)
        put("w0", inp["r_w0"][0][ps_]); put("a0", inp["r_a0"][0][ps_]); put("kk", inp["r_k_k"][0][ps_])
        put("ka", inp["r_k_a"][0][ps_]); put("rrk", inp["r_r_k"][0].reshape(-1)[ps_])
        put("gng", inp["r_gn_g"][0][ps_]); put("gnb", inp["r_gn_b"][0][ps_])
        wl = np.concatenate([W[:, qs], W[:, ks], W[:, 2048 + r * 256:2048 + (r + 1) * 256],
                             W[:, 3072 + r * 256:3072 + (r + 1) * 256], np.zeros((D, 8), np.float32),
                             W[:, RC0 + r * 256:RC0 + (r + 1) * 256],
                             W[:, RC0 + 1024 + r * 256:RC0 + 1024 + (r + 1) * 256],
                             W[:, RC0 + 2048 + r * 256:RC0 + 2048 + (r + 1) * 256],
                             W[:, RC0 + 3072:RC0 + 3328]], axis=1)
        assert wl.shape[1] == 2056
        wif = np.zeros((D, 8), np.float32)
        wif[:, 0] = W[:, 4096 + r]
        wif[:, 4] = W[:, 4100 + r]
        gbias = np.zeros((128, 8), np.float32)
        gbias[:, 0] = inp["m_i_bias"][0][r]
        gbias[:, 4] = inp["m_f_bias"][0][r]
        gbias[:, 5:8] = 30.0
        m = {"cst": _consts(), "pv": pv, "gbias": gbias,
             "rst": np.ascontiguousarray(np.broadcast_to((np.arange(512)[None, :] % 64 != 0), (128, 512)).astype(np.float32)),
             "w_loc": np.ascontiguousarray(wl), "w_if": wif, "w_gate": gates,
             "r_w2": np.ascontiguousarray(inp["r_w2"][0][:, ps_]), "r_a2": np.ascontiguousarray(inp["r_a2"][0][:, ps_]),
             "r_g2": np.ascontiguousarray(inp["r_g2"][0][:, ps_]),
             "x": np.ascontiguousarray(x[b, r * nbl * TB:(r + 1) * nbl * TB]),
             "oh": np.ascontiguousarray(np.broadcast_to(np.eye(4, dtype=np.float32)[r][None, :], (128, 4)))}
        for n in ("ffn1_w_gate", "ffn1_w_up", "ffn1_w_down", "ffn2_w_gate", "ffn2_w_up", "ffn2_w_down",
                  "w_branch_a", "w_branch_b", "w_out"):
            m[n] = np.ascontiguousarray(inp[n][0], dtype=np.float32)
        in_maps.append(m)
    groups = [list(range(4 * g, 4 * g + 4)) for g in range(ngrp)]
    nc = build(nbl, groups)
    res = run_bass_kernel_spmd(nc, in_maps, core_ids=list(range(4 * ngrp)))
    out = np.zeros((ngrp, 4 * nbl * TB, D), np.float32)
    for c in range(4 * ngrp):
        out[c // 4, (c % 4) * nbl * TB:(c % 4 + 1) * nbl * TB] = np.asarray(res.results[c]["out"])
    return out
```

```python
import contextlib
import os
import numpy as np
import concourse.bass as bass
import concourse.mybir as mybir
from concourse.bass_utils import run_bass_kernel_spmd

F32 = mybir.dt.float32
BF16 = mybir.dt.bfloat16
AF = mybir.ActivationFunctionType
ALU = mybir.AluOpType

D = 1024
DFF = 2816
NJ = DFF // 128
SEQ = 8192
TB = 512
ALPHA = 2.0 ** 0.25
LN_EPS = 1e-5
GN_EPS = 64e-5
NCORES = 8
WCOLS = 9480
RC0 = 4104
SAME_ENGINE_WAITS = os.environ.get("KSEW", "act,dve,pool,sp").split(",")


class Res:
    __slots__ = ("w", "r", "excl")

    def __init__(self, excl=False):
        self.w = None
        self.r = {}
        self.excl = excl


class Sched:
    NDS = 24

    def __init__(self, nc, stack):
        self.nc = nc
        self.E = {"pe": nc.tensor, "act": nc.scalar, "dve": nc.vector, "pool": nc.gpsimd, "sp": nc.sync}
        self.q = {k: [] for k in self.E}
        self.cnt = {k: 0 for k in self.E}
        self.esem = {k: stack.enter_context(nc.semaphore("s_" + k)) for k in self.E}
        self.dsem = [stack.enter_context(nc.semaphore("d%d" % i)) for i in range(self.NDS)]
        self.dval = [0] * self.NDS
        self.dnext = {"sp": 0, "pool": 0}
        self.dpool = {"sp": list(range(0, 8)), "pool": list(range(8, self.NDS))}
        self.waited = {k: {} for k in self.E}
        self.csem = stack.enter_context(nc.semaphore("cc"))
        self.cval = 0

    def _deps(self, eng, reads, writes, extra=()):
        deps = {}

        def add(t):
            if t is None:
                return
            k = (t[0], t[1])
            if deps.get(k, 0) < t[2]:
                deps[k] = t[2]

        for r in reads:
            add(r.w)
        for w in writes:
            add(w.w)
            for k, v in w.r.items():
                add((k[0], k[1], v))
        for t in extra:
            add(t)
        waits = []
        for k, v in deps.items():
            if k[0] == "e" and k[1] == eng and (eng == "pe" or eng not in SAME_ENGINE_WAITS):
                continue
            if self.waited[eng].get(k, 0) >= v:
                continue
            self.waited[eng][k] = v
            waits.append((k, v))
        return waits

    def _mark(self, tok, reads, writes):
        k = (tok[0], tok[1])
        for r in reads:
            if r.r.get(k, 0) < tok[2]:
                r.r[k] = tok[2]
        for w in writes:
            w.w = tok
            w.r = {}

    def op(self, eng, fn, reads=(), writes=()):
        ex = [r for r in reads if r.excl]
        if ex:
            writes = list(writes) + ex
        waits = self._deps(eng, reads, writes)
        self.cnt[eng] += 1
        tok = ("e", eng, self.cnt[eng])
        self.q[eng].append((waits, fn, None))
        self._mark(tok, reads, writes)
        return tok

    def dma(self, qeng, out, in_, reads=(), writes=()):
        pool = self.dpool[qeng]
        j = pool[self.dnext[qeng]]
        self.dnext[qeng] = (self.dnext[qeng] + 1) % len(pool)
        prev = ("d", j, self.dval[j]) if self.dval[j] else None
        extra = [prev] if prev else []
        waits = self._deps(qeng, reads, writes, extra=extra)
        self.dval[j] += 16
        tok = ("d", j, self.dval[j])
        self.q[qeng].append((waits, (out, in_), j))
        self._mark(tok, reads, writes)
        return tok

    def cc(self, kind, ins, outs, groups, reads=(), writes=()):
        waits = self._deps("pool", reads, writes)
        self.cval += 1
        tok = ("c", 0, self.cval)
        self.q["pool"].append((waits, (kind, ins, outs, groups), "cc"))
        self._mark(tok, reads, writes)
        return tok

    def emit(self):
        nc = self.nc
        with nc.Block() as block:
            def run(kind):
                def body(eng):
                    for waits, fn, dj in self.q[kind]:
                        for k, v in waits:
                            sem = self.esem[k[1]] if k[0] == "e" else (self.csem if k[0] == "c" else self.dsem[k[1]])
                            eng.wait_ge(sem, v)
                        if dj is None:
                            fn(eng).then_inc(self.esem[kind], 1)
                        elif dj == "cc":
                            eng.collective_compute(fn[0], ALU.bypass, replica_groups=fn[3], ins=[fn[1]], outs=[fn[2]]).then_inc(self.csem, 1)
                        else:
                            eng.dma_start(out=fn[0], in_=fn[1]).then_inc(self.dsem[dj], 16)
                return body
            block.tensor(run("pe"))
            block.scalar(run("act"))
            block.vector(run("dve"))
            block.gpsimd(run("pool"))
            block.sync(run("sp"))


class Ring:
    def __init__(self, items):
        self.items = items
        self.i = 0

    def next(self):
        it = self.items[self.i]
        self.i = (self.i + 1) % len(self.items)
        return it


PV = {}
_o = 0
for _n, _w in [("ln1_g", 8), ("ln1_b", 8), ("ln2_g", 8), ("ln2_b", 8), ("ln3_g", 8), ("ln3_b", 8),
               ("cw0", 4), ("cw1", 4), ("cw2", 4), ("cw3", 4), ("cb", 4), ("mng", 2),
               ("mu", 8), ("omu", 8), ("w0", 2), ("a0", 2), ("kk", 2), ("ka", 2), ("rrk", 2),
               ("gng", 2), ("gnb", 2)]:
    PV[_n] = _o
    _o += _w
NPV = _o
C_ID, C_OD, C_MUS, C_TRI, C_ONE, C_MLS, C_IST, C_BO, NCST = 0, 128, 256, 384, 512, 640, 768, 832, 960


def build(nbl, groups):
    nc = bass.Bass("TRN2", target_bir_lowering=False)

    def din(name, shape):
        return nc.dram_tensor(name, list(shape), F32, kind="ExternalInput").ap()

    NBT = 4 * nbl
    x_d = din("x", [nbl * TB, D])
    cst_d = din("cst", [128, NCST])
    pv_d = din("pv", [128, NPV])
    gb_d = din("gbias", [128, 8])
    rst_d = din("rst", [128, 512])
    w1g = din("ffn1_w_gate", [D, DFF]); w1u = din("ffn1_w_up", [D, DFF]); w1d = din("ffn1_w_down", [DFF, D])
    w2g = din("ffn2_w_gate", [D, DFF]); w2u = din("ffn2_w_up", [D, DFF]); w2d = din("ffn2_w_down", [DFF, D])
    win = din("w_loc", [D, 2056])
    wif_d = din("w_if", [D, 8])
    wgate = din("w_gate", [D, 2048])
    oh_d = din("oh", [128, 4])
    wa_d = din("w_branch_a", [D, D]); wb_d = din("w_branch_b", [D, D]); wo_d = din("w_out", [D, D])
    rw2_d = din("r_w2", [64, 256]); ra2_d = din("r_a2", [64, 256]); rg2_d = din("r_g2", [128, 256])
    out_d = nc.dram_tensor("out", [nbl * TB, D], F32, kind="ExternalOutput").ap()
    x1f_loc = nc.dram_tensor("x1f_loc", [nbl * 128, 8 * TB], F32).ap()
    x1b_loc = [[nc.dram_tensor("x1bl_%d_%d" % (k, h), [128, 4 * TB], BF16).ap() for h in range(2)] for k in range(nbl)]
    x1b_all = [[nc.dram_tensor("x1ba_%d_%d" % (k, h), [4 * 128, 4 * TB], BF16).ap() for h in range(2)] for k in range(nbl)]
    hy_loc = [nc.dram_tensor("hyl_%d" % k, [128, 4 * TB], BF16).ap() for k in range(NBT)]
    hy_all = [nc.dram_tensor("hya_%d" % k, [4 * 128, 4 * TB], BF16).ap() for k in range(NBT)]

    with contextlib.ExitStack() as st:
        S = Sched(nc, st)
        _n = [0]

        def sb(shape, dt=F32):
            _n[0] += 1
            return st.enter_context(nc.sbuf_tensor("sb%d" % _n[0], list(shape), dt))

        def ring(n, shape, dt=F32):
            return Ring([(sb(shape, dt), Res()) for _ in range(n)])

        banks = [st.enter_context(nc.psum_tensor("ps%d" % i, [128, 512], F32)) for i in range(8)]
        psr = Ring([(banks[i], Res(True)) for i in range(7)])
        pyo_bank = (banks[7], Res(True))

        cst = sb([128, NCST]); cst_r = Res()
        cstb = sb([128, NCST], BF16); cstb_r = Res()
        pv = sb([128, NPV]); pv_r = Res()
        gbias = sb([128, 8]); gb_r = Res()
        S.dma("sp", cst[:, :], cst_d[:, :], writes=[cst_r])
        S.dma("sp", pv[:, :], pv_d[:, :], writes=[pv_r])
        S.dma("sp", gbias[:, :], gb_d[:, :], writes=[gb_r])
        S.op("act", lambda e: e.copy(cstb[:, :], cst[:, :]), reads=[cst_r], writes=[cstb_r])
        ident = cst[:, C_ID:C_ID + 128]
        identb = cstb[:, C_ID:C_ID + 128]
        onesdb = cstb[:, C_OD:C_OD + 128]
        tri = cst[:, C_TRI:C_TRI + 128]
        ones = cst[:, C_ONE:C_ONE + 128]
        istb = cstb[:, C_IST:C_IST + 64]
        bob = cstb[:, C_BO:C_BO + 128]
        rstb = sb([128, 512], BF16)
        S.dma("pool", rstb[:, :], rst_d[:, :], writes=[cstb_r])
        rst = rstb[:, :]
        epsln = sb([128, 1]); eps4 = sb([128, 1]); epsgn = sb([128, 1]); eps_r = Res()
        S.op("dve", lambda e: e.memset(epsln[:, :], LN_EPS), writes=[eps_r])
        S.op("dve", lambda e: e.memset(eps4[:, :], 4 * LN_EPS), writes=[eps_r])
        S.op("dve", lambda e: e.memset(epsgn[:, :], GN_EPS), writes=[eps_r])

        def pvc(name, i=0):
            c = PV[name] + i
            return pv[:, c:c + 1]

        rw2a2 = sb([128, 256], BF16); rw_r = Res()
        rg2 = sb([128, 256], BF16)
        S.dma("pool", rw2a2[0:64, :], rw2_d[:, :], writes=[rw_r])
        S.dma("pool", rw2a2[64:128, :], ra2_d[:, :], writes=[rw_r])
        S.dma("pool", rg2[:, :], rg2_d[:, :], writes=[rw_r])

        xT = sb([128, 8, TB]); xTb = sb([128, 8, TB], BF16); xT_r = [Res() for _ in range(8)]
        x1T = sb([128, 8, TB]); x1Tb = sb([128, 8, TB], BF16); x1_r = [Res() for _ in range(8)]
        x2T = xT; x2Tb = xTb; x2_r = xT_r
        x3T = xT; x3Tb = xTb; x3_r = xT_r
        aT = sb([128, NJ, TB], BF16); aT_r = [Res() for _ in range(NJ)]
        zT = sb([128, 8, TB]); zT_r = [Res() for _ in range(8)]
        xtok = ring(1, [128, D])
        otok = xtok
        wgu = ring(2, [128, 2, 8, 256], BF16)
        wdb = ring(3, [128, 512], BF16)
        wpj = ring(3, [128, 8, 128], BF16)
        tA = ring(3, [128, TB])
        tB = ring(2, [128, TB], BF16)
        mean_r = Res(); rstd_r = Res()
        out_r = Res()

        def load_xT(blk):
            for t in range(TB // 128):
                xt, xr = xtok.next()
                r0 = blk * TB + t * 128
                S.dma("sp", xt[:, :], x_d[r0:r0 + 128, :], writes=[xr])
                for half in range(2):
                    ps, pr = psr.next()
                    for q in range(4):
                        kc = half * 4 + q
                        S.op("pe", lambda e, ps=ps, xt=xt, kc=kc, q=q: e.matmul(
                            ps[:, q * 128:(q + 1) * 128], xt[:, kc * 128:(kc + 1) * 128], ident,
                            start=True, stop=True), reads=[xr, cst_r], writes=[pr])
                    psv = ps[:, :].rearrange("p (q n) -> p q n", q=4)
                    S.op("act", lambda e, psv=psv, half=half, t=t: e.copy(
                        xT[:, half * 4:half * 4 + 4, t * 128:(t + 1) * 128], psv),
                        reads=[pr], writes=xT_r[half * 4:half * 4 + 4])
                    S.op("dve", lambda e, psv=psv, half=half, t=t: e.tensor_copy(
                        xTb[:, half * 4:half * 4 + 4, t * 128:(t + 1) * 128], psv),
                        reads=[pr], writes=xT_r[half * 4:half * 4 + 4])

        def layer_norm(gname, bname, eps_t, outT, outTb, out_rs):
            psm, pmr = psr.next()
            pss, ssr = psr.next()
            for dc in range(8):
                tb, tr = tB.next()
                S.op("act", lambda e, tb=tb, dc=dc: e.copy(tb[:, :], zT[:, dc, :]), reads=[zT_r[dc]], writes=[tr])
                S.op("pe", lambda e, tb=tb, dc=dc: e.matmul(psm[:, :], onesdb, tb[:, :], start=(dc == 0), stop=(dc == 7)),
                     reads=[cstb_r, tr], writes=[pmr])
                tb2, tr2 = tB.next()
                S.op("act", lambda e, tb2=tb2, dc=dc: e.activation(tb2[:, :], zT[:, dc, :], AF.Square),
                     reads=[zT_r[dc]], writes=[tr2])
                S.op("pe", lambda e, tb2=tb2, dc=dc: e.matmul(pss[:, :], onesdb, tb2[:, :], start=(dc == 0), stop=(dc == 7)),
                     reads=[cstb_r, tr2], writes=[ssr])
            S.op("act", lambda e: e.copy(mean_sb[:, :], psm[:, :]), reads=[pmr], writes=[mean_r])
            t1, r1 = tA.next()
            S.op("dve", lambda e, t1=t1: e.tensor_tensor(t1[:, :], mean_sb[:, :], mean_sb[:, :], ALU.mult),
                 reads=[mean_r], writes=[r1])
            t2, r2 = tA.next()
            S.op("dve", lambda e, t1=t1, t2=t2: e.tensor_tensor(t2[:, :], pss[:, :], t1[:, :], ALU.subtract),
                 reads=[ssr, r1], writes=[r2])
            S.op("dve", lambda e, t2=t2: e.tensor_scalar_max(t2[:, :], t2[:, :], 0.0), reads=[r2], writes=[r2])
            t3, r3 = tA.next()
            S.op("act", lambda e, t2=t2, t3=t3: e.activation(t3[:, :], t2[:, :], AF.Sqrt, bias=eps_t[:, 0:1]),
                 reads=[r2, eps_r], writes=[r3])
            S.op("dve", lambda e, t3=t3: e.reciprocal(rstd_sb[:, :], t3[:, :]), reads=[r3], writes=[rstd_r])
            for dc in range(8):
                ta, tar = tA.next()
                S.op("dve", lambda e, ta=ta, dc=dc: e.tensor_tensor(ta[:, :], zT[:, dc, :], mean_sb[:, :], ALU.subtract),
                     reads=[zT_r[dc], mean_r], writes=[tar])
                S.op("dve", lambda e, ta=ta: e.tensor_tensor(ta[:, :], ta[:, :], rstd_sb[:, :], ALU.mult),
                     reads=[tar, rstd_r], writes=[tar])
                S.op("act", lambda e, ta=ta, dc=dc: e.activation(
                    outT[:, dc, :], ta[:, :], AF.Identity, scale=pvc(gname, dc), bias=pvc(bname, dc)),
                    reads=[tar, pv_r], writes=[out_rs[dc]])
                S.op("act", lambda e, ta=ta, dc=dc: e.activation(
                    outTb[:, dc, :], ta[:, :], AF.Identity, scale=pvc(gname, dc), bias=pvc(bname, dc)),
                    reads=[tar, pv_r], writes=[out_rs[dc]])

        def ffn_ln(inT, inTb, in_rs, Wg, Wu, Wd, gname, bname, outT, outTb, out_rs):
            Wg_v = Wg.rearrange("(kc p) f -> p kc f", p=128)
            Wu_v = Wu.rearrange("(kc p) f -> p kc f", p=128)
            Wd_v = Wd.rearrange("(j p) d -> p j d", p=128)
            for j in range(NJ):
                if j % 2 == 0:
                    wb, wr = wgu.next()
                    S.dma("pool", wb[:, 0], Wg_v[:, :, j * 128:(j + 2) * 128], writes=[wr])
                    S.dma("pool", wb[:, 1], Wu_v[:, :, j * 128:(j + 2) * 128], writes=[wr])
                jo = (j % 2) * 128
                psg, pgr = psr.next()
                psu, pur = psr.next()
                for gu, (ps, prr) in enumerate(((psg, pgr), (psu, pur))):
                    for kc in range(8):
                        S.op("pe", lambda e, ps=ps, wb=wb, kc=kc, gu=gu, jo=jo: e.matmul(
                            ps[:, :], wb[:, gu, kc, jo:jo + 128], inTb[:, kc, :], start=(kc == 0), stop=(kc == 7)),
                            reads=[wr, in_rs[kc]], writes=[prr])
                tb, tr = tA.next()
                S.op("act", lambda e, tb=tb, ps=psg: e.activation(tb[:, :], ps[:, :], AF.Silu), reads=[pgr], writes=[tr])
                S.op("dve", lambda e, tb=tb, ps=psu, j=j: e.tensor_tensor(aT[:, j, :], tb[:, :], ps[:, :], ALU.mult),
                     reads=[tr, pur], writes=[aT_r[j]])
            for half in range(2):
                pss_ = [psr.next() for _ in range(4)]
                for j in range(NJ):
                    wb, wr = wdb.next()
                    S.dma("pool", wb[:, :], Wd_v[:, j, half * 512:(half + 1) * 512], writes=[wr])
                    for q in range(4):
                        ps, pr = pss_[q]
                        S.op("pe", lambda e, ps=ps, wb=wb, j=j, q=q: e.matmul(
                            ps[:, :], wb[:, q * 128:(q + 1) * 128], aT[:, j, :], start=(j == 0), stop=(j == NJ - 1)),
                            reads=[wr, aT_r[j]], writes=[pr])
                for q in range(4):
                    dc = half * 4 + q
                    ps, pr = pss_[q]
                    S.op("dve", lambda e, ps=ps, dc=dc: e.scalar_tensor_tensor(
                        zT[:, dc, :], inT[:, dc, :], 2.0 * ALPHA, ps[:, :], ALU.mult, ALU.add),
                        reads=[pr, in_rs[dc]], writes=[zT_r[dc]])
            layer_norm(gname, bname, eps4, outT, outTb, out_rs)

        def store_T(blk, srcT, src_rs):
            for t in range(TB // 128):
                ot, orr = otok.next()
                for half in range(2):
                    ps, pr = psr.next()
                    for q in range(4):
                        kc = half * 4 + q
                        S.op("pe", lambda e, ps=ps, kc=kc, q=q, t=t: e.matmul(
                            ps[:, q * 128:(q + 1) * 128], srcT[:, kc, t * 128:(t + 1) * 128], ident,
                            start=True, stop=True), reads=[src_rs[kc], cst_r], writes=[pr])
                    S.op("act", lambda e, ps=ps, ot=ot, half=half: e.copy(ot[:, half * 512:(half + 1) * 512], ps[:, :]),
                         reads=[pr], writes=[orr])
                r0 = blk * TB + t * 128
                S.dma("sp", out_d[r0:r0 + 128, :], ot[:, :], reads=[orr], writes=[out_r])

        def projT(Wd_ap, col0, inTb, in_rs, ncols=128, wring=None, pring=None):
            wb, wr = (wring or wpj).next()
            Wv = Wd_ap.rearrange("(kc p) f -> p kc f", p=128)
            S.dma("pool", wb[:, :, 0:ncols], Wv[:, :, col0:col0 + ncols], writes=[wr])
            ps, pr = (pring or psr).next()
            for kc in range(8):
                src = inTb(kc) if callable(inTb) else inTb[:, kc, :]
                S.op("pe", lambda e, ps=ps, wb=wb, kc=kc, src=src: e.matmul(
                    ps[0:ncols, :], wb[:, kc, 0:ncols], src, start=(kc == 0), stop=(kc == 7)),
                    reads=[wr, in_rs[kc]], writes=[pr])
            return ps, pr

        NT = TB // 128
        carry = sb([128, 4, 3]); carry_r = [Res() for _ in range(4)]
        S.op("dve", lambda e: e.memset(carry[:, :, :], 0.0), writes=carry_r)
        cwork = ring(2, [128, 3 + TB])
        mean_sb = cwork.items[0][0][:, 0:TB]; rstd_sb = cwork.items[1][0][:, 0:TB]
        qkT = zT[:, :, :].rearrange("p a b -> p (a b)").bitcast(BF16).rearrange("p (c n) -> p c n", n=TB)
        qk_r = [zT_r[c // 2] for c in range(16)]
        sigmo = sb([128, 2, TB], BF16); sigmo_r = [Res() for _ in range(2)]
        sga = sb([128, 8, TB], BF16); sga_r = [Res() for _ in range(8)]
        sgb = sb([128, 8, TB], BF16); sgb_r = [Res() for _ in range(8)]
        vt = sb([128, NT, 1, 258], BF16); vt_r = [[Res() for _ in range(1)] for _ in range(NT)]
        S.op("dve", lambda e: e.memset(vt[:, :, :, :], 1.0), writes=[r for rr in vt_r for r in rr])
        wv = ring(1, [128, 8, 256], BF16)
        wif = sb([128, 8, 8], BF16); wif_r = Res()
        S.dma("pool", wif[:, :, :], wif_d.rearrange("(kc p) f -> p kc f", p=128), writes=[wif_r])
        gts = sb([128, NT, 24]); gts_r = [Res() for _ in range(NT)]
        Cst = sb([128, 1, 2, 258]); C_r = [Res() for _ in range(1)]
        Cb = sb([128, 1, 2, 258], BF16); Cb_r = [Res() for _ in range(1)]
        S.op("dve", lambda e: e.memset(Cst[:, :, :, :], 0.0), writes=C_r)
        S.op("dve", lambda e: e.memset(Cb[:, :, :, :], 0.0), writes=Cb_r)
        hyT = sb([128, 4, TB], BF16); hm_r = [Res() for _ in range(2)]
        hmTb = hyT[:, 0:2, :]
        HYf = sb([128, 4, 4, TB], BF16); hyf_r = [Res() for _ in range(4)]
        yrTb = hyT[:, 2:4, :]; yr_r = [Res() for _ in range(2)]
        mgTb = sga; mg_r = sga_r
        smr = ring(4, [128, 128], BF16)
        ktk = ring(4, [128, 256], BF16)
        sm6 = ring(4, [128, 8])

        def mlstm(blk):
            Wv = win.rearrange("(kc p) f -> p kc f", p=128)
            for c in range(4):
                ps, pr = projT(win, c * 128, x1Tb, x1_r, wring=wpjM, pring=psrM)
                wk, wkr = cwork.next()
                S.op("act", lambda e, wk=wk, c=c: e.copy(wk[:, 0:3], carry[:, c, :]), reads=[carry_r[c]], writes=[wkr])
                yield
                S.op("act", lambda e, wk=wk, ps=ps: e.copy(wk[:, 3:3 + TB], ps[:, :]), reads=[pr], writes=[wkr])
                yield
                S.op("act", lambda e, wk=wk, c=c: e.copy(carry[:, c, :], wk[:, TB:TB + 3]), reads=[wkr], writes=[carry_r[c]])
                yield
                ta, tar = tAM.next()
                S.op("dve", lambda e, wk=wk, ta=ta, c=c: e.tensor_scalar(
                    ta[:, :], wk[:, 0:TB], pvc("cw0", c), pvc("cb", c), ALU.mult, ALU.add),
                    reads=[wkr, pv_r], writes=[tar])
                yield
                for j in (1, 2, 3):
                    S.op("dve", lambda e, wk=wk, ta=ta, c=c, j=j: e.scalar_tensor_tensor(
                        ta[:, :], wk[:, j:j + TB], pvc("cw%d" % j, c), ta[:, :], ALU.mult, ALU.add),
                        reads=[wkr, pv_r, tar], writes=[tar])
                    yield
                S.op("act", lambda e, ta=ta, c=c: e.activation(qkT[:, c, :], ta[:, :], AF.Silu),
                     reads=[tar], writes=[qk_r[c]])
                yield
            for c in range(2):
                ps, pr = projT(win, 768 + c * 128, x1Tb, x1_r, wring=wpjM, pring=psrM)
                S.op("act", lambda e, ps=ps, c=c: e.activation(sigmo[:, c, :], ps[:, :], AF.Sigmoid),
                     reads=[pr], writes=[sigmo_r[c]])
                yield
            for t in range(NT):
                ps, pr = psrM.next()
                for kc in range(8):
                    S.op("pe", lambda e, ps=ps, kc=kc, t=t: e.matmul(
                        ps[:, 0:8], x1Tb[:, kc, t * 128:(t + 1) * 128], wif[:, kc, :], start=(kc == 0), stop=(kc == 7)),
                        reads=[wif_r, x1_r[kc]], writes=[pr])
                    yield
                g = gts[:, t, :]
                gr = gts_r[t]
                S.op("dve", lambda e, g=g, ps=ps: e.tensor_tensor(g[:, 12:20], ps[:, 0:8], gbias[:, :], ALU.add),
                     reads=[pr, gb_r], writes=[gr])
                yield
                S.op("act", lambda e, g=g: e.activation(g[:, 20:24], g[:, 16:20], AF.Exp, scale=-1.0), reads=[gr], writes=[gr])
                yield
                S.op("act", lambda e, g=g: e.activation(g[:, 16:20], g[:, 20:24], AF.Ln, bias=1.0), reads=[gr], writes=[gr])
                yield
                S.op("dve", lambda e, g=g: e.tensor_scalar_mul(g[:, 16:20], g[:, 16:20], -1.0), reads=[gr], writes=[gr])
                yield
                ps2, pr2 = psrM.next()
                S.op("pe", lambda e, ps2=ps2, g=g: e.matmul(ps2[:, 0:4], tri, g[:, 16:20], start=True, stop=True),
                     reads=[gr, cst_r], writes=[pr2])
                yield
                S.op("pe", lambda e, ps2=ps2, g=g: e.matmul(ps2[:, 4:8], ones, g[:, 16:20], start=True, stop=True),
                     reads=[gr, cst_r], writes=[pr2])
                yield
                S.op("dve", lambda e, g=g, ps2=ps2: e.tensor_tensor(g[:, 20:24], g[:, 12:16], ps2[:, 0:4], ALU.subtract),
                     reads=[gr, pr2], writes=[gr])
                yield
                S.op("act", lambda e, g=g: e.activation(g[:, 0:4], g[:, 20:24], AF.Exp), reads=[gr], writes=[gr])
                yield
                S.op("act", lambda e, g=g, ps2=ps2: e.activation(g[:, 4:8], ps2[:, 0:4], AF.Exp, scale=-1.0),
                     reads=[gr, pr2], writes=[gr])
                yield
                S.op("act", lambda e, g=g, ps2=ps2: e.activation(g[:, 8:12], ps2[:, 4:8], AF.Exp), reads=[gr, pr2], writes=[gr])
                yield
            for h in range(1):
                wb, wr = wv.next()
                S.dma("pool", wb[:, :, :], Wv[:, :, 512:768], writes=[wr])
                yield
                for t in range(NT):
                    ps, pr = psrM.next()
                    for kc in range(8):
                        S.op("pe", lambda e, ps=ps, kc=kc, t=t, wb=wb: e.matmul(
                            ps[:, 0:256], x1Tb[:, kc, t * 128:(t + 1) * 128], wb[:, kc, :], start=(kc == 0), stop=(kc == 7)),
                            reads=[wr, x1_r[kc]], writes=[pr])
                        yield
                    S.op("act", lambda e, ps=ps, t=t, h=h: e.copy(vt[:, t, h, 0:256], ps[:, 0:256]),
                         reads=[pr], writes=[vt_r[t][h]])
                    yield
            for t in range(NT):
                tc_ = slice(t * 128, (t + 1) * 128)
                g = gts[:, t, :]
                gr = gts_r[t]
                def head_chain(t, h, tc_, g, gr):
                    qc = [h * 2, h * 2 + 1]
                    kc_ = [2 + h * 2, 2 + h * 2 + 1]
                    ps, pr = psrM.next()
                    for i in range(2):
                        S.op("pe", lambda e, ps=ps, i=i, kc_=kc_, qc=qc, tc_=tc_: e.matmul(
                            ps[:, 0:128], qkT[:, kc_[i], tc_], qkT[:, qc[i], tc_], start=(i == 0), stop=(i == 1)),
                            reads=[qk_r[kc_[i]], qk_r[qc[i]]], writes=[pr])
                        yield
                    sm, smrr = smr.next()
                    S.op("dve", lambda e, sm=sm, ps=ps, h=h, g=g: e.scalar_tensor_tensor(
                        sm[:, :], ps[:, 0:128], g[:, h:h + 1], tri, ALU.mult, ALU.mult),
                        reads=[pr, gr, cst_r], writes=[smrr])
                    yield
                    po, por = psrM.next()
                    S.op("pe", lambda e, po=po, sm=sm, t=t, h=h: e.matmul(
                        po[:, 0:258], sm[:, :], vt[:, t, h, :], start=True, stop=False),
                        reads=[smrr, vt_r[t][h]], writes=[por])
                    yield
                    for i in range(2):
                        S.op("pe", lambda e, po=po, i=i, h=h, qc=qc, tc_=tc_: e.matmul(
                            po[:, 0:258], qkT[:, qc[i], tc_], Cb[:, h, i, :], start=False, stop=(i == 1)),
                            reads=[qk_r[qc[i]], Cb_r[h]], writes=[por])
                        yield
                    s6, s6r = sm6.next()
                    S.op("act", lambda e, s6=s6, po=po: e.activation(
                        s6[:, 0:1], po[:, 256:257], AF.Abs, scale=1.0 / 16.0), reads=[por], writes=[s6r])
                    yield
                    S.op("dve", lambda e, s6=s6, g=g, h=h: e.tensor_tensor(s6[:, 0:1], s6[:, 0:1], g[:, 4 + h:5 + h], ALU.max),
                         reads=[s6r, gr], writes=[s6r])
                    yield
                    S.op("dve", lambda e, s6=s6: e.reciprocal(s6[:, 1:2], s6[:, 0:1]), reads=[s6r], writes=[s6r])
                    yield
                    hb, hbr = hh.next()
                    S.op("dve", lambda e, hb=hb, po=po, s6=s6: e.tensor_scalar(
                        hb[:, :], po[:, 0:256], s6[:, 1:2], 1.0 / 16.0, ALU.mult, ALU.mult), reads=[por, s6r], writes=[hbr])
                    yield
                    S.op("dve", lambda e, hb=hb, s6=s6: e.bn_stats(s6[:, 2:8], hb[:, :]), reads=[hbr], writes=[s6r])
                    yield
                    S.op("dve", lambda e, s6=s6: e.bn_aggr(s6[:, 0:2], s6[:, 2:8]), reads=[s6r], writes=[s6r])
                    yield
                    S.op("act", lambda e, s6=s6: e.activation(s6[:, 2:3], s6[:, 1:2], AF.Sqrt, bias=epsln[:, 0:1]),
                         reads=[s6r, eps_r], writes=[s6r])
                    yield
                    S.op("dve", lambda e, s6=s6: e.reciprocal(s6[:, 3:4], s6[:, 2:3]), reads=[s6r], writes=[s6r])
                    yield
                    hn, hnr = hnb.next()
                    S.op("dve", lambda e, hn=hn, hb=hb, s6=s6: e.tensor_scalar(
                        hn[:, :], hb[:, :], s6[:, 0:1], s6[:, 3:4], ALU.subtract, ALU.mult), reads=[hbr, s6r], writes=[hnr])
                    yield
                    for i in range(2):
                        pt, ptr = psrM.next()
                        S.op("pe", lambda e, pt=pt, hn=hn, i=i: e.matmul(
                            pt[:, 0:128], hn[:, i * 128:(i + 1) * 128], identb, start=True, stop=True),
                            reads=[hnr, cstb_r], writes=[ptr])
                        yield
                        S.op("dve", lambda e, pt=pt, h=h, i=i, tc_=tc_: e.scalar_tensor_tensor(
                            hmTb[:, h * 2 + i, tc_], pt[:, 0:128], pvc("mng", h * 2 + i), sigmo[:, h * 2 + i, tc_],
                            ALU.mult, ALU.mult), reads=[ptr, pv_r, sigmo_r[h * 2 + i]], writes=[hm_r[h * 2 + i]])
                        yield
                    kk_, kkr = ktk.next()
                    for i in range(2):
                        pt, ptr = psrM.next()
                        S.op("pe", lambda e, pt=pt, i=i, kc_=kc_, tc_=tc_: e.matmul(
                            pt[:, 0:128], qkT[:, kc_[i], tc_], identb, start=True, stop=True),
                            reads=[qk_r[kc_[i]], cstb_r], writes=[ptr])
                        yield
                        S.op("act", lambda e, pt=pt, kk_=kk_, i=i, g=g, h=h: e.activation(
                            kk_[:, i * 128:(i + 1) * 128], pt[:, 0:128], AF.Identity, scale=g[:, h:h + 1]),
                            reads=[ptr, gr], writes=[kkr])
                        yield
                    for i in range(2):
                        pc, pcr = psrM.next()
                        S.op("pe", lambda e, pc=pc, kk_=kk_, i=i, t=t, h=h: e.matmul(
                            pc[:, 0:258], kk_[:, i * 128:(i + 1) * 128], vt[:, t, h, :], start=True, stop=True),
                            reads=[kkr, vt_r[t][h]], writes=[pcr])
                        yield
                        S.op("dve", lambda e, h=h, i=i, g=g: e.tensor_scalar_mul(Cst[:, h, i, :], Cst[:, h, i, :], g[:, 8 + h:9 + h]),
                             reads=[C_r[h], gr], writes=[C_r[h]])
                        yield
                        S.op("dve", lambda e, pc=pc, h=h, i=i, g=g: e.scalar_tensor_tensor(
                            Cst[:, h, i, :], pc[:, 0:258], g[:, 8 + h:9 + h], Cst[:, h, i, :], ALU.mult, ALU.add),
                            reads=[pcr, gr, C_r[h]], writes=[C_r[h]])
                        yield
                        S.op("act", lambda e, h=h, i=i: e.copy(Cb[:, h, i, :], Cst[:, h, i, :]), reads=[C_r[h]], writes=[Cb_r[h]])
                        yield

                gens = [head_chain(t, h, tc_, g, gr) for h in range(1)]
                while gens:
                    for gg in list(gens):
                        try:
                            next(gg)
                            yield
                        except StopIteration:
                            gens.remove(gg)

        NCH = TB // 64
        rcar = sb([128, 8, 1]); rcar_r = [Res() for _ in range(8)]
        S.op("dve", lambda e: e.memset(rcar[:, :, :], 0.0), writes=rcar_r)
        psrM = Ring(psr.items[0:3]); psrR = Ring(psr.items[3:7])
        tAM = Ring([(x1T[:, i, :], Res()) for i in range(2)])
        rwork = Ring([(x1T[:, 2 + 2 * i:4 + 2 * i, :].rearrange("p a b -> p (a b)")[:, 0:1 + TB], Res()) for i in range(2)])
        wpjM = Ring([(x1T[:, 6 + i, :].bitcast(BF16).rearrange("p (k c) -> p k c", c=128), Res()) for i in range(2)])
        scr_r = [r for _, r in tAM.items + rwork.items + wpjM.items]
        lowT = sb([128, 2, TB], BF16); low_r = [Res(), Res()]
        rtmp = Ring([(aT[:, 2 * i:2 * i + 2, :].rearrange("p a b -> p (a b)").bitcast(F32), Res()) for i in range(10)])
        ARbd = sb([128, NCH, 256], BF16); Bbd = sb([128, NCH, 128], BF16); Kbd = sb([128, NCH, 128], BF16)
        Vbd = sb([128, NCH, 128], BF16); Ynbd = sb([128, 128], BF16)
        bd_r = Res(); ynbd_r = Res()
        for tns in (ARbd, Bbd, Kbd, Vbd):
            S.op("dve", lambda e, tns=tns: e.memset(tns[:, :, :], 0.0), writes=[bd_r])
        S.op("dve", lambda e: e.memset(Ynbd[:, :], 0.0), writes=[ynbd_r])
        gam = sb([128, NCH]); gam_r = Res()
        Hst = sb([128, 2, 64]); H_r = [Res() for _ in range(2)]
        Hb = sb([128, 2, 64], BF16); Hb_r = [Res() for _ in range(2)]
        S.op("dve", lambda e: e.memset(Hst[:, :, :], 0.0), writes=H_r)
        S.op("dve", lambda e: e.memset(Hb[:, :, :], 0.0), writes=Hb_r)
        vst = sb([128, NCH, 64], BF16); vst_r = Res()
        btk = sb([128, NCH, 256], BF16); btk_r = Res()
        ub = ring(4, [128, 64], BF16)
        gn6 = ring(4, [128, 8])
        htmp = ring(2, [128, 64])
        maskAR = cst[:, C_MUS:C_MUS + 256]; mask_r = cst_r
        mls = cst[:, C_MLS:C_MLS + 128]

        xTv = xT[:, :, :].rearrange("p a b -> p (a b)").bitcast(BF16)
        mAK = xTv[:, 0:NCH * 512].rearrange("p (n c) -> p n c", c=512)
        Xs = xTv[:, 4096:4096 + NCH * 128].rearrange("p (n c) -> p n c", c=128)
        xTbv = xTb[:, :, :].rearrange("p a b -> p (a b)")
        PW = [[xTbv[:, (b * 8 + n) * 128:(b * 8 + n + 1) * 128] for n in range(NCH)] for b in range(2)]
        PWT = [[xTbv[:, 2048 + (b * 8 + n) * 128:2048 + (b * 8 + n + 1) * 128] for n in range(NCH)] for b in range(2)]
        hh = Ring([(xTv[:, 5120 + i * 512:5120 + (i + 1) * 512].bitcast(F32), Res()) for i in range(4)])
        hnb = Ring([(xTv[:, 7168 + i * 256:7168 + (i + 1) * 256], Res()) for i in range(4)])
        mA_r = [Res() for _ in range(NCH)]; mK_r = [Res() for _ in range(NCH)]; X_r = [Res() for _ in range(NCH)]
        PW_r = [[Res() for _ in range(NCH)] for _ in range(2)]; PWT_r = [[Res() for _ in range(NCH)] for _ in range(2)]

        def v3(ap):
            return ap.rearrange("p (n l) -> p n l", l=64)

        def shifted(ci, ps, pr):
            wk, wkr = rwork.next()
            S.op("act", lambda e, wk=wk: e.copy(wk[:, 0:1], rcar[:, ci, :]), reads=[rcar_r[ci]], writes=[wkr])
            S.op("act", lambda e, wk=wk, ps=ps: e.copy(wk[:, 1:1 + TB], ps[:, :]), reads=[pr], writes=[wkr])
            S.op("act", lambda e, wk=wk: e.copy(rcar[:, ci, :], wk[:, TB:TB + 1]), reads=[wkr], writes=[rcar_r[ci]])
            ta, tar = rtmp.next()
            S.op("dve", lambda e, wk=wk, ta=ta: e.tensor_scalar_mul(ta[:, :], wk[:, 1:1 + TB], pvc("omu", ci)),
                 reads=[wkr, pv_r], writes=[tar])
            S.op("dve", lambda e, wk=wk, ta=ta: e.scalar_tensor_tensor(
                ta[:, :], wk[:, 0:TB], pvc("mu", ci), ta[:, :], ALU.mult, ALU.add), reads=[wkr, pv_r, tar], writes=[tar])
            return ta, tar

        krw = int(os.environ.get("KRW", "9"))

        def rwkv(blk):
            ps, pr = projT(win, 1800, x1Tb, x1_r, pring=psrR)
            ta, tar = shifted(6, ps, pr)
            S.op("act", lambda e, ta=ta: e.activation(lowT[0:64, 0, :], ta[0:64, :], AF.Tanh), reads=[tar], writes=[low_r[0]])
            yield
            S.op("act", lambda e, ta=ta: e.copy(lowT[64:128, 0, :], ta[64:128, :]), reads=[tar], writes=[low_r[0]])
            yield
            ps, pr = projT(win, 1928, x1Tb, x1_r, pring=psrR)
            ta, tar = shifted(7, ps, pr)
            S.op("act", lambda e, ta=ta: e.activation(lowT[:, 1, :], ta[:, :], AF.Sigmoid), reads=[tar], writes=[low_r[1]])
            yield
            for p in range(2):
                cs = slice(p * 128, (p + 1) * 128)
                ps, pr = projT(win, 1032 + p * 128, x1Tb, x1_r, pring=psrR)
                r_, r_r = shifted(p, ps, pr)
                ps, pr = projT(win, 1288 + p * 128, x1Tb, x1_r, pring=psrR)
                k_, k_r = shifted(2 + p, ps, pr)
                ps, pr = projT(win, 1544 + p * 128, x1Tb, x1_r, pring=psrR)
                v_, v_r = shifted(4 + p, ps, pr)
                pw, pwr = psrR.next()
                S.op("pe", lambda e, pw=pw, cs=cs: e.matmul(pw[:, :], rw2a2[0:64, cs], lowT[0:64, 0, :], start=True, stop=True),
                     reads=[rw_r, low_r[0]], writes=[pwr])
                yield
                lw, lwr = rtmp.next()
                S.op("act", lambda e, lw=lw, pw=pw, p=p: e.activation(lw[:, :], pw[:, :], AF.Sigmoid, bias=pvc("w0", p)),
                     reads=[pwr, pv_r], writes=[lwr])
                yield
                S.op("dve", lambda e, lw=lw: e.tensor_scalar_mul(lw[:, :], lw[:, :], -float(np.exp(-0.5))), reads=[lwr], writes=[lwr])
                yield
                pa, par = psrR.next()
                S.op("pe", lambda e, pa=pa, cs=cs: e.matmul(pa[:, :], rw2a2[64:128, cs], lowT[64:128, 0, :], start=True, stop=True),
                     reads=[rw_r, low_r[0]], writes=[par])
                yield
                a_, a_r = rtmp.next()
                S.op("act", lambda e, a_=a_, pa=pa, p=p: e.activation(a_[:, :], pa[:, :], AF.Sigmoid, bias=pvc("a0", p)),
                     reads=[par, pv_r], writes=[a_r])
                yield
                pg, pgr = psrR.next()
                S.op("pe", lambda e, pg=pg, cs=cs: e.matmul(pg[:, :], rg2[:, cs], lowT[:, 1, :], start=True, stop=True),
                     reads=[rw_r, low_r[1]], writes=[pgr])
                yield
                g_, g_r = rtmp.next()
                S.op("act", lambda e, g_=g_, pg=pg: e.copy(g_[:, :], pg[:, :]), reads=[pgr], writes=[g_r])
                yield
                kk, kkr = rtmp.next()
                S.op("dve", lambda e, kk=kk, k_=k_, p=p: e.tensor_scalar_mul(kk[:, :], k_[:, :], pvc("kk", p)),
                     reads=[k_r, pv_r], writes=[kkr])
                yield
                sq, sqr = tB.next()
                S.op("act", lambda e, sq=sq, kk=kk: e.activation(sq[:, :], kk[:, :], AF.Square), reads=[kkr], writes=[sqr])
                yield
                pq, pqr = psrR.next()
                S.op("pe", lambda e, pq=pq, sq=sq: e.matmul(pq[:, :], bob, sq[:, :], start=True, stop=True),
                     reads=[cstb_r, sqr], writes=[pqr])
                yield
                t1, t1r = rtmp.next()
                S.op("act", lambda e, t1=t1, pq=pq: e.activation(t1[:, :], pq[:, :], AF.Sqrt), reads=[pqr], writes=[t1r])
                yield
                S.op("dve", lambda e, t1=t1: e.tensor_scalar_max(t1[:, :], t1[:, :], 1e-12), reads=[t1r], writes=[t1r])
                yield
                S.op("dve", lambda e, t1=t1: e.reciprocal(t1[:, :], t1[:, :]), reads=[t1r], writes=[t1r])
                yield
                S.op("dve", lambda e, t1=t1, kk=kk: e.tensor_tensor(kk[:, :], kk[:, :], t1[:, :], ALU.mult),
                     reads=[t1r, kkr], writes=[kkr])
                yield
                S.op("dve", lambda e, t1=t1, a_=a_, p=p: e.tensor_scalar(t1[:, :], a_[:, :], 1.0, pvc("ka", p), ALU.subtract, ALU.mult),
                     reads=[a_r, pv_r], writes=[t1r])
                yield
                S.op("dve", lambda e, t1=t1, k_=k_: e.scalar_tensor_tensor(k_[:, :], t1[:, :], 1.0, k_[:, :], ALU.add, ALU.mult),
                     reads=[t1r, k_r], writes=[k_r])
                yield
                t2, t2r = tB.next()
                S.op("dve", lambda e, t2=t2, r_=r_, k_=k_, p=p: e.scalar_tensor_tensor(
                    t2[:, :], r_[:, :], pvc("rrk", p), k_[:, :], ALU.mult, ALU.mult), reads=[r_r, k_r, pv_r], writes=[t2r])
                yield
                pb, pbr = psrR.next()
                S.op("pe", lambda e, pb=pb, t2=t2: e.matmul(pb[:, :], bob, t2[:, :], start=True, stop=True),
                     reads=[cstb_r, t2r], writes=[pbr])
                yield
                bon, bonr = rtmp.next()
                S.op("dve", lambda e, bon=bon, pb=pb, v_=v_: e.tensor_tensor(bon[:, :], pb[:, :], v_[:, :], ALU.mult),
                     reads=[pbr, v_r], writes=[bonr])
                yield
                cl, clr = rtmp.next()
                S.op("dve", lambda e, cl=cl, lw=lw: e.tensor_tensor_scan(cl[:, :], rst, lw[:, :], 0.0, ALU.mult, ALU.add),
                     reads=[lwr, cstb_r], writes=[clr])
                yield
                e1, e1r = tA.next()
                S.op("act", lambda e, e1=e1, cl=cl: e.activation(e1[:, :], cl[:, :], AF.Exp), reads=[clr], writes=[e1r])
                yield
                S.op("act", lambda e, e1=e1: e.copy(gam[:, :], v3(e1[:, :])[:, :, 63]), reads=[e1r], writes=[gam_r])
                yield
                for hf in range(2):
                    hs = slice(hf * 64, hf * 64 + 64)
                    S.op("dve", lambda e, hs=hs, hf=hf, r_=r_, e1=e1: e.tensor_tensor(
                        ARbd[hs, :, 128 + hf * 64:128 + hf * 64 + 64], v3(r_[hs, :]), v3(e1[hs, :]), ALU.mult),
                        reads=[r_r, e1r], writes=[bd_r])
                    yield
                e2, e2r = tA.next()
                S.op("act", lambda e, e2=e2, cl=cl: e.activation(e2[:, :], cl[:, :], AF.Exp, scale=-1.0), reads=[clr], writes=[e2r])
                yield
                S.op("dve", lambda e, t1=t1, kk=kk, a_=a_: e.tensor_tensor(t1[:, :], kk[:, :], a_[:, :], ALU.mult),
                     reads=[kkr, a_r], writes=[t1r])
                yield
                for hf in range(2):
                    hs = slice(hf * 64, hf * 64 + 64)
                    S.op("dve", lambda e, hs=hs, hf=hf, t1=t1, e2=e2: e.tensor_tensor(
                        Bbd[hs, :, hf * 64:hf * 64 + 64], v3(t1[hs, :]), v3(e2[hs, :]), ALU.mult), reads=[t1r, e2r], writes=[bd_r])
                    yield
                    S.op("dve", lambda e, hs=hs, hf=hf, k_=k_, e2=e2: e.tensor_tensor(
                        Kbd[hs, :, hf * 64:hf * 64 + 64], v3(k_[hs, :]), v3(e2[hs, :]), ALU.mult), reads=[k_r, e2r], writes=[bd_r])
                    yield
                    S.op("act", lambda e, hs=hs, hf=hf, v_=v_: e.copy(Vbd[hs, :, hf * 64:hf * 64 + 64], v3(v_[hs, :])),
                         reads=[v_r], writes=[bd_r])
                    yield
                S.op("dve", lambda e, cl=cl, lw=lw: e.tensor_tensor(cl[:, :], cl[:, :], lw[:, :], ALU.subtract),
                     reads=[clr, lwr], writes=[clr])
                yield
                e3, e3r = tA.next()
                S.op("act", lambda e, e3=e3, cl=cl: e.activation(e3[:, :], cl[:, :], AF.Exp), reads=[clr], writes=[e3r])
                yield
                for hf in range(2):
                    hs = slice(hf * 64, hf * 64 + 64)
                    S.op("dve", lambda e, hs=hs, hf=hf, kk=kk, e3=e3: e.scalar_tensor_tensor(
                        ARbd[hs, :, hf * 64:hf * 64 + 64], v3(kk[hs, :]), -1.0, v3(e3[hs, :]), ALU.mult, ALU.mult),
                        reads=[kkr, e3r], writes=[bd_r])
                    yield
                if krw <= 1:
                    S.op("dve", lambda e, p=p: e.memset(yrTb[:, p, :], 0.0), writes=[yr_r[p]])
                    yield
                    continue
                pv_, pvr_ = psrR.next()
                for n in range(NCH):
                    S.op("pe", lambda e, n=n, pv_=pv_: e.matmul(pv_[:, n * 64:(n + 1) * 64], Vbd[:, n, :], istb, start=True, stop=True),
                         reads=[bd_r, cstb_r], writes=[pvr_])
                    yield
                S.op("act", lambda e, pv_=pv_: e.copy(vst[:, :, :], v3(pv_[:, :])), reads=[pvr_], writes=[vst_r])
                yield
                for n0 in range(0, NCH, 2):
                    pt, ptr = psrR.next()
                    for n in (n0, n0 + 1):
                        o = (n - n0) * 256
                        S.op("pe", lambda e, n=n, pt=pt, o=o: e.matmul(pt[:, o:o + 128], Bbd[:, n, :], identb, start=True, stop=True),
                             reads=[bd_r, cstb_r], writes=[ptr])
                        yield
                        S.op("pe", lambda e, n=n, pt=pt, o=o: e.matmul(pt[:, o + 128:o + 256], Kbd[:, n, :], identb, start=True, stop=True),
                             reads=[bd_r, cstb_r], writes=[ptr])
                        yield
                    S.op("act", lambda e, n0=n0, pt=pt: e.copy(btk[:, n0:n0 + 2, :], pt[:, :].rearrange("p (n l) -> p n l", l=256)),
                         reads=[ptr], writes=[btk_r])
                    yield
                if krw <= 2:
                    S.op("dve", lambda e, p=p: e.memset(yrTb[:, p, :], 0.0), writes=[yr_r[p]])
                    yield
                    continue
                pyo, pyor = pyo_bank
                for g0 in range(0, NCH, 4):
                    G = list(range(g0, min(g0 + 4, NCH)))
                    pAs = {}
                    for n in G:
                        pA, pAr = psrR.next()
                        pAs[n] = (pA, pAr)
                        S.op("pe", lambda e, pA=pA, n=n: e.matmul(pA[:, 0:256], Bbd[:, n, :], ARbd[:, n, :], start=True, stop=True),
                             reads=[bd_r], writes=[pAr])
                        yield
                        S.op("pe", lambda e, pA=pA, n=n: e.matmul(pA[:, 256:512], Kbd[:, n, :], ARbd[:, n, :], start=True, stop=True),
                             reads=[bd_r], writes=[pAr])
                        yield
                    for n in G:
                        pA, pAr = pAs[n]
                        S.op("dve", lambda e, pA=pA, n=n: e.tensor_tensor(mAK[:, n, 0:256], pA[:, 0:256], maskAR[:, :], ALU.mult),
                             reads=[pAr, mask_r], writes=[mA_r[n]])
                        yield
                        S.op("dve", lambda e, pA=pA, n=n: e.tensor_tensor(mAK[:, n, 256:512], pA[:, 256:512], maskAR[:, :], ALU.mult),
                             reads=[pAr, mask_r], writes=[mK_r[n]])
                        yield
                    pTs = {}
                    for n in G:
                        pT, pTr = psrR.next()
                        pTs[n] = (pT, pTr)
                        S.op("pe", lambda e, pT=pT, n=n: e.matmul(pT[:, 0:128], ARbd[:, n, 0:128], Bbd[:, n, :], start=True, stop=True),
                             reads=[bd_r], writes=[pTr])
                        yield
                    for n in G:
                        pT, pTr = pTs[n]
                        S.op("dve", lambda e, pT=pT, n=n: e.tensor_tensor(PWT[0][n], pT[:, 0:128], mls, ALU.mult),
                             reads=[pTr, cst_r], writes=[PWT_r[0][n]])
                        yield
                        S.op("dve", lambda e, n=n: e.tensor_tensor(Xs[:, n, :], mAK[:, n, 0:128], identb, ALU.add),
                             reads=[mA_r[n], cstb_r], writes=[X_r[n]])
                        yield
                    for j in range(1, 6):
                        b0, b1 = (j - 1) % 2, j % 2
                        p2s = {}
                        for n in G:
                            cur = mAK[:, n, 0:128] if j == 1 else PW[b0][n]
                            curr = mA_r[n] if j == 1 else PW_r[b0][n]
                            ct, ctr_ = PWT[b0][n], PWT_r[b0][n]
                            p2, p2r = psrR.next()
                            p2s[n] = (p2, p2r)
                            if j < 5:
                                S.op("pe", lambda e, p2=p2, cur=cur, ct=ct: e.matmul(p2[:, 0:128], ct, cur, start=True, stop=True),
                                     reads=[curr, ctr_], writes=[p2r])
                                yield
                            S.op("pe", lambda e, p2=p2, cur=cur, ct=ct: e.matmul(p2[:, 128:256], cur, ct, start=True, stop=True),
                                 reads=[curr, ctr_], writes=[p2r])
                            yield
                        for n in G:
                            p2, p2r = p2s[n]
                            S.op("dve", lambda e, p2=p2, n=n, b1=b1: e.tensor_copy(PWT[b1][n], p2[:, 128:256]),
                                 reads=[p2r], writes=[PWT_r[b1][n]])
                            yield
                            if j < 5:
                                S.op("act", lambda e, p2=p2, n=n, b1=b1: e.copy(PW[b1][n], p2[:, 0:128]),
                                     reads=[p2r], writes=[PW_r[b1][n]])
                                yield
                        pxs = {}
                        for n in G:
                            px, pxr = psrR.next()
                            pxs[n] = (px, pxr)
                            S.op("pe", lambda e, px=px, n=n, b1=b1: e.matmul(px[:, 0:128], PWT[b1][n], Xs[:, n, :], start=True, stop=True),
                                 reads=[PWT_r[b1][n], X_r[n]], writes=[pxr])
                            yield
                        for n in G:
                            px, pxr = pxs[n]
                            S.op("dve", lambda e, px=px, n=n: e.tensor_tensor(Xs[:, n, :], px[:, 0:128], Xs[:, n, :], ALU.add),
                                 reads=[pxr, X_r[n]], writes=[X_r[n]])
                            yield
                for n in range(NCH):
                    pw_, pw_r = psrR.next()
                    S.op("pe", lambda e, pw_=pw_, n=n, p=p: e.matmul(pw_[:, 0:64], ARbd[:, n, 0:128], Hb[:, p, :], start=True, stop=False),
                         reads=[bd_r, Hb_r[p]], writes=[pw_r])
                    yield
                    S.op("pe", lambda e, pw_=pw_, n=n: e.matmul(pw_[:, 0:64], mAK[:, n, 256:384], vst[:, n, :], start=False, stop=True),
                         reads=[mK_r[n], vst_r], writes=[pw_r])
                    yield
                    w_b, wbr = ub.next()
                    S.op("act", lambda e, w_b=w_b, pw_=pw_: e.copy(w_b[:, :], pw_[:, 0:64]), reads=[pw_r], writes=[wbr])
                    yield
                    pu, pur_ = psrR.next()
                    S.op("pe", lambda e, pu=pu, n=n, w_b=w_b: e.matmul(pu[:, 0:64], Xs[:, n, :], w_b[:, :], start=True, stop=True),
                         reads=[X_r[n], wbr], writes=[pur_])
                    yield
                    u_b, ubr = ub.next()
                    S.op("dve", lambda e, u_b=u_b, pu=pu: e.tensor_copy(u_b[:, :], pu[:, 0:64]), reads=[pur_], writes=[ubr])
                    yield
                    ph, phr = psrR.next()
                    S.op("pe", lambda e, ph=ph, n=n, u_b=u_b: e.matmul(ph[:, 0:64], btk[:, n, 0:128], u_b[:, :], start=True, stop=False),
                         reads=[btk_r, ubr], writes=[phr])
                    yield
                    S.op("pe", lambda e, ph=ph, n=n: e.matmul(ph[:, 0:64], btk[:, n, 128:256], vst[:, n, :], start=False, stop=True),
                         reads=[btk_r, vst_r], writes=[phr])
                    yield
                    py, pyr = psrR.next()
                    S.op("pe", lambda e, py=py, n=n, p=p: e.matmul(py[:, 0:64], ARbd[:, n, 128:256], Hb[:, p, :], start=True, stop=False),
                         reads=[bd_r, Hb_r[p]], writes=[pyr])
                    yield
                    S.op("pe", lambda e, py=py, n=n, u_b=u_b: e.matmul(py[:, 0:64], mAK[:, n, 128:256], u_b[:, :], start=False, stop=False),
                         reads=[mA_r[n], ubr], writes=[pyr])
                    yield
                    S.op("pe", lambda e, py=py, n=n: e.matmul(py[:, 0:64], mAK[:, n, 384:512], vst[:, n, :], start=False, stop=True),
                         reads=[mK_r[n], vst_r], writes=[pyr])
                    yield
                    ht, htr = htmp.next()
                    S.op("dve", lambda e, ht=ht, ph=ph, p=p: e.tensor_tensor(ht[:, :], ph[:, 0:64], Hst[:, p, :], ALU.add),
                         reads=[phr, H_r[p]], writes=[htr])
                    yield
                    S.op("act", lambda e, ht=ht, p=p, n=n: e.activation(Hb[:, p, :], ht[:, :], AF.Identity, scale=gam[:, n:n + 1]),
                         reads=[htr, gam_r], writes=[Hb_r[p]])
                    yield
                    S.op("act", lambda e, ht=ht, p=p, n=n: e.activation(Hst[:, p, :], ht[:, :], AF.Identity, scale=gam[:, n:n + 1]),
                         reads=[htr, gam_r], writes=[H_r[p]])
                    yield
                    g6, g6r = gn6.next()
                    S.op("dve", lambda e, g6=g6, py=py: e.bn_stats(g6[:, 2:8], py[:, 0:64]), reads=[pyr], writes=[g6r])
                    yield
                    S.op("dve", lambda e, g6=g6: e.bn_aggr(g6[:, 0:2], g6[:, 2:8]), reads=[g6r], writes=[g6r])
                    yield
                    S.op("act", lambda e, g6=g6: e.activation(g6[:, 2:3], g6[:, 1:2], AF.Sqrt, bias=epsgn[:, 0:1]),
                         reads=[g6r, eps_r], writes=[g6r])
                    yield
                    S.op("dve", lambda e, g6=g6: e.reciprocal(g6[:, 3:4], g6[:, 2:3]), reads=[g6r], writes=[g6r])
                    yield
                    for hf in range(2):
                        hs = slice(hf * 64, hf * 64 + 64)
                        S.op("dve", lambda e, hs=hs, hf=hf, py=py, g6=g6: e.tensor_scalar(
                            Ynbd[hs, hf * 64:hf * 64 + 64], py[hs, 0:64], g6[hs, 0:1], g6[hs, 3:4], ALU.subtract, ALU.mult),
                            reads=[pyr, g6r], writes=[ynbd_r])
                        yield
                    S.op("pe", lambda e, n=n, pyo=pyo: e.matmul(pyo[:, n * 64:(n + 1) * 64], Ynbd[:, :], istb, start=True, stop=True),
                         reads=[ynbd_r, cstb_r], writes=[pyor])
                    yield
                if krw <= 5:
                    S.op("dve", lambda e, p=p: e.memset(yrTb[:, p, :], 0.0), writes=[yr_r[p]])
                    yield
                    continue
                S.op("dve", lambda e, t1=t1, pyo=pyo, p=p: e.tensor_scalar(
                    t1[:, :], pyo[:, :], pvc("gng", p), pvc("gnb", p), ALU.mult, ALU.add), reads=[pyor, pv_r], writes=[t1r])
                yield
                S.op("dve", lambda e, t1=t1, bon=bon: e.tensor_tensor(t1[:, :], t1[:, :], bon[:, :], ALU.add),
                     reads=[t1r, bonr], writes=[t1r])
                yield
                S.op("dve", lambda e, t1=t1, g_=g_, p=p: e.tensor_tensor(yrTb[:, p, :], t1[:, :], g_[:, :], ALU.mult),
                     reads=[t1r, g_r], writes=[yr_r[p]])
                yield

        def merge_out(blk):
            for c in range(8):
                ps, pr = projT(wgate, c * 128, x1Tb, x1_r)
                S.op("act", lambda e, ps=ps, c=c: e.activation(sga[:, c, :], ps[:, :], AF.Sigmoid), reads=[pr], writes=[sga_r[c]])
                ps, pr = projT(wgate, 1024 + c * 128, x1Tb, x1_r)
                S.op("act", lambda e, ps=ps, c=c: e.activation(sgb[:, c, :], ps[:, :], AF.Sigmoid), reads=[pr], writes=[sgb_r[c]])
            for c in range(8):
                psa, par = projT(wa_d, c * 128, lambda kc: HYf[:, kc // 2, kc % 2, :], [hyf_r[kc // 2] for kc in range(8)])
                psb, pbr = projT(wb_d, c * 128, lambda kc: HYf[:, kc // 2, 2 + kc % 2, :], [hyf_r[kc // 2] for kc in range(8)])
                ta, tar = tA.next()
                S.op("dve", lambda e, ta=ta, psa=psa, c=c: e.tensor_tensor(ta[:, :], psa[:, :], sga[:, c, :], ALU.mult),
                     reads=[par, sga_r[c]], writes=[tar])
                tb, tbr = tA.next()
                S.op("dve", lambda e, tb=tb, psb=psb, c=c: e.tensor_tensor(tb[:, :], psb[:, :], sgb[:, c, :], ALU.mult),
                     reads=[pbr, sgb_r[c]], writes=[tbr])
                S.op("dve", lambda e, ta=ta, tb=tb, c=c: e.tensor_tensor(mgTb[:, c, :], ta[:, :], tb[:, :], ALU.add),
                     reads=[tar, tbr], writes=[mg_r[c]])
            for c in range(8):
                ps, pr = projT(wo_d, c * 128, mgTb, mg_r)
                S.op("dve", lambda e, ps=ps, c=c: e.scalar_tensor_tensor(
                    zT[:, c, :], x1T[:, c, :], ALPHA, ps[:, :], ALU.mult, ALU.add), reads=[pr, x1_r[c]], writes=[zT_r[c]])
            layer_norm("ln2_g", "ln2_b", epsln, x2T, x2Tb, x2_r)

        x1f_r = Res()
        x1bl_r = [[Res(), Res()] for _ in range(nbl)]; x1ba_r = [[Res(), Res()] for _ in range(nbl)]
        hyl_r = [Res() for _ in range(NBT)]; hya_r = [Res() for _ in range(NBT)]
        oh = sb([128, 4]); oh_r = Res()
        S.dma("sp", oh[:, :], oh_d[:, :], writes=[oh_r])

        def half(t, h):
            return t[:, 4 * h:4 * h + 4, :].rearrange("p a b -> p (a b)")

        for blk in range(nbl):
            load_xT(blk)
            ffn_ln(xT, xTb, xT_r, w1g, w1u, w1d, "ln1_g", "ln1_b", x1T, x1Tb, x1_r)
            rows = slice(blk * 128, (blk + 1) * 128)
            S.dma("sp", x1f_loc[rows, :], x1T[:, :, :].rearrange("p a b -> p (a b)"), reads=x1_r, writes=[x1f_r])
            for h in range(2):
                S.dma("sp", x1b_loc[blk][h][:, :], half(x1Tb, h), reads=x1_r, writes=[x1bl_r[blk][h]])
                S.cc("AllGather", x1b_loc[blk][h][:, :], x1b_all[blk][h][:, :], groups,
                     reads=[x1bl_r[blk][h]], writes=[x1ba_r[blk][h]])
        for gb in range(NBT):
            i, blk = gb // nbl, gb % nbl
            for h in range(2):
                S.dma("sp", half(x1Tb, h), x1b_all[blk][h][i * 128:(i + 1) * 128, :], reads=[x1ba_r[blk][h]],
                      writes=x1_r[4 * h:4 * h + 4])
            if gb == 0:
                S.op("dve", lambda e: e.memset(x1T[:, 0, 0:8], 0.0), writes=x1_r + scr_r)
            gens = [mlstm(gb), rwkv(gb)]
            while gens:
                for gg in list(gens):
                    try:
                        next(gg)
                    except StopIteration:
                        gens.remove(gg)
            S.dma("sp", hy_loc[gb][:, :], hyT[:, :, :].rearrange("p a b -> p (a b)"), reads=hm_r + yr_r, writes=[hyl_r[gb]])
            S.cc("AllGather", hy_loc[gb][:, :], hy_all[gb][:, :], groups, reads=[hyl_r[gb]], writes=[hya_r[gb]])
        for blk in range(nbl):
            rows = slice(blk * 128, (blk + 1) * 128)
            S.dma("sp", x1T[:, :, :].rearrange("p a b -> p (a b)"), x1f_loc[rows, :], reads=[x1f_r], writes=x1_r + scr_r)
            for h in range(2):
                S.dma("sp", half(x1Tb, h), x1b_loc[blk][h][:, :], reads=[x1bl_r[blk][h]], writes=x1_r[4 * h:4 * h + 4])
            for i in range(4):
                dst = HYf[:, i, :, :].rearrange("p a b -> p (a b)")
                for j in range(4):
                    k = j * nbl + blk
                    stg = sgb[:, 4 * (j % 2):4 * (j % 2) + 4, :].rearrange("p a b -> p (a b)")
                    stg_r = sgb_r[4 * (j % 2):4 * (j % 2) + 4]
                    S.dma("sp", stg, hy_all[k][i * 128:(i + 1) * 128, :], reads=[hya_r[k]], writes=stg_r)
                    if j == 0:
                        S.op("dve", lambda e, dst=dst, stg=stg, j=j: e.tensor_scalar_mul(dst, stg, oh[:, j:j + 1]),
                             reads=stg_r + [oh_r], writes=[hyf_r[i]])
                    else:
                        S.op("dve", lambda e, dst=dst, stg=stg, j=j: e.scalar_tensor_tensor(
                            dst, stg, oh[:, j:j + 1], dst, ALU.mult, ALU.add),
                            reads=stg_r + [oh_r, hyf_r[i]], writes=[hyf_r[i]])
            merge_out(blk)
            ffn_ln(x2T, x2Tb, x2_r, w2g, w2u, w2d, "ln3_g", "ln3_b", x3T, x3Tb, x3_r)
            store_T(blk, x3T, x3_r)

        S.op("sp", lambda e: e.nop(), reads=[out_r])
        S.emit()
    return nc


def _consts():
    c = np.zeros((128, NCST), np.float32)
    i = np.arange(128)
    c[:, C_ID:C_ID + 128] = np.eye(128)
    c[:, C_OD:C_OD + 128] = 1.0 / 1024.0
    c[:, C_TRI:C_TRI + 128] = (i[:, None] <= i[None, :])
    c[:, C_ONE:C_ONE + 128] = 1.0
    c[:, C_MUS:C_MUS + 128] = (i[:, None] < i[None, :])
    c[:, C_MLS:C_MLS + 128] = (i[:, None] > i[None, :])
    c[:, C_IST:C_IST + 64] = np.concatenate([np.eye(64), np.eye(64)], axis=0)
    c[:, C_BO:C_BO + 128] = ((i[:, None] // 64) == (i[None, :] // 64))
    return c


def _fm(v):
    v = np.asarray(v, np.float32).reshape(-1, 128)
    return np.ascontiguousarray(v.T)


def kernel(**inp):
    nbl = int(os.environ.get("KNBL", "4"))
    ngrp = int(os.environ.get("KNGRP", "2"))
    x = np.asarray(inp["x"], np.float32)
    W = np.asarray(inp["w_in"][0], np.float32)
    gates = np.ascontiguousarray(W[:, 7432:9480])
    cw = inp["m_conv_w"][0]; cbv = inp["m_conv_b"][0]
    mu = inp["r_mu"][0]
    in_maps = []
    for c in range(4 * ngrp):
        b, r = c // 4, c % 4
        pv = np.zeros((128, NPV), np.float32)

        def put(name, arr):
            a = _fm(arr)
            pv[:, PV[name]:PV[name] + a.shape[1]] = a

        for n in ("ln1_g", "ln1_b", "ln2_g", "ln2_b", "ln3_g", "ln3_b"):
            put(n, inp[n][0])
        qs = slice(r * 256, (r + 1) * 256)
        ks = slice(1024 + r * 256, 1024 + (r + 1) * 256)
        for j in range(4):
            put("cw%d" % j, np.concatenate([cw[j][qs], cw[j][ks]]))
        put("cb", np.concatenate([cbv[qs], cbv[ks]]))
        put("mng", inp["m_norm_g"][0][qs])
        ps_ = slice(r * 256, (r + 1) * 256)
        mul = np.concatenate([mu[0:1024][ps_], mu[1024:2048][ps_], mu[2048:3072][ps_], mu[3072:3328]])
        put("mu", mul)
        put("omu", 1.0 - mul)
        put("w0", inp["r_w0"][0][ps_]); put("a0", inp["r_a0"][0][ps_]); put("kk", inp["r_k_k"][0][ps_])
        put("ka", inp["r_k_a"][0][ps_]); put("rrk", inp["r_r_k"][0].reshape(-1)[ps_])
        put("gng", inp["r_gn_g"][0][ps_]); put("gnb", inp["r_gn_b"][0][ps_])
        wl = np.concatenate([W[:, qs], W[:, ks], W[:, 2048 + r * 256:2048 + (r + 1) * 256],
                             W[:, 3072 + r * 256:3072 + (r + 1) * 256], np.zeros((D, 8), np.float32),
                             W[:, RC0 + r * 256:RC0 + (r + 1) * 256],
                             W[:, RC0 + 1024 + r * 256:RC0 + 1024 + (r + 1) * 256],
                             W[:, RC0 + 2048 + r * 256:RC0 + 2048 + (r + 1) * 256],
                             W[:, RC0 + 3072:RC0 + 3328]], axis=1)
        assert wl.shape[1] == 2056
        wif = np.zeros((D, 8), np.float32)
        wif[:, 0] = W[:, 4096 + r]
        wif[:, 4] = W[:, 4100 + r]
        gbias = np.zeros((128, 8), np.float32)
        gbias[:, 0] = inp["m_i_bias"][0][r]
        gbias[:, 4] = inp["m_f_bias"][0][r]
        gbias[:, 5:8] = 30.0
        m = {"cst": _consts(), "pv": pv, "gbias": gbias,
             "rst": np.ascontiguousarray(np.broadcast_to((np.arange(512)[None, :] % 64 != 0), (128, 512)).astype(np.float32)),
             "w_loc": np.ascontiguousarray(wl), "w_if": wif, "w_gate": gates,
             "r_w2": np.ascontiguousarray(inp["r_w2"][0][:, ps_]), "r_a2": np.ascontiguousarray(inp["r_a2"][0][:, ps_]),
             "r_g2": np.ascontiguousarray(inp["r_g2"][0][:, ps_]),
             "x": np.ascontiguousarray(x[b, r * nbl * TB:(r + 1) * nbl * TB]),
             "oh": np.ascontiguousarray(np.broadcast_to(np.eye(4, dtype=np.float32)[r][None, :], (128, 4)))}
        for n in ("ffn1_w_gate", "ffn1_w_up", "ffn1_w_down", "ffn2_w_gate", "ffn2_w_up", "ffn2_w_down",
                  "w_branch_a", "w_branch_b", "w_out"):
            m[n] = np.ascontiguousarray(inp[n][0], dtype=np.float32)
        in_maps.append(m)
    groups = [list(range(4 * g, 4 * g + 4)) for g in range(ngrp)]
    nc = build(nbl, groups)
    res = run_bass_kernel_spmd(nc, in_maps, core_ids=list(range(4 * ngrp)))
    out = np.zeros((ngrp, 4 * nbl * TB, D), np.float32)
    for c in range(4 * ngrp):
        out[c // 4, (c % 4) * nbl * TB:(c % 4 + 1) * nbl * TB] = np.asarray(res.results[c]["out"])
    return out
```

```python
import contextlib
import os
import numpy as np
import concourse.bass as bass
import concourse.mybir as mybir
from concourse.bass_utils import run_bass_kernel_spmd

F32 = mybir.dt.float32
BF16 = mybir.dt.bfloat16
AF = mybir.ActivationFunctionType
ALU = mybir.AluOpType

D = 1024
DFF = 2816
NJ = DFF // 128
SEQ = 8192
TB = 512
ALPHA = 2.0 ** 0.25
LN_EPS = 1e-5
GN_EPS = 64e-5
NCORES = 8
WCOLS = 9480
RC0 = 4104
SAME_ENGINE_WAITS = os.environ.get("KSEW", "act,dve,pool,sp").split(",")


class Res:
    __slots__ = ("w", "r", "excl")

    def __init__(self, excl=False):
        self.w = None
        self.r = {}
        self.excl = excl


class Sched:
    NDS = 24

    def __init__(self, nc, stack):
        self.nc = nc
        self.E = {"pe": nc.tensor, "act": nc.scalar, "dve": nc.vector, "pool": nc.gpsimd, "sp": nc.sync}
        self.q = {k: [] for k in self.E}
        self.cnt = {k: 0 for k in self.E}
        self.esem = {k: stack.enter_context(nc.semaphore("s_" + k)) for k in self.E}
        self.dsem = [stack.enter_context(nc.semaphore("d%d" % i)) for i in range(self.NDS)]
        self.dval = [0] * self.NDS
        self.dnext = {"sp": 0, "pool": 0}
        self.dpool = {"sp": list(range(0, 8)), "pool": list(range(8, self.NDS))}
        self.waited = {k: {} for k in self.E}
        self.csem = stack.enter_context(nc.semaphore("cc"))
        self.cval = 0

    def _deps(self, eng, reads, writes, extra=()):
        deps = {}

        def add(t):
            if t is None:
                return
            k = (t[0], t[1])
            if deps.get(k, 0) < t[2]:
                deps[k] = t[2]

        for r in reads:
            add(r.w)
        for w in writes:
            add(w.w)
            for k, v in w.r.items():
                add((k[0], k[1], v))
        for t in extra:
            add(t)
        waits = []
        for k, v in deps.items():
            if k[0] == "e" and k[1] == eng and (eng == "pe" or eng not in SAME_ENGINE_WAITS):
                continue
            if self.waited[eng].get(k, 0) >= v:
                continue
            self.waited[eng][k] = v
            waits.append((k, v))
        return waits

    def _mark(self, tok, reads, writes):
        k = (tok[0], tok[1])
        for r in reads:
            if r.r.get(k, 0) < tok[2]:
                r.r[k] = tok[2]
        for w in writes:
            w.w = tok
            w.r = {}

    def op(self, eng, fn, reads=(), writes=()):
        ex = [r for r in reads if r.excl]
        if ex:
            writes = list(writes) + ex
        waits = self._deps(eng, reads, writes)
        self.cnt[eng] += 1
        tok = ("e", eng, self.cnt[eng])
        self.q[eng].append((waits, fn, None))
        self._mark(tok, reads, writes)
        return tok

    def dma(self, qeng, out, in_, reads=(), writes=()):
        pool = self.dpool[qeng]
        j = pool[self.dnext[qeng]]
        self.dnext[qeng] = (self.dnext[qeng] + 1) % len(pool)
        prev = ("d", j, self.dval[j]) if self.dval[j] else None
        extra = [prev] if prev else []
        waits = self._deps(qeng, reads, writes, extra=extra)
        self.dval[j] += 16
        tok = ("d", j, self.dval[j])
        self.q[qeng].append((waits, (out, in_), j))
        self._mark(tok, reads, writes)
        return tok

    def cc(self, kind, ins, outs, groups, reads=(), writes=()):
        waits = self._deps("pool", reads, writes)
        self.cval += 1
        tok = ("c", 0, self.cval)
        self.q["pool"].append((waits, (kind, ins, outs, groups), "cc"))
        self._mark(tok, reads, writes)
        return tok

    def emit(self):
        nc = self.nc
        with nc.Block() as block:
            def run(kind):
                def body(eng):
                    for waits, fn, dj in self.q[kind]:
                        for k, v in waits:
                            sem = self.esem[k[1]] if k[0] == "e" else (self.csem if k[0] == "c" else self.dsem[k[1]])
                            eng.wait_ge(sem, v)
                        if dj is None:
                            fn(eng).then_inc(self.esem[kind], 1)
                        elif dj == "cc":
                            eng.collective_compute(fn[0], ALU.bypass, replica_groups=fn[3], ins=[fn[1]], outs=[fn[2]]).then_inc(self.csem, 1)
                        else:
                            eng.dma_start(out=fn[0], in_=fn[1]).then_inc(self.dsem[dj], 16)
                return body
            block.tensor(run("pe"))
            block.scalar(run("act"))
            block.vector(run("dve"))
            block.gpsimd(run("pool"))
            block.sync(run("sp"))


class Ring:
    def __init__(self, items):
        self.items = items
        self.i = 0

    def next(self):
        it = self.items[self.i]
        self.i = (self.i + 1) % len(self.items)
        return it


PV = {}
_o = 0
for _n, _w in [("ln1_g", 8), ("ln1_b", 8), ("ln2_g", 8), ("ln2_b", 8), ("ln3_g", 8), ("ln3_b", 8),
               ("cw0", 4), ("cw1", 4), ("cw2", 4), ("cw3", 4), ("cb", 4), ("mng", 2),
               ("mu", 8), ("omu", 8), ("w0", 2), ("a0", 2), ("kk", 2), ("ka", 2), ("rrk", 2),
               ("gng", 2), ("gnb", 2)]:
    PV[_n] = _o
    _o += _w
NPV = _o
C_ID, C_OD, C_MUS, C_TRI, C_ONE, C_MLS, C_IST, C_BO, NCST = 0, 128, 256, 384, 512, 640, 768, 832, 960


def build(nbl, groups):
    nc = bass.Bass("TRN2", target_bir_lowering=False)

    def din(name, shape):
        return nc.dram_tensor(name, list(shape), F32, kind="ExternalInput").ap()

    NBT = 4 * nbl
    x_d = din("x", [nbl * TB, D])
    cst_d = din("cst", [128, NCST])
    pv_d = din("pv", [128, NPV])
    gb_d = din("gbias", [128, 8])
    rst_d = din("rst", [128, 512])
    w1g = din("ffn1_w_gate", [D, DFF]); w1u = din("ffn1_w_up", [D, DFF]); w1d = din("ffn1_w_down", [DFF, D])
    w2g = din("ffn2_w_gate", [D, DFF]); w2u = din("ffn2_w_up", [D, DFF]); w2d = din("ffn2_w_down", [DFF, D])
    win = din("w_loc", [D, 2048])
    winT = din("w_loc_t", [2048, D])
    wif_d = din("w_if", [D, 8])
    wgate = din("w_gate", [2048, D])
    oh_d = din("oh", [128, 4])
    wa_d = din("w_branch_a", [D, D]); wb_d = din("w_branch_b", [D, D]); wo_d = din("w_out", [D, D])
    rw2_d = din("r_w2", [64, 256]); ra2_d = din("r_a2", [64, 256]); rg2_d = din("r_g2", [128, 256])
    out_d = nc.dram_tensor("out", [nbl * TB, D], F32, kind="ExternalOutput").ap()
    x1f_loc = nc.dram_tensor("x1f_loc", [nbl * 128, 8 * TB], F32).ap()
    x1b_loc = [[nc.dram_tensor("x1bl_%d_%d" % (k, h), [128, 4 * TB], BF16).ap() for h in range(2)] for k in range(nbl)]
    x1b_all = [[nc.dram_tensor("x1ba_%d_%d" % (k, h), [4 * 128, 4 * TB], BF16).ap() for h in range(2)] for k in range(nbl)]
    hy_loc = [nc.dram_tensor("hyl_%d" % k, [128, 4 * TB], BF16).ap() for k in range(NBT)]
    hy_all = [nc.dram_tensor("hya_%d" % k, [4 * 128, 4 * TB], BF16).ap() for k in range(NBT)]

    with contextlib.ExitStack() as st:
        S = Sched(nc, st)
        _n = [0]

        def sb(shape, dt=F32):
            _n[0] += 1
            return st.enter_context(nc.sbuf_tensor("sb%d" % _n[0], list(shape), dt))

        def ring(n, shape, dt=F32):
            return Ring([(sb(shape, dt), Res()) for _ in range(n)])

        banks = [st.enter_context(nc.psum_tensor("ps%d" % i, [128, 512], F32)) for i in range(8)]
        psr = Ring([(banks[i], Res(True)) for i in range(7)])
        pyo_bank = (banks[7], Res(True))

        cst = sb([128, NCST]); cst_r = Res()
        cstb = sb([128, NCST], BF16); cstb_r = Res()
        pv = sb([128, NPV]); pv_r = Res()
        gbias = sb([128, 8]); gb_r = Res()
        S.dma("sp", cst[:, :], cst_d[:, :], writes=[cst_r])
        S.dma("sp", pv[:, :], pv_d[:, :], writes=[pv_r])
        S.dma("sp", gbias[:, :], gb_d[:, :], writes=[gb_r])
        S.op("act", lambda e: e.copy(cstb[:, :], cst[:, :]), reads=[cst_r], writes=[cstb_r])
        ident = cst[:, C_ID:C_ID + 128]
        identb = cstb[:, C_ID:C_ID + 128]
        onesdb = cstb[:, C_OD:C_OD + 128]
        tri = cst[:, C_TRI:C_TRI + 128]
        ones = cst[:, C_ONE:C_ONE + 128]
        istb = cstb[:, C_IST:C_IST + 64]
        bob = cstb[:, C_BO:C_BO + 128]
        rstb = sb([128, 512], BF16)
        S.dma("pool", rstb[:, :], rst_d[:, :], writes=[cstb_r])
        rst = rstb[:, :]
        epsln = sb([128, 1]); eps4 = sb([128, 1]); epsgn = sb([128, 1]); eps_r = Res()
        S.op("dve", lambda e: e.memset(epsln[:, :], LN_EPS), writes=[eps_r])
        S.op("dve", lambda e: e.memset(eps4[:, :], 4 * LN_EPS), writes=[eps_r])
        S.op("dve", lambda e: e.memset(epsgn[:, :], GN_EPS), writes=[eps_r])

        def pvc(name, i=0):
            c = PV[name] + i
            return pv[:, c:c + 1]

        rw2a2 = sb([128, 256], BF16); rw_r = Res()
        rg2 = sb([128, 256], BF16)
        S.dma("pool", rw2a2[0:64, :], rw2_d[:, :], writes=[rw_r])
        S.dma("pool", rw2a2[64:128, :], ra2_d[:, :], writes=[rw_r])
        S.dma("pool", rg2[:, :], rg2_d[:, :], writes=[rw_r])

        xT = sb([128, 8, TB]); xTb = sb([128, 8, TB], BF16); xT_r = [Res() for _ in range(8)]
        x1T = sb([128, 8, TB]); x1Tb = sb([128, 8, TB], BF16); x1_r = [Res() for _ in range(8)]
        x2T = xT; x2Tb = xTb; x2_r = xT_r
        x3T = xT; x3Tb = xTb; x3_r = xT_r
        aT = sb([128, NJ, TB], BF16); aT_r = [Res() for _ in range(NJ)]
        zT = sb([128, 8, TB]); zT_r = [Res() for _ in range(8)]
        xtok = ring(1, [128, D])
        otok = xtok
        wgu = ring(2, [128, 2, 8, 256], BF16)
        wdb = ring(3, [128, 512], BF16)
        wpj = ring(3, [128, 8, 128], BF16)
        tA = ring(3, [128, TB])
        tB = ring(2, [128, TB], BF16)
        mean_r = Res(); rstd_r = Res()
        out_r = Res()

        def load_xT(blk):
            for t in range(TB // 128):
                xt, xr = xtok.next()
                r0 = blk * TB + t * 128
                S.dma("sp", xt[:, :], x_d[r0:r0 + 128, :], writes=[xr])
                for half in range(2):
                    ps, pr = psr.next()
                    for q in range(4):
                        kc = half * 4 + q
                        S.op("pe", lambda e, ps=ps, xt=xt, kc=kc, q=q: e.matmul(
                            ps[:, q * 128:(q + 1) * 128], xt[:, kc * 128:(kc + 1) * 128], ident,
                            start=True, stop=True), reads=[xr, cst_r], writes=[pr])
                    psv = ps[:, :].rearrange("p (q n) -> p q n", q=4)
                    S.op("act", lambda e, psv=psv, half=half, t=t: e.copy(
                        xT[:, half * 4:half * 4 + 4, t * 128:(t + 1) * 128], psv),
                        reads=[pr], writes=xT_r[half * 4:half * 4 + 4])
                    S.op("dve", lambda e, psv=psv, half=half, t=t: e.tensor_copy(
                        xTb[:, half * 4:half * 4 + 4, t * 128:(t + 1) * 128], psv),
                        reads=[pr], writes=xT_r[half * 4:half * 4 + 4])

        def layer_norm(gname, bname, eps_t, outT, outTb, out_rs):
            psm, pmr = psr.next()
            pss, ssr = psr.next()
            for dc in range(8):
                tb, tr = tB.next()
                S.op("act", lambda e, tb=tb, dc=dc: e.copy(tb[:, :], zT[:, dc, :]), reads=[zT_r[dc]], writes=[tr])
                S.op("pe", lambda e, tb=tb, dc=dc: e.matmul(psm[:, :], onesdb, tb[:, :], start=(dc == 0), stop=(dc == 7)),
                     reads=[cstb_r, tr], writes=[pmr])
                tb2, tr2 = tB.next()
                S.op("act", lambda e, tb2=tb2, dc=dc: e.activation(tb2[:, :], zT[:, dc, :], AF.Square),
                     reads=[zT_r[dc]], writes=[tr2])
                S.op("pe", lambda e, tb2=tb2, dc=dc: e.matmul(pss[:, :], onesdb, tb2[:, :], start=(dc == 0), stop=(dc == 7)),
                     reads=[cstb_r, tr2], writes=[ssr])
            S.op("act", lambda e: e.copy(mean_sb[:, :], psm[:, :]), reads=[pmr], writes=[mean_r])
            t1, r1 = tA.next()
            S.op("dve", lambda e, t1=t1: e.tensor_tensor(t1[:, :], mean_sb[:, :], mean_sb[:, :], ALU.mult),
                 reads=[mean_r], writes=[r1])
            t2, r2 = tA.next()
            S.op("dve", lambda e, t1=t1, t2=t2: e.tensor_tensor(t2[:, :], pss[:, :], t1[:, :], ALU.subtract),
                 reads=[ssr, r1], writes=[r2])
            S.op("dve", lambda e, t2=t2: e.tensor_scalar_max(t2[:, :], t2[:, :], 0.0), reads=[r2], writes=[r2])
            t3, r3 = tA.next()
            S.op("act", lambda e, t2=t2, t3=t3: e.activation(t3[:, :], t2[:, :], AF.Sqrt, bias=eps_t[:, 0:1]),
                 reads=[r2, eps_r], writes=[r3])
            S.op("dve", lambda e, t3=t3: e.reciprocal(rstd_sb[:, :], t3[:, :]), reads=[r3], writes=[rstd_r])
            for dc in range(8):
                ta, tar = tA.next()
                S.op("dve", lambda e, ta=ta, dc=dc: e.tensor_tensor(ta[:, :], zT[:, dc, :], mean_sb[:, :], ALU.subtract),
                     reads=[zT_r[dc], mean_r], writes=[tar])
                S.op("dve", lambda e, ta=ta: e.tensor_tensor(ta[:, :], ta[:, :], rstd_sb[:, :], ALU.mult),
                     reads=[tar, rstd_r], writes=[tar])
                S.op("act", lambda e, ta=ta, dc=dc: e.activation(
                    outT[:, dc, :], ta[:, :], AF.Identity, scale=pvc(gname, dc), bias=pvc(bname, dc)),
                    reads=[tar, pv_r], writes=[out_rs[dc]])
                S.op("act", lambda e, ta=ta, dc=dc: e.activation(
                    outTb[:, dc, :], ta[:, :], AF.Identity, scale=pvc(gname, dc), bias=pvc(bname, dc)),
                    reads=[tar, pv_r], writes=[out_rs[dc]])

        def ffn_ln(inT, inTb, in_rs, Wg, Wu, Wd, gname, bname, outT, outTb, out_rs):
            Wg_v = Wg.rearrange("(kc p) f -> p kc f", p=128)
            Wu_v = Wu.rearrange("(kc p) f -> p kc f", p=128)
            Wd_v = Wd.rearrange("(j p) d -> p j d", p=128)
            for j in range(NJ):
                if j % 2 == 0:
                    wb, wr = wgu.next()
                    S.dma("pool", wb[:, 0], Wg_v[:, :, j * 128:(j + 2) * 128], writes=[wr])
                    S.dma("pool", wb[:, 1], Wu_v[:, :, j * 128:(j + 2) * 128], writes=[wr])
                jo = (j % 2) * 128
                psg, pgr = psr.next()
                psu, pur = psr.next()
                for gu, (ps, prr) in enumerate(((psg, pgr), (psu, pur))):
                    for kc in range(8):
                        S.op("pe", lambda e, ps=ps, wb=wb, kc=kc, gu=gu, jo=jo: e.matmul(
                            ps[:, :], wb[:, gu, kc, jo:jo + 128], inTb[:, kc, :], start=(kc == 0), stop=(kc == 7)),
                            reads=[wr, in_rs[kc]], writes=[prr])
                tb, tr = tA.next()
                S.op("act", lambda e, tb=tb, ps=psg: e.activation(tb[:, :], ps[:, :], AF.Silu), reads=[pgr], writes=[tr])
                S.op("dve", lambda e, tb=tb, ps=psu, j=j: e.tensor_tensor(aT[:, j, :], tb[:, :], ps[:, :], ALU.mult),
                     reads=[tr, pur], writes=[aT_r[j]])
            for half in range(2):
                pss_ = [psr.next() for _ in range(4)]
                for j in range(NJ):
                    wb, wr = wdb.next()
                    S.dma("pool", wb[:, :], Wd_v[:, j, half * 512:(half + 1) * 512], writes=[wr])
                    for q in range(4):
                        ps, pr = pss_[q]
                        S.op("pe", lambda e, ps=ps, wb=wb, j=j, q=q: e.matmul(
                            ps[:, :], wb[:, q * 128:(q + 1) * 128], aT[:, j, :], start=(j == 0), stop=(j == NJ - 1)),
                            reads=[wr, aT_r[j]], writes=[pr])
                for q in range(4):
                    dc = half * 4 + q
                    ps, pr = pss_[q]
                    S.op("dve", lambda e, ps=ps, dc=dc: e.scalar_tensor_tensor(
                        zT[:, dc, :], inT[:, dc, :], 2.0 * ALPHA, ps[:, :], ALU.mult, ALU.add),
                        reads=[pr, in_rs[dc]], writes=[zT_r[dc]])
            layer_norm(gname, bname, eps4, outT, outTb, out_rs)

        def store_T(blk, srcT, src_rs):
            for t in range(TB // 128):
                ot, orr = otok.next()
                for half in range(2):
                    ps, pr = psr.next()
                    for q in range(4):
                        kc = half * 4 + q
                        S.op("pe", lambda e, ps=ps, kc=kc, q=q, t=t: e.matmul(
                            ps[:, q * 128:(q + 1) * 128], srcT[:, kc, t * 128:(t + 1) * 128], ident,
                            start=True, stop=True), reads=[src_rs[kc], cst_r], writes=[pr])
                    S.op("act", lambda e, ps=ps, ot=ot, half=half: e.copy(ot[:, half * 512:(half + 1) * 512], ps[:, :]),
                         reads=[pr], writes=[orr])
                r0 = blk * TB + t * 128
                S.dma("sp", out_d[r0:r0 + 128, :], ot[:, :], reads=[orr], writes=[out_r])

        def projT(Wd_ap, col0, inTb, in_rs, ncols=128, wring=None, pring=None):
            wb, wr = (wring or wpj).next()
            S.dma("pool", wb[:, :, :].rearrange("p k c -> p (k c)"), Wd_ap[col0:col0 + 128, :], writes=[wr])
            ps, pr = (pring or psr).next()
            for kc in range(8):
                src = inTb(kc) if callable(inTb) else inTb[:, kc, :]
                S.op("pe", lambda e, ps=ps, wb=wb, kc=kc, src=src: e.matmul(
                    ps[0:ncols, :], wb[:, kc, 0:ncols], src, start=(kc == 0), stop=(kc == 7)),
                    reads=[wr, in_rs[kc]], writes=[pr])
            return ps, pr

        NT = TB // 128
        carry = sb([128, 4, 3]); carry_r = [Res() for _ in range(4)]
        S.op("dve", lambda e: e.memset(carry[:, :, :], 0.0), writes=carry_r)
        cwork = ring(2, [128, 3 + TB])
        mean_sb = cwork.items[0][0][:, 0:TB]; rstd_sb = cwork.items[1][0][:, 0:TB]
        qkT = zT[:, :, :].rearrange("p a b -> p (a b)").bitcast(BF16).rearrange("p (c n) -> p c n", n=TB)
        qk_r = [zT_r[c // 2] for c in range(16)]
        sigmo = sb([128, 2, TB], BF16); sigmo_r = [Res() for _ in range(2)]
        sga = sb([128, 8, TB], BF16); sga_r = [Res() for _ in range(8)]
        sgb = sb([128, 8, TB], BF16); sgb_r = [Res() for _ in range(8)]
        vt = sb([128, NT, 1, 258], BF16); vt_r = [[Res() for _ in range(1)] for _ in range(NT)]
        S.op("dve", lambda e: e.memset(vt[:, :, :, :], 1.0), writes=[r for rr in vt_r for r in rr])
        wv = ring(1, [128, 8, 256], BF16)
        wif = sb([128, 8, 8], BF16); wif_r = Res()
        S.dma("pool", wif[:, :, :], wif_d.rearrange("(kc p) f -> p kc f", p=128), writes=[wif_r])
        gts = sb([128, NT, 24]); gts_r = [Res() for _ in range(NT)]
        Cst = sb([128, 1, 2, 258]); C_r = [Res() for _ in range(1)]
        Cb = sb([128, 1, 2, 258], BF16); Cb_r = [Res() for _ in range(1)]
        S.op("dve", lambda e: e.memset(Cst[:, :, :, :], 0.0), writes=C_r)
        S.op("dve", lambda e: e.memset(Cb[:, :, :, :], 0.0), writes=Cb_r)
        hyT = sb([128, 4, TB], BF16); hm_r = [Res() for _ in range(2)]
        hmTb = hyT[:, 0:2, :]
        HYf = sb([128, 4, 4, TB], BF16); hyf_r = [Res() for _ in range(4)]
        yrTb = hyT[:, 2:4, :]; yr_r = [Res() for _ in range(2)]
        mgTb = sga; mg_r = sga_r
        smr = ring(4, [128, 128], BF16)
        ktk = ring(4, [128, 256], BF16)
        sm6 = ring(4, [128, 8])

        def mlstm(blk):
            Wv = win.rearrange("(kc p) f -> p kc f", p=128)
            for c in range(4):
                ps, pr = projT(winT, c * 128, x1Tb, x1_r, wring=wpjM, pring=psrM)
                wk, wkr = cwork.next()
                S.op("act", lambda e, wk=wk, c=c: e.copy(wk[:, 0:3], carry[:, c, :]), reads=[carry_r[c]], writes=[wkr])
                yield
                S.op("act", lambda e, wk=wk, ps=ps: e.copy(wk[:, 3:3 + TB], ps[:, :]), reads=[pr], writes=[wkr])
                yield
                S.op("act", lambda e, wk=wk, c=c: e.copy(carry[:, c, :], wk[:, TB:TB + 3]), reads=[wkr], writes=[carry_r[c]])
                yield
                ta, tar = tAM.next()
                S.op("dve", lambda e, wk=wk, ta=ta, c=c: e.tensor_scalar(
                    ta[:, :], wk[:, 0:TB], pvc("cw0", c), pvc("cb", c), ALU.mult, ALU.add),
                    reads=[wkr, pv_r], writes=[tar])
                yield
                for j in (1, 2, 3):
                    S.op("dve", lambda e, wk=wk, ta=ta, c=c, j=j: e.scalar_tensor_tensor(
                        ta[:, :], wk[:, j:j + TB], pvc("cw%d" % j, c), ta[:, :], ALU.mult, ALU.add),
                        reads=[wkr, pv_r, tar], writes=[tar])
                    yield
                S.op("act", lambda e, ta=ta, c=c: e.activation(qkT[:, c, :], ta[:, :], AF.Silu),
                     reads=[tar], writes=[qk_r[c]])
                yield
            for c in range(2):
                ps, pr = projT(winT, 768 + c * 128, x1Tb, x1_r, wring=wpjM, pring=psrM)
                S.op("act", lambda e, ps=ps, c=c: e.activation(sigmo[:, c, :], ps[:, :], AF.Sigmoid),
                     reads=[pr], writes=[sigmo_r[c]])
                yield
            for t in range(NT):
                ps, pr = psrM.next()
                for kc in range(8):
                    S.op("pe", lambda e, ps=ps, kc=kc, t=t: e.matmul(
                        ps[:, 0:8], x1Tb[:, kc, t * 128:(t + 1) * 128], wif[:, kc, :], start=(kc == 0), stop=(kc == 7)),
                        reads=[wif_r, x1_r[kc]], writes=[pr])
                    yield
                g = gts[:, t, :]
                gr = gts_r[t]
                S.op("dve", lambda e, g=g, ps=ps: e.tensor_tensor(g[:, 12:20], ps[:, 0:8], gbias[:, :], ALU.add),
                     reads=[pr, gb_r], writes=[gr])
                yield
                S.op("act", lambda e, g=g: e.activation(g[:, 20:24], g[:, 16:20], AF.Exp, scale=-1.0), reads=[gr], writes=[gr])
                yield
                S.op("act", lambda e, g=g: e.activation(g[:, 16:20], g[:, 20:24], AF.Ln, bias=1.0), reads=[gr], writes=[gr])
                yield
                S.op("dve", lambda e, g=g: e.tensor_scalar_mul(g[:, 16:20], g[:, 16:20], -1.0), reads=[gr], writes=[gr])
                yield
                ps2, pr2 = psrM.next()
                S.op("pe", lambda e, ps2=ps2, g=g: e.matmul(ps2[:, 0:4], tri, g[:, 16:20], start=True, stop=True),
                     reads=[gr, cst_r], writes=[pr2])
                yield
                S.op("pe", lambda e, ps2=ps2, g=g: e.matmul(ps2[:, 4:8], ones, g[:, 16:20], start=True, stop=True),
                     reads=[gr, cst_r], writes=[pr2])
                yield
                S.op("dve", lambda e, g=g, ps2=ps2: e.tensor_tensor(g[:, 20:24], g[:, 12:16], ps2[:, 0:4], ALU.subtract),
                     reads=[gr, pr2], writes=[gr])
                yield
                S.op("act", lambda e, g=g: e.activation(g[:, 0:4], g[:, 20:24], AF.Exp), reads=[gr], writes=[gr])
                yield
                S.op("act", lambda e, g=g, ps2=ps2: e.activation(g[:, 4:8], ps2[:, 0:4], AF.Exp, scale=-1.0),
                     reads=[gr, pr2], writes=[gr])
                yield
                S.op("act", lambda e, g=g, ps2=ps2: e.activation(g[:, 8:12], ps2[:, 4:8], AF.Exp), reads=[gr, pr2], writes=[gr])
                yield
            for h in range(1):
                wb, wr = wv.next()
                S.dma("pool", wb[:, :, :], Wv[:, :, 512:768], writes=[wr])
                yield
                for t in range(NT):
                    ps, pr = psrM.next()
                    for kc in range(8):
                        S.op("pe", lambda e, ps=ps, kc=kc, t=t, wb=wb: e.matmul(
                            ps[:, 0:256], x1Tb[:, kc, t * 128:(t + 1) * 128], wb[:, kc, :], start=(kc == 0), stop=(kc == 7)),
                            reads=[wr, x1_r[kc]], writes=[pr])
                        yield
                    S.op("act", lambda e, ps=ps, t=t, h=h: e.copy(vt[:, t, h, 0:256], ps[:, 0:256]),
                         reads=[pr], writes=[vt_r[t][h]])
                    yield
            for t in range(NT):
                tc_ = slice(t * 128, (t + 1) * 128)
                g = gts[:, t, :]
                gr = gts_r[t]
                def head_chain(t, h, tc_, g, gr):
                    qc = [h * 2, h * 2 + 1]
                    kc_ = [2 + h * 2, 2 + h * 2 + 1]
                    ps, pr = psrM.next()
                    for i in range(2):
                        S.op("pe", lambda e, ps=ps, i=i, kc_=kc_, qc=qc, tc_=tc_: e.matmul(
                            ps[:, 0:128], qkT[:, kc_[i], tc_], qkT[:, qc[i], tc_], start=(i == 0), stop=(i == 1)),
                            reads=[qk_r[kc_[i]], qk_r[qc[i]]], writes=[pr])
                        yield
                    sm, smrr = smr.next()
                    S.op("dve", lambda e, sm=sm, ps=ps, h=h, g=g: e.scalar_tensor_tensor(
                        sm[:, :], ps[:, 0:128], g[:, h:h + 1], tri, ALU.mult, ALU.mult),
                        reads=[pr, gr, cst_r], writes=[smrr])
                    yield
                    po, por = psrM.next()
                    S.op("pe", lambda e, po=po, sm=sm, t=t, h=h: e.matmul(
                        po[:, 0:258], sm[:, :], vt[:, t, h, :], start=True, stop=False),
                        reads=[smrr, vt_r[t][h]], writes=[por])
                    yield
                    for i in range(2):
                        S.op("pe", lambda e, po=po, i=i, h=h, qc=qc, tc_=tc_: e.matmul(
                            po[:, 0:258], qkT[:, qc[i], tc_], Cb[:, h, i, :], start=False, stop=(i == 1)),
                            reads=[qk_r[qc[i]], Cb_r[h]], writes=[por])
                        yield
                    s6, s6r = sm6.next()
                    S.op("act", lambda e, s6=s6, po=po: e.activation(
                        s6[:, 0:1], po[:, 256:257], AF.Abs, scale=1.0 / 16.0), reads=[por], writes=[s6r])
                    yield
                    S.op("dve", lambda e, s6=s6, g=g, h=h: e.tensor_tensor(s6[:, 0:1], s6[:, 0:1], g[:, 4 + h:5 + h], ALU.max),
                         reads=[s6r, gr], writes=[s6r])
                    yield
                    S.op("dve", lambda e, s6=s6: e.reciprocal(s6[:, 1:2], s6[:, 0:1]), reads=[s6r], writes=[s6r])
                    yield
                    hb, hbr = hh.next()
                    S.op("dve", lambda e, hb=hb, po=po, s6=s6: e.tensor_scalar(
                        hb[:, :], po[:, 0:256], s6[:, 1:2], 1.0 / 16.0, ALU.mult, ALU.mult), reads=[por, s6r], writes=[hbr])
                    yield
                    S.op("dve", lambda e, hb=hb, s6=s6: e.bn_stats(s6[:, 2:8], hb[:, :]), reads=[hbr], writes=[s6r])
                    yield
                    S.op("dve", lambda e, s6=s6: e.bn_aggr(s6[:, 0:2], s6[:, 2:8]), reads=[s6r], writes=[s6r])
                    yield
                    S.op("act", lambda e, s6=s6: e.activation(s6[:, 2:3], s6[:, 1:2], AF.Sqrt, bias=epsln[:, 0:1]),
                         reads=[s6r, eps_r], writes=[s6r])
                    yield
                    S.op("dve", lambda e, s6=s6: e.reciprocal(s6[:, 3:4], s6[:, 2:3]), reads=[s6r], writes=[s6r])
                    yield
                    hn, hnr = hnb.next()
                    S.op("dve", lambda e, hn=hn, hb=hb, s6=s6: e.tensor_scalar(
                        hn[:, :], hb[:, :], s6[:, 0:1], s6[:, 3:4], ALU.subtract, ALU.mult), reads=[hbr, s6r], writes=[hnr])
                    yield
                    for i in range(2):
                        pt, ptr = psrM.next()
                        S.op("pe", lambda e, pt=pt, hn=hn, i=i: e.matmul(
                            pt[:, 0:128], hn[:, i * 128:(i + 1) * 128], identb, start=True, stop=True),
                            reads=[hnr, cstb_r], writes=[ptr])
                        yield
                        S.op("dve", lambda e, pt=pt, h=h, i=i, tc_=tc_: e.scalar_tensor_tensor(
                            hmTb[:, h * 2 + i, tc_], pt[:, 0:128], pvc("mng", h * 2 + i), sigmo[:, h * 2 + i, tc_],
                            ALU.mult, ALU.mult), reads=[ptr, pv_r, sigmo_r[h * 2 + i]], writes=[hm_r[h * 2 + i]])
                        yield
                    kk_, kkr = ktk.next()
                    for i in range(2):
                        pt, ptr = psrM.next()
                        S.op("pe", lambda e, pt=pt, i=i, kc_=kc_, tc_=tc_: e.matmul(
                            pt[:, 0:128], qkT[:, kc_[i], tc_], identb, start=True, stop=True),
                            reads=[qk_r[kc_[i]], cstb_r], writes=[ptr])
                        yield
                        S.op("act", lambda e, pt=pt, kk_=kk_, i=i, g=g, h=h: e.activation(
                            kk_[:, i * 128:(i + 1) * 128], pt[:, 0:128], AF.Identity, scale=g[:, h:h + 1]),
                            reads=[ptr, gr], writes=[kkr])
                        yield
                    for i in range(2):
                        pc, pcr = psrM.next()
                        S.op("pe", lambda e, pc=pc, kk_=kk_, i=i, t=t, h=h: e.matmul(
                            pc[:, 0:258], kk_[:, i * 128:(i + 1) * 128], vt[:, t, h, :], start=True, stop=True),
                            reads=[kkr, vt_r[t][h]], writes=[pcr])
                        yield
                        S.op("dve", lambda e, h=h, i=i, g=g: e.tensor_scalar_mul(Cst[:, h, i, :], Cst[:, h, i, :], g[:, 8 + h:9 + h]),
                             reads=[C_r[h], gr], writes=[C_r[h]])
                        yield
                        S.op("dve", lambda e, pc=pc, h=h, i=i, g=g: e.scalar_tensor_tensor(
                            Cst[:, h, i, :], pc[:, 0:258], g[:, 8 + h:9 + h], Cst[:, h, i, :], ALU.mult, ALU.add),
                            reads=[pcr, gr, C_r[h]], writes=[C_r[h]])
                        yield
                        S.op("act", lambda e, h=h, i=i: e.copy(Cb[:, h, i, :], Cst[:, h, i, :]), reads=[C_r[h]], writes=[Cb_r[h]])
                        yield

                gens = [head_chain(t, h, tc_, g, gr) for h in range(1)]
                while gens:
                    for gg in list(gens):
                        try:
                            next(gg)
                            yield
                        except StopIteration:
                            gens.remove(gg)

        NCH = TB // 64
        rcar = sb([128, 8, 1]); rcar_r = [Res() for _ in range(8)]
        S.op("dve", lambda e: e.memset(rcar[:, :, :], 0.0), writes=rcar_r)
        psrM = Ring(psr.items[0:3]); psrR = Ring(psr.items[3:7])
        tAM = Ring([(x1T[:, i, :], Res()) for i in range(2)])
        rwork = Ring([(x1T[:, 2 + 2 * i:4 + 2 * i, :].rearrange("p a b -> p (a b)")[:, 0:1 + TB], Res()) for i in range(2)])
        wpjM = Ring([(x1T[:, 6 + i, :].bitcast(BF16).rearrange("p (k c) -> p k c", c=128), Res()) for i in range(2)])
        scr_r = [r for _, r in tAM.items + rwork.items + wpjM.items]
        lowT = sb([128, 2, TB], BF16); low_r = [Res(), Res()]
        rtmp = Ring([(aT[:, 2 * i:2 * i + 2, :].rearrange("p a b -> p (a b)").bitcast(F32), Res()) for i in range(10)])
        ARbd = sb([128, NCH, 256], BF16); Bbd = sb([128, NCH, 128], BF16); Kbd = sb([128, NCH, 128], BF16)
        Vbd = sb([128, NCH, 128], BF16); Ynbd = sb([128, 128], BF16)
        bd_r = Res(); ynbd_r = Res()
        for tns in (ARbd, Bbd, Kbd, Vbd):
            S.op("dve", lambda e, tns=tns: e.memset(tns[:, :, :], 0.0), writes=[bd_r])
        S.op("dve", lambda e: e.memset(Ynbd[:, :], 0.0), writes=[ynbd_r])
        gam = sb([128, NCH]); gam_r = Res()
        Hst = sb([128, 2, 64]); H_r = [Res() for _ in range(2)]
        Hb = sb([128, 2, 64], BF16); Hb_r = [Res() for _ in range(2)]
        S.op("dve", lambda e: e.memset(Hst[:, :, :], 0.0), writes=H_r)
        S.op("dve", lambda e: e.memset(Hb[:, :, :], 0.0), writes=Hb_r)
        vst = sb([128, NCH, 64], BF16); vst_r = Res()
        btk = sb([128, NCH, 256], BF16); btk_r = Res()
        ub = ring(4, [128, 64], BF16)
        gn6 = ring(4, [128, 8])
        htmp = ring(2, [128, 64])
        maskAR = cst[:, C_MUS:C_MUS + 256]; mask_r = cst_r
        mls = cst[:, C_MLS:C_MLS + 128]

        xTv = xT[:, :, :].rearrange("p a b -> p (a b)").bitcast(BF16)
        mAK = xTv[:, 0:NCH * 512].rearrange("p (n c) -> p n c", c=512)
        Xs = xTv[:, 4096:4096 + NCH * 128].rearrange("p (n c) -> p n c", c=128)
        xTbv = xTb[:, :, :].rearrange("p a b -> p (a b)")
        PW = [[xTbv[:, (b * 8 + n) * 128:(b * 8 + n + 1) * 128] for n in range(NCH)] for b in range(2)]
        PWT = [[xTbv[:, 2048 + (b * 8 + n) * 128:2048 + (b * 8 + n + 1) * 128] for n in range(NCH)] for b in range(2)]
        hh = Ring([(xTv[:, 5120 + i * 512:5120 + (i + 1) * 512].bitcast(F32), Res()) for i in range(4)])
        hnb = Ring([(xTv[:, 7168 + i * 256:7168 + (i + 1) * 256], Res()) for i in range(4)])
        mA_r = [Res() for _ in range(NCH)]; mK_r = [Res() for _ in range(NCH)]; X_r = [Res() for _ in range(NCH)]
        PW_r = [[Res() for _ in range(NCH)] for _ in range(2)]; PWT_r = [[Res() for _ in range(NCH)] for _ in range(2)]

        def v3(ap):
            return ap.rearrange("p (n l) -> p n l", l=64)

        def shifted(ci, ps, pr):
            wk, wkr = rwork.next()
            S.op("act", lambda e, wk=wk: e.copy(wk[:, 0:1], rcar[:, ci, :]), reads=[rcar_r[ci]], writes=[wkr])
            S.op("act", lambda e, wk=wk, ps=ps: e.copy(wk[:, 1:1 + TB], ps[:, :]), reads=[pr], writes=[wkr])
            S.op("act", lambda e, wk=wk: e.copy(rcar[:, ci, :], wk[:, TB:TB + 1]), reads=[wkr], writes=[rcar_r[ci]])
            ta, tar = rtmp.next()
            S.op("dve", lambda e, wk=wk, ta=ta: e.tensor_scalar_mul(ta[:, :], wk[:, 1:1 + TB], pvc("omu", ci)),
                 reads=[wkr, pv_r], writes=[tar])
            S.op("dve", lambda e, wk=wk, ta=ta: e.scalar_tensor_tensor(
                ta[:, :], wk[:, 0:TB], pvc("mu", ci), ta[:, :], ALU.mult, ALU.add), reads=[wkr, pv_r, tar], writes=[tar])
            return ta, tar

        krw = int(os.environ.get("KRW", "9"))

        def rwkv(blk):
            ps, pr = projT(winT, 1792, x1Tb, x1_r, pring=psrR)
            ta, tar = shifted(6, ps, pr)
            S.op("act", lambda e, ta=ta: e.activation(lowT[0:64, 0, :], ta[0:64, :], AF.Tanh), reads=[tar], writes=[low_r[0]])
            yield
            S.op("act", lambda e, ta=ta: e.copy(lowT[64:128, 0, :], ta[64:128, :]), reads=[tar], writes=[low_r[0]])
            yield
            ps, pr = projT(winT, 1920, x1Tb, x1_r, pring=psrR)
            ta, tar = shifted(7, ps, pr)
            S.op("act", lambda e, ta=ta: e.activation(lowT[:, 1, :], ta[:, :], AF.Sigmoid), reads=[tar], writes=[low_r[1]])
            yield
            for p in range(2):
                cs = slice(p * 128, (p + 1) * 128)
                ps, pr = projT(winT, 1024 + p * 128, x1Tb, x1_r, pring=psrR)
                r_, r_r = shifted(p, ps, pr)
                ps, pr = projT(winT, 1280 + p * 128, x1Tb, x1_r, pring=psrR)
                k_, k_r = shifted(2 + p, ps, pr)
                ps, pr = projT(winT, 1536 + p * 128, x1Tb, x1_r, pring=psrR)
                v_, v_r = shifted(4 + p, ps, pr)
                pw, pwr = psrR.next()
                S.op("pe", lambda e, pw=pw, cs=cs: e.matmul(pw[:, :], rw2a2[0:64, cs], lowT[0:64, 0, :], start=True, stop=True),
                     reads=[rw_r, low_r[0]], writes=[pwr])
                yield
                lw, lwr = rtmp.next()
                S.op("act", lambda e, lw=lw, pw=pw, p=p: e.activation(lw[:, :], pw[:, :], AF.Sigmoid, bias=pvc("w0", p)),
                     reads=[pwr, pv_r], writes=[lwr])
                yield
                S.op("dve", lambda e, lw=lw: e.tensor_scalar_mul(lw[:, :], lw[:, :], -float(np.exp(-0.5))), reads=[lwr], writes=[lwr])
                yield
                pa, par = psrR.next()
                S.op("pe", lambda e, pa=pa, cs=cs: e.matmul(pa[:, :], rw2a2[64:128, cs], lowT[64:128, 0, :], start=True, stop=True),
                     reads=[rw_r, low_r[0]], writes=[par])
                yield
                a_, a_r = rtmp.next()
                S.op("act", lambda e, a_=a_, pa=pa, p=p: e.activation(a_[:, :], pa[:, :], AF.Sigmoid, bias=pvc("a0", p)),
                     reads=[par, pv_r], writes=[a_r])
                yield
                pg, pgr = psrR.next()
                S.op("pe", lambda e, pg=pg, cs=cs: e.matmul(pg[:, :], rg2[:, cs], lowT[:, 1, :], start=True, stop=True),
                     reads=[rw_r, low_r[1]], writes=[pgr])
                yield
                g_, g_r = rtmp.next()
                S.op("act", lambda e, g_=g_, pg=pg: e.copy(g_[:, :], pg[:, :]), reads=[pgr], writes=[g_r])
                yield
                kk, kkr = rtmp.next()
                S.op("dve", lambda e, kk=kk, k_=k_, p=p: e.tensor_scalar_mul(kk[:, :], k_[:, :], pvc("kk", p)),
                     reads=[k_r, pv_r], writes=[kkr])
                yield
                sq, sqr = tB.next()
                S.op("act", lambda e, sq=sq, kk=kk: e.activation(sq[:, :], kk[:, :], AF.Square), reads=[kkr], writes=[sqr])
                yield
                pq, pqr = psrR.next()
                S.op("pe", lambda e, pq=pq, sq=sq: e.matmul(pq[:, :], bob, sq[:, :], start=True, stop=True),
                     reads=[cstb_r, sqr], writes=[pqr])
                yield
                t1, t1r = rtmp.next()
                S.op("act", lambda e, t1=t1, pq=pq: e.activation(t1[:, :], pq[:, :], AF.Sqrt), reads=[pqr], writes=[t1r])
                yield
                S.op("dve", lambda e, t1=t1: e.tensor_scalar_max(t1[:, :], t1[:, :], 1e-12), reads=[t1r], writes=[t1r])
                yield
                S.op("dve", lambda e, t1=t1: e.reciprocal(t1[:, :], t1[:, :]), reads=[t1r], writes=[t1r])
                yield
                S.op("dve", lambda e, t1=t1, kk=kk: e.tensor_tensor(kk[:, :], kk[:, :], t1[:, :], ALU.mult),
                     reads=[t1r, kkr], writes=[kkr])
                yield
                S.op("dve", lambda e, t1=t1, a_=a_, p=p: e.tensor_scalar(t1[:, :], a_[:, :], 1.0, pvc("ka", p), ALU.subtract, ALU.mult),
                     reads=[a_r, pv_r], writes=[t1r])
                yield
                S.op("dve", lambda e, t1=t1, k_=k_: e.scalar_tensor_tensor(k_[:, :], t1[:, :], 1.0, k_[:, :], ALU.add, ALU.mult),
                     reads=[t1r, k_r], writes=[k_r])
                yield
                t2, t2r = tB.next()
                S.op("dve", lambda e, t2=t2, r_=r_, k_=k_, p=p: e.scalar_tensor_tensor(
                    t2[:, :], r_[:, :], pvc("rrk", p), k_[:, :], ALU.mult, ALU.mult), reads=[r_r, k_r, pv_r], writes=[t2r])
                yield
                pb, pbr = psrR.next()
                S.op("pe", lambda e, pb=pb, t2=t2: e.matmul(pb[:, :], bob, t2[:, :], start=True, stop=True),
                     reads=[cstb_r, t2r], writes=[pbr])
                yield
                bon, bonr = rtmp.next()
                S.op("dve", lambda e, bon=bon, pb=pb, v_=v_: e.tensor_tensor(bon[:, :], pb[:, :], v_[:, :], ALU.mult),
                     reads=[pbr, v_r], writes=[bonr])
                yield
                cl, clr = rtmp.next()
                S.op("dve", lambda e, cl=cl, lw=lw: e.tensor_tensor_scan(cl[:, :], rst, lw[:, :], 0.0, ALU.mult, ALU.add),
                     reads=[lwr, cstb_r], writes=[clr])
                yield
                e1, e1r = tA.next()
                S.op("act", lambda e, e1=e1, cl=cl: e.activation(e1[:, :], cl[:, :], AF.Exp), reads=[clr], writes=[e1r])
                yield
                S.op("act", lambda e, e1=e1: e.copy(gam[:, :], v3(e1[:, :])[:, :, 63]), reads=[e1r], writes=[gam_r])
                yield
                for hf in range(2):
                    hs = slice(hf * 64, hf * 64 + 64)
                    S.op("dve", lambda e, hs=hs, hf=hf, r_=r_, e1=e1: e.tensor_tensor(
                        ARbd[hs, :, 128 + hf * 64:128 + hf * 64 + 64], v3(r_[hs, :]), v3(e1[hs, :]), ALU.mult),
                        reads=[r_r, e1r], writes=[bd_r])
                    yield
                e2, e2r = tA.next()
                S.op("act", lambda e, e2=e2, cl=cl: e.activation(e2[:, :], cl[:, :], AF.Exp, scale=-1.0), reads=[clr], writes=[e2r])
                yield
                S.op("dve", lambda e, t1=t1, kk=kk, a_=a_: e.tensor_tensor(t1[:, :], kk[:, :], a_[:, :], ALU.mult),
                     reads=[kkr, a_r], writes=[t1r])
                yield
                for hf in range(2):
                    hs = slice(hf * 64, hf * 64 + 64)
                    S.op("dve", lambda e, hs=hs, hf=hf, t1=t1, e2=e2: e.tensor_tensor(
                        Bbd[hs, :, hf * 64:hf * 64 + 64], v3(t1[hs, :]), v3(e2[hs, :]), ALU.mult), reads=[t1r, e2r], writes=[bd_r])
                    yield
                    S.op("dve", lambda e, hs=hs, hf=hf, k_=k_, e2=e2: e.tensor_tensor(
                        Kbd[hs, :, hf * 64:hf * 64 + 64], v3(k_[hs, :]), v3(e2[hs, :]), ALU.mult), reads=[k_r, e2r], writes=[bd_r])
                    yield
                    S.op("act", lambda e, hs=hs, hf=hf, v_=v_: e.copy(Vbd[hs, :, hf * 64:hf * 64 + 64], v3(v_[hs, :])),
                         reads=[v_r], writes=[bd_r])
                    yield
                S.op("dve", lambda e, cl=cl, lw=lw: e.tensor_tensor(cl[:, :], cl[:, :], lw[:, :], ALU.subtract),
                     reads=[clr, lwr], writes=[clr])
                yield
                e3, e3r = tA.next()
                S.op("act", lambda e, e3=e3, cl=cl: e.activation(e3[:, :], cl[:, :], AF.Exp), reads=[clr], writes=[e3r])
                yield
                for hf in range(2):
                    hs = slice(hf * 64, hf * 64 + 64)
                    S.op("dve", lambda e, hs=hs, hf=hf, kk=kk, e3=e3: e.scalar_tensor_tensor(
                        ARbd[hs, :, hf * 64:hf * 64 + 64], v3(kk[hs, :]), -1.0, v3(e3[hs, :]), ALU.mult, ALU.mult),
                        reads=[kkr, e3r], writes=[bd_r])
                    yield
                if krw <= 1:
                    S.op("dve", lambda e, p=p: e.memset(yrTb[:, p, :], 0.0), writes=[yr_r[p]])
                    yield
                    continue
                pv_, pvr_ = psrR.next()
                for n in range(NCH):
                    S.op("pe", lambda e, n=n, pv_=pv_: e.matmul(pv_[:, n * 64:(n + 1) * 64], Vbd[:, n, :], istb, start=True, stop=True),
                         reads=[bd_r, cstb_r], writes=[pvr_])
                    yield
                S.op("act", lambda e, pv_=pv_: e.copy(vst[:, :, :], v3(pv_[:, :])), reads=[pvr_], writes=[vst_r])
                yield
                for n0 in range(0, NCH, 2):
                    pt, ptr = psrR.next()
                    for n in (n0, n0 + 1):
                        o = (n - n0) * 256
                        S.op("pe", lambda e, n=n, pt=pt, o=o: e.matmul(pt[:, o:o + 128], Bbd[:, n, :], identb, start=True, stop=True),
                             reads=[bd_r, cstb_r], writes=[ptr])
                        yield
                        S.op("pe", lambda e, n=n, pt=pt, o=o: e.matmul(pt[:, o + 128:o + 256], Kbd[:, n, :], identb, start=True, stop=True),
                             reads=[bd_r, cstb_r], writes=[ptr])
                        yield
                    S.op("act", lambda e, n0=n0, pt=pt: e.copy(btk[:, n0:n0 + 2, :], pt[:, :].rearrange("p (n l) -> p n l", l=256)),
                         reads=[ptr], writes=[btk_r])
                    yield
                if krw <= 2:
                    S.op("dve", lambda e, p=p: e.memset(yrTb[:, p, :], 0.0), writes=[yr_r[p]])
                    yield
                    continue
                pyo, pyor = pyo_bank
                for g0 in range(0, NCH, 4):
                    G = list(range(g0, min(g0 + 4, NCH)))
                    pAs = {}
                    for n in G:
                        pA, pAr = psrR.next()
                        pAs[n] = (pA, pAr)
                        S.op("pe", lambda e, pA=pA, n=n: e.matmul(pA[:, 0:256], Bbd[:, n, :], ARbd[:, n, :], start=True, stop=True),
                             reads=[bd_r], writes=[pAr])
                        yield
                        S.op("pe", lambda e, pA=pA, n=n: e.matmul(pA[:, 256:512], Kbd[:, n, :], ARbd[:, n, :], start=True, stop=True),
                             reads=[bd_r], writes=[pAr])
                        yield
                    for n in G:
                        pA, pAr = pAs[n]
                        S.op("dve", lambda e, pA=pA, n=n: e.tensor_tensor(mAK[:, n, 0:256], pA[:, 0:256], maskAR[:, :], ALU.mult),
                             reads=[pAr, mask_r], writes=[mA_r[n]])
                        yield
                        S.op("dve", lambda e, pA=pA, n=n: e.tensor_tensor(mAK[:, n, 256:512], pA[:, 256:512], maskAR[:, :], ALU.mult),
                             reads=[pAr, mask_r], writes=[mK_r[n]])
                        yield
                    pTs = {}
                    for n in G:
                        pT, pTr = psrR.next()
                        pTs[n] = (pT, pTr)
                        S.op("pe", lambda e, pT=pT, n=n: e.matmul(pT[:, 0:128], ARbd[:, n, 0:128], Bbd[:, n, :], start=True, stop=True),
                             reads=[bd_r], writes=[pTr])
                        yield
                    for n in G:
                        pT, pTr = pTs[n]
                        S.op("dve", lambda e, pT=pT, n=n: e.tensor_tensor(PWT[0][n], pT[:, 0:128], mls, ALU.mult),
                             reads=[pTr, cst_r], writes=[PWT_r[0][n]])
                        yield
                        S.op("dve", lambda e, n=n: e.tensor_tensor(Xs[:, n, :], mAK[:, n, 0:128], identb, ALU.add),
                             reads=[mA_r[n], cstb_r], writes=[X_r[n]])
                        yield
                    for j in range(1, 6):
                        b0, b1 = (j - 1) % 2, j % 2
                        p2s = {}
                        for n in G:
                            cur = mAK[:, n, 0:128] if j == 1 else PW[b0][n]
                            curr = mA_r[n] if j == 1 else PW_r[b0][n]
                            ct, ctr_ = PWT[b0][n], PWT_r[b0][n]
                            p2, p2r = psrR.next()
                            p2s[n] = (p2, p2r)
                            if j < 5:
                                S.op("pe", lambda e, p2=p2, cur=cur, ct=ct: e.matmul(p2[:, 0:128], ct, cur, start=True, stop=True),
                                     reads=[curr, ctr_], writes=[p2r])
                                yield
                            S.op("pe", lambda e, p2=p2, cur=cur, ct=ct: e.matmul(p2[:, 128:256], cur, ct, start=True, stop=True),
                                 reads=[curr, ctr_], writes=[p2r])
                            yield
                        for n in G:
                            p2, p2r = p2s[n]
                            S.op("dve", lambda e, p2=p2, n=n, b1=b1: e.tensor_copy(PWT[b1][n], p2[:, 128:256]),
                                 reads=[p2r], writes=[PWT_r[b1][n]])
                            yield
                            if j < 5:
                                S.op("act", lambda e, p2=p2, n=n, b1=b1: e.copy(PW[b1][n], p2[:, 0:128]),
                                     reads=[p2r], writes=[PW_r[b1][n]])
                                yield
                        pxs = {}
                        for n in G:
                            px, pxr = psrR.next()
                            pxs[n] = (px, pxr)
                            S.op("pe", lambda e, px=px, n=n, b1=b1: e.matmul(px[:, 0:128], PWT[b1][n], Xs[:, n, :], start=True, stop=True),
                                 reads=[PWT_r[b1][n], X_r[n]], writes=[pxr])
                            yield
                        for n in G:
                            px, pxr = pxs[n]
                            S.op("dve", lambda e, px=px, n=n: e.tensor_tensor(Xs[:, n, :], px[:, 0:128], Xs[:, n, :], ALU.add),
                                 reads=[pxr, X_r[n]], writes=[X_r[n]])
                            yield
                for n in range(NCH):
                    pw_, pw_r = psrR.next()
                    S.op("pe", lambda e, pw_=pw_, n=n, p=p: e.matmul(pw_[:, 0:64], ARbd[:, n, 0:128], Hb[:, p, :], start=True, stop=False),
                         reads=[bd_r, Hb_r[p]], writes=[pw_r])
                    yield
                    S.op("pe", lambda e, pw_=pw_, n=n: e.matmul(pw_[:, 0:64], mAK[:, n, 256:384], vst[:, n, :], start=False, stop=True),
                         reads=[mK_r[n], vst_r], writes=[pw_r])
                    yield
                    w_b, wbr = ub.next()
                    S.op("act", lambda e, w_b=w_b, pw_=pw_: e.copy(w_b[:, :], pw_[:, 0:64]), reads=[pw_r], writes=[wbr])
                    yield
                    pu, pur_ = psrR.next()
                    S.op("pe", lambda e, pu=pu, n=n, w_b=w_b: e.matmul(pu[:, 0:64], Xs[:, n, :], w_b[:, :], start=True, stop=True),
                         reads=[X_r[n], wbr], writes=[pur_])
                    yield
                    u_b, ubr = ub.next()
                    S.op("dve", lambda e, u_b=u_b, pu=pu: e.tensor_copy(u_b[:, :], pu[:, 0:64]), reads=[pur_], writes=[ubr])
                    yield
                    ph, phr = psrR.next()
                    S.op("pe", lambda e, ph=ph, n=n, u_b=u_b: e.matmul(ph[:, 0:64], btk[:, n, 0:128], u_b[:, :], start=True, stop=False),
                         reads=[btk_r, ubr], writes=[phr])
                    yield
                    S.op("pe", lambda e, ph=ph, n=n: e.matmul(ph[:, 0:64], btk[:, n, 128:256], vst[:, n, :], start=False, stop=True),
                         reads=[btk_r, vst_r], writes=[phr])
                    yield
                    py, pyr = psrR.next()
                    S.op("pe", lambda e, py=py, n=n, p=p: e.matmul(py[:, 0:64], ARbd[:, n, 128:256], Hb[:, p, :], start=True, stop=False),
                         reads=[bd_r, Hb_r[p]], writes=[pyr])
                    yield
                    S.op("pe", lambda e, py=py, n=n, u_b=u_b: e.matmul(py[:, 0:64], mAK[:, n, 128:256], u_b[:, :], start=False, stop=False),
                         reads=[mA_r[n], ubr], writes=[pyr])
                    yield
                    S.op("pe", lambda e, py=py, n=n: e.matmul(py[:, 0:64], mAK[:, n, 384:512], vst[:, n, :], start=False, stop=True),
                         reads=[mK_r[n], vst_r], writes=[pyr])
                    yield
                    ht, htr = htmp.next()
                    S.op("dve", lambda e, ht=ht, ph=ph, p=p: e.tensor_tensor(ht[:, :], ph[:, 0:64], Hst[:, p, :], ALU.add),
                         reads=[phr, H_r[p]], writes=[htr])
                    yield
                    S.op("act", lambda e, ht=ht, p=p, n=n: e.activation(Hb[:, p, :], ht[:, :], AF.Identity, scale=gam[:, n:n + 1]),
                         reads=[htr, gam_r], writes=[Hb_r[p]])
                    yield
                    S.op("act", lambda e, ht=ht, p=p, n=n: e.activation(Hst[:, p, :], ht[:, :], AF.Identity, scale=gam[:, n:n + 1]),
                         reads=[htr, gam_r], writes=[H_r[p]])
                    yield
                    g6, g6r = gn6.next()
                    S.op("dve", lambda e, g6=g6, py=py: e.bn_stats(g6[:, 2:8], py[:, 0:64]), reads=[pyr], writes=[g6r])
                    yield
                    S.op("dve", lambda e, g6=g6: e.bn_aggr(g6[:, 0:2], g6[:, 2:8]), reads=[g6r], writes=[g6r])
                    yield
                    S.op("act", lambda e, g6=g6: e.activation(g6[:, 2:3], g6[:, 1:2], AF.Sqrt, bias=epsgn[:, 0:1]),
                         reads=[g6r, eps_r], writes=[g6r])
                    yield
                    S.op("dve", lambda e, g6=g6: e.reciprocal(g6[:, 3:4], g6[:, 2:3]), reads=[g6r], writes=[g6r])
                    yield
                    for hf in range(2):
                        hs = slice(hf * 64, hf * 64 + 64)
                        S.op("dve", lambda e, hs=hs, hf=hf, py=py, g6=g6: e.tensor_scalar(
                            Ynbd[hs, hf * 64:hf * 64 + 64], py[hs, 0:64], g6[hs, 0:1], g6[hs, 3:4], ALU.subtract, ALU.mult),
                            reads=[pyr, g6r], writes=[ynbd_r])
                        yield
                    S.op("pe", lambda e, n=n, pyo=pyo: e.matmul(pyo[:, n * 64:(n + 1) * 64], Ynbd[:, :], istb, start=True, stop=True),
                         reads=[ynbd_r, cstb_r], writes=[pyor])
                    yield
                if krw <= 5:
                    S.op("dve", lambda e, p=p: e.memset(yrTb[:, p, :], 0.0), writes=[yr_r[p]])
                    yield
                    continue
                S.op("dve", lambda e, t1=t1, pyo=pyo, p=p: e.tensor_scalar(
                    t1[:, :], pyo[:, :], pvc("gng", p), pvc("gnb", p), ALU.mult, ALU.add), reads=[pyor, pv_r], writes=[t1r])
                yield
                S.op("dve", lambda e, t1=t1, bon=bon: e.tensor_tensor(t1[:, :], t1[:, :], bon[:, :], ALU.add),
                     reads=[t1r, bonr], writes=[t1r])
                yield
                S.op("dve", lambda e, t1=t1, g_=g_, p=p: e.tensor_tensor(yrTb[:, p, :], t1[:, :], g_[:, :], ALU.mult),
                     reads=[t1r, g_r], writes=[yr_r[p]])
                yield

        def merge_out(blk):
            for c in range(8):
                ps, pr = projT(wgate, c * 128, x1Tb, x1_r)
                S.op("act", lambda e, ps=ps, c=c: e.activation(sga[:, c, :], ps[:, :], AF.Sigmoid), reads=[pr], writes=[sga_r[c]])
                ps, pr = projT(wgate, 1024 + c * 128, x1Tb, x1_r)
                S.op("act", lambda e, ps=ps, c=c: e.activation(sgb[:, c, :], ps[:, :], AF.Sigmoid), reads=[pr], writes=[sgb_r[c]])
            for c in range(8):
                psa, par = projT(wa_d, c * 128, lambda kc: HYf[:, kc // 2, kc % 2, :], [hyf_r[kc // 2] for kc in range(8)])
                psb, pbr = projT(wb_d, c * 128, lambda kc: HYf[:, kc // 2, 2 + kc % 2, :], [hyf_r[kc // 2] for kc in range(8)])
                ta, tar = tA.next()
                S.op("dve", lambda e, ta=ta, psa=psa, c=c: e.tensor_tensor(ta[:, :], psa[:, :], sga[:, c, :], ALU.mult),
                     reads=[par, sga_r[c]], writes=[tar])
                tb, tbr = tA.next()
                S.op("dve", lambda e, tb=tb, psb=psb, c=c: e.tensor_tensor(tb[:, :], psb[:, :], sgb[:, c, :], ALU.mult),
                     reads=[pbr, sgb_r[c]], writes=[tbr])
                S.op("dve", lambda e, ta=ta, tb=tb, c=c: e.tensor_tensor(mgTb[:, c, :], ta[:, :], tb[:, :], ALU.add),
                     reads=[tar, tbr], writes=[mg_r[c]])
            for c in range(8):
                ps, pr = projT(wo_d, c * 128, mgTb, mg_r)
                S.op("dve", lambda e, ps=ps, c=c: e.scalar_tensor_tensor(
                    zT[:, c, :], x1T[:, c, :], ALPHA, ps[:, :], ALU.mult, ALU.add), reads=[pr, x1_r[c]], writes=[zT_r[c]])
            layer_norm("ln2_g", "ln2_b", epsln, x2T, x2Tb, x2_r)

        x1f_r = Res()
        x1bl_r = [[Res(), Res()] for _ in range(nbl)]; x1ba_r = [[Res(), Res()] for _ in range(nbl)]
        hyl_r = [Res() for _ in range(NBT)]; hya_r = [Res() for _ in range(NBT)]
        oh = sb([128, 4]); oh_r = Res()
        S.dma("sp", oh[:, :], oh_d[:, :], writes=[oh_r])

        def half(t, h):
            return t[:, 4 * h:4 * h + 4, :].rearrange("p a b -> p (a b)")

        for blk in range(nbl):
            load_xT(blk)
            ffn_ln(xT, xTb, xT_r, w1g, w1u, w1d, "ln1_g", "ln1_b", x1T, x1Tb, x1_r)
            rows = slice(blk * 128, (blk + 1) * 128)
            S.dma("sp", x1f_loc[rows, :], x1T[:, :, :].rearrange("p a b -> p (a b)"), reads=x1_r, writes=[x1f_r])
            for h in range(2):
                S.dma("sp", x1b_loc[blk][h][:, :], half(x1Tb, h), reads=x1_r, writes=[x1bl_r[blk][h]])
                S.cc("AllGather", x1b_loc[blk][h][:, :], x1b_all[blk][h][:, :], groups,
                     reads=[x1bl_r[blk][h]], writes=[x1ba_r[blk][h]])
        for gb in range(NBT):
            i, blk = gb // nbl, gb % nbl
            for h in range(2):
                S.dma("sp", half(x1Tb, h), x1b_all[blk][h][i * 128:(i + 1) * 128, :], reads=[x1ba_r[blk][h]],
                      writes=x1_r[4 * h:4 * h + 4])
            if gb == 0:
                S.op("dve", lambda e: e.memset(x1T[:, 0, 0:8], 0.0), writes=x1_r + scr_r)
            gens = [mlstm(gb), rwkv(gb)]
            while gens:
                for gg in list(gens):
                    try:
                        next(gg)
                    except StopIteration:
                        gens.remove(gg)
            S.dma("sp", hy_loc[gb][:, :], hyT[:, :, :].rearrange("p a b -> p (a b)"), reads=hm_r + yr_r, writes=[hyl_r[gb]])
            S.cc("AllGather", hy_loc[gb][:, :], hy_all[gb][:, :], groups, reads=[hyl_r[gb]], writes=[hya_r[gb]])
        for blk in range(nbl):
            rows = slice(blk * 128, (blk + 1) * 128)
            S.dma("sp", x1T[:, :, :].rearrange("p a b -> p (a b)"), x1f_loc[rows, :], reads=[x1f_r], writes=x1_r + scr_r)
            for h in range(2):
                S.dma("sp", half(x1Tb, h), x1b_loc[blk][h][:, :], reads=[x1bl_r[blk][h]], writes=x1_r[4 * h:4 * h + 4])
            for i in range(4):
                dst = HYf[:, i, :, :].rearrange("p a b -> p (a b)")
                for j in range(4):
                    k = j * nbl + blk
                    stg = sgb[:, 4 * (j % 2):4 * (j % 2) + 4, :].rearrange("p a b -> p (a b)")
                    stg_r = sgb_r[4 * (j % 2):4 * (j % 2) + 4]
                    S.dma("sp", stg, hy_all[k][i * 128:(i + 1) * 128, :], reads=[hya_r[k]], writes=stg_r)
                    if j == 0:
                        S.op("dve", lambda e, dst=dst, stg=stg, j=j: e.tensor_scalar_mul(dst, stg, oh[:, j:j + 1]),
                             reads=stg_r + [oh_r], writes=[hyf_r[i]])
                    else:
                        S.op("dve", lambda e, dst=dst, stg=stg, j=j: e.scalar_tensor_tensor(
                            dst, stg, oh[:, j:j + 1], dst, ALU.mult, ALU.add),
                            reads=stg_r + [oh_r, hyf_r[i]], writes=[hyf_r[i]])
            merge_out(blk)
            ffn_ln(x2T, x2Tb, x2_r, w2g, w2u, w2d, "ln3_g", "ln3_b", x3T, x3Tb, x3_r)
            store_T(blk, x3T, x3_r)

        S.op("sp", lambda e: e.nop(), reads=[out_r])
        S.emit()
    return nc


def _consts():
    c = np.zeros((128, NCST), np.float32)
    i = np.arange(128)
    c[:, C_ID:C_ID + 128] = np.eye(128)
    c[:, C_OD:C_OD + 128] = 1.0 / 1024.0
    c[:, C_TRI:C_TRI + 128] = (i[:, None] <= i[None, :])
    c[:, C_ONE:C_ONE + 128] = 1.0
    c[:, C_MUS:C_MUS + 128] = (i[:, None] < i[None, :])
    c[:, C_MLS:C_MLS + 128] = (i[:, None] > i[None, :])
    c[:, C_IST:C_IST + 64] = np.concatenate([np.eye(64), np.eye(64)], axis=0)
    c[:, C_BO:C_BO + 128] = ((i[:, None] // 64) == (i[None, :] // 64))
    return c


def _tile(Wm):
    F = Wm.shape[1]
    return np.ascontiguousarray(
        np.asarray(Wm, np.float32).reshape(8, 128, F // 128, 128).transpose(2, 1, 0, 3).reshape(F, 1024))


def _fm(v):
    v = np.asarray(v, np.float32).reshape(-1, 128)
    return np.ascontiguousarray(v.T)


def kernel(**inp):
    nbl = int(os.environ.get("KNBL", "4"))
    ngrp = int(os.environ.get("KNGRP", "2"))
    x = np.asarray(inp["x"], np.float32)
    W = np.asarray(inp["w_in"][0], np.float32)
    gates_t = _tile(W[:, 7432:9480])
    tiled = {n: _tile(np.asarray(inp[n][0], np.float32)) for n in ("w_branch_a", "w_branch_b", "w_out")}
    cw = inp["m_conv_w"][0]; cbv = inp["m_conv_b"][0]
    mu = inp["r_mu"][0]
    in_maps = []
    for c in range(4 * ngrp):
        b, r = c // 4, c % 4
        pv = np.zeros((128, NPV), np.float32)

        def put(name, arr):
            a = _fm(arr)
            pv[:, PV[name]:PV[name] + a.shape[1]] = a

        for n in ("ln1_g", "ln1_b", "ln2_g", "ln2_b", "ln3_g", "ln3_b"):
            put(n, inp[n][0])
        qs = slice(r * 256, (r + 1) * 256)
        ks = slice(1024 + r * 256, 1024 + (r + 1) * 256)
        for j in range(4):
            put("cw%d" % j, np.concatenate([cw[j][qs], cw[j][ks]]))
        put("cb", np.concatenate([cbv[qs], cbv[ks]]))
        put("mng", inp["m_norm_g"][0][qs])
        ps_ = slice(r * 256, (r + 1) * 256)
        mul = np.concatenate([mu[0:1024][ps_], mu[1024:2048][ps_], mu[2048:3072][ps_], mu[3072:3328]])
        put("mu", mul)
        put("omu", 1.0 - mul)
        put("w0", inp["r_w0"][0][ps_]); put("a0", inp["r_a0"][0][ps_]); put("kk", inp["r_k_k"][0][ps_])
        put("ka", inp["r_k_a"][0][ps_]); put("rrk", inp["r_r_k"][0].reshape(-1)[ps_])
        put("gng", inp["r_gn_g"][0][ps_]); put("gnb", inp["r_gn_b"][0][ps_])
        wl = np.concatenate([W[:, qs], W[:, ks], W[:, 2048 + r * 256:2048 + (r + 1) * 256],
                             W[:, 3072 + r * 256:3072 + (r + 1) * 256],
                             W[:, RC0 + r * 256:RC0 + (r + 1) * 256],
                             W[:, RC0 + 1024 + r * 256:RC0 + 1024 + (r + 1) * 256],
                             W[:, RC0 + 2048 + r * 256:RC0 + 2048 + (r + 1) * 256],
                             W[:, RC0 + 3072:RC0 + 3328]], axis=1)
        assert wl.shape[1] == 2048
        wif = np.zeros((D, 8), np.float32)
        wif[:, 0] = W[:, 4096 + r]
        wif[:, 4] = W[:, 4100 + r]
        gbias = np.zeros((128, 8), np.float32)
        gbias[:, 0] = inp["m_i_bias"][0][r]
        gbias[:, 4] = inp["m_f_bias"][0][r]
        gbias[:, 5:8] = 30.0
        m = {"cst": _consts(), "pv": pv, "gbias": gbias,
             "rst": np.ascontiguousarray(np.broadcast_to((np.arange(512)[None, :] % 64 != 0), (128, 512)).astype(np.float32)),
             "w_loc": np.ascontiguousarray(wl), "w_loc_t": _tile(wl), "w_if": wif, "w_gate": gates_t,
             "r_w2": np.ascontiguousarray(inp["r_w2"][0][:, ps_]), "r_a2": np.ascontiguousarray(inp["r_a2"][0][:, ps_]),
             "r_g2": np.ascontiguousarray(inp["r_g2"][0][:, ps_]),
             "x": np.ascontiguousarray(x[b, r * nbl * TB:(r + 1) * nbl * TB]),
             "oh": np.ascontiguousarray(np.broadcast_to(np.eye(4, dtype=np.float32)[r][None, :], (128, 4)))}
        for n in ("ffn1_w_gate", "ffn1_w_up", "ffn1_w_down", "ffn2_w_gate", "ffn2_w_up", "ffn2_w_down"):
            m[n] = np.ascontiguousarray(inp[n][0], dtype=np.float32)
        m.update(tiled)
        in_maps.append(m)
    groups = [list(range(4 * g, 4 * g + 4)) for g in range(ngrp)]
    nc = build(nbl, groups)
    res = run_bass_kernel_spmd(nc, in_maps, core_ids=list(range(4 * ngrp)))
    out = np.zeros((ngrp, 4 * nbl * TB, D), np.float32)
    for c in range(4 * ngrp):
        out[c // 4, (c % 4) * nbl * TB:(c % 4 + 1) * nbl * TB] = np.asarray(res.results[c]["out"])
    return out
```

```python
import contextlib
import os
import numpy as np
import concourse.bass as bass
import concourse.mybir as mybir
from concourse.bass_utils import run_bass_kernel_spmd

F32 = mybir.dt.float32
BF16 = mybir.dt.bfloat16
AF = mybir.ActivationFunctionType
ALU = mybir.AluOpType

D = 1024
DFF = 2816
NJ = DFF // 128
SEQ = 8192
TB = 512
ALPHA = 2.0 ** 0.25
LN_EPS = 1e-5
GN_EPS = 64e-5
NCORES = 8
WCOLS = 9480
RC0 = 4104
SAME_ENGINE_WAITS = os.environ.get("KSEW", "act,dve,pool,sp").split(",")


class Res:
    __slots__ = ("w", "r", "excl")

    def __init__(self, excl=False):
        self.w = None
        self.r = {}
        self.excl = excl


class Sched:
    NDS = 24

    def __init__(self, nc, stack):
        self.nc = nc
        self.E = {"pe": nc.tensor, "act": nc.scalar, "dve": nc.vector, "pool": nc.gpsimd, "sp": nc.sync}
        self.q = {k: [] for k in self.E}
        self.cnt = {k: 0 for k in self.E}
        self.esem = {k: stack.enter_context(nc.semaphore("s_" + k)) for k in self.E}
        self.dsem = [stack.enter_context(nc.semaphore("d%d" % i)) for i in range(self.NDS)]
        self.dval = [0] * self.NDS
        self.dnext = {"sp": 0, "pool": 0}
        self.dpool = {"sp": list(range(0, 8)), "pool": list(range(8, self.NDS))}
        self.waited = {k: {} for k in self.E}
        self.csem = stack.enter_context(nc.semaphore("cc"))
        self.cval = 0

    def _deps(self, eng, reads, writes, extra=()):
        deps = {}

        def add(t):
            if t is None:
                return
            k = (t[0], t[1])
            if deps.get(k, 0) < t[2]:
                deps[k] = t[2]

        for r in reads:
            add(r.w)
        for w in writes:
            add(w.w)
            for k, v in w.r.items():
                add((k[0], k[1], v))
        for t in extra:
            add(t)
        waits = []
        for k, v in deps.items():
            if k[0] == "e" and k[1] == eng and (eng == "pe" or eng not in SAME_ENGINE_WAITS):
                continue
            if self.waited[eng].get(k, 0) >= v:
                continue
            self.waited[eng][k] = v
            waits.append((k, v))
        return waits

    def _mark(self, tok, reads, writes):
        k = (tok[0], tok[1])
        for r in reads:
            if r.r.get(k, 0) < tok[2]:
                r.r[k] = tok[2]
        for w in writes:
            w.w = tok
            w.r = {}

    def op(self, eng, fn, reads=(), writes=()):
        ex = [r for r in reads if r.excl]
        if ex:
            writes = list(writes) + ex
        waits = self._deps(eng, reads, writes)
        self.cnt[eng] += 1
        tok = ("e", eng, self.cnt[eng])
        self.q[eng].append((waits, fn, None))
        self._mark(tok, reads, writes)
        return tok

    def dma(self, qeng, out, in_, reads=(), writes=()):
        pool = self.dpool[qeng]
        j = pool[self.dnext[qeng]]
        self.dnext[qeng] = (self.dnext[qeng] + 1) % len(pool)
        prev = ("d", j, self.dval[j]) if self.dval[j] else None
        extra = [prev] if prev else []
        waits = self._deps(qeng, reads, writes, extra=extra)
        self.dval[j] += 16
        tok = ("d", j, self.dval[j])
        self.q[qeng].append((waits, (out, in_), j))
        self._mark(tok, reads, writes)
        return tok

    def cc(self, kind, ins, outs, groups, reads=(), writes=()):
        waits = self._deps("pool", reads, writes)
        self.cval += 1
        tok = ("c", 0, self.cval)
        self.q["pool"].append((waits, (kind, ins, outs, groups), "cc"))
        self._mark(tok, reads, writes)
        return tok

    def emit(self):
        nc = self.nc
        with nc.Block() as block:
            def run(kind):
                def body(eng):
                    for waits, fn, dj in self.q[kind]:
                        for k, v in waits:
                            sem = self.esem[k[1]] if k[0] == "e" else (self.csem if k[0] == "c" else self.dsem[k[1]])
                            eng.wait_ge(sem, v)
                        if dj is None:
                            fn(eng).then_inc(self.esem[kind], 1)
                        elif dj == "cc":
                            eng.collective_compute(fn[0], ALU.bypass, replica_groups=fn[3], ins=[fn[1]], outs=[fn[2]]).then_inc(self.csem, 1)
                        else:
                            eng.dma_start(out=fn[0], in_=fn[1]).then_inc(self.dsem[dj], 16)
                return body
            block.tensor(run("pe"))
            block.scalar(run("act"))
            block.vector(run("dve"))
            block.gpsimd(run("pool"))
            block.sync(run("sp"))


class Ring:
    def __init__(self, items):
        self.items = items
        self.i = 0

    def next(self):
        it = self.items[self.i]
        self.i = (self.i + 1) % len(self.items)
        return it


PV = {}
_o = 0
for _n, _w in [("ln1_g", 8), ("ln1_b", 8), ("ln2_g", 8), ("ln2_b", 8), ("ln3_g", 8), ("ln3_b", 8),
               ("cw0", 4), ("cw1", 4), ("cw2", 4), ("cw3", 4), ("cb", 4), ("mng", 2),
               ("mu", 8), ("omu", 8), ("w0", 2), ("a0", 2), ("kk", 2), ("ka", 2), ("rrk", 2),
               ("gng", 2), ("gnb", 2)]:
    PV[_n] = _o
    _o += _w
NPV = _o
C_ID, C_OD, C_MUS, C_TRI, C_ONE, C_MLS, C_IST, C_BO, NCST = 0, 128, 256, 384, 512, 640, 768, 832, 960


def build(nbl, groups):
    nc = bass.Bass("TRN2", target_bir_lowering=False)

    def din(name, shape):
        return nc.dram_tensor(name, list(shape), F32, kind="ExternalInput").ap()

    NBT = 4 * nbl
    x_d = din("x", [nbl * TB, D])
    cst_d = din("cst", [128, NCST])
    pv_d = din("pv", [128, NPV])
    gb_d = din("gbias", [128, 8])
    rst_d = din("rst", [128, 512])
    w1g = din("ffn1_w_gate", [NJ // 2 * 128, 2048]); w1u = din("ffn1_w_up", [NJ // 2 * 128, 2048]); w1d = din("ffn1_w_down", [DFF, D])
    w2g = din("ffn2_w_gate", [NJ // 2 * 128, 2048]); w2u = din("ffn2_w_up", [NJ // 2 * 128, 2048]); w2d = din("ffn2_w_down", [DFF, D])
    win = din("w_loc", [D, 2048])
    winT = din("w_loc_t", [2048, D])
    wif_d = din("w_if", [D, 8])
    wgate = din("w_gate", [2048, D])
    oh_d = din("oh", [128, 4])
    wa_d = din("w_branch_a", [D, D]); wb_d = din("w_branch_b", [D, D]); wo_d = din("w_out", [D, D])
    rw2_d = din("r_w2", [64, 256]); ra2_d = din("r_a2", [64, 256]); rg2_d = din("r_g2", [128, 256])
    out_d = nc.dram_tensor("out", [nbl * TB, D], F32, kind="ExternalOutput").ap()
    x1f_loc = nc.dram_tensor("x1f_loc", [nbl * 128, 8 * TB], F32).ap()
    x1b_loc = [[nc.dram_tensor("x1bl_%d_%d" % (k, h), [128, 4 * TB], BF16).ap() for h in range(2)] for k in range(nbl)]
    x1b_all = [[nc.dram_tensor("x1ba_%d_%d" % (k, h), [4 * 128, 4 * TB], BF16).ap() for h in range(2)] for k in range(nbl)]
    hy_loc = [nc.dram_tensor("hyl_%d" % k, [128, 4 * TB], BF16).ap() for k in range(NBT)]
    hy_all = [nc.dram_tensor("hya_%d" % k, [4 * 128, 4 * TB], BF16).ap() for k in range(NBT)]

    with contextlib.ExitStack() as st:
        S = Sched(nc, st)
        _n = [0]

        def sb(shape, dt=F32):
            _n[0] += 1
            return st.enter_context(nc.sbuf_tensor("sb%d" % _n[0], list(shape), dt))

        def ring(n, shape, dt=F32):
            return Ring([(sb(shape, dt), Res()) for _ in range(n)])

        banks = [st.enter_context(nc.psum_tensor("ps%d" % i, [128, 512], F32)) for i in range(8)]
        psr = Ring([(banks[i], Res(True)) for i in range(7)])
        pyo_bank = (banks[7], Res(True))

        cst = sb([128, NCST]); cst_r = Res()
        cstb = sb([128, NCST], BF16); cstb_r = Res()
        pv = sb([128, NPV]); pv_r = Res()
        gbias = sb([128, 8]); gb_r = Res()
        S.dma("sp", cst[:, :], cst_d[:, :], writes=[cst_r])
        S.dma("sp", pv[:, :], pv_d[:, :], writes=[pv_r])
        S.dma("sp", gbias[:, :], gb_d[:, :], writes=[gb_r])
        S.op("act", lambda e: e.copy(cstb[:, :], cst[:, :]), reads=[cst_r], writes=[cstb_r])
        ident = cst[:, C_ID:C_ID + 128]
        identb = cstb[:, C_ID:C_ID + 128]
        onesdb = cstb[:, C_OD:C_OD + 128]
        tri = cst[:, C_TRI:C_TRI + 128]
        ones = cst[:, C_ONE:C_ONE + 128]
        istb = cstb[:, C_IST:C_IST + 64]
        bob = cstb[:, C_BO:C_BO + 128]
        rstb = sb([128, 512], BF16)
        S.dma("pool", rstb[:, :], rst_d[:, :], writes=[cstb_r])
        rst = rstb[:, :]
        epsln = sb([128, 1]); eps4 = sb([128, 1]); epsgn = sb([128, 1]); eps_r = Res()
        S.op("dve", lambda e: e.memset(epsln[:, :], LN_EPS), writes=[eps_r])
        S.op("dve", lambda e: e.memset(eps4[:, :], 4 * LN_EPS), writes=[eps_r])
        S.op("dve", lambda e: e.memset(epsgn[:, :], GN_EPS), writes=[eps_r])

        def pvc(name, i=0):
            c = PV[name] + i
            return pv[:, c:c + 1]

        rw2a2 = sb([128, 256], BF16); rw_r = Res()
        rg2 = sb([128, 256], BF16)
        S.dma("pool", rw2a2[0:64, :], rw2_d[:, :], writes=[rw_r])
        S.dma("pool", rw2a2[64:128, :], ra2_d[:, :], writes=[rw_r])
        S.dma("pool", rg2[:, :], rg2_d[:, :], writes=[rw_r])

        xT = sb([128, 8, TB]); xTb = sb([128, 8, TB], BF16); xT_r = [Res() for _ in range(8)]
        x1T = sb([128, 8, TB]); x1Tb = sb([128, 8, TB], BF16); x1_r = [Res() for _ in range(8)]
        x2T = xT; x2Tb = xTb; x2_r = xT_r
        x3T = xT; x3Tb = xTb; x3_r = xT_r
        aT = sb([128, NJ, TB], BF16); aT_r = [Res() for _ in range(NJ)]
        zT = sb([128, 8, TB]); zT_r = [Res() for _ in range(8)]
        xtok = ring(1, [128, D])
        otok = xtok
        wgu = ring(2, [128, 2, 8, 256], BF16)
        wdb = ring(3, [128, 512], BF16)
        wpj = ring(3, [128, 8, 128], BF16)
        tA = ring(3, [128, TB])
        tB = ring(2, [128, TB], BF16)
        mean_r = Res(); rstd_r = Res()
        out_r = Res()

        def load_xT(blk):
            for t in range(TB // 128):
                xt, xr = xtok.next()
                r0 = blk * TB + t * 128
                S.dma("sp", xt[:, :], x_d[r0:r0 + 128, :], writes=[xr])
                for half in range(2):
                    ps, pr = psr.next()
                    for q in range(4):
                        kc = half * 4 + q
                        S.op("pe", lambda e, ps=ps, xt=xt, kc=kc, q=q: e.matmul(
                            ps[:, q * 128:(q + 1) * 128], xt[:, kc * 128:(kc + 1) * 128], ident,
                            start=True, stop=True), reads=[xr, cst_r], writes=[pr])
                    psv = ps[:, :].rearrange("p (q n) -> p q n", q=4)
                    S.op("act", lambda e, psv=psv, half=half, t=t: e.copy(
                        xT[:, half * 4:half * 4 + 4, t * 128:(t + 1) * 128], psv),
                        reads=[pr], writes=xT_r[half * 4:half * 4 + 4])
                    S.op("dve", lambda e, psv=psv, half=half, t=t: e.tensor_copy(
                        xTb[:, half * 4:half * 4 + 4, t * 128:(t + 1) * 128], psv),
                        reads=[pr], writes=xT_r[half * 4:half * 4 + 4])

        def layer_norm(gname, bname, eps_t, outT, outTb, out_rs):
            psm, pmr = psr.next()
            pss, ssr = psr.next()
            for dc in range(8):
                tb, tr = tB.next()
                S.op("act", lambda e, tb=tb, dc=dc: e.copy(tb[:, :], zT[:, dc, :]), reads=[zT_r[dc]], writes=[tr])
                S.op("pe", lambda e, tb=tb, dc=dc: e.matmul(psm[:, :], onesdb, tb[:, :], start=(dc == 0), stop=(dc == 7)),
                     reads=[cstb_r, tr], writes=[pmr])
                tb2, tr2 = tB.next()
                S.op("act", lambda e, tb2=tb2, dc=dc: e.activation(tb2[:, :], zT[:, dc, :], AF.Square),
                     reads=[zT_r[dc]], writes=[tr2])
                S.op("pe", lambda e, tb2=tb2, dc=dc: e.matmul(pss[:, :], onesdb, tb2[:, :], start=(dc == 0), stop=(dc == 7)),
                     reads=[cstb_r, tr2], writes=[ssr])
            S.op("act", lambda e: e.copy(mean_sb[:, :], psm[:, :]), reads=[pmr], writes=[mean_r])
            t1, r1 = tA.next()
            S.op("dve", lambda e, t1=t1: e.tensor_tensor(t1[:, :], mean_sb[:, :], mean_sb[:, :], ALU.mult),
                 reads=[mean_r], writes=[r1])
            t2, r2 = tA.next()
            S.op("dve", lambda e, t1=t1, t2=t2: e.tensor_tensor(t2[:, :], pss[:, :], t1[:, :], ALU.subtract),
                 reads=[ssr, r1], writes=[r2])
            S.op("dve", lambda e, t2=t2: e.tensor_scalar_max(t2[:, :], t2[:, :], 0.0), reads=[r2], writes=[r2])
            t3, r3 = tA.next()
            S.op("act", lambda e, t2=t2, t3=t3: e.activation(t3[:, :], t2[:, :], AF.Sqrt, bias=eps_t[:, 0:1]),
                 reads=[r2, eps_r], writes=[r3])
            S.op("dve", lambda e, t3=t3: e.reciprocal(rstd_sb[:, :], t3[:, :]), reads=[r3], writes=[rstd_r])
            for dc in range(8):
                ta, tar = tA.next()
                S.op("dve", lambda e, ta=ta, dc=dc: e.tensor_tensor(ta[:, :], zT[:, dc, :], mean_sb[:, :], ALU.subtract),
                     reads=[zT_r[dc], mean_r], writes=[tar])
                S.op("dve", lambda e, ta=ta: e.tensor_tensor(ta[:, :], ta[:, :], rstd_sb[:, :], ALU.mult),
                     reads=[tar, rstd_r], writes=[tar])
                S.op("act", lambda e, ta=ta, dc=dc: e.activation(
                    outT[:, dc, :], ta[:, :], AF.Identity, scale=pvc(gname, dc), bias=pvc(bname, dc)),
                    reads=[tar, pv_r], writes=[out_rs[dc]])
                S.op("act", lambda e, ta=ta, dc=dc: e.activation(
                    outTb[:, dc, :], ta[:, :], AF.Identity, scale=pvc(gname, dc), bias=pvc(bname, dc)),
                    reads=[tar, pv_r], writes=[out_rs[dc]])

        def ffn_ln(inT, inTb, in_rs, Wg, Wu, Wd, gname, bname, outT, outTb, out_rs):
            Wd_v = Wd.rearrange("(j p) d -> p j d", p=128)
            for j in range(NJ):
                if j % 2 == 0:
                    wb, wr = wgu.next()
                    g0 = (j // 2) * 128
                    S.dma("pool", wb[:, 0].rearrange("p k c -> p (k c)"), Wg[g0:g0 + 128, :], writes=[wr])
                    S.dma("pool", wb[:, 1].rearrange("p k c -> p (k c)"), Wu[g0:g0 + 128, :], writes=[wr])
                jo = (j % 2) * 128
                psg, pgr = psr.next()
                psu, pur = psr.next()
                for gu, (ps, prr) in enumerate(((psg, pgr), (psu, pur))):
                    for kc in range(8):
                        S.op("pe", lambda e, ps=ps, wb=wb, kc=kc, gu=gu, jo=jo: e.matmul(
                            ps[:, :], wb[:, gu, kc, jo:jo + 128], inTb[:, kc, :], start=(kc == 0), stop=(kc == 7)),
                            reads=[wr, in_rs[kc]], writes=[prr])
                tb, tr = tA.next()
                S.op("act", lambda e, tb=tb, ps=psg: e.activation(tb[:, :], ps[:, :], AF.Silu), reads=[pgr], writes=[tr])
                S.op("dve", lambda e, tb=tb, ps=psu, j=j: e.tensor_tensor(aT[:, j, :], tb[:, :], ps[:, :], ALU.mult),
                     reads=[tr, pur], writes=[aT_r[j]])
            for half in range(2):
                pss_ = [psr.next() for _ in range(4)]
                for j in range(NJ):
                    wb, wr = wdb.next()
                    S.dma("pool", wb[:, :], Wd_v[:, j, half * 512:(half + 1) * 512], writes=[wr])
                    for q in range(4):
                        ps, pr = pss_[q]
                        S.op("pe", lambda e, ps=ps, wb=wb, j=j, q=q: e.matmul(
                            ps[:, :], wb[:, q * 128:(q + 1) * 128], aT[:, j, :], start=(j == 0), stop=(j == NJ - 1)),
                            reads=[wr, aT_r[j]], writes=[pr])
                for q in range(4):
                    dc = half * 4 + q
                    ps, pr = pss_[q]
                    S.op("dve", lambda e, ps=ps, dc=dc: e.scalar_tensor_tensor(
                        zT[:, dc, :], inT[:, dc, :], 2.0 * ALPHA, ps[:, :], ALU.mult, ALU.add),
                        reads=[pr, in_rs[dc]], writes=[zT_r[dc]])
            layer_norm(gname, bname, eps4, outT, outTb, out_rs)

        def store_T(blk, srcT, src_rs):
            for t in range(TB // 128):
                ot, orr = otok.next()
                for half in range(2):
                    ps, pr = psr.next()
                    for q in range(4):
                        kc = half * 4 + q
                        S.op("pe", lambda e, ps=ps, kc=kc, q=q, t=t: e.matmul(
                            ps[:, q * 128:(q + 1) * 128], srcT[:, kc, t * 128:(t + 1) * 128], ident,
                            start=True, stop=True), reads=[src_rs[kc], cst_r], writes=[pr])
                    S.op("act", lambda e, ps=ps, ot=ot, half=half: e.copy(ot[:, half * 512:(half + 1) * 512], ps[:, :]),
                         reads=[pr], writes=[orr])
                r0 = blk * TB + t * 128
                S.dma("sp", out_d[r0:r0 + 128, :], ot[:, :], reads=[orr], writes=[out_r])

        def projT(Wd_ap, col0, inTb, in_rs, ncols=128, wring=None, pring=None):
            wb, wr = (wring or wpj).next()
            S.dma("pool", wb[:, :, :].rearrange("p k c -> p (k c)"), Wd_ap[col0:col0 + 128, :], writes=[wr])
            ps, pr = (pring or psr).next()
            for kc in range(8):
                src = inTb(kc) if callable(inTb) else inTb[:, kc, :]
                S.op("pe", lambda e, ps=ps, wb=wb, kc=kc, src=src: e.matmul(
                    ps[0:ncols, :], wb[:, kc, 0:ncols], src, start=(kc == 0), stop=(kc == 7)),
                    reads=[wr, in_rs[kc]], writes=[pr])
            return ps, pr

        NT = TB // 128
        carry = sb([128, 4, 3]); carry_r = [Res() for _ in range(4)]
        S.op("dve", lambda e: e.memset(carry[:, :, :], 0.0), writes=carry_r)
        cwork = ring(2, [128, 3 + TB])
        mean_sb = cwork.items[0][0][:, 0:TB]; rstd_sb = cwork.items[1][0][:, 0:TB]
        qkT = zT[:, :, :].rearrange("p a b -> p (a b)").bitcast(BF16).rearrange("p (c n) -> p c n", n=TB)
        qk_r = [zT_r[c // 2] for c in range(16)]
        sigmo = sb([128, 2, TB], BF16); sigmo_r = [Res() for _ in range(2)]
        sga = sb([128, 8, TB], BF16); sga_r = [Res() for _ in range(8)]
        sgb = sb([128, 8, TB], BF16); sgb_r = [Res() for _ in range(8)]
        vt = sb([128, NT, 1, 258], BF16); vt_r = [[Res() for _ in range(1)] for _ in range(NT)]
        S.op("dve", lambda e: e.memset(vt[:, :, :, :], 1.0), writes=[r for rr in vt_r for r in rr])
        wv = ring(1, [128, 8, 256], BF16)
        wif = sb([128, 8, 8], BF16); wif_r = Res()
        S.dma("pool", wif[:, :, :], wif_d.rearrange("(kc p) f -> p kc f", p=128), writes=[wif_r])
        gts = sb([128, NT, 24]); gts_r = [Res() for _ in range(NT)]
        Cst = sb([128, 1, 2, 258]); C_r = [Res() for _ in range(1)]
        Cb = sb([128, 1, 2, 258], BF16); Cb_r = [Res() for _ in range(1)]
        S.op("dve", lambda e: e.memset(Cst[:, :, :, :], 0.0), writes=C_r)
        S.op("dve", lambda e: e.memset(Cb[:, :, :, :], 0.0), writes=Cb_r)
        hyT = sb([128, 4, TB], BF16); hm_r = [Res() for _ in range(2)]
        hmTb = hyT[:, 0:2, :]
        HYf = sb([128, 4, 4, TB], BF16); hyf_r = [Res() for _ in range(4)]
        yrTb = hyT[:, 2:4, :]; yr_r = [Res() for _ in range(2)]
        mgTb = sga; mg_r = sga_r
        smr = ring(4, [128, 128], BF16)
        ktk = ring(4, [128, 256], BF16)
        sm6 = ring(4, [128, 8])

        def mlstm(blk):
            Wv = win.rearrange("(kc p) f -> p kc f", p=128)
            for c in range(4):
                ps, pr = projT(winT, c * 128, x1Tb, x1_r, wring=wpjM, pring=psrM)
                wk, wkr = cwork.next()
                S.op("act", lambda e, wk=wk, c=c: e.copy(wk[:, 0:3], carry[:, c, :]), reads=[carry_r[c]], writes=[wkr])
                yield
                S.op("act", lambda e, wk=wk, ps=ps: e.copy(wk[:, 3:3 + TB], ps[:, :]), reads=[pr], writes=[wkr])
                yield
                S.op("act", lambda e, wk=wk, c=c: e.copy(carry[:, c, :], wk[:, TB:TB + 3]), reads=[wkr], writes=[carry_r[c]])
                yield
                ta, tar = tAM.next()
                S.op("dve", lambda e, wk=wk, ta=ta, c=c: e.tensor_scalar(
                    ta[:, :], wk[:, 0:TB], pvc("cw0", c), pvc("cb", c), ALU.mult, ALU.add),
                    reads=[wkr, pv_r], writes=[tar])
                yield
                for j in (1, 2, 3):
                    S.op("dve", lambda e, wk=wk, ta=ta, c=c, j=j: e.scalar_tensor_tensor(
                        ta[:, :], wk[:, j:j + TB], pvc("cw%d" % j, c), ta[:, :], ALU.mult, ALU.add),
                        reads=[wkr, pv_r, tar], writes=[tar])
                    yield
                S.op("act", lambda e, ta=ta, c=c: e.activation(qkT[:, c, :], ta[:, :], AF.Silu),
                     reads=[tar], writes=[qk_r[c]])
                yield
            for c in range(2):
                ps, pr = projT(winT, 768 + c * 128, x1Tb, x1_r, wring=wpjM, pring=psrM)
                S.op("act", lambda e, ps=ps, c=c: e.activation(sigmo[:, c, :], ps[:, :], AF.Sigmoid),
                     reads=[pr], writes=[sigmo_r[c]])
                yield
            for t in range(NT):
                ps, pr = psrM.next()
                for kc in range(8):
                    S.op("pe", lambda e, ps=ps, kc=kc, t=t: e.matmul(
                        ps[:, 0:8], x1Tb[:, kc, t * 128:(t + 1) * 128], wif[:, kc, :], start=(kc == 0), stop=(kc == 7)),
                        reads=[wif_r, x1_r[kc]], writes=[pr])
                    yield
                g = gts[:, t, :]
                gr = gts_r[t]
                S.op("dve", lambda e, g=g, ps=ps: e.tensor_tensor(g[:, 12:20], ps[:, 0:8], gbias[:, :], ALU.add),
                     reads=[pr, gb_r], writes=[gr])
                yield
                S.op("act", lambda e, g=g: e.activation(g[:, 20:24], g[:, 16:20], AF.Exp, scale=-1.0), reads=[gr], writes=[gr])
                yield
                S.op("act", lambda e, g=g: e.activation(g[:, 16:20], g[:, 20:24], AF.Ln, bias=1.0), reads=[gr], writes=[gr])
                yield
                S.op("dve", lambda e, g=g: e.tensor_scalar_mul(g[:, 16:20], g[:, 16:20], -1.0), reads=[gr], writes=[gr])
                yield
                ps2, pr2 = psrM.next()
                S.op("pe", lambda e, ps2=ps2, g=g: e.matmul(ps2[:, 0:4], tri, g[:, 16:20], start=True, stop=True),
                     reads=[gr, cst_r], writes=[pr2])
                yield
                S.op("pe", lambda e, ps2=ps2, g=g: e.matmul(ps2[:, 4:8], ones, g[:, 16:20], start=True, stop=True),
                     reads=[gr, cst_r], writes=[pr2])
                yield
                S.op("dve", lambda e, g=g, ps2=ps2: e.tensor_tensor(g[:, 20:24], g[:, 12:16], ps2[:, 0:4], ALU.subtract),
                     reads=[gr, pr2], writes=[gr])
                yield
                S.op("act", lambda e, g=g: e.activation(g[:, 0:4], g[:, 20:24], AF.Exp), reads=[gr], writes=[gr])
                yield
                S.op("act", lambda e, g=g, ps2=ps2: e.activation(g[:, 4:8], ps2[:, 0:4], AF.Exp, scale=-1.0),
                     reads=[gr, pr2], writes=[gr])
                yield
                S.op("act", lambda e, g=g, ps2=ps2: e.activation(g[:, 8:12], ps2[:, 4:8], AF.Exp), reads=[gr, pr2], writes=[gr])
                yield
            for h in range(1):
                wb, wr = wv.next()
                S.dma("pool", wb[:, :, :], Wv[:, :, 512:768], writes=[wr])
                yield
                for t in range(NT):
                    ps, pr = psrM.next()
                    for kc in range(8):
                        S.op("pe", lambda e, ps=ps, kc=kc, t=t, wb=wb: e.matmul(
                            ps[:, 0:256], x1Tb[:, kc, t * 128:(t + 1) * 128], wb[:, kc, :], start=(kc == 0), stop=(kc == 7)),
                            reads=[wr, x1_r[kc]], writes=[pr])
                        yield
                    S.op("act", lambda e, ps=ps, t=t, h=h: e.copy(vt[:, t, h, 0:256], ps[:, 0:256]),
                         reads=[pr], writes=[vt_r[t][h]])
                    yield
            for t in range(NT):
                tc_ = slice(t * 128, (t + 1) * 128)
                g = gts[:, t, :]
                gr = gts_r[t]
                def head_chain(t, h, tc_, g, gr):
                    qc = [h * 2, h * 2 + 1]
                    kc_ = [2 + h * 2, 2 + h * 2 + 1]
                    ps, pr = psrM.next()
                    for i in range(2):
                        S.op("pe", lambda e, ps=ps, i=i, kc_=kc_, qc=qc, tc_=tc_: e.matmul(
                            ps[:, 0:128], qkT[:, kc_[i], tc_], qkT[:, qc[i], tc_], start=(i == 0), stop=(i == 1)),
                            reads=[qk_r[kc_[i]], qk_r[qc[i]]], writes=[pr])
                        yield
                    sm, smrr = smr.next()
                    S.op("dve", lambda e, sm=sm, ps=ps, h=h, g=g: e.scalar_tensor_tensor(
                        sm[:, :], ps[:, 0:128], g[:, h:h + 1], tri, ALU.mult, ALU.mult),
                        reads=[pr, gr, cst_r], writes=[smrr])
                    yield
                    po, por = psrM.next()
                    S.op("pe", lambda e, po=po, sm=sm, t=t, h=h: e.matmul(
                        po[:, 0:258], sm[:, :], vt[:, t, h, :], start=True, stop=False),
                        reads=[smrr, vt_r[t][h]], writes=[por])
                    yield
                    for i in range(2):
                        S.op("pe", lambda e, po=po, i=i, h=h, qc=qc, tc_=tc_: e.matmul(
                            po[:, 0:258], qkT[:, qc[i], tc_], Cb[:, h, i, :], start=False, stop=(i == 1)),
                            reads=[qk_r[qc[i]], Cb_r[h]], writes=[por])
                        yield
                    s6, s6r = sm6.next()
                    S.op("act", lambda e, s6=s6, po=po: e.activation(
                        s6[:, 0:1], po[:, 256:257], AF.Abs, scale=1.0 / 16.0), reads=[por], writes=[s6r])
                    yield
                    S.op("dve", lambda e, s6=s6, g=g, h=h: e.tensor_tensor(s6[:, 0:1], s6[:, 0:1], g[:, 4 + h:5 + h], ALU.max),
                         reads=[s6r, gr], writes=[s6r])
                    yield
                    S.op("dve", lambda e, s6=s6: e.reciprocal(s6[:, 1:2], s6[:, 0:1]), reads=[s6r], writes=[s6r])
                    yield
                    hb, hbr = hh.next()
                    S.op("dve", lambda e, hb=hb, po=po, s6=s6: e.tensor_scalar(
                        hb[:, :], po[:, 0:256], s6[:, 1:2], 1.0 / 16.0, ALU.mult, ALU.mult), reads=[por, s6r], writes=[hbr])
                    yield
                    S.op("dve", lambda e, hb=hb, s6=s6: e.bn_stats(s6[:, 2:8], hb[:, :]), reads=[hbr], writes=[s6r])
                    yield
                    S.op("dve", lambda e, s6=s6: e.bn_aggr(s6[:, 0:2], s6[:, 2:8]), reads=[s6r], writes=[s6r])
                    yield
                    S.op("act", lambda e, s6=s6: e.activation(s6[:, 2:3], s6[:, 1:2], AF.Sqrt, bias=epsln[:, 0:1]),
                         reads=[s6r, eps_r], writes=[s6r])
                    yield
                    S.op("dve", lambda e, s6=s6: e.reciprocal(s6[:, 3:4], s6[:, 2:3]), reads=[s6r], writes=[s6r])
                    yield
                    hn, hnr = hnb.next()
                    S.op("dve", lambda e, hn=hn, hb=hb, s6=s6: e.tensor_scalar(
                        hn[:, :], hb[:, :], s6[:, 0:1], s6[:, 3:4], ALU.subtract, ALU.mult), reads=[hbr, s6r], writes=[hnr])
                    yield
                    for i in range(2):
                        pt, ptr = psrM.next()
                        S.op("pe", lambda e, pt=pt, hn=hn, i=i: e.matmul(
                            pt[:, 0:128], hn[:, i * 128:(i + 1) * 128], identb, start=True, stop=True),
                            reads=[hnr, cstb_r], writes=[ptr])
                        yield
                        S.op("dve", lambda e, pt=pt, h=h, i=i, tc_=tc_: e.scalar_tensor_tensor(
                            hmTb[:, h * 2 + i, tc_], pt[:, 0:128], pvc("mng", h * 2 + i), sigmo[:, h * 2 + i, tc_],
                            ALU.mult, ALU.mult), reads=[ptr, pv_r, sigmo_r[h * 2 + i]], writes=[hm_r[h * 2 + i]])
                        yield
                    kk_, kkr = ktk.next()
                    for i in range(2):
                        pt, ptr = psrM.next()
                        S.op("pe", lambda e, pt=pt, i=i, kc_=kc_, tc_=tc_: e.matmul(
                            pt[:, 0:128], qkT[:, kc_[i], tc_], identb, start=True, stop=True),
                            reads=[qk_r[kc_[i]], cstb_r], writes=[ptr])
                        yield
                        S.op("act", lambda e, pt=pt, kk_=kk_, i=i, g=g, h=h: e.activation(
                            kk_[:, i * 128:(i + 1) * 128], pt[:, 0:128], AF.Identity, scale=g[:, h:h + 1]),
                            reads=[ptr, gr], writes=[kkr])
                        yield
                    for i in range(2):
                        pc, pcr = psrM.next()
                        S.op("pe", lambda e, pc=pc, kk_=kk_, i=i, t=t, h=h: e.matmul(
                            pc[:, 0:258], kk_[:, i * 128:(i + 1) * 128], vt[:, t, h, :], start=True, stop=True),
                            reads=[kkr, vt_r[t][h]], writes=[pcr])
                        yield
                        S.op("dve", lambda e, h=h, i=i, g=g: e.tensor_scalar_mul(Cst[:, h, i, :], Cst[:, h, i, :], g[:, 8 + h:9 + h]),
                             reads=[C_r[h], gr], writes=[C_r[h]])
                        yield
                        S.op("dve", lambda e, pc=pc, h=h, i=i, g=g: e.scalar_tensor_tensor(
                            Cst[:, h, i, :], pc[:, 0:258], g[:, 8 + h:9 + h], Cst[:, h, i, :], ALU.mult, ALU.add),
                            reads=[pcr, gr, C_r[h]], writes=[C_r[h]])
                        yield
                        S.op("act", lambda e, h=h, i=i: e.copy(Cb[:, h, i, :], Cst[:, h, i, :]), reads=[C_r[h]], writes=[Cb_r[h]])
                        yield

                gens = [head_chain(t, h, tc_, g, gr) for h in range(1)]
                while gens:
                    for gg in list(gens):
                        try:
                            next(gg)
                            yield
                        except StopIteration:
                            gens.remove(gg)

        NCH = TB // 64
        rcar = sb([128, 8, 1]); rcar_r = [Res() for _ in range(8)]
        S.op("dve", lambda e: e.memset(rcar[:, :, :], 0.0), writes=rcar_r)
        psrM = Ring(psr.items[0:3]); psrR = Ring(psr.items[3:7])
        tAM = Ring([(x1T[:, i, :], Res()) for i in range(2)])
        rwork = Ring([(x1T[:, 2 + 2 * i:4 + 2 * i, :].rearrange("p a b -> p (a b)")[:, 0:1 + TB], Res()) for i in range(2)])
        wpjM = Ring([(x1T[:, 6 + i, :].bitcast(BF16).rearrange("p (k c) -> p k c", c=128), Res()) for i in range(2)])
        scr_r = [r for _, r in tAM.items + rwork.items + wpjM.items]
        lowT = sb([128, 2, TB], BF16); low_r = [Res(), Res()]
        rtmp = Ring([(aT[:, 2 * i:2 * i + 2, :].rearrange("p a b -> p (a b)").bitcast(F32), Res()) for i in range(10)])
        ARbd = sb([128, NCH, 256], BF16); Bbd = sb([128, NCH, 128], BF16); Kbd = sb([128, NCH, 128], BF16)
        Vbd = sb([128, NCH, 128], BF16); Ynbd = sb([128, 128], BF16)
        bd_r = Res(); ynbd_r = Res()
        for tns in (ARbd, Bbd, Kbd, Vbd):
            S.op("dve", lambda e, tns=tns: e.memset(tns[:, :, :], 0.0), writes=[bd_r])
        S.op("dve", lambda e: e.memset(Ynbd[:, :], 0.0), writes=[ynbd_r])
        gam = sb([128, NCH]); gam_r = Res()
        Hst = sb([128, 2, 64]); H_r = [Res() for _ in range(2)]
        Hb = sb([128, 2, 64], BF16); Hb_r = [Res() for _ in range(2)]
        S.op("dve", lambda e: e.memset(Hst[:, :, :], 0.0), writes=H_r)
        S.op("dve", lambda e: e.memset(Hb[:, :, :], 0.0), writes=Hb_r)
        vst = sb([128, NCH, 64], BF16); vst_r = Res()
        btk = sb([128, NCH, 256], BF16); btk_r = Res()
        ub = ring(4, [128, 64], BF16)
        gn6 = ring(4, [128, 8])
        htmp = ring(2, [128, 64])
        maskAR = cst[:, C_MUS:C_MUS + 256]; mask_r = cst_r
        mls = cst[:, C_MLS:C_MLS + 128]

        xTv = xT[:, :, :].rearrange("p a b -> p (a b)").bitcast(BF16)
        mAK = xTv[:, 0:NCH * 512].rearrange("p (n c) -> p n c", c=512)
        Xs = xTv[:, 4096:4096 + NCH * 128].rearrange("p (n c) -> p n c", c=128)
        xTbv = xTb[:, :, :].rearrange("p a b -> p (a b)")
        PW = [[xTbv[:, (b * 8 + n) * 128:(b * 8 + n + 1) * 128] for n in range(NCH)] for b in range(2)]
        PWT = [[xTbv[:, 2048 + (b * 8 + n) * 128:2048 + (b * 8 + n + 1) * 128] for n in range(NCH)] for b in range(2)]
        hh = Ring([(xTv[:, 5120 + i * 512:5120 + (i + 1) * 512].bitcast(F32), Res()) for i in range(4)])
        hnb = Ring([(xTv[:, 7168 + i * 256:7168 + (i + 1) * 256], Res()) for i in range(4)])
        mA_r = [Res() for _ in range(NCH)]; mK_r = [Res() for _ in range(NCH)]; X_r = [Res() for _ in range(NCH)]
        PW_r = [[Res() for _ in range(NCH)] for _ in range(2)]; PWT_r = [[Res() for _ in range(NCH)] for _ in range(2)]

        def v3(ap):
            return ap.rearrange("p (n l) -> p n l", l=64)

        def shifted(ci, ps, pr):
            wk, wkr = rwork.next()
            S.op("act", lambda e, wk=wk: e.copy(wk[:, 0:1], rcar[:, ci, :]), reads=[rcar_r[ci]], writes=[wkr])
            S.op("act", lambda e, wk=wk, ps=ps: e.copy(wk[:, 1:1 + TB], ps[:, :]), reads=[pr], writes=[wkr])
            S.op("act", lambda e, wk=wk: e.copy(rcar[:, ci, :], wk[:, TB:TB + 1]), reads=[wkr], writes=[rcar_r[ci]])
            ta, tar = rtmp.next()
            S.op("dve", lambda e, wk=wk, ta=ta: e.tensor_scalar_mul(ta[:, :], wk[:, 1:1 + TB], pvc("omu", ci)),
                 reads=[wkr, pv_r], writes=[tar])
            S.op("dve", lambda e, wk=wk, ta=ta: e.scalar_tensor_tensor(
                ta[:, :], wk[:, 0:TB], pvc("mu", ci), ta[:, :], ALU.mult, ALU.add), reads=[wkr, pv_r, tar], writes=[tar])
            return ta, tar

        krw = int(os.environ.get("KRW", "9"))

        def rwkv(blk):
            ps, pr = projT(winT, 1792, x1Tb, x1_r, pring=psrR)
            ta, tar = shifted(6, ps, pr)
            S.op("act", lambda e, ta=ta: e.activation(lowT[0:64, 0, :], ta[0:64, :], AF.Tanh), reads=[tar], writes=[low_r[0]])
            yield
            S.op("act", lambda e, ta=ta: e.copy(lowT[64:128, 0, :], ta[64:128, :]), reads=[tar], writes=[low_r[0]])
            yield
            ps, pr = projT(winT, 1920, x1Tb, x1_r, pring=psrR)
            ta, tar = shifted(7, ps, pr)
            S.op("act", lambda e, ta=ta: e.activation(lowT[:, 1, :], ta[:, :], AF.Sigmoid), reads=[tar], writes=[low_r[1]])
            yield
            for p in range(2):
                cs = slice(p * 128, (p + 1) * 128)
                ps, pr = projT(winT, 1024 + p * 128, x1Tb, x1_r, pring=psrR)
                r_, r_r = shifted(p, ps, pr)
                ps, pr = projT(winT, 1280 + p * 128, x1Tb, x1_r, pring=psrR)
                k_, k_r = shifted(2 + p, ps, pr)
                ps, pr = projT(winT, 1536 + p * 128, x1Tb, x1_r, pring=psrR)
                v_, v_r = shifted(4 + p, ps, pr)
                pw, pwr = psrR.next()
                S.op("pe", lambda e, pw=pw, cs=cs: e.matmul(pw[:, :], rw2a2[0:64, cs], lowT[0:64, 0, :], start=True, stop=True),
                     reads=[rw_r, low_r[0]], writes=[pwr])
                yield
                lw, lwr = rtmp.next()
                S.op("act", lambda e, lw=lw, pw=pw, p=p: e.activation(lw[:, :], pw[:, :], AF.Sigmoid, bias=pvc("w0", p)),
                     reads=[pwr, pv_r], writes=[lwr])
                yield
                S.op("dve", lambda e, lw=lw: e.tensor_scalar_mul(lw[:, :], lw[:, :], -float(np.exp(-0.5))), reads=[lwr], writes=[lwr])
                yield
                pa, par = psrR.next()
                S.op("pe", lambda e, pa=pa, cs=cs: e.matmul(pa[:, :], rw2a2[64:128, cs], lowT[64:128, 0, :], start=True, stop=True),
                     reads=[rw_r, low_r[0]], writes=[par])
                yield
                a_, a_r = rtmp.next()
                S.op("act", lambda e, a_=a_, pa=pa, p=p: e.activation(a_[:, :], pa[:, :], AF.Sigmoid, bias=pvc("a0", p)),
                     reads=[par, pv_r], writes=[a_r])
                yield
                pg, pgr = psrR.next()
                S.op("pe", lambda e, pg=pg, cs=cs: e.matmul(pg[:, :], rg2[:, cs], lowT[:, 1, :], start=True, stop=True),
                     reads=[rw_r, low_r[1]], writes=[pgr])
                yield
                g_, g_r = rtmp.next()
                S.op("act", lambda e, g_=g_, pg=pg: e.copy(g_[:, :], pg[:, :]), reads=[pgr], writes=[g_r])
                yield
                kk, kkr = rtmp.next()
                S.op("dve", lambda e, kk=kk, k_=k_, p=p: e.tensor_scalar_mul(kk[:, :], k_[:, :], pvc("kk", p)),
                     reads=[k_r, pv_r], writes=[kkr])
                yield
                sq, sqr = tB.next()
                S.op("act", lambda e, sq=sq, kk=kk: e.activation(sq[:, :], kk[:, :], AF.Square), reads=[kkr], writes=[sqr])
                yield
                pq, pqr = psrR.next()
                S.op("pe", lambda e, pq=pq, sq=sq: e.matmul(pq[:, :], bob, sq[:, :], start=True, stop=True),
                     reads=[cstb_r, sqr], writes=[pqr])
                yield
                t1, t1r = rtmp.next()
                S.op("act", lambda e, t1=t1, pq=pq: e.activation(t1[:, :], pq[:, :], AF.Sqrt), reads=[pqr], writes=[t1r])
                yield
                S.op("dve", lambda e, t1=t1: e.tensor_scalar_max(t1[:, :], t1[:, :], 1e-12), reads=[t1r], writes=[t1r])
                yield
                S.op("dve", lambda e, t1=t1: e.reciprocal(t1[:, :], t1[:, :]), reads=[t1r], writes=[t1r])
                yield
                S.op("dve", lambda e, t1=t1, kk=kk: e.tensor_tensor(kk[:, :], kk[:, :], t1[:, :], ALU.mult),
                     reads=[t1r, kkr], writes=[kkr])
                yield
                S.op("dve", lambda e, t1=t1, a_=a_, p=p: e.tensor_scalar(t1[:, :], a_[:, :], 1.0, pvc("ka", p), ALU.subtract, ALU.mult),
                     reads=[a_r, pv_r], writes=[t1r])
                yield
                S.op("dve", lambda e, t1=t1, k_=k_: e.scalar_tensor_tensor(k_[:, :], t1[:, :], 1.0, k_[:, :], ALU.add, ALU.mult),
                     reads=[t1r, k_r], writes=[k_r])
                yield
                t2, t2r = tB.next()
                S.op("dve", lambda e, t2=t2, r_=r_, k_=k_, p=p: e.scalar_tensor_tensor(
                    t2[:, :], r_[:, :], pvc("rrk", p), k_[:, :], ALU.mult, ALU.mult), reads=[r_r, k_r, pv_r], writes=[t2r])
                yield
                pb, pbr = psrR.next()
                S.op("pe", lambda e, pb=pb, t2=t2: e.matmul(pb[:, :], bob, t2[:, :], start=True, stop=True),
                     reads=[cstb_r, t2r], writes=[pbr])
                yield
                bon, bonr = rtmp.next()
                S.op("dve", lambda e, bon=bon, pb=pb, v_=v_: e.tensor_tensor(bon[:, :], pb[:, :], v_[:, :], ALU.mult),
                     reads=[pbr, v_r], writes=[bonr])
                yield
                cl, clr = rtmp.next()
                S.op("dve", lambda e, cl=cl, lw=lw: e.tensor_tensor_scan(cl[:, :], rst, lw[:, :], 0.0, ALU.mult, ALU.add),
                     reads=[lwr, cstb_r], writes=[clr])
                yield
                e1, e1r = tA.next()
                S.op("act", lambda e, e1=e1, cl=cl: e.activation(e1[:, :], cl[:, :], AF.Exp), reads=[clr], writes=[e1r])
                yield
                S.op("act", lambda e, e1=e1: e.copy(gam[:, :], v3(e1[:, :])[:, :, 63]), reads=[e1r], writes=[gam_r])
                yield
                for hf in range(2):
                    hs = slice(hf * 64, hf * 64 + 64)
                    S.op("dve", lambda e, hs=hs, hf=hf, r_=r_, e1=e1: e.tensor_tensor(
                        ARbd[hs, :, 128 + hf * 64:128 + hf * 64 + 64], v3(r_[hs, :]), v3(e1[hs, :]), ALU.mult),
                        reads=[r_r, e1r], writes=[bd_r])
                    yield
                e2, e2r = tA.next()
                S.op("act", lambda e, e2=e2, cl=cl: e.activation(e2[:, :], cl[:, :], AF.Exp, scale=-1.0), reads=[clr], writes=[e2r])
                yield
                S.op("dve", lambda e, t1=t1, kk=kk, a_=a_: e.tensor_tensor(t1[:, :], kk[:, :], a_[:, :], ALU.mult),
                     reads=[kkr, a_r], writes=[t1r])
                yield
                for hf in range(2):
                    hs = slice(hf * 64, hf * 64 + 64)
                    S.op("dve", lambda e, hs=hs, hf=hf, t1=t1, e2=e2: e.tensor_tensor(
                        Bbd[hs, :, hf * 64:hf * 64 + 64], v3(t1[hs, :]), v3(e2[hs, :]), ALU.mult), reads=[t1r, e2r], writes=[bd_r])
                    yield
                    S.op("dve", lambda e, hs=hs, hf=hf, k_=k_, e2=e2: e.tensor_tensor(
                        Kbd[hs, :, hf * 64:hf * 64 + 64], v3(k_[hs, :]), v3(e2[hs, :]), ALU.mult), reads=[k_r, e2r], writes=[bd_r])
                    yield
                    S.op("act", lambda e, hs=hs, hf=hf, v_=v_: e.copy(Vbd[hs, :, hf * 64:hf * 64 + 64], v3(v_[hs, :])),
                         reads=[v_r], writes=[bd_r])
                    yield
                S.op("dve", lambda e, cl=cl, lw=lw: e.tensor_tensor(cl[:, :], cl[:, :], lw[:, :], ALU.subtract),
                     reads=[clr, lwr], writes=[clr])
                yield
                e3, e3r = tA.next()
                S.op("act", lambda e, e3=e3, cl=cl: e.activation(e3[:, :], cl[:, :], AF.Exp), reads=[clr], writes=[e3r])
                yield
                for hf in range(2):
                    hs = slice(hf * 64, hf * 64 + 64)
                    S.op("dve", lambda e, hs=hs, hf=hf, kk=kk, e3=e3: e.scalar_tensor_tensor(
                        ARbd[hs, :, hf * 64:hf * 64 + 64], v3(kk[hs, :]), -1.0, v3(e3[hs, :]), ALU.mult, ALU.mult),
                        reads=[kkr, e3r], writes=[bd_r])
                    yield
                if krw <= 1:
                    S.op("dve", lambda e, p=p: e.memset(yrTb[:, p, :], 0.0), writes=[yr_r[p]])
                    yield
                    continue
                pv_, pvr_ = psrR.next()
                for n in range(NCH):
                    S.op("pe", lambda e, n=n, pv_=pv_: e.matmul(pv_[:, n * 64:(n + 1) * 64], Vbd[:, n, :], istb, start=True, stop=True),
                         reads=[bd_r, cstb_r], writes=[pvr_])
                    yield
                S.op("act", lambda e, pv_=pv_: e.copy(vst[:, :, :], v3(pv_[:, :])), reads=[pvr_], writes=[vst_r])
                yield
                for n0 in range(0, NCH, 2):
                    pt, ptr = psrR.next()
                    for n in (n0, n0 + 1):
                        o = (n - n0) * 256
                        S.op("pe", lambda e, n=n, pt=pt, o=o: e.matmul(pt[:, o:o + 128], Bbd[:, n, :], identb, start=True, stop=True),
                             reads=[bd_r, cstb_r], writes=[ptr])
                        yield
                        S.op("pe", lambda e, n=n, pt=pt, o=o: e.matmul(pt[:, o + 128:o + 256], Kbd[:, n, :], identb, start=True, stop=True),
                             reads=[bd_r, cstb_r], writes=[ptr])
                        yield
                    S.op("act", lambda e, n0=n0, pt=pt: e.copy(btk[:, n0:n0 + 2, :], pt[:, :].rearrange("p (n l) -> p n l", l=256)),
                         reads=[ptr], writes=[btk_r])
                    yield
                if krw <= 2:
                    S.op("dve", lambda e, p=p: e.memset(yrTb[:, p, :], 0.0), writes=[yr_r[p]])
                    yield
                    continue
                pyo, pyor = pyo_bank
                for g0 in range(0, NCH, 4):
                    G = list(range(g0, min(g0 + 4, NCH)))
                    pAs = {}
                    for n in G:
                        pA, pAr = psrR.next()
                        pAs[n] = (pA, pAr)
                        S.op("pe", lambda e, pA=pA, n=n: e.matmul(pA[:, 0:256], Bbd[:, n, :], ARbd[:, n, :], start=True, stop=True),
                             reads=[bd_r], writes=[pAr])
                        yield
                        S.op("pe", lambda e, pA=pA, n=n: e.matmul(pA[:, 256:512], Kbd[:, n, :], ARbd[:, n, :], start=True, stop=True),
                             reads=[bd_r], writes=[pAr])
                        yield
                    for n in G:
                        pA, pAr = pAs[n]
                        S.op("dve", lambda e, pA=pA, n=n: e.tensor_tensor(mAK[:, n, 0:256], pA[:, 0:256], maskAR[:, :], ALU.mult),
                             reads=[pAr, mask_r], writes=[mA_r[n]])
                        yield
                        S.op("dve", lambda e, pA=pA, n=n: e.tensor_tensor(mAK[:, n, 256:512], pA[:, 256:512], maskAR[:, :], ALU.mult),
                             reads=[pAr, mask_r], writes=[mK_r[n]])
                        yield
                    pTs = {}
                    for n in G:
                        pT, pTr = psrR.next()
                        pTs[n] = (pT, pTr)
                        S.op("pe", lambda e, pT=pT, n=n: e.matmul(pT[:, 0:128], ARbd[:, n, 0:128], Bbd[:, n, :], start=True, stop=True),
                             reads=[bd_r], writes=[pTr])
                        yield
                    for n in G:
                        pT, pTr = pTs[n]
                        S.op("dve", lambda e, pT=pT, n=n: e.tensor_tensor(PWT[0][n], pT[:, 0:128], mls, ALU.mult),
                             reads=[pTr, cst_r], writes=[PWT_r[0][n]])
                        yield
                        S.op("dve", lambda e, n=n: e.tensor_tensor(Xs[:, n, :], mAK[:, n, 0:128], identb, ALU.add),
                             reads=[mA_r[n], cstb_r], writes=[X_r[n]])
                        yield
                    for j in range(1, 6):
                        b0, b1 = (j - 1) % 2, j % 2
                        p2s = {}
                        for n in G:
                            cur = mAK[:, n, 0:128] if j == 1 else PW[b0][n]
                            curr = mA_r[n] if j == 1 else PW_r[b0][n]
                            ct, ctr_ = PWT[b0][n], PWT_r[b0][n]
                            p2, p2r = psrR.next()
                            p2s[n] = (p2, p2r)
                            if j < 5:
                                S.op("pe", lambda e, p2=p2, cur=cur, ct=ct: e.matmul(p2[:, 0:128], ct, cur, start=True, stop=True),
                                     reads=[curr, ctr_], writes=[p2r])
                                yield
                            S.op("pe", lambda e, p2=p2, cur=cur, ct=ct: e.matmul(p2[:, 128:256], cur, ct, start=True, stop=True),
                                 reads=[curr, ctr_], writes=[p2r])
                            yield
                        for n in G:
                            p2, p2r = p2s[n]
                            S.op("dve", lambda e, p2=p2, n=n, b1=b1: e.tensor_copy(PWT[b1][n], p2[:, 128:256]),
                                 reads=[p2r], writes=[PWT_r[b1][n]])
                            yield
                            if j < 5:
                                S.op("act", lambda e, p2=p2, n=n, b1=b1: e.copy(PW[b1][n], p2[:, 0:128]),
                                     reads=[p2r], writes=[PW_r[b1][n]])
                                yield
                        pxs = {}
                        for n in G:
                            px, pxr = psrR.next()
                            pxs[n] = (px, pxr)
                            S.op("pe", lambda e, px=px, n=n, b1=b1: e.matmul(px[:, 0:128], PWT[b1][n], Xs[:, n, :], start=True, stop=True),
                                 reads=[PWT_r[b1][n], X_r[n]], writes=[pxr])
                            yield
                        for n in G:
                            px, pxr = pxs[n]
                            S.op("dve", lambda e, px=px, n=n: e.tensor_tensor(Xs[:, n, :], px[:, 0:128], Xs[:, n, :], ALU.add),
                                 reads=[pxr, X_r[n]], writes=[X_r[n]])
                            yield
                for n in range(NCH):
                    pw_, pw_r = psrR.next()
                    S.op("pe", lambda e, pw_=pw_, n=n, p=p: e.matmul(pw_[:, 0:64], ARbd[:, n, 0:128], Hb[:, p, :], start=True, stop=False),
                         reads=[bd_r, Hb_r[p]], writes=[pw_r])
                    yield
                    S.op("pe", lambda e, pw_=pw_, n=n: e.matmul(pw_[:, 0:64], mAK[:, n, 256:384], vst[:, n, :], start=False, stop=True),
                         reads=[mK_r[n], vst_r], writes=[pw_r])
                    yield
                    w_b, wbr = ub.next()
                    S.op("act", lambda e, w_b=w_b, pw_=pw_: e.copy(w_b[:, :], pw_[:, 0:64]), reads=[pw_r], writes=[wbr])
                    yield
                    pu, pur_ = psrR.next()
                    S.op("pe", lambda e, pu=pu, n=n, w_b=w_b: e.matmul(pu[:, 0:64], Xs[:, n, :], w_b[:, :], start=True, stop=True),
                         reads=[X_r[n], wbr], writes=[pur_])
                    yield
                    u_b, ubr = ub.next()
                    S.op("dve", lambda e, u_b=u_b, pu=pu: e.tensor_copy(u_b[:, :], pu[:, 0:64]), reads=[pur_], writes=[ubr])
                    yield
                    ph, phr = psrR.next()
                    S.op("pe", lambda e, ph=ph, n=n, u_b=u_b: e.matmul(ph[:, 0:64], btk[:, n, 0:128], u_b[:, :], start=True, stop=False),
                         reads=[btk_r, ubr], writes=[phr])
                    yield
                    S.op("pe", lambda e, ph=ph, n=n: e.matmul(ph[:, 0:64], btk[:, n, 128:256], vst[:, n, :], start=False, stop=True),
                         reads=[btk_r, vst_r], writes=[phr])
                    yield
                    py, pyr = psrR.next()
                    S.op("pe", lambda e, py=py, n=n, p=p: e.matmul(py[:, 0:64], ARbd[:, n, 128:256], Hb[:, p, :], start=True, stop=False),
                         reads=[bd_r, Hb_r[p]], writes=[pyr])
                    yield
                    S.op("pe", lambda e, py=py, n=n, u_b=u_b: e.matmul(py[:, 0:64], mAK[:, n, 128:256], u_b[:, :], start=False, stop=False),
                         reads=[mA_r[n], ubr], writes=[pyr])
                    yield
                    S.op("pe", lambda e, py=py, n=n: e.matmul(py[:, 0:64], mAK[:, n, 384:512], vst[:, n, :], start=False, stop=True),
                         reads=[mK_r[n], vst_r], writes=[pyr])
                    yield
                    ht, htr = htmp.next()
                    S.op("dve", lambda e, ht=ht, ph=ph, p=p: e.tensor_tensor(ht[:, :], ph[:, 0:64], Hst[:, p, :], ALU.add),
                         reads=[phr, H_r[p]], writes=[htr])
                    yield
                    S.op("act", lambda e, ht=ht, p=p, n=n: e.activation(Hb[:, p, :], ht[:, :], AF.Identity, scale=gam[:, n:n + 1]),
                         reads=[htr, gam_r], writes=[Hb_r[p]])
                    yield
                    S.op("act", lambda e, ht=ht, p=p, n=n: e.activation(Hst[:, p, :], ht[:, :], AF.Identity, scale=gam[:, n:n + 1]),
                         reads=[htr, gam_r], writes=[H_r[p]])
                    yield
                    g6, g6r = gn6.next()
                    S.op("dve", lambda e, g6=g6, py=py: e.bn_stats(g6[:, 2:8], py[:, 0:64]), reads=[pyr], writes=[g6r])
                    yield
                    S.op("dve", lambda e, g6=g6: e.bn_aggr(g6[:, 0:2], g6[:, 2:8]), reads=[g6r], writes=[g6r])
                    yield
                    S.op("act", lambda e, g6=g6: e.activation(g6[:, 2:3], g6[:, 1:2], AF.Sqrt, bias=epsgn[:, 0:1]),
                         reads=[g6r, eps_r], writes=[g6r])
                    yield
                    S.op("dve", lambda e, g6=g6: e.reciprocal(g6[:, 3:4], g6[:, 2:3]), reads=[g6r], writes=[g6r])
                    yield
                    for hf in range(2):
                        hs = slice(hf * 64, hf * 64 + 64)
                        S.op("dve", lambda e, hs=hs, hf=hf, py=py, g6=g6: e.tensor_scalar(
                            Ynbd[hs, hf * 64:hf * 64 + 64], py[hs, 0:64], g6[hs, 0:1], g6[hs, 3:4], ALU.subtract, ALU.mult),
                            reads=[pyr, g6r], writes=[ynbd_r])
                        yield
                    S.op("pe", lambda e, n=n, pyo=pyo: e.matmul(pyo[:, n * 64:(n + 1) * 64], Ynbd[:, :], istb, start=True, stop=True),
                         reads=[ynbd_r, cstb_r], writes=[pyor])
                    yield
                if krw <= 5:
                    S.op("dve", lambda e, p=p: e.memset(yrTb[:, p, :], 0.0), writes=[yr_r[p]])
                    yield
                    continue
                S.op("dve", lambda e, t1=t1, pyo=pyo, p=p: e.tensor_scalar(
                    t1[:, :], pyo[:, :], pvc("gng", p), pvc("gnb", p), ALU.mult, ALU.add), reads=[pyor, pv_r], writes=[t1r])
                yield
                S.op("dve", lambda e, t1=t1, bon=bon: e.tensor_tensor(t1[:, :], t1[:, :], bon[:, :], ALU.add),
                     reads=[t1r, bonr], writes=[t1r])
                yield
                S.op("dve", lambda e, t1=t1, g_=g_, p=p: e.tensor_tensor(yrTb[:, p, :], t1[:, :], g_[:, :], ALU.mult),
                     reads=[t1r, g_r], writes=[yr_r[p]])
                yield

        def merge_out(blk):
            for c in range(8):
                ps, pr = projT(wgate, c * 128, x1Tb, x1_r)
                S.op("act", lambda e, ps=ps, c=c: e.activation(sga[:, c, :], ps[:, :], AF.Sigmoid), reads=[pr], writes=[sga_r[c]])
                ps, pr = projT(wgate, 1024 + c * 128, x1Tb, x1_r)
                S.op("act", lambda e, ps=ps, c=c: e.activation(sgb[:, c, :], ps[:, :], AF.Sigmoid), reads=[pr], writes=[sgb_r[c]])
            for c in range(8):
                psa, par = projT(wa_d, c * 128, lambda kc: HYf[:, kc // 2, kc % 2, :], [hyf_r[kc // 2] for kc in range(8)])
                psb, pbr = projT(wb_d, c * 128, lambda kc: HYf[:, kc // 2, 2 + kc % 2, :], [hyf_r[kc // 2] for kc in range(8)])
                ta, tar = tA.next()
                S.op("dve", lambda e, ta=ta, psa=psa, c=c: e.tensor_tensor(ta[:, :], psa[:, :], sga[:, c, :], ALU.mult),
                     reads=[par, sga_r[c]], writes=[tar])
                tb, tbr = tA.next()
                S.op("dve", lambda e, tb=tb, psb=psb, c=c: e.tensor_tensor(tb[:, :], psb[:, :], sgb[:, c, :], ALU.mult),
                     reads=[pbr, sgb_r[c]], writes=[tbr])
                S.op("dve", lambda e, ta=ta, tb=tb, c=c: e.tensor_tensor(mgTb[:, c, :], ta[:, :], tb[:, :], ALU.add),
                     reads=[tar, tbr], writes=[mg_r[c]])
            for c in range(8):
                ps, pr = projT(wo_d, c * 128, mgTb, mg_r)
                S.op("dve", lambda e, ps=ps, c=c: e.scalar_tensor_tensor(
                    zT[:, c, :], x1T[:, c, :], ALPHA, ps[:, :], ALU.mult, ALU.add), reads=[pr, x1_r[c]], writes=[zT_r[c]])
            layer_norm("ln2_g", "ln2_b", epsln, x2T, x2Tb, x2_r)

        x1f_r = Res()
        x1bl_r = [[Res(), Res()] for _ in range(nbl)]; x1ba_r = [[Res(), Res()] for _ in range(nbl)]
        hyl_r = [Res() for _ in range(NBT)]; hya_r = [Res() for _ in range(NBT)]
        oh = sb([128, 4]); oh_r = Res()
        S.dma("sp", oh[:, :], oh_d[:, :], writes=[oh_r])

        def half(t, h):
            return t[:, 4 * h:4 * h + 4, :].rearrange("p a b -> p (a b)")

        for blk in range(nbl):
            load_xT(blk)
            ffn_ln(xT, xTb, xT_r, w1g, w1u, w1d, "ln1_g", "ln1_b", x1T, x1Tb, x1_r)
            rows = slice(blk * 128, (blk + 1) * 128)
            S.dma("sp", x1f_loc[rows, :], x1T[:, :, :].rearrange("p a b -> p (a b)"), reads=x1_r, writes=[x1f_r])
            for h in range(2):
                S.dma("sp", x1b_loc[blk][h][:, :], half(x1Tb, h), reads=x1_r, writes=[x1bl_r[blk][h]])
                S.cc("AllGather", x1b_loc[blk][h][:, :], x1b_all[blk][h][:, :], groups,
                     reads=[x1bl_r[blk][h]], writes=[x1ba_r[blk][h]])
        for gb in range(NBT):
            i, blk = gb // nbl, gb % nbl
            for h in range(2):
                S.dma("sp", half(x1Tb, h), x1b_all[blk][h][i * 128:(i + 1) * 128, :], reads=[x1ba_r[blk][h]],
                      writes=x1_r[4 * h:4 * h + 4])
            if gb == 0:
                S.op("dve", lambda e: e.memset(x1T[:, 0, 0:8], 0.0), writes=x1_r + scr_r)
            gens = [mlstm(gb), rwkv(gb)]
            while gens:
                for gg in list(gens):
                    try:
                        next(gg)
                    except StopIteration:
                        gens.remove(gg)
            S.dma("sp", hy_loc[gb][:, :], hyT[:, :, :].rearrange("p a b -> p (a b)"), reads=hm_r + yr_r, writes=[hyl_r[gb]])
            S.cc("AllGather", hy_loc[gb][:, :], hy_all[gb][:, :], groups, reads=[hyl_r[gb]], writes=[hya_r[gb]])
        for blk in range(nbl):
            rows = slice(blk * 128, (blk + 1) * 128)
            S.dma("sp", x1T[:, :, :].rearrange("p a b -> p (a b)"), x1f_loc[rows, :], reads=[x1f_r], writes=x1_r + scr_r)
            for h in range(2):
                S.dma("sp", half(x1Tb, h), x1b_loc[blk][h][:, :], reads=[x1bl_r[blk][h]], writes=x1_r[4 * h:4 * h + 4])
            for i in range(4):
                dst = HYf[:, i, :, :].rearrange("p a b -> p (a b)")
                for j in range(4):
                    k = j * nbl + blk
                    stg = sgb[:, 4 * (j % 2):4 * (j % 2) + 4, :].rearrange("p a b -> p (a b)")
                    stg_r = sgb_r[4 * (j % 2):4 * (j % 2) + 4]
                    S.dma("sp", stg, hy_all[k][i * 128:(i + 1) * 128, :], reads=[hya_r[k]], writes=stg_r)
                    if j == 0:
                        S.op("dve", lambda e, dst=dst, stg=stg, j=j: e.tensor_scalar_mul(dst, stg, oh[:, j:j + 1]),
                             reads=stg_r + [oh_r], writes=[hyf_r[i]])
                    else:
                        S.op("dve", lambda e, dst=dst, stg=stg, j=j: e.scalar_tensor_tensor(
                            dst, stg, oh[:, j:j + 1], dst, ALU.mult, ALU.add),
                            reads=stg_r + [oh_r, hyf_r[i]], writes=[hyf_r[i]])
            merge_out(blk)
            ffn_ln(x2T, x2Tb, x2_r, w2g, w2u, w2d, "ln3_g", "ln3_b", x3T, x3Tb, x3_r)
            store_T(blk, x3T, x3_r)

        S.op("sp", lambda e: e.nop(), reads=[out_r])
        S.emit()
    return nc


def _consts():
    c = np.zeros((128, NCST), np.float32)
    i = np.arange(128)
    c[:, C_ID:C_ID + 128] = np.eye(128)
    c[:, C_OD:C_OD + 128] = 1.0 / 1024.0
    c[:, C_TRI:C_TRI + 128] = (i[:, None] <= i[None, :])
    c[:, C_ONE:C_ONE + 128] = 1.0
    c[:, C_MUS:C_MUS + 128] = (i[:, None] < i[None, :])
    c[:, C_MLS:C_MLS + 128] = (i[:, None] > i[None, :])
    c[:, C_IST:C_IST + 64] = np.concatenate([np.eye(64), np.eye(64)], axis=0)
    c[:, C_BO:C_BO + 128] = ((i[:, None] // 64) == (i[None, :] // 64))
    return c


def _tile(Wm, cw=128):
    F = Wm.shape[1]
    return np.ascontiguousarray(
        np.asarray(Wm, np.float32).reshape(8, 128, F // cw, cw).transpose(2, 1, 0, 3).reshape(F // cw * 128, 8 * cw))


def _fm(v):
    v = np.asarray(v, np.float32).reshape(-1, 128)
    return np.ascontiguousarray(v.T)


def kernel(**inp):
    nbl = int(os.environ.get("KNBL", "4"))
    ngrp = int(os.environ.get("KNGRP", "2"))
    x = np.asarray(inp["x"], np.float32)
    W = np.asarray(inp["w_in"][0], np.float32)
    gates_t = _tile(W[:, 7432:9480])
    tiled_gu = {n: _tile(np.asarray(inp[n][0], np.float32), 256) for n in ("ffn1_w_gate", "ffn1_w_up", "ffn2_w_gate", "ffn2_w_up")}
    tiled = {n: _tile(np.asarray(inp[n][0], np.float32)) for n in ("w_branch_a", "w_branch_b", "w_out")}
    cw = inp["m_conv_w"][0]; cbv = inp["m_conv_b"][0]
    mu = inp["r_mu"][0]
    in_maps = []
    for c in range(4 * ngrp):
        b, r = c // 4, c % 4
        pv = np.zeros((128, NPV), np.float32)

        def put(name, arr):
            a = _fm(arr)
            pv[:, PV[name]:PV[name] + a.shape[1]] = a

        for n in ("ln1_g", "ln1_b", "ln2_g", "ln2_b", "ln3_g", "ln3_b"):
            put(n, inp[n][0])
        qs = slice(r * 256, (r + 1) * 256)
        ks = slice(1024 + r * 256, 1024 + (r + 1) * 256)
        for j in range(4):
            put("cw%d" % j, np.concatenate([cw[j][qs], cw[j][ks]]))
        put("cb", np.concatenate([cbv[qs], cbv[ks]]))
        put("mng", inp["m_norm_g"][0][qs])
        ps_ = slice(r * 256, (r + 1) * 256)
        mul = np.concatenate([mu[0:1024][ps_], mu[1024:2048][ps_], mu[2048:3072][ps_], mu[3072:3328]])
        put("mu", mul)
        put("omu", 1.0 - mul)
        put("w0", inp["r_w0"][0][ps_]); put("a0", inp["r_a0"][0][ps_]); put("kk", inp["r_k_k"][0][ps_])
        put("ka", inp["r_k_a"][0][ps_]); put("rrk", inp["r_r_k"][0].reshape(-1)[ps_])
        put("gng", inp["r_gn_g"][0][ps_]); put("gnb", inp["r_gn_b"][0][ps_])
        wl = np.concatenate([W[:, qs], W[:, ks], W[:, 2048 + r * 256:2048 + (r + 1) * 256],
                             W[:, 3072 + r * 256:3072 + (r + 1) * 256],
                             W[:, RC0 + r * 256:RC0 + (r + 1) * 256],
                             W[:, RC0 + 1024 + r * 256:RC0 + 1024 + (r + 1) * 256],
                             W[:, RC0 + 2048 + r * 256:RC0 + 2048 + (r + 1) * 256],
                             W[:, RC0 + 3072:RC0 + 3328]], axis=1)
        assert wl.shape[1] == 2048
        wif = np.zeros((D, 8), np.float32)
        wif[:, 0] = W[:, 4096 + r]
        wif[:, 4] = W[:, 4100 + r]
        gbias = np.zeros((128, 8), np.float32)
        gbias[:, 0] = inp["m_i_bias"][0][r]
        gbias[:, 4] = inp["m_f_bias"][0][r]
        gbias[:, 5:8] = 30.0
        m = {"cst": _consts(), "pv": pv, "gbias": gbias,
             "rst": np.ascontiguousarray(np.broadcast_to((np.arange(512)[None, :] % 64 != 0), (128, 512)).astype(np.float32)),
             "w_loc": np.ascontiguousarray(wl), "w_loc_t": _tile(wl), "w_if": wif, "w_gate": gates_t,
             "r_w2": np.ascontiguousarray(inp["r_w2"][0][:, ps_]), "r_a2": np.ascontiguousarray(inp["r_a2"][0][:, ps_]),
             "r_g2": np.ascontiguousarray(inp["r_g2"][0][:, ps_]),
             "x": np.ascontiguousarray(x[b, r * nbl * TB:(r + 1) * nbl * TB]),
             "oh": np.ascontiguousarray(np.broadcast_to(np.eye(4, dtype=np.float32)[r][None, :], (128, 4)))}
        for n in ("ffn1_w_down", "ffn2_w_down"):
            m[n] = np.ascontiguousarray(inp[n][0], dtype=np.float32)
        m.update(tiled_gu)
        m.update(tiled)
        in_maps.append(m)
    groups = [list(range(4 * g, 4 * g + 4)) for g in range(ngrp)]
    nc = build(nbl, groups)
    res = run_bass_kernel_spmd(nc, in_maps, core_ids=list(range(4 * ngrp)))
    out = np.zeros((ngrp, 4 * nbl * TB, D), np.float32)
    for c in range(4 * ngrp):
        out[c // 4, (c % 4) * nbl * TB:(c % 4 + 1) * nbl * TB] = np.asarray(res.results[c]["out"])
    return out
```
